# Optimizing a Trainium2 kernel written in Bass

```python
import math
import jax, jax.numpy as jnp
from jax import lax
import numpy as np


D_MODEL = 1024
BATCH = 1
SEQ = 16384
DEPTH = 2

N_META = 16
SB_HEAD_DIM = 64
SB_WIDTH = D_MODEL // 2
SB_HEADS = SB_WIDTH // SB_HEAD_DIM
SB_BLOCK = 128
S5_WIDTH = D_MODEL - SB_WIDTH
S5_GROUP = 16
S5_GROUPS = S5_WIDTH // S5_GROUP
S5_STATE = 64
DN_HEAD_DIM = 128
DN_WIDTH = D_MODEL
DN_HEADS = DN_WIDTH // DN_HEAD_DIM
DN_CONV = 4
DN_CHUNK = 64
D_FF = 4 * D_MODEL
N_EVEN = (DEPTH + 1) // 2
N_ODD = DEPTH // 2
EPS = 1e-6

kernel_name = 'hybrid_stickbreak_s5_gated_deltanet'


def rms_norm(x, g):
    xf = x.astype(jnp.float32)
    y = xf * lax.rsqrt(jnp.mean(xf * xf, axis=-1, keepdims=True) + EPS)
    return (y * g.astype(jnp.float32)).astype(x.dtype)


def l2_norm(x):
    return x * lax.rsqrt(jnp.sum(x * x, axis=-1, keepdims=True) + EPS)


def stick_breaking_attention(q, k, v):
    b, l, h, d = q.shape
    pad = (-N_META) % SB_BLOCK
    lp = l + pad
    nb = lp // SB_BLOCK
    to_heads = lambda t: jnp.pad(t.transpose(0, 2, 1, 3), ((0, 0), (0, 0), (pad, 0), (0, 0)))
    qh, kh, vh = to_heads(q), to_heads(k), to_heads(v)
    q_blocks = qh.reshape(b, h, nb, SB_BLOCK, d).transpose(2, 0, 1, 3, 4)
    key_pos = jnp.arange(lp)
    scale = d ** -0.5

    def block(args):
        qb, bi = args
        q_pos = bi * SB_BLOCK + jnp.arange(SB_BLOCK)
        valid = (key_pos[None, :] < q_pos[:, None]) & (key_pos[None, :] >= pad)
        z = jnp.einsum('bhqd,bhkd->bhqk', qb, kh) * scale
        log_keep = jnp.where(valid, jax.nn.log_sigmoid(-z), 0.0)
        log_surv = lax.cumsum(log_keep, axis=3, reverse=True) - log_keep
        w = jnp.exp(jnp.where(valid, jax.nn.log_sigmoid(z) + log_surv, -jnp.inf))
        return jnp.einsum('bhqk,bhkd->bhqd', w, vh)

    out = lax.map(block, (q_blocks, jnp.arange(nb)))
    out = out.transpose(1, 0, 3, 2, 4).reshape(b, lp, h, d)
    return out[:, pad:]


def _complex_affine_combine(e1, e2):
    a1r, a1i, b1r, b1i = e1
    a2r, a2i, b2r, b2i = e2
    ar = a1r * a2r - a1i * a2i
    ai = a1r * a2i + a1i * a2r
    br = a2r * b1r - a2i * b1i + b2r
    bi = a2r * b1i + a2i * b1r + b2i
    return (ar, ai, br, bi)


def s5_glu(u, lam_re, lam_im, log_dt, b_re, b_im, c_re, c_im, d_skip, w_glu, b_glu):
    f32 = jnp.float32
    b, l, _ = u.shape
    lr = jnp.minimum(lam_re.astype(f32), -1e-4)
    li = lam_im.astype(f32)
    dt = jnp.exp(log_dt.astype(f32))[:, None]
    mag = jnp.exp(lr * dt)
    ang = li * dt
    abar_re, abar_im = mag * jnp.cos(ang), mag * jnp.sin(ang)
    den = lr * lr + li * li
    nr, ni = abar_re - 1.0, abar_im
    coef_re = (nr * lr + ni * li) / den
    coef_im = (ni * lr - nr * li) / den
    br, bim = b_re.astype(f32), b_im.astype(f32)
    bbar_re = coef_re[..., None] * br - coef_im[..., None] * bim
    bbar_im = coef_re[..., None] * bim + coef_im[..., None] * br
    ug = u.reshape(b, l, S5_GROUPS, S5_GROUP)
    bu_re = jnp.einsum('blgp,gnp->blgn', ug, bbar_re)
    bu_im = jnp.einsum('blgp,gnp->blgn', ug, bbar_im)
    a_re = jnp.broadcast_to(abar_re, bu_re.shape)
    a_im = jnp.broadcast_to(abar_im, bu_im.shape)
    _, _, x_re, x_im = lax.associative_scan(_complex_affine_combine, (a_re, a_im, bu_re, bu_im), axis=1)
    y = (jnp.einsum('blgn,gpn->blgp', x_re, c_re.astype(f32))
         - jnp.einsum('blgn,gpn->blgp', x_im, c_im.astype(f32)))
    y = y.reshape(b, l, S5_WIDTH) + d_skip.astype(f32) * u
    hact = jax.nn.gelu(y)
    return hact * jax.nn.sigmoid(hact @ w_glu.astype(f32) + b_glu.astype(f32))


def sb_s5_mixer(h, w_in, w_out, sb_norm_g, lam_re, lam_im, log_dt, b_re, b_im, c_re, c_im,
                d_skip, w_glu, b_glu, s5_norm_g):
    b, l, _ = h.shape
    proj = h @ w_in
    qkv = proj[..., :3 * SB_WIDTH].astype(jnp.float32).reshape(b, l, 3, SB_HEADS, SB_HEAD_DIM)
    o_sb = stick_breaking_attention(qkv[:, :, 0], qkv[:, :, 1], qkv[:, :, 2]).reshape(b, l, SB_WIDTH)
    o_s5 = s5_glu(proj[..., 3 * SB_WIDTH:].astype(jnp.float32), lam_re, lam_im, log_dt,
                  b_re, b_im, c_re, c_im, d_skip, w_glu, b_glu)
    merged = jnp.concatenate([rms_norm(o_sb, sb_norm_g), rms_norm(o_s5, s5_norm_g)], axis=-1)
    return merged.astype(h.dtype) @ w_out


def causal_conv(x, w):
    kw = w.shape[0]
    return lax.conv_general_dilated(x, w[:, None, :].astype(x.dtype), window_strides=(1,),
                                    padding=[(kw - 1, 0)], dimension_numbers=('NWC', 'WIO', 'NWC'),
                                    feature_group_count=x.shape[-1])


def gated_delta_rule(q, k, v, g, beta):
    f32 = jnp.float32
    b, l, h, dk = q.shape
    dv = v.shape[-1]
    cs = DN_CHUNK
    pad = (-N_META) % cs
    lp = l + pad
    nc = lp // cs

    def chunks(t):
        t = jnp.moveaxis(t, 2, 1)
        t = jnp.pad(t, [(0, 0), (0, 0), (pad, 0)] + [(0, 0)] * (t.ndim - 3))
        return t.reshape(t.shape[:2] + (nc, cs) + t.shape[3:])

    q, k, v, g, beta = chunks(q * dk ** -0.5), chunks(k), chunks(v), chunks(g), chunks(beta)
    gcum = jnp.cumsum(g, axis=-1)
    idx = jnp.arange(cs)
    incl = idx[:, None] >= idx[None, :]
    strict = idx[:, None] > idx[None, :]
    decay = jnp.exp(jnp.where(incl, gcum[..., :, None] - gcum[..., None, :], -jnp.inf))
    kk = jnp.einsum('bhncd,bhnsd->bhncs', k, k)
    lower = jnp.where(strict, beta[..., :, None] * kk * decay, 0.0)
    rhs = jnp.concatenate([v * beta[..., None], k * (beta * jnp.exp(gcum))[..., None]], axis=-1)
    sol = lax.linalg.triangular_solve(jnp.eye(cs, dtype=f32) + lower, rhs, left_side=True, lower=True)
    value, k_cumdecay = sol[..., :dv], sol[..., dv:]
    attn_intra = jnp.einsum('bhncd,bhnsd->bhncs', q, k) * decay
    q_g = q * jnp.exp(gcum)[..., None]
    g_last = gcum[..., -1]
    k_tail = k * jnp.exp(g_last[..., None] - gcum)[..., None]

    def step(state, inp):
        qg_c, kcd_c, val_c, intra_c, kt_c, gl_c = inp
        v_new = val_c - jnp.einsum('bhcd,bhde->bhce', kcd_c, state)
        o_c = jnp.einsum('bhcd,bhde->bhce', qg_c, state) + jnp.einsum('bhcs,bhse->bhce', intra_c, v_new)
        state = state * jnp.exp(gl_c)[..., None, None] + jnp.einsum('bhcd,bhce->bhde', kt_c, v_new)
        return state, o_c

    xs = tuple(jnp.moveaxis(t, 2, 0) for t in (q_g, k_cumdecay, value, attn_intra, k_tail, g_last))
    s0 = jnp.zeros((b, h, dk, dv), f32)
    _, out = lax.scan(step, s0, xs)
    out = out.transpose(1, 0, 3, 2, 4).reshape(b, lp, h, dv)
    return out[:, pad:]


def gated_deltanet_mixer(h, w_in, conv_w, a_log, dt_bias, norm_g, w_out):
    f32 = jnp.float32
    b, l, _ = h.shape
    wd, nh, hd = DN_WIDTH, DN_HEADS, DN_HEAD_DIM
    proj = h @ w_in
    qkv = jax.nn.silu(causal_conv(proj[..., :3 * wd], conv_w)).astype(f32)
    z = proj[..., 3 * wd:4 * wd].astype(f32).reshape(b, l, nh, hd)
    a = proj[..., 4 * wd:4 * wd + nh].astype(f32)
    bb = proj[..., 4 * wd + nh:].astype(f32)
    q, k, v = jnp.split(qkv, 3, axis=-1)
    q = l2_norm(q.reshape(b, l, nh, hd))
    k = l2_norm(k.reshape(b, l, nh, hd))
    v = v.reshape(b, l, nh, hd)
    beta = jax.nn.sigmoid(bb)
    g = -jnp.exp(a_log.astype(f32)) * jax.nn.softplus(a + dt_bias.astype(f32))
    o = gated_delta_rule(q, k, v, g, beta)
    o = rms_norm(o, norm_g) * jax.nn.silu(z)
    return o.reshape(b, l, wd).astype(h.dtype) @ w_out


def sq_relu_mlp(h, w1, w2):
    return jnp.square(jax.nn.relu(h @ w1)) @ w2


def setup_inputs(seed: int = 0) -> dict:
    key = jax.random.key(seed)
    ks = iter(jax.random.split(key, 40))
    f32 = jnp.float32
    nrm = lambda shape, s: jax.random.normal(next(ks), shape, f32) * s
    gain = lambda shape: 1.0 + nrm(shape, 0.02)
    log_uniform = lambda shape, lo, hi: jax.random.uniform(next(ks), shape, f32, math.log(lo), math.log(hi))
    dn_dt = jnp.exp(log_uniform((N_ODD, DN_HEADS), 1e-3, 1e-1))
    return {
        'x': nrm((BATCH, SEQ, D_MODEL), 1.0),
        'meta_tokens': nrm((N_META, D_MODEL), 1.0),
        'pre_mix_norm': gain((DEPTH, D_MODEL)),
        'post_mix_norm': gain((DEPTH, D_MODEL)),
        'pre_mlp_norm': gain((DEPTH, D_MODEL)),
        'post_mlp_norm': gain((DEPTH, D_MODEL)),
        'mlp_w1': nrm((DEPTH, D_MODEL, D_FF), D_MODEL ** -0.5),
        'mlp_w2': nrm((DEPTH, D_FF, D_MODEL), D_FF ** -0.5),
        'w_in_even': nrm((N_EVEN, D_MODEL, 3 * SB_WIDTH + S5_WIDTH), D_MODEL ** -0.5),
        'w_out_even': nrm((N_EVEN, SB_WIDTH + S5_WIDTH, D_MODEL), (SB_WIDTH + S5_WIDTH) ** -0.5),
        'sb_out_norm': gain((N_EVEN, SB_WIDTH)),
        's5_lambda_re': -0.5 + nrm((N_EVEN, S5_GROUPS, S5_STATE), 0.01),
        's5_lambda_im': jnp.pi * jnp.arange(S5_STATE, dtype=f32)[None, None, :] + nrm((N_EVEN, S5_GROUPS, S5_STATE), 0.01),
        's5_log_dt': log_uniform((N_EVEN, S5_GROUPS), 1e-3, 1e-1),
        's5_b_re': nrm((N_EVEN, S5_GROUPS, S5_STATE, S5_GROUP), (2 * S5_GROUP) ** -0.5),
        's5_b_im': nrm((N_EVEN, S5_GROUPS, S5_STATE, S5_GROUP), (2 * S5_GROUP) ** -0.5),
        's5_c_re': nrm((N_EVEN, S5_GROUPS, S5_GROUP, S5_STATE), S5_STATE ** -0.5),
        's5_c_im': nrm((N_EVEN, S5_GROUPS, S5_GROUP, S5_STATE), S5_STATE ** -0.5),
        's5_d': nrm((N_EVEN, S5_WIDTH), 0.5),
        's5_w_glu': nrm((N_EVEN, S5_WIDTH, S5_WIDTH), S5_WIDTH ** -0.5),
        's5_b_glu': nrm((N_EVEN, S5_WIDTH), 0.01),
        's5_out_norm': gain((N_EVEN, S5_WIDTH)),
        'w_in_odd': nrm((N_ODD, D_MODEL, 4 * DN_WIDTH + 2 * DN_HEADS), D_MODEL ** -0.5),
        'dn_conv_w': nrm((N_ODD, DN_CONV, 3 * DN_WIDTH), DN_CONV ** -0.5),
        'dn_a_log': jnp.log(jax.random.uniform(next(ks), (N_ODD, DN_HEADS), f32, 1.0, 16.0)),
        'dn_dt_bias': dn_dt + jnp.log(-jnp.expm1(-dn_dt)),
        'dn_out_norm': gain((N_ODD, DN_HEAD_DIM)),
        'w_out_odd': nrm((N_ODD, DN_WIDTH, D_MODEL), DN_WIDTH ** -0.5),
    }


def reference(x, meta_tokens, pre_mix_norm, post_mix_norm, pre_mlp_norm, post_mlp_norm, mlp_w1, mlp_w2,
              w_in_even, w_out_even, sb_out_norm, s5_lambda_re, s5_lambda_im, s5_log_dt, s5_b_re, s5_b_im,
              s5_c_re, s5_c_im, s5_d, s5_w_glu, s5_b_glu, s5_out_norm,
              w_in_odd, dn_conv_w, dn_a_log, dn_dt_bias, dn_out_norm, w_out_odd):
    b = x.shape[0]
    meta = jnp.broadcast_to(meta_tokens.astype(x.dtype)[None], (b, N_META, D_MODEL))
    hs = jnp.concatenate([meta, x], axis=1)
    for i in range(DEPTH):
        j = i // 2
        hn = rms_norm(hs, pre_mix_norm[i])
        if i % 2 == 0:
            mix = sb_s5_mixer(hn, w_in_even[j], w_out_even[j], sb_out_norm[j], s5_lambda_re[j], s5_lambda_im[j],
                              s5_log_dt[j], s5_b_re[j], s5_b_im[j], s5_c_re[j], s5_c_im[j], s5_d[j],
                              s5_w_glu[j], s5_b_glu[j], s5_out_norm[j])
        else:
            mix = gated_deltanet_mixer(hn, w_in_odd[j], dn_conv_w[j], dn_a_log[j], dn_dt_bias[j],
                                       dn_out_norm[j], w_out_odd[j])
        hs = hs + rms_norm(mix, post_mix_norm[i])
        hn = rms_norm(hs, pre_mlp_norm[i])
        hs = hs + rms_norm(sq_relu_mlp(hn, mlp_w1[i], mlp_w2[i]), post_mlp_norm[i])
    return hs[:, N_META:]
```

```python
from contextlib import ExitStack
import numpy as np
import concourse.bass as bass
import concourse.mybir as mybir
from concourse.bass_utils import run_bass_kernel_spmd

F32 = mybir.dt.float32
BF16 = mybir.dt.bfloat16
ALU = mybir.AluOpType
AF = mybir.ActivationFunctionType

NCORES = 8
D = 1024
DFF = 4096
NT = 129
LP = NT * 128
PADF = 112
NMETA = 16
TPC = 17
EPS = 1e-6


class Res:
    __slots__ = ("name", "lw", "rd", "psum")

    def __init__(self, name="", psum=False):
        self.name = name
        self.lw = None
        self.rd = {}
        self.psum = psum


class Prog:
    ENG = ("pe", "act", "dve", "pool", "sp")
    NDMA = 6

    def __init__(self, nc, es):
        self.nc = nc
        self.es = es
        self.lists = {k: [] for k in self.ENG}
        self.count = {k: 0 for k in self.ENG}
        self.sem = {}
        for k in self.ENG:
            self.sem[k] = es.enter_context(nc.semaphore("s_" + k))
        self.dsem = {}
        self.dcount = {}
        for q in ("sp", "pool", "act"):
            self.dsem[q] = [es.enter_context(nc.semaphore(f"d_{q}{i}")) for i in range(self.NDMA)]
            self.dcount[q] = 0
        self.waited = {}
        self.eobj = {"pe": nc.tensor, "act": nc.scalar, "dve": nc.vector, "pool": nc.gpsimd, "sp": nc.sync}

    def sb(self, name, shape, dt):
        return self.es.enter_context(self.nc.sbuf_tensor(name, list(shape), dt))

    def ps(self, name, shape, dt):
        return self.es.enter_context(self.nc.psum_tensor(name, list(shape), dt))

    def _semobj(self, key):
        if isinstance(key, tuple):
            return self.dsem[key[0]][key[1]]
        return self.sem[key]

    def _wait(self, eng, key, val):
        if key == eng and eng == "pe":
            return
        k = (eng, key)
        if self.waited.get(k, 0) >= val:
            return
        self.waited[k] = val
        so = self._semobj(key)
        self.eobj[eng].wait_ge(so, val)

    def _deps(self, eng, r, w):
        for x in r:
            if x.lw is not None:
                self._wait(eng, *x.lw)
            if x.psum:
                for key, val in x.rd.items():
                    if key != eng:
                        self._wait(eng, key, val)
        for x in w:
            if x.lw is not None:
                self._wait(eng, *x.lw)
            for key, val in x.rd.items():
                self._wait(eng, key, val)

    def _mark(self, tok, r, w):
        key, val = tok
        for x in r:
            if x.rd.get(key, 0) < val:
                x.rd[key] = val
        for x in w:
            x.lw = tok
            x.rd = {}

    def op(self, eng, fn, r=(), w=()):
        self._deps(eng, r, w)
        self.count[eng] += 1
        seq = self.count[eng]
        so = self.sem[eng]
        fn(self.eobj[eng]).then_inc(so, 1)
        self._mark((eng, seq), r, w)

    def dma(self, q, out, in_, r=(), w=(), **kw):
        i = self.dcount[q]
        self.dcount[q] += 1
        slot = i % self.NDMA
        key = (q, slot)
        val = 16 * (i // self.NDMA + 1)
        if i >= self.NDMA:
            self._wait(q, key, val - 16)
        self._deps(q, r, w)
        so = self.dsem[q][slot]
        self.eobj[q].dma_start(out=out, in_=in_, **kw).then_inc(so, 16)
        self._mark((key, val), r, w)

    def finish(self):
        for q in ("sp", "pool", "act"):
            n = self.dcount[q]
            for slot in range(min(n, self.NDMA)):
                last_i = ((n - 1 - slot) // self.NDMA) * self.NDMA + slot
                self._wait(q, (q, slot), 16 * (last_i // self.NDMA + 1))


class Ctx:
    def __init__(self, nc, es):
        self.P = Prog(nc, es)
        self.nc = nc
        P = self.P
        self.ident = P.sb("ident", [128, 128], BF16)
        self.R_ident = Res("ident")
        P.op("pool", lambda e: e.memset(self.ident[:, :], 1.0), w=[self.R_ident])
        P.op("pool", lambda e: e.affine_select(out=self.ident[:, :], in_=self.ident[:, :], pattern=[[-1, 128]],
                                                 compare_op=ALU.is_equal, fill=0.0, base=0, channel_multiplier=1),
             r=[self.R_ident], w=[self.R_ident])
        self.ones_bf = P.sb("ones_bf", [128, 128], BF16)
        self.R_ones = Res("ones")
        P.op("pool", lambda e: e.memset(self.ones_bf[:, :], 1.0), w=[self.R_ones])
        self.rr = 0

    def rstd(self, ss_ap, out_ap, R_ss, R_out, n, eng2="dve"):
        P = self.P
        P.op("dve", lambda e: e.tensor_scalar(out=out_ap, in0=ss_ap, scalar1=1.0 / n, scalar2=EPS,
                                               op0=ALU.mult, op1=ALU.add), r=[R_ss], w=[R_out])
        P.op("act", lambda e: e.activation(out=out_ap, in_=out_ap, func=AF.Sqrt), r=[R_out], w=[R_out])
        P.op("dve", lambda e: e.reciprocal(out=out_ap, in_=out_ap), r=[R_out], w=[R_out])

    def cast_eng(self):
        self.rr += 1
        return ("dve", "pool", "act")[self.rr % 3]

    def load_weight(self, dram, wb, R_wb, stage, R_stage, nk, ncols, gsc=None, R_g=None, col0=0, colw=None):
        P = self.P
        idx = 0
        for k in range(nk):
            for c0 in range(0, ncols, 1024):
                cw = min(1024, ncols - c0)
                st, Rs = stage[idx % len(stage)], R_stage[idx % len(stage)]
                idx += 1
                P.dma("sp", st[:, 0:cw], dram[k * 128:(k + 1) * 128, col0 + c0:col0 + c0 + cw], w=[Rs])
                eng = self.cast_eng()
                rs = [Rs] + ([R_g] if gsc is not None else [])
                if gsc is not None:
                    if eng == "act":
                        P.op("act", lambda e, st=st, k=k, c0=c0, cw=cw: e.activation(
                            out=wb[:, k, c0:c0 + cw], in_=st[:, 0:cw], func=AF.Copy, scale=gsc[:, k:k + 1]), r=rs, w=[R_wb])
                    else:
                        P.op(eng, lambda e, st=st, k=k, c0=c0, cw=cw: e.tensor_scalar(
                            out=wb[:, k, c0:c0 + cw], in0=st[:, 0:cw], scalar1=gsc[:, k:k + 1], scalar2=None,
                            op0=ALU.mult), r=rs, w=[R_wb])
                else:
                    if eng == "act":
                        P.op("act", lambda e, st=st, k=k, c0=c0, cw=cw: e.activation(
                            out=wb[:, k, c0:c0 + cw], in_=st[:, 0:cw], func=AF.Copy), r=rs, w=[R_wb])
                    else:
                        P.op(eng, lambda e, st=st, k=k, c0=c0, cw=cw: e.tensor_copy(
                            out=wb[:, k, c0:c0 + cw], in_=st[:, 0:cw]), r=rs, w=[R_wb])


def build_post(kind):
    even = kind == "even"
    nc = bass.Bass("TRN2", target_bir_lowering=False)
    dr = lambda n, s, k="ExternalInput": nc.dram_tensor(n, list(s), F32, kind=k).ap()
    rows = TPC * 128
    hs_d = dr("hs", [rows, D])
    if even:
        osb_d = dr("osb", [rows, 512])
        ys5_d = dr("ys5", [rows, 512])
        wglu_d = dr("wglu", [512, 512])
        bglu_d = dr("bglu", [512])
        gmerge_d = dr("gmerge", [D])
    else:
        og_d = dr("og", [rows, D])
    wout_d = dr("wout", [D, D])
    w1_d = dr("w1", [D, DFF])
    w2_d = dr("w2", [DFF, D])
    g1_d = dr("g_postmix", [D])
    g2_d = dr("g_premlp", [D])
    g3_d = dr("g_postmlp", [D])
    out_d = dr("out", [rows, D], "ExternalOutput")

    with ExitStack() as es:
        C = Ctx(nc, es)
        P = C.P
        w1b = P.sb("w1b", [128, 8, DFF], BF16)
        w2b = P.sb("w2b", [128, 32, D], BF16)
        woutb = P.sb("woutb", [128, 8, D], BF16)
        R_w1, R_w2, R_wout = Res("w1"), Res("w2"), Res("wout")
        stage = [P.sb("stage0", [128, 1024], F32), P.sb("stage1", [128, 1024], F32)]
        R_stage = [Res("st0"), Res("st1")]
        gsm = P.sb("gsm", [128, 3, 8], F32)
        R_gsm = Res("gsm")
        G1 = P.sb("G1", [128, D], F32)
        G3 = P.sb("G3", [128, D], F32)
        R_G = Res("G")
        hs = P.sb("hs_t", [128, D], F32)
        mg = P.sb("mg_t", [128, D], F32)
        xn = P.sb("xn_t", [128, D], BF16)
        xnT = P.sb("xnT_t", [128, 8, 128], BF16)
        h1T = P.sb("h1T_t", [128, 32, 128], BF16)
        rl = P.sb("rl_t", [128, 512], BF16)
        tmp = P.sb("tmp_t", [128, 512], F32)
        st = P.sb("stat", [128, 8], F32)
        R_hs, R_mg, R_xn, R_xnT, R_h1T, R_rl, R_tmp = (Res(n) for n in ("hs", "mg", "xn", "xnT", "h1T", "rl", "tmp"))
        R_st = [Res(f"st{i}") for i in range(8)]
        pT = P.ps("pT", [128, 8, 128], BF16)
        pA = P.ps("pA", [128, 512], F32)
        pB = P.ps("pB", [128, 512], F32)
        pH = [P.ps("pH0", [128, 4, 128], F32), P.ps("pH1", [128, 4, 128], F32)]
        R_pT, R_pA, R_pB = Res("pT", True), Res("pA", True), Res("pB", True)
        R_pH = [Res("pH0", True), Res("pH1", True)]
        junk = stage[0]
        R_junk = R_stage[0]

        P.dma("sp", G1[:, :], g1_d.partition_broadcast(128), w=[R_G])
        P.dma("sp", G3[:, :], g3_d.partition_broadcast(128), w=[R_G])
        P.dma("sp", gsm[:, 0, :], g2_d.rearrange("(k p) -> p k", p=128), w=[R_gsm], allow_slow_non_contiguous=True)
        if even:
            P.dma("sp", gsm[:, 1, :], gmerge_d.rearrange("(k p) -> p k", p=128), w=[R_gsm], allow_slow_non_contiguous=True)
            wglub = P.sb("wglub", [128, 4, 512], BF16)
            bglub = P.sb("bglub", [1, 512], BF16)
            R_wglu, R_bglu = Res("wglu"), Res("bglu")
            P.dma("sp", tmp[0:1, :], bglu_d.rearrange("(o n) -> o n", o=1), w=[R_tmp])
            P.op("dve", lambda e: e.tensor_copy(out=bglub[:, :], in_=tmp[0:1, :]), r=[R_tmp], w=[R_bglu])
            C.load_weight(wglu_d, wglub, R_wglu, stage, R_stage, 4, 512)
            C.load_weight(wout_d, woutb, R_wout, stage, R_stage, 8, D, gsc=gsm[:, 1, :], R_g=R_gsm)
        else:
            C.load_weight(wout_d, woutb, R_wout, stage, R_stage, 8, D)
        C.load_weight(w1_d, w1b, R_w1, stage, R_stage, 8, DFF, gsc=gsm[:, 0, :], R_g=R_gsm)
        C.load_weight(w2_d, w2b, R_w2, stage, R_stage, 32, D)

        def transpose8(nblk):
            for k in range(nblk):
                P.op("pe", lambda e, k=k: e.transpose(out=pT[:, k, :], in_=xn[:, k * 128:(k + 1) * 128], identity=C.ident[:, :]),
                     r=[R_xn, C.R_ident], w=[R_pT])
            P.op("dve", lambda e: e.tensor_copy(out=xnT[:, 0:nblk, :], in_=pT[:, 0:nblk, :]), r=[R_pT], w=[R_xnT])

        def norm_residual(G):
            P.op("act", lambda e: e.activation(out=junk[:, 0:512], in_=pA[:, :], func=AF.Square, accum_out=st[:, 0:1]),
                 r=[R_pA], w=[R_junk, R_st[0]])
            P.op("act", lambda e: e.activation(out=junk[:, 512:1024], in_=pB[:, :], func=AF.Square, accum_out=st[:, 1:2]),
                 r=[R_pB], w=[R_junk, R_st[1]])
            P.op("dve", lambda e: e.tensor_tensor(out=st[:, 0:1], in0=st[:, 0:1], in1=st[:, 1:2], op=ALU.add),
                 r=[R_st[0], R_st[1]], w=[R_st[0]])
            C.rstd(st[:, 0:1], st[:, 0:1], R_st[0], R_st[0], D)
            for h, (pp, Rp) in enumerate(((pA, R_pA), (pB, R_pB))):
                sl = slice(h * 512, (h + 1) * 512)
                P.op("dve", lambda e, pp=pp, sl=sl: e.scalar_tensor_tensor(out=tmp[:, :], in0=pp[:, :], scalar=st[:, 0:1], in1=G[:, sl],
                                                                          op0=ALU.mult, op1=ALU.mult),
                     r=[Rp, R_st[0], R_G], w=[R_tmp])
                P.op("pool", lambda e, sl=sl: e.tensor_tensor(out=hs[:, sl], in0=hs[:, sl], in1=tmp[:, :], op=ALU.add),
                     r=[R_hs, R_tmp], w=[R_hs])

        for t in range(TPC):
            rsl = slice(t * 128, (t + 1) * 128)
            P.dma("sp", hs[:, :], hs_d[rsl, :], w=[R_hs])
            if even:
                P.dma("sp", mg[:, 0:512], osb_d[rsl, :], w=[R_mg])
                P.dma("sp", mg[:, 512:1024], ys5_d[rsl, :], w=[R_mg])
                y = mg[:, 512:1024]
                P.op("act", lambda e: e.activation(out=tmp[:, :], in_=y, func=AF.Square), r=[R_mg], w=[R_tmp])
                P.op("dve", lambda e: e.tensor_scalar(out=tmp[:, :], in0=tmp[:, :], scalar1=0.044715, scalar2=1.0, op0=ALU.mult, op1=ALU.add),
                     r=[R_tmp], w=[R_tmp])
                P.op("dve", lambda e: e.tensor_tensor(out=tmp[:, :], in0=tmp[:, :], in1=y, op=ALU.mult), r=[R_tmp, R_mg], w=[R_tmp])
                P.op("act", lambda e: e.activation(out=tmp[:, :], in_=tmp[:, :], func=AF.Sigmoid, scale=1.5957691216057308), r=[R_tmp], w=[R_tmp])
                P.op("dve", lambda e: e.tensor_tensor(out=y, in0=tmp[:, :], in1=y, op=ALU.mult), r=[R_tmp, R_mg], w=[R_mg])
                P.op("act", lambda e: e.activation(out=xn[:, 0:512], in_=y, func=AF.Copy), r=[R_mg], w=[R_xn])
                transpose8(4)
                for k in range(4):
                    P.op("pe", lambda e, k=k: e.matmul(out=pA[:, :], lhsT=xnT[:, k, :], rhs=wglub[:, k, :], start=(k == 0), stop=False),
                         r=[R_xnT, R_wglu], w=[R_pA])
                P.op("pe", lambda e: e.matmul(out=pA[:, :], lhsT=C.ones_bf[0:1, :], rhs=bglub[0:1, :], start=False, stop=True),
                     r=[C.R_ones, R_bglu], w=[R_pA])
                P.op("act", lambda e: e.activation(out=tmp[:, :], in_=pA[:, :], func=AF.Sigmoid), r=[R_pA], w=[R_tmp])
                P.op("dve", lambda e: e.tensor_tensor(out=y, in0=tmp[:, :], in1=y, op=ALU.mult), r=[R_tmp, R_mg], w=[R_mg])
                for h in range(2):
                    sl = slice(h * 512, (h + 1) * 512)
                    P.op("act", lambda e, sl=sl, h=h: e.activation(out=junk[:, sl], in_=mg[:, sl], func=AF.Square, accum_out=st[:, 2 + h:3 + h]),
                         r=[R_mg], w=[R_junk, R_st[2 + h]])
                    C.rstd(st[:, 2 + h:3 + h], st[:, 2 + h:3 + h], R_st[2 + h], R_st[2 + h], 512)
                    P.op("act", lambda e, sl=sl, h=h: e.activation(out=xn[:, sl], in_=mg[:, sl], func=AF.Copy, scale=st[:, 2 + h:3 + h]),
                         r=[R_mg, R_st[2 + h]], w=[R_xn])
            else:
                P.dma("sp", mg[:, :], og_d[rsl, :], w=[R_mg])
                P.op("act", lambda e: e.activation(out=xn[:, :], in_=mg[:, :], func=AF.Copy), r=[R_mg], w=[R_xn])
            transpose8(8)
            for k in range(8):
                P.op("pe", lambda e, k=k: e.matmul(out=pA[:, :], lhsT=xnT[:, k, :], rhs=woutb[:, k, 0:512], start=(k == 0), stop=(k == 7)),
                     r=[R_xnT, R_wout], w=[R_pA])
            for k in range(8):
                P.op("pe", lambda e, k=k: e.matmul(out=pB[:, :], lhsT=xnT[:, k, :], rhs=woutb[:, k, 512:1024], start=(k == 0), stop=(k == 7)),
                     r=[R_xnT, R_wout], w=[R_pB])
            norm_residual(G1)
            P.op("act", lambda e: e.activation(out=junk[:, :], in_=hs[:, :], func=AF.Square, accum_out=st[:, 4:5]), r=[R_hs], w=[R_junk, R_st[4]])
            C.rstd(st[:, 4:5], st[:, 4:5], R_st[4], R_st[4], D)
            P.op("act", lambda e: e.activation(out=xn[:, :], in_=hs[:, :], func=AF.Copy, scale=st[:, 4:5]), r=[R_hs, R_st[4]], w=[R_xn])
            transpose8(8)
            for fg in range(8):
                ph, Rph = pH[fg % 2], R_pH[fg % 2]
                for j in range(4):
                    fb = fg * 4 + j
                    for k in range(8):
                        P.op("pe", lambda e, ph=ph, j=j, fb=fb, k=k: e.matmul(out=ph[:, j, :], lhsT=w1b[:, k, fb * 128:(fb + 1) * 128], rhs=xnT[:, k, :],
                                                                               start=(k == 0), stop=(k == 7)),
                             r=[R_w1, R_xnT], w=[Rph])
                P.op("act", lambda e, ph=ph: e.activation(out=rl[:, :], in_=ph[:, :, :].rearrange("p a b -> p (a b)"), func=AF.Relu), r=[Rph], w=[R_rl])
                P.op("pool", lambda e, fg=fg: e.tensor_tensor(out=h1T[:, fg * 4:(fg + 1) * 4, :].rearrange("p a b -> p (a b)"), in0=rl[:, :], in1=rl[:, :], op=ALU.mult),
                     r=[R_rl], w=[R_h1T])
            for fb in range(32):
                P.op("pe", lambda e, fb=fb: e.matmul(out=pA[:, :], lhsT=h1T[:, fb, :], rhs=w2b[:, fb, 0:512], start=(fb == 0), stop=(fb == 31)),
                     r=[R_h1T, R_w2], w=[R_pA])
            for fb in range(32):
                P.op("pe", lambda e, fb=fb: e.matmul(out=pB[:, :], lhsT=h1T[:, fb, :], rhs=w2b[:, fb, 512:1024], start=(fb == 0), stop=(fb == 31)),
                     r=[R_h1T, R_w2], w=[R_pB])
            norm_residual(G3)
            P.dma("pool", out_d[rsl, :], hs[:, :], r=[R_hs])
        P.finish()
    return nc


def tok_shard(a_pad, c):
    return np.ascontiguousarray(np.concatenate([a_pad[0:128], a_pad[128 * (1 + 16 * c):128 * (17 + 16 * c)]], axis=0))


def tok_unshard(outs):
    parts = [outs[0][0:128]] + [o[128:] for o in outs]
    return np.concatenate(parts, axis=0)


_CACHE = {}


def get_prog(name, builder, *args):
    if name not in _CACHE:
        _CACHE[name] = builder(*args)
    return _CACHE[name]


def run_post(kind, hs_pad, mix_in, W):
    nc = get_prog("post_" + kind, build_post, kind)
    in_maps = []
    for c in range(NCORES):
        m = {"hs": tok_shard(hs_pad, c)}
        if kind == "even":
            m["osb"] = tok_shard(mix_in[0], c)
            m["ys5"] = tok_shard(mix_in[1], c)
        else:
            m["og"] = tok_shard(mix_in, c)
        m.update(W)
        in_maps.append(m)
    res = run_bass_kernel_spmd(nc, in_maps, core_ids=list(range(NCORES)))
    return tok_unshard([r["out"] for r in res.results])


NG = 33


def grp_cols(gi):
    return (gi * 512, 512 if gi < 32 else 128)


def build_mix_even():
    nc = bass.Bass("TRN2", target_bir_lowering=False)
    dr = lambda n, s, k="ExternalInput": nc.dram_tensor(n, list(s), F32, kind=k).ap()
    hs_d = dr("hs", [LP, D])
    wh_d = dr("wh", [D, 384])
    g_d = dr("g_pre", [D])
    lre_d = dr("lam_re", [4, 64])
    lim_d = dr("lam_im", [4, 64])
    ldt_d = dr("log_dt", [4])
    bre_d = dr("b_re", [4, 64, 16])
    bim_d = dr("b_im", [4, 64, 16])
    cre_d = dr("c_re", [4, 16, 64])
    cim_d = dr("c_im", [4, 16, 64])
    dsk_d = dr("d_skip", [64])
    osb_d = dr("osbT", [64, LP], "ExternalOutput")
    ys_d = dr("ysT", [64, LP], "ExternalOutput")

    with ExitStack() as es:
        C = Ctx(nc, es)
        P = C.P
        QQ = P.sb("QQ", [128, LP], BF16)
        KK = P.sb("KK", [128, LP], BF16)
        vA = P.sb("vA", [128, NT, 64], BF16)
        R_qT = [Res(f"qT{g}") for g in range(NG)]
        R_kT = [Res(f"kT{g}") for g in range(NG)]
        R_v = [Res(f"v{g}") for g in range(NG)]
        whb = P.sb("whb", [128, 8, 384], BF16)
        R_wh = Res("wh")
        gsm = P.sb("gsm", [128, 8], F32)
        R_gsm = Res("gsm")
        stage = [P.sb("stage0", [128, 1024], F32), P.sb("stage1", [128, 1024], F32)]
        R_stage = [Res("st0"), Res("st1")]
        banks = [P.ps(f"bk{i}", [128, 512], F32) for i in range(7)]
        R_bk = [Res(f"bk{i}", True) for i in range(7)]
        pT = P.ps("pT", [128, 8, 128], BF16)
        R_pT = Res("pT", True)

        P.dma("sp", gsm[:, :], g_d.rearrange("(k p) -> p k", p=128), w=[R_gsm], allow_slow_non_contiguous=True)
        C.load_weight(wh_d, whb, R_wh, stage, R_stage, 8, 384, gsc=gsm, R_g=R_gsm)

        sp_ = P.sb("s5p", [128, 2, 24], F32)
        R_sp = Res("s5p")
        LR, LI, DT, AR, AI, IR, II, T0, T1, T2, T3, CR, CI, NLR = range(14)
        spi = P.sb("s5pi", [128, 2], mybir.dt.int32)
        col = lambda c: sp_[:, :, c]
        for sb in range(2):
            P.dma("sp", sp_[:, sb, LR:LR + 1], lre_d[2 * sb:2 * sb + 2, :].rearrange("g (n o) -> (g n) o", o=1), w=[R_sp])
            P.dma("sp", sp_[:, sb, LI:LI + 1], lim_d[2 * sb:2 * sb + 2, :].rearrange("g (n o) -> (g n) o", o=1), w=[R_sp])
            for gl in range(2):
                P.dma("sp", sp_[gl * 64:(gl + 1) * 64, sb, DT:DT + 1],
                      ldt_d[2 * sb + gl:2 * sb + gl + 1].rearrange("(o n) -> o n", o=1).partition_broadcast(64), w=[R_sp])
        so = lambda eng, fn: P.op(eng, fn, r=[R_sp], w=[R_sp])
        so("act", lambda e: e.activation(out=col(DT), in_=col(DT), func=AF.Exp))
        so("dve", lambda e: e.tensor_scalar(out=col(LR), in0=col(LR), scalar1=-1e-4, scalar2=None, op0=ALU.min))
        so("dve", lambda e: e.tensor_tensor(out=col(T0), in0=col(LR), in1=col(DT), op=ALU.mult))
        so("dve", lambda e: e.tensor_scalar(out=col(NLR), in0=col(T0), scalar1=-1.0, scalar2=None, op0=ALU.mult))
        so("dve", lambda e: e.tensor_tensor(out=col(T1), in0=col(LI), in1=col(DT), op=ALU.mult))
        so("dve", lambda e: e.tensor_scalar(out=col(T2), in0=col(T1), scalar1=1.0 / (2 * np.pi), scalar2=None, op0=ALU.mult))
        P.op("dve", lambda e: e.tensor_copy(out=spi[:, :], in_=col(T2)), r=[R_sp], w=[R_sp])
        P.op("dve", lambda e: e.tensor_copy(out=col(T2), in_=spi[:, :]), r=[R_sp], w=[R_sp])
        so("dve", lambda e: e.scalar_tensor_tensor(out=col(T1), in0=col(T2), scalar=-2 * np.pi, in1=col(T1), op0=ALU.mult, op1=ALU.add))
        so("dve", lambda e: e.tensor_scalar(out=col(T1), in0=col(T1), scalar1=0.5, scalar2=None, op0=ALU.mult))
        so("dve", lambda e: e.tensor_scalar(out=col(T2), in0=col(T1), scalar1=np.pi / 2, scalar2=None, op0=ALU.add))
        so("act", lambda e: e.activation(out=col(T1), in_=col(T1), func=AF.Sin))
        so("act", lambda e: e.activation(out=col(T2), in_=col(T2), func=AF.Sin))
        so("act", lambda e: e.activation(out=col(T3), in_=col(T0), func=AF.Exp))
        so("dve", lambda e: e.tensor_tensor(out=col(AI), in0=col(T1), in1=col(T2), op=ALU.mult))
        so("dve", lambda e: e.tensor_scalar(out=col(AI), in0=col(AI), scalar1=2.0, scalar2=None, op0=ALU.mult))
        so("dve", lambda e: e.tensor_tensor(out=col(AR), in0=col(T1), in1=col(T1), op=ALU.mult))
        so("dve", lambda e: e.tensor_scalar(out=col(AR), in0=col(AR), scalar1=-2.0, scalar2=1.0, op0=ALU.mult, op1=ALU.add))
        so("act", lambda e: e.activation(out=col(T0), in_=col(NLR), func=AF.Exp))
        so("dve", lambda e: e.tensor_tensor(out=col(IR), in0=col(AR), in1=col(T0), op=ALU.mult))
        so("dve", lambda e: e.tensor_tensor(out=col(II), in0=col(AI), in1=col(T0), op=ALU.mult))
        so("dve", lambda e: e.tensor_scalar(out=col(II), in0=col(II), scalar1=-1.0, scalar2=None, op0=ALU.mult))
        so("dve", lambda e: e.tensor_tensor(out=col(AR), in0=col(AR), in1=col(T3), op=ALU.mult))
        so("dve", lambda e: e.tensor_tensor(out=col(AI), in0=col(AI), in1=col(T3), op=ALU.mult))
        so("dve", lambda e: e.tensor_tensor(out=col(T0), in0=col(LR), in1=col(LR), op=ALU.mult))
        so("dve", lambda e: e.tensor_tensor(out=col(T1), in0=col(LI), in1=col(LI), op=ALU.mult))
        so("dve", lambda e: e.tensor_tensor(out=col(T0), in0=col(T0), in1=col(T1), op=ALU.add))
        so("dve", lambda e: e.reciprocal(out=col(T0), in_=col(T0)))
        so("dve", lambda e: e.tensor_scalar(out=col(T1), in0=col(AR), scalar1=-1.0, scalar2=None, op0=ALU.add))
        so("dve", lambda e: e.tensor_tensor(out=col(T2), in0=col(T1), in1=col(LR), op=ALU.mult))
        so("dve", lambda e: e.tensor_tensor(out=col(T3), in0=col(AI), in1=col(LI), op=ALU.mult))
        so("dve", lambda e: e.tensor_tensor(out=col(T2), in0=col(T2), in1=col(T3), op=ALU.add))
        so("dve", lambda e: e.tensor_tensor(out=col(CR), in0=col(T2), in1=col(T0), op=ALU.mult))
        so("dve", lambda e: e.tensor_tensor(out=col(T2), in0=col(AI), in1=col(LR), op=ALU.mult))
        so("dve", lambda e: e.tensor_tensor(out=col(T3), in0=col(T1), in1=col(LI), op=ALU.mult))
        so("dve", lambda e: e.tensor_tensor(out=col(T2), in0=col(T2), in1=col(T3), op=ALU.subtract))
        so("dve", lambda e: e.tensor_tensor(out=col(CI), in0=col(T2), in1=col(T0), op=ALU.mult))

        TB = {n: P.sb("tb_" + n, [128, 2, 512], F32) for n in ("pr", "pi", "ir", "ii")}
        R_tb = Res("tb")
        ptmp = P.sb("ptmp", [128, 256], F32)
        R_ptmp = Res("ptmp")
        pw = P.sb("pw", [128, 2, 2, 2], F32)
        R_pw = Res("pw")
        pw2 = P.sb("pw2", [128, 4], F32)
        for sb in range(2):
            for wi, (tr, ti, cr_, ci_) in enumerate((("pr", "pi", AR, AI), ("ir", "ii", IR, II))):
                P.op("pool", lambda e, tr=tr, sb=sb: e.memset(TB[tr][:, sb, 0:1], 1.0), w=[R_tb])
                P.op("pool", lambda e, ti=ti, sb=sb: e.memset(TB[ti][:, sb, 0:1], 0.0), w=[R_tb])
                P.op("dve", lambda e, sb=sb, wi=wi, cr_=cr_: e.tensor_copy(out=pw[:, sb, wi, 0:1], in_=sp_[:, sb, cr_:cr_ + 1]), r=[R_sp], w=[R_pw])
                P.op("dve", lambda e, sb=sb, wi=wi, ci_=ci_: e.tensor_copy(out=pw[:, sb, wi, 1:2], in_=sp_[:, sb, ci_:ci_ + 1]), r=[R_sp], w=[R_pw])
                n = 1
                while n < 512:
                    cr = pw[:, sb, wi, 0:1]
                    ci = pw[:, sb, wi, 1:2]
                    src_r = TB[tr][:, sb, 0:n]
                    src_i = TB[ti][:, sb, 0:n]
                    dst_r = TB[tr][:, sb, n:2 * n]
                    dst_i = TB[ti][:, sb, n:2 * n]
                    tm = ptmp[:, 0:n]
                    P.op("dve", lambda e, tm=tm, src_i=src_i, ci=ci: e.tensor_scalar(out=tm, in0=src_i, scalar1=ci, scalar2=None, op0=ALU.mult),
                         r=[R_tb, R_pw], w=[R_ptmp])
                    P.op("dve", lambda e, dst_r=dst_r, src_r=src_r, cr=cr, tm=tm: e.scalar_tensor_tensor(out=dst_r, in0=src_r, scalar=cr, in1=tm, op0=ALU.mult, op1=ALU.subtract),
                         r=[R_tb, R_pw, R_ptmp], w=[R_tb])
                    P.op("dve", lambda e, tm=tm, src_i=src_i, cr=cr: e.tensor_scalar(out=tm, in0=src_i, scalar1=cr, scalar2=None, op0=ALU.mult),
                         r=[R_tb, R_pw], w=[R_ptmp])
                    P.op("dve", lambda e, dst_i=dst_i, src_r=src_r, ci=ci, tm=tm: e.scalar_tensor_tensor(out=dst_i, in0=src_r, scalar=ci, in1=tm, op0=ALU.mult, op1=ALU.add),
                         r=[R_tb, R_pw, R_ptmp], w=[R_tb])
                    n *= 2
                    if n < 512:
                        P.op("dve", lambda e, cr=cr, ci=ci: e.tensor_tensor(out=pw2[:, 0:1], in0=cr, in1=cr, op=ALU.mult), r=[R_pw], w=[R_ptmp])
                        P.op("dve", lambda e, cr=cr, ci=ci: e.tensor_tensor(out=pw2[:, 1:2], in0=ci, in1=ci, op=ALU.mult), r=[R_pw], w=[R_ptmp])
                        P.op("dve", lambda e, cr=cr, ci=ci: e.tensor_tensor(out=pw2[:, 2:3], in0=cr, in1=ci, op=ALU.mult), r=[R_pw], w=[R_ptmp])
                        P.op("dve", lambda e, cr=cr: e.tensor_tensor(out=cr, in0=pw2[:, 0:1], in1=pw2[:, 1:2], op=ALU.subtract), r=[R_ptmp], w=[R_pw])
                        P.op("dve", lambda e, ci=ci: e.tensor_scalar(out=ci, in0=pw2[:, 2:3], scalar1=2.0, scalar2=None, op0=ALU.mult), r=[R_ptmp], w=[R_pw])

        braw = P.sb("braw", [128, 2, 2, 16], F32)
        R_braw = Res("braw")
        Bfull = P.sb("Bfull", [128, 2, 2, 64], F32)
        R_Bfull = Res("Bfull")
        BbT = P.sb("BbT", [64, 2, 2, 128], F32)
        R_BbT = Res("BbT")
        CT = P.sb("CT", [128, 2, 2, 64], F32)
        R_CT = Res("CT")
        identf = P.sb("identf", [128, 128], F32)
        R_if = Res("identf")
        P.op("dve", lambda e: e.tensor_copy(out=identf[:, :], in_=C.ident[:, :]), r=[C.R_ident], w=[R_if])
        P.op("pool", lambda e: e.memset(Bfull[:, :, :, :].rearrange("p a b c -> p (a b c)"), 0.0), w=[R_Bfull])
        P.op("pool", lambda e: e.memset(CT[:, :, :, :].rearrange("p a b c -> p (a b c)"), 0.0), w=[R_CT])
        for sb in range(2):
            P.dma("sp", braw[:, sb, 0, :], bre_d[2 * sb:2 * sb + 2].rearrange("g n p -> (g n) p"), w=[R_braw])
            P.dma("sp", braw[:, sb, 1, :], bim_d[2 * sb:2 * sb + 2].rearrange("g n p -> (g n) p"), w=[R_braw])
            for gl in range(2):
                g = 2 * sb + gl
                ps_ = slice(gl * 64, (gl + 1) * 64)
                cs = slice(g * 16, (g + 1) * 16)
                P.dma("sp", CT[ps_, sb, 0, cs], cre_d[g].rearrange("p n -> n p"), w=[R_CT], allow_slow_non_contiguous=True)
                P.dma("sp", CT[ps_, sb, 1, cs], cim_d[g].rearrange("p n -> n p"), w=[R_CT], allow_slow_non_contiguous=True)
                crr = sp_[ps_, sb, CR:CR + 1]
                cii = sp_[ps_, sb, CI:CI + 1]
                tm = ptmp[ps_, 0:16]
                P.op("dve", lambda e, tm=tm, sb=sb, ps_=ps_, cii=cii: e.tensor_scalar(out=tm, in0=braw[ps_, sb, 1, :], scalar1=cii, scalar2=None, op0=ALU.mult),
                     r=[R_braw, R_sp], w=[R_ptmp])
                P.op("dve", lambda e, tm=tm, sb=sb, ps_=ps_, cs=cs, crr=crr: e.scalar_tensor_tensor(out=Bfull[ps_, sb, 0, cs], in0=braw[ps_, sb, 0, :], scalar=crr, in1=tm,
                                                                                              op0=ALU.mult, op1=ALU.subtract),
                     r=[R_braw, R_sp, R_ptmp], w=[R_Bfull])
                P.op("dve", lambda e, tm=tm, sb=sb, ps_=ps_, cii=cii: e.tensor_scalar(out=tm, in0=braw[ps_, sb, 0, :], scalar1=cii, scalar2=None, op0=ALU.mult),
                     r=[R_braw, R_sp], w=[R_ptmp])
                P.op("dve", lambda e, tm=tm, sb=sb, ps_=ps_, cs=cs, crr=crr: e.scalar_tensor_tensor(out=Bfull[ps_, sb, 1, cs], in0=braw[ps_, sb, 1, :], scalar=crr, in1=tm,
                                                                                              op0=ALU.mult, op1=ALU.add),
                     r=[R_braw, R_sp, R_ptmp], w=[R_Bfull])
            P.op("dve", lambda e, sb=sb: e.tensor_scalar(out=CT[:, sb, 1, :], in0=CT[:, sb, 1, :], scalar1=-1.0, scalar2=None, op0=ALU.mult), r=[R_CT], w=[R_CT])
            for ri in range(2):
                P.op("pe", lambda e, sb=sb, ri=ri: e.transpose(out=banks[0][0:64, 0:128], in_=Bfull[:, sb, ri, :], identity=identf[:, :]),
                     r=[R_Bfull, R_if], w=[R_bk[0]])
                P.op("dve", lambda e, sb=sb, ri=ri: e.tensor_copy(out=BbT[:, sb, ri, :], in_=banks[0][0:64, 0:128]), r=[R_bk[0]], w=[R_BbT])
        dsk = P.sb("dsk", [64, 1], F32)
        R_dsk = Res("dsk")
        P.dma("sp", dsk[:, :], dsk_d.rearrange("(n o) -> n o", o=1), w=[R_dsk])
        onesf = P.sb("onesf", [128, 512], F32)
        R_onesf = Res("onesf")
        P.op("pool", lambda e: e.memset(onesf[:, :], 1.0), w=[R_onesf])
        carry = P.sb("carry", [128, 2, 2], F32)
        R_carry = Res("carry")
        P.op("pool", lambda e: e.memset(carry[:, :, :].rearrange("p a b -> p (a b)"), 0.0), w=[R_carry])

        hsb = [P.sb("hsb0", [128, D], F32), P.sb("hsb1", [128, D], F32)]
        R_hsb = [Res("hsb0"), Res("hsb1")]
        xn = P.sb("xn", [128, D], BF16)
        R_xn = Res("xn")
        xnT = P.sb("xnT", [128, 8, 512], BF16)
        R_xnT = Res("xnT")
        stt = P.sb("stt", [128, 2], F32)
        R_stt = [Res("stt0"), Res("stt1")]
        uT = P.sb("uT", [64, 512], F32)
        R_uT = Res("uT")
        S5 = {n: P.sb("s5_" + n, [128, 512], F32) for n in ("sre", "sim", "cre", "cim", "t1", "t2", "wre", "wim", "xre", "xim")}
        R_S5 = {n: Res("s5_" + n) for n in S5}
        ysb = P.sb("ysb", [64, 512], F32)
        R_ysb = Res("ysb")
        junk = stage[0]
        R_junk = R_stage[0]
        pV, pQ, pK, pU, pBr, pBi, pY = banks
        R_pV, R_pQ, R_pK, R_pU, R_pBr, R_pBi, R_pY = R_bk

        tile_i = 0
        for gi in range(NG):
            c0, ncol = grp_cols(gi)
            ntl = ncol // 128
            for tl in range(ntl):
                t = gi * 4 + tl
                hb, Rh = hsb[tile_i % 2], R_hsb[tile_i % 2]
                tile_i += 1
                P.dma("sp", hb[:, :], hs_d[t * 128:(t + 1) * 128, :], w=[Rh])
                P.op("act", lambda e, hb=hb: e.activation(out=junk[:, :], in_=hb[:, :], func=AF.Square, accum_out=stt[:, 0:1]), r=[Rh], w=[R_junk, R_stt[0]])
                C.rstd(stt[:, 0:1], stt[:, 0:1], R_stt[0], R_stt[0], D)
                P.op("act", lambda e, hb=hb: e.activation(out=xn[:, :], in_=hb[:, :], func=AF.Copy, scale=stt[:, 0:1]), r=[Rh, R_stt[0]], w=[R_xn])
                for k in range(8):
                    P.op("pe", lambda e, k=k: e.transpose(out=pT[:, k, :], in_=xn[:, k * 128:(k + 1) * 128], identity=C.ident[:, :]),
                         r=[R_xn, C.R_ident], w=[R_pT])
                P.op("dve", lambda e, tl=tl: e.tensor_copy(out=xnT[:, :, tl * 128:(tl + 1) * 128], in_=pT[:, :, :]), r=[R_pT], w=[R_xnT])
                for k in range(8):
                    P.op("pe", lambda e, k=k, tl=tl: e.matmul(out=pV[:, 0:64], lhsT=xnT[:, k, tl * 128:(tl + 1) * 128], rhs=whb[:, k, 256:320],
                                                               start=(k == 0), stop=(k == 7)), r=[R_xnT, R_wh], w=[R_pV])
                P.op("act", lambda e, t=t: e.activation(out=vA[:, t, :], in_=pV[:, 0:64], func=AF.Copy), r=[R_pV], w=[R_v[gi]])
            cs = slice(c0, c0 + ncol)
            for (pp, Rp, wc, wn) in ((pQ, R_pQ, 0, 128), (pK, R_pK, 128, 128), (pU, R_pU, 320, 64)):
                for k in range(8):
                    P.op("pe", lambda e, pp=pp, k=k, wc=wc, wn=wn: e.matmul(out=pp[0:wn, 0:ncol], lhsT=whb[:, k, wc:wc + wn], rhs=xnT[:, k, 0:ncol],
                                                                             start=(k == 0), stop=(k == 7)), r=[R_xnT, R_wh], w=[Rp])
            P.op("act", lambda e, cs=cs: e.activation(out=QQ[:, cs], in_=pQ[:, 0:ncol], func=AF.Copy, scale=0.125), r=[R_pQ], w=[R_qT[gi]])
            P.op("dve", lambda e, cs=cs: e.tensor_copy(out=KK[0:64, cs], in_=pK[0:64, 0:ncol]), r=[R_pK], w=[R_kT[gi]])
            P.op("dve", lambda e, cs=cs: e.tensor_scalar(out=KK[64:128, cs], in0=pK[64:128, 0:ncol], scalar1=-1.0, scalar2=None, op0=ALU.mult), r=[R_pK], w=[R_kT[gi]])
            P.op("act", lambda e: e.activation(out=uT[:, 0:ncol], in_=pU[0:64, 0:ncol], func=AF.Copy), r=[R_pU], w=[R_uT])
            for sb in range(2):
                nn = ncol
                P.op("pe", lambda e, sb=sb: e.matmul(out=pBr[:, 0:nn], lhsT=BbT[:, sb, 0, :], rhs=uT[:, 0:nn], start=True, stop=True), r=[R_BbT, R_uT], w=[R_pBr])
                P.op("pe", lambda e, sb=sb: e.matmul(out=pBi[:, 0:nn], lhsT=BbT[:, sb, 1, :], rhs=uT[:, 0:nn], start=True, stop=True), r=[R_BbT, R_uT], w=[R_pBi])
                A = lambda n: S5[n][:, 0:nn]
                P.op("act", lambda e: e.activation(out=A("sre"), in_=pBr[:, 0:nn], func=AF.Copy), r=[R_pBr], w=[R_S5["sre"]])
                P.op("act", lambda e: e.activation(out=A("sim"), in_=pBi[:, 0:nn], func=AF.Copy), r=[R_pBi], w=[R_S5["sim"]])
                ir_, ii_ = TB["ir"][:, sb, 0:nn], TB["ii"][:, sb, 0:nn]
                pr_, pi_ = TB["pr"][:, sb, 0:nn], TB["pi"][:, sb, 0:nn]
                P.op("dve", lambda e: e.tensor_tensor(out=A("t1"), in0=A("sim"), in1=ii_, op=ALU.mult), r=[R_S5["sim"], R_tb], w=[R_S5["t1"]])
                P.op("dve", lambda e: e.tensor_tensor(out=A("cre"), in0=A("sre"), in1=ir_, op=ALU.mult), r=[R_S5["sre"], R_tb], w=[R_S5["cre"]])
                P.op("dve", lambda e: e.tensor_tensor(out=A("cre"), in0=A("cre"), in1=A("t1"), op=ALU.subtract), r=[R_S5["cre"], R_S5["t1"]], w=[R_S5["cre"]])
                P.op("pool", lambda e: e.tensor_tensor(out=A("t2"), in0=A("sim"), in1=ir_, op=ALU.mult), r=[R_S5["sim"], R_tb], w=[R_S5["t2"]])
                P.op("pool", lambda e: e.tensor_tensor(out=A("cim"), in0=A("sre"), in1=ii_, op=ALU.mult), r=[R_S5["sre"], R_tb], w=[R_S5["cim"]])
                P.op("pool", lambda e: e.tensor_tensor(out=A("cim"), in0=A("cim"), in1=A("t2"), op=ALU.add), r=[R_S5["cim"], R_S5["t2"]], w=[R_S5["cim"]])
                P.op("dve", lambda e, sb=sb: e.tensor_tensor_scan(out=A("wre"), data0=onesf[:, 0:nn], data1=A("cre"), initial=carry[:, sb, 0:1], op0=ALU.mult, op1=ALU.add),
                     r=[R_onesf, R_S5["cre"], R_carry], w=[R_S5["wre"]])
                P.op("dve", lambda e, sb=sb: e.tensor_tensor_scan(out=A("wim"), data0=onesf[:, 0:nn], data1=A("cim"), initial=carry[:, sb, 1:2], op0=ALU.mult, op1=ALU.add),
                     r=[R_onesf, R_S5["cim"], R_carry], w=[R_S5["wim"]])
                P.op("dve", lambda e: e.tensor_tensor(out=A("t1"), in0=A("wim"), in1=pi_, op=ALU.mult), r=[R_S5["wim"], R_tb], w=[R_S5["t1"]])
                P.op("dve", lambda e: e.tensor_tensor(out=A("xre"), in0=A("wre"), in1=pr_, op=ALU.mult), r=[R_S5["wre"], R_tb], w=[R_S5["xre"]])
                P.op("dve", lambda e: e.tensor_tensor(out=A("xre"), in0=A("xre"), in1=A("t1"), op=ALU.subtract), r=[R_S5["xre"], R_S5["t1"]], w=[R_S5["xre"]])
                P.op("pool", lambda e: e.tensor_tensor(out=A("t2"), in0=A("wim"), in1=pr_, op=ALU.mult), r=[R_S5["wim"], R_tb], w=[R_S5["t2"]])
                P.op("pool", lambda e: e.tensor_tensor(out=A("xim"), in0=A("wre"), in1=pi_, op=ALU.mult), r=[R_S5["wre"], R_tb], w=[R_S5["xim"]])
                P.op("pool", lambda e: e.tensor_tensor(out=A("xim"), in0=A("xim"), in1=A("t2"), op=ALU.add), r=[R_S5["xim"], R_S5["t2"]], w=[R_S5["xim"]])
                xer, xei = S5["xre"][:, nn - 1:nn], S5["xim"][:, nn - 1:nn]
                ar_, ai_ = sp_[:, sb, AR:AR + 1], sp_[:, sb, AI:AI + 1]
                P.op("dve", lambda e: e.tensor_tensor(out=pw2[:, 0:1], in0=xei, in1=ai_, op=ALU.mult), r=[R_S5["xim"], R_sp], w=[R_ptmp])
                P.op("dve", lambda e: e.tensor_tensor(out=pw2[:, 1:2], in0=xei, in1=ar_, op=ALU.mult), r=[R_S5["xim"], R_sp], w=[R_ptmp])
                P.op("dve", lambda e, sb=sb: e.scalar_tensor_tensor(out=carry[:, sb, 0:1], in0=xer, scalar=ar_, in1=pw2[:, 0:1], op0=ALU.mult, op1=ALU.subtract),
                     r=[R_S5["xre"], R_sp, R_ptmp], w=[R_carry])
                P.op("dve", lambda e, sb=sb: e.scalar_tensor_tensor(out=carry[:, sb, 1:2], in0=xer, scalar=ai_, in1=pw2[:, 1:2], op0=ALU.mult, op1=ALU.add),
                     r=[R_S5["xre"], R_sp, R_ptmp], w=[R_carry])
                P.op("pe", lambda e, sb=sb: e.matmul(out=pY[0:64, 0:nn], lhsT=CT[:, sb, 0, :], rhs=A("xre"), start=(sb == 0), stop=False), r=[R_CT, R_S5["xre"]], w=[R_pY])
                P.op("pe", lambda e, sb=sb: e.matmul(out=pY[0:64, 0:nn], lhsT=CT[:, sb, 1, :], rhs=A("xim"), start=False, stop=(sb == 1)), r=[R_CT, R_S5["xim"]], w=[R_pY])
            P.op("dve", lambda e: e.scalar_tensor_tensor(out=ysb[:, 0:ncol], in0=uT[:, 0:ncol], scalar=dsk[:, 0:1], in1=pY[0:64, 0:ncol], op0=ALU.mult, op1=ALU.add),
                 r=[R_uT, R_dsk, R_pY], w=[R_ysb])
            P.dma("pool", ys_d[:, cs], ysb[:, 0:ncol], r=[R_ysb])

        Tincl = P.sb("Tincl", [128, 128], BF16)
        R_Ti = Res("Tincl")
        P.op("pool", lambda e: e.memset(Tincl[:, :], 1.0), w=[R_Ti])
        P.op("pool", lambda e: e.affine_select(out=Tincl[:, :], in_=Tincl[:, :], pattern=[[-1, 128]], compare_op=ALU.is_ge, fill=0.0, base=0, channel_multiplier=1),
             r=[R_Ti], w=[R_Ti])
        onesb512 = P.sb("onesb512", [128, 512], BF16)
        R_o512 = Res("o512")
        P.op("pool", lambda e: e.memset(onesb512[:, :], 1.0), w=[R_o512])
        masks = P.sb("masks", [128, 4, 512], BF16)
        R_masks = Res("masks")
        for i in range(4):
            P.op("pool", lambda e, i=i: e.affine_select(out=masks[:, i, :], in_=onesb512[:, :], pattern=[[1, 512]], compare_op=ALU.is_gt, fill=0.0,
                                                        base=-128 * i, channel_multiplier=-1), r=[R_o512], w=[R_masks])
        padcol = P.sb("padcol", [128, 1], F32)
        R_pad = Res("padcol")
        P.op("pool", lambda e: e.affine_select(out=padcol[:, :], in_=onesf[:, 0:1], pattern=[[0, 1]], compare_op=ALU.is_ge, fill=0.0, base=-PADF, channel_multiplier=1),
             r=[R_onesf], w=[R_pad])
        NB = 3
        eb = [P.sb(f"eb{i}", [128, 512], F32) for i in range(NB)]
        R_eb = [Res(f"eb{i}") for i in range(NB)]
        spb = [P.sb(f"spb{i}", [128, 512], BF16) for i in range(NB)]
        R_spb = [Res(f"spb{i}") for i in range(NB)]
        wb_ = [P.sb("wbb0", [128, 512], BF16), P.sb("wbb1", [128, 512], BF16)]
        R_wb = [Res("wbb0"), Res("wbb1")]
        S16 = [P.sb("S16a", [128, 512], BF16), P.sb("S16b", [128, 512], BF16)]
        R_S16 = [Res("S16a"), Res("S16b")]
        osb_sb = P.sb("osb_sb", [64, 512], F32)
        R_osb = Res("osb_sb")
        pz = [banks[0], banks[1], banks[5]]
        R_pz = [R_bk[0], R_bk[1], R_bk[5]]
        pc = [banks[2], banks[3]]
        R_pc = [R_bk[2], R_bk[3]]
        po = banks[4]
        R_po = R_bk[4]
        iters = []
        for Q in range(NG):
            jmax = min(4 * Q + 3, NT - 1)
            for j in range(jmax, -1, -1):
                iters.append((Q, j, jmax))

        def maskop(buf, Rbuf, Q, j, nq):
            diag = j >= 4 * Q
            if not (diag or j == 0):
                return
            mk = masks[:, j - 4 * Q, 0:nq] if diag else onesb512[:, 0:nq]
            if j == 0:
                P.op("dve", lambda e: e.scalar_tensor_tensor(out=buf[:, 0:nq], in0=buf[:, 0:nq], scalar=padcol[:, 0:1], in1=mk,
                                                             op0=ALU.mult, op1=ALU.mult), r=[Rbuf, R_pad, R_masks, R_o512], w=[Rbuf])
            else:
                P.op("dve", lambda e: e.tensor_tensor(out=buf[:, 0:nq], in0=buf[:, 0:nq], in1=mk, op=ALU.mult), r=[Rbuf, R_masks], w=[Rbuf])

        def stageA(i):
            Q, j, jmax = iters[i]
            q0, nq = grp_cols(Q)
            qs = slice(q0, q0 + nq)
            ks = slice(j * 128, (j + 1) * 128)
            b = i % NB
            P.op("pe", lambda e: e.matmul(out=pz[b][:, 0:nq], lhsT=KK[0:64, ks], rhs=QQ[0:64, qs], start=True, stop=True),
                 r=[R_kT[j // 4], R_qT[Q]], w=[R_pz[b]])
            P.op("act", lambda e: e.activation(out=eb[b][:, 0:nq], in_=pz[b][:, 0:nq], func=AF.Exp), r=[R_pz[b]], w=[R_eb[b]])
            P.op("act", lambda e: e.activation(out=spb[b][:, 0:nq], in_=eb[b][:, 0:nq], func=AF.Ln, bias=1.0), r=[R_eb[b]], w=[R_spb[b]])
            maskop(spb[b], R_spb[b], Q, j, nq)

        def stageB(i):
            Q, j, jmax = iters[i]
            q0, nq = grp_cols(Q)
            qs = slice(q0, q0 + nq)
            ks = slice(j * 128, (j + 1) * 128)
            b = i % NB
            c = i % 2
            first = (j == jmax)
            sidx = (jmax - j) % 2
            P.op("pe", lambda e: e.matmul(out=pc[c][:, 0:nq], lhsT=Tincl[:, :], rhs=spb[b][:, 0:nq], start=True, stop=False),
                 r=[R_Ti, R_spb[b]], w=[R_pc[c]])
            if not first:
                P.op("pe", lambda e: e.matmul(out=pc[c][:, 0:nq], lhsT=C.ones_bf[:, :], rhs=S16[sidx][:, 0:nq], start=False, stop=False),
                     r=[C.R_ones, R_S16[sidx]], w=[R_pc[c]])
            P.op("pe", lambda e: e.matmul(out=pc[c][:, 0:nq], lhsT=KK[64:128, ks], rhs=QQ[64:128, qs], start=False, stop=True),
                 r=[R_kT[j // 4], R_qT[Q]], w=[R_pc[c]])
            P.op("act", lambda e: e.activation(out=wb_[c][:, 0:nq], in_=pc[c][:, 0:nq], func=AF.Exp, scale=-1.0), r=[R_pc[c]], w=[R_wb[c]])
            maskop(wb_[c], R_wb[c], Q, j, nq)
            if j > 0:
                if first:
                    P.op("pool", lambda e: e.tensor_copy(out=S16[1 - sidx][:, 0:nq], in_=spb[b][:, 0:nq]), r=[R_spb[b]], w=[R_S16[1 - sidx]])
                else:
                    P.op("pool", lambda e: e.tensor_tensor(out=S16[1 - sidx][:, 0:nq], in0=S16[sidx][:, 0:nq], in1=spb[b][:, 0:nq], op=ALU.add),
                         r=[R_S16[sidx], R_spb[b]], w=[R_S16[1 - sidx]])
            P.op("pe", lambda e: e.matmul(out=po[0:64, 0:nq], lhsT=vA[:, j, :], rhs=wb_[c][:, 0:nq], start=first, stop=(j == 0)),
                 r=[R_v[j // 4], R_wb[c]], w=[R_po])
            if j == 0:
                P.op("dve", lambda e: e.tensor_copy(out=osb_sb[:, 0:nq], in_=po[0:64, 0:nq]), r=[R_po], w=[R_osb])
                P.dma("pool", osb_d[:, qs], osb_sb[:, 0:nq], r=[R_osb])

        n_it = len(iters)
        LA = 2
        for i in range(min(LA, n_it)):
            stageA(i)
        for i in range(n_it):
            if i + LA < n_it:
                stageA(i + LA)
            stageB(i)
        P.finish()
    return nc


def run_mix_even(hs_pad, I):
    nc = get_prog("mix_even", build_mix_even)
    w = I["w_in_even"][0]
    in_maps = []
    for h in range(NCORES):
        wq, wk = w[:, h * 64:(h + 1) * 64], w[:, 512 + h * 64:512 + (h + 1) * 64]
        wh = np.concatenate([wq, wq, wk, wk, w[:, 1024 + h * 64:1024 + (h + 1) * 64], w[:, 1536 + h * 64:1536 + (h + 1) * 64]], axis=1)
        gs = slice(4 * h, 4 * h + 4)
        in_maps.append({
            "hs": hs_pad, "wh": np.ascontiguousarray(wh), "g_pre": I["pre_mix_norm"][0],
            "lam_re": np.ascontiguousarray(I["s5_lambda_re"][0, gs]), "lam_im": np.ascontiguousarray(I["s5_lambda_im"][0, gs]),
            "log_dt": np.ascontiguousarray(I["s5_log_dt"][0, gs]),
            "b_re": np.ascontiguousarray(I["s5_b_re"][0, gs]), "b_im": np.ascontiguousarray(I["s5_b_im"][0, gs]),
            "c_re": np.ascontiguousarray(I["s5_c_re"][0, gs]), "c_im": np.ascontiguousarray(I["s5_c_im"][0, gs]),
            "d_skip": np.ascontiguousarray(I["s5_d"][0, h * 64:(h + 1) * 64]),
        })
    res = run_bass_kernel_spmd(nc, in_maps, core_ids=list(range(NCORES)))
    osb = np.concatenate([r["osbT"].T for r in res.results], axis=1)
    ys = np.concatenate([r["ysT"].T for r in res.results], axis=1)
    return np.ascontiguousarray(osb), np.ascontiguousarray(ys)


def build_mix_odd(ng_limit=NG):
    nc = bass.Bass("TRN2", target_bir_lowering=False)
    dr = lambda n, s, k="ExternalInput": nc.dram_tensor(n, list(s), F32, kind=k).ap()
    hs_d = dr("hs", [LP, D])
    wh_d = dr("wh", [D, 516])
    g_d = dr("g_pre", [D])
    cw_d = dr("convw", [128, 12])
    alog_d = dr("a_log", [1])
    dtb_d = dr("dt_bias", [1])
    gdn_d = dr("g_dn", [128])
    og_d = dr("og", [LP, 128], "ExternalOutput")

    with ExitStack() as es:
        C = Ctx(nc, es)
        P = C.P
        whb = P.sb("whb", [128, 8, 516], BF16)
        R_wh = Res("wh")
        gsm = P.sb("gsm", [128, 8], F32)
        R_gsm = Res("gsm")
        stage = [P.sb("stage0", [128, 1024], F32), P.sb("stage1", [128, 1024], F32)]
        R_stage = [Res("st0"), Res("st1")]
        bk = [P.ps(f"bk{i}", [128, 512], F32) for i in range(7)]
        R_bk = [Res(f"bk{i}", True) for i in range(7)]
        pT = P.ps("pT", [128, 8, 128], BF16)
        R_pT = Res("pT", True)
        P.dma("sp", gsm[:, :], g_d.rearrange("(k p) -> p k", p=128), w=[R_gsm], allow_slow_non_contiguous=True)
        C.load_weight(wh_d, whb, R_wh, stage, R_stage, 8, 516, gsc=gsm, R_g=R_gsm)
        cst = P.sb("cst", [128, 16], F32)
        R_cst = Res("cst")
        P.dma("sp", cst[:, 0:12], cw_d, w=[R_cst])
        P.dma("sp", cst[:, 12:13], dtb_d.rearrange("(o n) -> o n", o=1).partition_broadcast(128), w=[R_cst])
        P.dma("sp", cst[:, 13:14], alog_d.rearrange("(o n) -> o n", o=1).partition_broadcast(128), w=[R_cst])
        P.op("act", lambda e: e.activation(out=cst[:, 13:14], in_=cst[:, 13:14], func=AF.Exp), r=[R_cst], w=[R_cst])
        P.op("dve", lambda e: e.tensor_scalar(out=cst[:, 13:14], in0=cst[:, 13:14], scalar1=-1.0, scalar2=None, op0=ALU.mult), r=[R_cst], w=[R_cst])
        Gdn = P.sb("Gdn", [128, 128], F32)
        R_Gdn = Res("Gdn")
        P.dma("sp", Gdn[:, :], gdn_d.partition_broadcast(128), w=[R_Gdn])
        onesf = P.sb("onesf", [128, 128], F32)
        R_onesf = Res("onesf")
        P.op("pool", lambda e: e.memset(onesf[:, :], 1.0), w=[R_onesf])
        identf = P.sb("identf", [128, 128], F32)
        R_if = Res("identf")
        P.op("dve", lambda e: e.tensor_copy(out=identf[:, :], in_=C.ident[:, :]), r=[C.R_ident], w=[R_if])
        maskS = P.sb("maskS", [128, 128], F32)
        maskST = P.sb("maskST", [128, 128], F32)
        TriX = P.sb("TriX", [128, 130], F32)
        R_mk = Res("masks")
        P.op("pool", lambda e: e.affine_select(out=maskS[:, :], in_=onesf[:, :], pattern=[[-1, 128]], compare_op=ALU.is_gt, fill=0.0, base=0, channel_multiplier=1),
             r=[R_onesf], w=[R_mk])
        P.op("pool", lambda e: e.memset(maskS[64:128, 0:64], 0.0), r=[R_mk], w=[R_mk])
        P.op("pool", lambda e: e.affine_select(out=maskST[:, :], in_=onesf[:, :], pattern=[[1, 128]], compare_op=ALU.is_gt, fill=0.0, base=0, channel_multiplier=-1),
             r=[R_onesf], w=[R_mk])
        P.op("pool", lambda e: e.memset(maskST[0:64, 64:128], 0.0), r=[R_mk], w=[R_mk])
        P.op("pool", lambda e: e.affine_select(out=TriX[:, 0:128], in_=onesf[:, :], pattern=[[1, 128]], compare_op=ALU.is_ge, fill=0.0, base=0, channel_multiplier=-1),
             r=[R_onesf], w=[R_mk])
        P.op("pool", lambda e: e.memset(TriX[0:64, 64:128], 0.0), r=[R_mk], w=[R_mk])
        P.op("pool", lambda e: e.memset(TriX[:, 128:130], 0.0), r=[R_mk], w=[R_mk])
        P.op("pool", lambda e: e.memset(TriX[0:64, 128:129], 1.0), r=[R_mk], w=[R_mk])
        P.op("pool", lambda e: e.memset(TriX[64:128, 129:130], 1.0), r=[R_mk], w=[R_mk])
        maskIT = TriX
        mblk = P.sb("mblk", [128, 3, 128], F32)
        for bi, bs in enumerate((16, 32, 64)):
            nb_ = 128 // bs
            P.op("pool", lambda e: e.affine_select(out=mblk[:, bi, :], in_=onesf[:, :], pattern=[[-bs, nb_], [0, bs]], compare_op=ALU.is_ge, fill=0.0,
                                                   base=0, channel_multiplier=1), r=[R_onesf, R_mk], w=[R_mk])
            P.op("pool", lambda e: e.affine_select(out=mblk[:, bi, :], in_=mblk[:, bi, :], pattern=[[bs, nb_], [0, bs]], compare_op=ALU.is_ge, fill=0.0,
                                                   base=bs - 1, channel_multiplier=-1), r=[R_mk], w=[R_mk])
        P.op("pool", lambda e: e.tensor_tensor(out=mblk[:, 2, :], in0=mblk[:, 2, :], in1=mblk[:, 1, :], op=ALU.subtract), r=[R_mk], w=[R_mk])
        P.op("pool", lambda e: e.tensor_tensor(out=mblk[:, 1, :], in0=mblk[:, 1, :], in1=mblk[:, 0, :], op=ALU.subtract), r=[R_mk], w=[R_mk])

        hsb = [P.sb("hsb0", [128, D], F32), P.sb("hsb1", [128, D], F32)]
        R_hsb = [Res("hsb0"), Res("hsb1")]
        xn = P.sb("xn", [128, D], BF16)
        R_xn = Res("xn")
        xnT = P.sb("xnT", [128, 8, 512], BF16)
        R_xnT = Res("xnT")
        stt = P.sb("stt", [128, 4], F32)
        R_stt = Res("stt")
        raw = P.sb("raw", [128, 3, 515], F32)
        R_raw = [Res("rawq"), Res("rawk"), Res("rawv")]
        P.op("pool", lambda e: e.memset(raw[:, :, :].rearrange("p a b -> p (a b)"), 0.0), w=R_raw)
        acc = P.sb("acc", [128, 512], F32)
        R_acc = Res("acc")
        sil = P.sb("sil", [128, 3, 512], F32)
        R_sil = [Res("silq"), Res("silk"), Res("silv")]
        sq = P.sb("sq", [128, 512], F32)
        R_sq = Res("sq")
        rn = P.sb("rn", [128, 512], F32)
        R_rn = Res("rn")
        qnT = P.sb("qnT", [128, 512], BF16)
        knT = P.sb("knT", [128, 512], BF16)
        kbT = P.sb("kbT", [128, 512], BF16)
        vT = P.sb("vT", [128, 512], BF16)
        R_qnT, R_knT, R_kbT, R_vT = Res("qnT"), Res("knT"), Res("kbT"), Res("vT")
        brow = P.sb("brow", [1, 512], BF16)
        R_brow = Res("brow")
        gz = P.sb("gz", [128, 4, 128], F32)
        R_gz = Res("gz")
        cols = P.sb("cols", [128, 4, 8], F32)
        R_cols = [Res(f"cols{i}") for i in range(4)]
        egl = P.sb("egl", [128, 2], F32)
        R_egl = Res("egl")
        Gbc = P.sb("Gbc", [128, 128], F32)
        R_Gbc = Res("Gbc")
        ktl = P.sb("ktl", [128, 2, 128], BF16)
        ind = P.sb("ind", [128, 2], F32)
        R_ind = Res("ind")
        P.op("pool", lambda e: e.memset(ind[:, :], 0.0), w=[R_ind])
        P.op("pool", lambda e: e.memset(ind[0:64, 0:1], 1.0), r=[R_ind], w=[R_ind])
        P.op("pool", lambda e: e.memset(ind[64:128, 1:2], 1.0), r=[R_ind], w=[R_ind])
        osb_ = P.sb("o_sb", [128, 128], F32)
        R_osb_ = Res("o_sb")
        bv = P.sb("bv", [128, 128], F32)
        R_ktl, R_bv = Res("ktl"), Res("bv")
        egb = P.sb("egb", [128, 128], F32)
        R_egb = Res("egb")
        qg = P.sb("qg", [128, 128], BF16)
        R_qg = Res("qg")
        Dm = P.sb("Dm", [128, 128], F32)
        DTm = P.sb("DTm", [128, 128], F32)
        DTI = P.sb("DTI", [128, 128], F32)
        R_Dm, R_DTm, R_DTI = Res("Dm"), Res("DTm"), Res("DTI")
        NW = 14
        Wk = P.sb("Wk", [128, NW, 128], F32)
        R_Wk = [Res(f"Wk{i}") for i in range(NW)]
        ATb = P.sb("ATb", [128, 128], BF16)
        R_ATb = Res("ATb")
        aiT = P.sb("aiT", [128, 128], BF16)
        R_aiT = Res("aiT")
        rt = P.sb("rt", [128, 128], BF16)
        vn = P.sb("vn", [128, 128], BF16)
        R_rt, R_vn = Res("rt"), Res("vn")
        S32 = P.sb("S32", [128, 128], F32)
        Sbf = P.sb("Sbf", [128, 128], BF16)
        R_S32, R_Sbf = Res("S32"), Res("Sbf")
        P.op("pool", lambda e: e.memset(S32[:, :], 0.0), w=[R_S32])
        P.op("pool", lambda e: e.memset(Sbf[:, :], 0.0), w=[R_Sbf])
        ogs = P.sb("ogs", [128, 128], F32)
        R_ogs = Res("ogs")
        junk = stage[0]
        R_junk = R_stage[0]
        pQKV = [bk[0], bk[1], bk[2]]
        R_pQKV = [R_bk[0], R_bk[1], R_bk[2]]

        tile_i = 0
        for gi in range(ng_limit):
            c0, ncol = grp_cols(gi)
            ntl = ncol // 128
            N = ncol
            for tl in range(ntl):
                t = gi * 4 + tl
                hb, Rh = hsb[tile_i % 2], R_hsb[tile_i % 2]
                tile_i += 1
                tc_ = slice(tl * 128, (tl + 1) * 128)
                P.dma("sp", hb[:, :], hs_d[t * 128:(t + 1) * 128, :], w=[Rh])
                P.op("act", lambda e: e.activation(out=junk[:, :], in_=hb[:, :], func=AF.Square, accum_out=stt[:, 0:1]), r=[Rh], w=[R_junk, R_stt])
                C.rstd(stt[:, 0:1], stt[:, 0:1], R_stt, R_stt, D)
                P.op("act", lambda e: e.activation(out=xn[:, :], in_=hb[:, :], func=AF.Copy, scale=stt[:, 0:1]), r=[Rh, R_stt], w=[R_xn])
                for k in range(8):
                    P.op("pe", lambda e: e.transpose(out=pT[:, k, :], in_=xn[:, k * 128:(k + 1) * 128], identity=C.ident[:, :]), r=[R_xn, C.R_ident], w=[R_pT])
                P.op("dve", lambda e: e.tensor_copy(out=xnT[:, :, tc_], in_=pT[:, :, :]), r=[R_pT], w=[R_xnT])
                for k in range(8):
                    P.op("pe", lambda e: e.matmul(out=bk[5][:, 0:132], lhsT=xnT[:, k, tc_], rhs=whb[:, k, 384:516], start=(k == 0), stop=(k == 7)),
                         r=[R_xnT, R_wh], w=[R_bk[5]])
                P.op("act", lambda e: e.activation(out=gz[:, tl, :], in_=bk[5][:, 0:128], func=AF.Silu), r=[R_bk[5]], w=[R_gz])
                P.op("pool", lambda e: e.tensor_tensor(out=gz[:, tl, :], in0=gz[:, tl, :], in1=Gdn[:, :], op=ALU.mult), r=[R_gz, R_Gdn], w=[R_gz])
                cc = lambda i: cols[:, tl, i:i + 1]
                Rc = R_cols[tl]
                P.op("act", lambda e: e.activation(out=cc(1), in_=bk[5][:, 130:131], func=AF.Sigmoid), r=[R_bk[5]], w=[Rc])
                P.op("dve", lambda e: e.tensor_tensor(out=cc(7), in0=bk[5][:, 128:129], in1=cst[:, 12:13], op=ALU.add), r=[R_bk[5], R_cst], w=[Rc])
                P.op("dve", lambda e: e.tensor_scalar(out=cc(0), in0=cc(7), scalar1=-1.0, scalar2=None, op0=ALU.mult), r=[Rc], w=[Rc])
                P.op("dve", lambda e: e.tensor_tensor(out=cc(0), in0=cc(0), in1=cc(7), op=ALU.max), r=[Rc], w=[Rc])
                P.op("act", lambda e: e.activation(out=cc(0), in_=cc(0), func=AF.Exp, scale=-1.0), r=[Rc], w=[Rc])
                P.op("dve", lambda e: e.tensor_scalar(out=cc(6), in0=cc(0), scalar1=2.0, scalar2=None, op0=ALU.add), r=[Rc], w=[Rc])
                P.op("dve", lambda e: e.reciprocal(out=cc(6), in_=cc(6)), r=[Rc], w=[Rc])
                P.op("dve", lambda e: e.tensor_tensor(out=cc(6), in0=cc(6), in1=cc(0), op=ALU.mult), r=[Rc], w=[Rc])
                P.op("dve", lambda e: e.tensor_tensor(out=cc(0), in0=cc(6), in1=cc(6), op=ALU.mult), r=[Rc], w=[Rc])
                P.op("dve", lambda e: e.tensor_scalar(out=cc(5), in0=cc(0), scalar1=1.0 / 15, scalar2=1.0 / 13, op0=ALU.mult, op1=ALU.add), r=[Rc], w=[Rc])
                for cf in (1.0 / 11, 1.0 / 9, 1.0 / 7, 1.0 / 5, 1.0 / 3, 1.0):
                    P.op("dve", lambda e: e.tensor_tensor(out=cc(5), in0=cc(5), in1=cc(0), op=ALU.mult), r=[Rc], w=[Rc])
                    P.op("dve", lambda e: e.tensor_scalar(out=cc(5), in0=cc(5), scalar1=cf, scalar2=None, op0=ALU.add), r=[Rc], w=[Rc])
                P.op("dve", lambda e: e.tensor_tensor(out=cc(5), in0=cc(5), in1=cc(6), op=ALU.mult), r=[Rc], w=[Rc])
                P.op("dve", lambda e: e.tensor_scalar(out=cc(7), in0=cc(7), scalar1=0.0, scalar2=None, op0=ALU.max), r=[Rc], w=[Rc])
                P.op("dve", lambda e: e.scalar_tensor_tensor(out=cc(0), in0=cc(5), scalar=2.0, in1=cc(7), op0=ALU.mult, op1=ALU.add), r=[Rc], w=[Rc])
                P.op("dve", lambda e: e.tensor_tensor(out=cc(0), in0=cc(0), in1=cst[:, 13:14], op=ALU.mult), r=[Rc, R_cst], w=[Rc])
            for wi in range(3):
                for k in range(8):
                    P.op("pe", lambda e: e.matmul(out=pQKV[wi][:, 0:N], lhsT=whb[:, k, wi * 128:(wi + 1) * 128], rhs=xnT[:, k, 0:N], start=(k == 0), stop=(k == 7)),
                         r=[R_xnT, R_wh], w=[R_pQKV[wi]])
            for k in range(8):
                P.op("pe", lambda e: e.matmul(out=bk[3][0:1, 0:N], lhsT=whb[:, k, 514:515], rhs=xnT[:, k, 0:N], start=(k == 0), stop=(k == 7)),
                     r=[R_xnT, R_wh], w=[R_bk[3]])
            P.op("act", lambda e: e.activation(out=brow[:, 0:N], in_=bk[3][0:1, 0:N], func=AF.Sigmoid), r=[R_bk[3]], w=[R_brow])
            P.op("pe", lambda e: e.matmul(out=bk[4][:, 0:N], lhsT=C.ones_bf[0:1, :], rhs=brow[0:1, 0:N], start=True, stop=True), r=[C.R_ones, R_brow], w=[R_bk[4]])
            for wi in range(3):
                P.op("act", lambda e: e.activation(out=raw[:, wi, 3:3 + N], in_=pQKV[wi][:, 0:N], func=AF.Copy), r=[R_pQKV[wi]], w=[R_raw[wi]])
                eng = "dve" if wi != 1 else "pool"
                P.op(eng, lambda e: e.tensor_scalar(out=acc[:, 0:N], in0=raw[:, wi, 0:N], scalar1=cst[:, wi * 4:wi * 4 + 1], scalar2=None, op0=ALU.mult),
                     r=[R_raw[wi], R_cst], w=[R_acc])
                for j in range(1, 4):
                    P.op("dve", lambda e: e.scalar_tensor_tensor(out=acc[:, 0:N], in0=raw[:, wi, j:j + N], scalar=cst[:, wi * 4 + j:wi * 4 + j + 1], in1=acc[:, 0:N],
                                                                 op0=ALU.mult, op1=ALU.add), r=[R_raw[wi], R_cst, R_acc], w=[R_acc])
                P.op("act", lambda e: e.activation(out=sil[:, wi, 0:N], in_=acc[:, 0:N], func=AF.Silu), r=[R_acc], w=[R_sil[wi]])
                P.op("pool", lambda e: e.tensor_copy(out=raw[:, wi, 0:3], in_=raw[:, wi, N:N + 3]), r=[R_raw[wi]], w=[R_raw[wi]])
            for wi, (dst, Rd, sc) in enumerate(((qnT, R_qnT, 128 ** -0.5), (knT, R_knT, 1.0))):
                P.op("act", lambda e: e.activation(out=sq[:, 0:N], in_=sil[:, wi, 0:N], func=AF.Square), r=[R_sil[wi]], w=[R_sq])
                P.op("pe", lambda e: e.matmul(out=bk[6][:, 0:N], lhsT=onesf[:, :], rhs=sq[:, 0:N], start=True, stop=True), r=[R_onesf, R_sq], w=[R_bk[6]])
                P.op("dve", lambda e: e.tensor_scalar(out=rn[:, 0:N], in0=bk[6][:, 0:N], scalar1=EPS, scalar2=None, op0=ALU.add), r=[R_bk[6]], w=[R_rn])
                P.op("act", lambda e: e.activation(out=rn[:, 0:N], in_=rn[:, 0:N], func=AF.Sqrt), r=[R_rn], w=[R_rn])
                P.op("dve", lambda e: e.reciprocal(out=rn[:, 0:N], in_=rn[:, 0:N]), r=[R_rn], w=[R_rn])
                P.op("dve", lambda e: e.scalar_tensor_tensor(out=dst[:, 0:N], in0=sil[:, wi, 0:N], scalar=sc, in1=rn[:, 0:N], op0=ALU.mult, op1=ALU.mult),
                     r=[R_sil[wi], R_rn], w=[Rd])
            P.op("dve", lambda e: e.tensor_tensor(out=kbT[:, 0:N], in0=knT[:, 0:N], in1=bk[4][:, 0:N], op=ALU.mult), r=[R_knT, R_bk[4]], w=[R_kbT])
            P.op("pool", lambda e: e.tensor_copy(out=vT[:, 0:N], in_=sil[:, 2, 0:N]), r=[R_sil[2]], w=[R_vT])

            for tl in range(ntl):
                t = gi * 4 + tl
                tc_ = slice(tl * 128, (tl + 1) * 128)
                cc = lambda i: cols[:, tl, i:i + 1]
                Rc = R_cols[tl]
                P.op("dve", lambda e: e.tensor_scalar(out=Gbc[:, :], in0=onesf[:, :], scalar1=cc(0), scalar2=None, op0=ALU.mult), r=[R_onesf, Rc], w=[R_Gbc])
                P.op("pe", lambda e: e.matmul(out=bk[6][:, 0:130], lhsT=Gbc[:, :], rhs=TriX[:, :], start=True, stop=True), r=[R_Gbc, R_mk], w=[R_bk[6]])
                P.op("pe", lambda e: e.matmul(out=bk[6][:, 256:258], lhsT=TriX[:, 0:128], rhs=cols[:, tl, 0:2], start=True, stop=True), r=[R_mk, Rc], w=[R_bk[6]])
                P.op("dve", lambda e: e.tensor_copy(out=cc(2), in_=bk[6][:, 256:257]), r=[R_bk[6]], w=[Rc])
                P.op("dve", lambda e: e.tensor_scalar(out=cc(6), in0=bk[6][:, 256:257], scalar1=-1.0, scalar2=None, op0=ALU.mult), r=[R_bk[6]], w=[Rc])
                P.op("dve", lambda e: e.tensor_copy(out=cols[0:64, tl, 3:4], in_=bk[6][0:64, 128:129]), r=[R_bk[6]], w=[Rc])
                P.op("dve", lambda e: e.tensor_copy(out=cols[64:128, tl, 3:4], in_=bk[6][64:128, 129:130]), r=[R_bk[6]], w=[Rc])
                P.op("dve", lambda e: e.tensor_tensor(out=cc(4), in0=cc(3), in1=cc(2), op=ALU.subtract), r=[Rc], w=[Rc])
                P.op("act", lambda e: e.activation(out=cc(4), in_=cc(4), func=AF.Exp), r=[Rc], w=[Rc])
                P.op("act", lambda e: e.activation(out=cc(7), in_=cc(2), func=AF.Exp), r=[Rc], w=[Rc])
                P.op("dve", lambda e: e.scalar_tensor_tensor(out=cc(5), in0=cc(7), scalar=-1.0, in1=cc(1), op0=ALU.mult, op1=ALU.mult), r=[Rc], w=[Rc])
                P.op("act", lambda e: e.activation(out=egl[:, :], in_=bk[6][:, 128:130], func=AF.Exp), r=[R_bk[6]], w=[R_egl])
                P.op("act", lambda e: e.activation(out=egb[:, :], in_=bk[6][:, 0:128], func=AF.Exp), r=[R_bk[6]], w=[R_egb])
                P.op("dve", lambda e: e.tensor_scalar(out=Dm[:, :], in0=bk[6][:, 0:128], scalar1=cc(2), scalar2=None, op0=ALU.subtract), r=[R_bk[6], Rc], w=[R_Dm])
                P.op("dve", lambda e: e.tensor_scalar(out=DTm[:, :], in0=Dm[:, :], scalar1=0.0, scalar2=None, op0=ALU.min), r=[R_Dm], w=[R_DTm])
                P.op("dve", lambda e: e.tensor_scalar(out=Dm[:, :], in0=Dm[:, :], scalar1=0.0, scalar2=-1.0, op0=ALU.max, op1=ALU.mult), r=[R_Dm], w=[R_Dm])
                P.op("act", lambda e: e.activation(out=Dm[:, :], in_=Dm[:, :], func=AF.Exp), r=[R_Dm], w=[R_Dm])
                P.op("act", lambda e: e.activation(out=DTm[:, :], in_=DTm[:, :], func=AF.Exp), r=[R_DTm], w=[R_DTm])
                P.op("pool", lambda e: e.tensor_tensor(out=Dm[:, :], in0=Dm[:, :], in1=maskS[:, :], op=ALU.mult), r=[R_Dm, R_mk], w=[R_Dm])
                P.op("pool", lambda e: e.tensor_tensor(out=DTI[:, :], in0=DTm[:, :], in1=maskIT[:, 0:128], op=ALU.mult), r=[R_DTm, R_mk], w=[R_DTI])
                P.op("pool", lambda e: e.tensor_tensor(out=DTm[:, :], in0=DTm[:, :], in1=maskST[:, :], op=ALU.mult), r=[R_DTm, R_mk], w=[R_DTm])
                P.op("dve", lambda e: e.tensor_tensor(out=qg[:, :], in0=qnT[:, tc_], in1=egb[:, :], op=ALU.mult), r=[R_qnT, R_egb], w=[R_qg])
                P.op("pe", lambda e: e.transpose(out=pT[:, 0, :], in_=knT[:, tc_], identity=C.ident[:, :]), r=[R_knT, C.R_ident], w=[R_pT])
                P.op("pe", lambda e: e.transpose(out=pT[:, 1, :], in_=vT[:, tc_], identity=C.ident[:, :]), r=[R_vT, C.R_ident], w=[R_pT])
                for ch in range(2):
                    P.op("dve", lambda e: e.tensor_scalar(out=ktl[:, ch, :], in0=pT[:, 0, :], scalar1=cc(4), scalar2=ind[:, ch:ch + 1], op0=ALU.mult, op1=ALU.mult),
                         r=[R_pT, Rc, R_ind], w=[R_ktl])
                P.op("dve", lambda e: e.tensor_scalar(out=bv[:, :], in0=pT[:, 1, :], scalar1=cc(1), scalar2=None, op0=ALU.mult), r=[R_pT, Rc], w=[R_bv])
                P.op("pe", lambda e: e.matmul(out=bk[0][:, 0:128], lhsT=kbT[:, tc_], rhs=knT[:, tc_], start=True, stop=True), r=[R_kbT, R_knT], w=[R_bk[0]])
                P.op("pe", lambda e: e.matmul(out=bk[0][:, 128:256], lhsT=knT[:, tc_], rhs=kbT[:, tc_], start=True, stop=True), r=[R_kbT, R_knT], w=[R_bk[0]])
                P.op("pe", lambda e: e.matmul(out=bk[0][:, 256:384], lhsT=knT[:, tc_], rhs=qnT[:, tc_], start=True, stop=True), r=[R_qnT, R_knT], w=[R_bk[0]])
                (LF, LTF, L_, LT_, O32, O32T, O64, O64T, X_, XT_, L2_, L2T_, Y_, Y2_) = range(NW)
                W = lambda i: Wk[:, i, :]
                P.op("dve", lambda e: e.tensor_tensor(out=W(LF), in0=bk[0][:, 0:128], in1=Dm[:, :], op=ALU.mult), r=[R_bk[0], R_Dm], w=[R_Wk[LF]])
                P.op("dve", lambda e: e.tensor_tensor(out=W(LTF), in0=bk[0][:, 128:256], in1=DTm[:, :], op=ALU.mult), r=[R_bk[0], R_DTm], w=[R_Wk[LTF]])
                P.op("dve", lambda e: e.tensor_tensor(out=aiT[:, :], in0=bk[0][:, 256:384], in1=DTI[:, :], op=ALU.mult), r=[R_bk[0], R_DTI], w=[R_aiT])
                for (dst, src, mi) in ((L_, LF, 0), (LT_, LTF, 0), (O32, LF, 1), (O32T, LTF, 1), (O64, LF, 2), (O64T, LTF, 2)):
                    P.op("pool", lambda e: e.tensor_tensor(out=W(dst), in0=W(src), in1=mblk[:, mi, :], op=ALU.mult), r=[R_Wk[src], R_mk], w=[R_Wk[dst]])
                P.op("pool", lambda e: e.tensor_tensor(out=W(X_), in0=identf[:, :], in1=W(L_), op=ALU.subtract), r=[R_if, R_Wk[L_]], w=[R_Wk[X_]])
                P.op("pool", lambda e: e.tensor_tensor(out=W(XT_), in0=identf[:, :], in1=W(LT_), op=ALU.subtract), r=[R_if, R_Wk[LT_]], w=[R_Wk[XT_]])

                def mm(out_ap, Rout, li, ri):
                    P.op("pe", lambda e: e.matmul(out=out_ap, lhsT=W(li), rhs=W(ri), start=True, stop=True), r=[R_Wk[li], R_Wk[ri]], w=[Rout])

                cl, clt, nl, nlt = L_, LT_, L2_, L2T_
                for lvl in range(3):
                    last = lvl == 2
                    mm(bk[1][:, 0:128], R_bk[1], clt, cl)
                    if not last:
                        mm(bk[1][:, 128:256], R_bk[1], cl, clt)
                    P.op("act", lambda e: e.activation(out=W(nl), in_=bk[1][:, 0:128], func=AF.Copy), r=[R_bk[1]], w=[R_Wk[nl]])
                    if not last:
                        P.op("act", lambda e: e.activation(out=W(nlt), in_=bk[1][:, 128:256], func=AF.Copy), r=[R_bk[1]], w=[R_Wk[nlt]])
                    mm(bk[2][:, 0:128], R_bk[2], nl, XT_)
                    mm(bk[2][:, 128:256], R_bk[2], XT_, nl)
                    P.op("dve", lambda e: e.tensor_tensor(out=W(XT_), in0=bk[2][:, 0:128], in1=W(XT_), op=ALU.add), r=[R_bk[2], R_Wk[XT_]], w=[R_Wk[XT_]])
                    P.op("dve", lambda e: e.tensor_tensor(out=W(X_), in0=bk[2][:, 128:256], in1=W(X_), op=ALU.add), r=[R_bk[2], R_Wk[X_]], w=[R_Wk[X_]])
                    cl, clt, nl, nlt = nl, nlt, cl, clt
                mm(bk[1][:, 0:128], R_bk[1], O32T, X_)
                mm(bk[1][:, 128:256], R_bk[1], O32, XT_)
                P.op("act", lambda e: e.activation(out=W(Y_), in_=bk[1][:, 0:128], func=AF.Copy), r=[R_bk[1]], w=[R_Wk[Y_]])
                P.op("act", lambda e: e.activation(out=W(Y2_), in_=bk[1][:, 128:256], func=AF.Copy), r=[R_bk[1]], w=[R_Wk[Y2_]])
                mm(bk[2][:, 0:128], R_bk[2], XT_, Y_)
                mm(bk[2][:, 128:256], R_bk[2], X_, Y2_)
                P.op("dve", lambda e: e.tensor_tensor(out=W(X_), in0=W(X_), in1=bk[2][:, 0:128], op=ALU.subtract), r=[R_bk[2], R_Wk[X_]], w=[R_Wk[X_]])
                P.op("dve", lambda e: e.tensor_tensor(out=W(XT_), in0=W(XT_), in1=bk[2][:, 128:256], op=ALU.subtract), r=[R_bk[2], R_Wk[XT_]], w=[R_Wk[XT_]])
                mm(bk[1][:, 0:128], R_bk[1], O64, XT_)
                P.op("act", lambda e: e.activation(out=W(Y2_), in_=bk[1][:, 0:128], func=AF.Copy), r=[R_bk[1]], w=[R_Wk[Y2_]])
                mm(bk[2][:, 0:128], R_bk[2], X_, Y2_)
                P.op("dve", lambda e: e.tensor_tensor(out=ATb[:, :], in0=W(XT_), in1=bk[2][:, 0:128], op=ALU.subtract), r=[R_bk[2], R_Wk[XT_]], w=[R_ATb])
                AT = ATb
                R_AT = R_ATb
                for ch in range(2):
                    ps_ = slice(ch * 64, (ch + 1) * 64)
                    pb = bk[3] if ch == 0 else bk[5]
                    R_pb = R_bk[3] if ch == 0 else R_bk[5]
                    P.op("pe", lambda e: e.matmul(out=pb[:, 0:128], lhsT=knT[:, tc_], rhs=Sbf[:, :], start=True, stop=True), r=[R_knT, R_Sbf], w=[R_pb])
                    P.op("dve", lambda e: e.scalar_tensor_tensor(out=rt[:, :], in0=pb[:, 0:128], scalar=cc(5), in1=bv[:, :], op0=ALU.mult, op1=ALU.add),
                         r=[R_pb, Rc, R_bv], w=[R_rt])
                    P.op("pe", lambda e: e.matmul(out=pb[:, 128:256], lhsT=AT[:, :], rhs=rt[:, :], start=True, stop=True), r=[R_AT, R_rt], w=[R_pb])
                    P.op("act", lambda e: e.activation(out=vn[:, :], in_=pb[:, 128:256], func=AF.Copy), r=[R_pb], w=[R_vn])
                    P.op("pe", lambda e: e.matmul(out=bk[4][:, ch * 128:(ch + 1) * 128], lhsT=qg[:, :], rhs=Sbf[:, :], start=True, stop=False), r=[R_qg, R_Sbf], w=[R_bk[4]])
                    P.op("pe", lambda e: e.matmul(out=bk[4][:, ch * 128:(ch + 1) * 128], lhsT=aiT[:, :], rhs=vn[:, :], start=False, stop=True), r=[R_aiT, R_vn], w=[R_bk[4]])
                    P.op("pe", lambda e: e.matmul(out=pb[:, 256:384], lhsT=ktl[:, ch, :], rhs=vn[:, :], start=True, stop=True), r=[R_ktl, R_vn], w=[R_pb])
                    P.op("dve", lambda e: e.scalar_tensor_tensor(out=S32[:, :], in0=S32[:, :], scalar=egl[:, ch:ch + 1], in1=pb[:, 256:384], op0=ALU.mult, op1=ALU.add),
                         r=[R_S32, R_egl, R_pb], w=[R_S32])
                    P.op("act", lambda e: e.activation(out=Sbf[:, :], in_=S32[:, :], func=AF.Copy), r=[R_S32], w=[R_Sbf])
                    P.op("dve", lambda e: e.tensor_copy(out=osb_[ps_, :], in_=bk[4][ps_, ch * 128:(ch + 1) * 128]), r=[R_bk[4]], w=[R_osb_])
                P.op("act", lambda e: e.activation(out=junk[:, 0:128], in_=osb_[:, :], func=AF.Square, accum_out=stt[:, 1:2]), r=[R_osb_], w=[R_junk, R_stt])
                C.rstd(stt[:, 1:2], stt[:, 1:2], R_stt, R_stt, 128)
                P.op("dve", lambda e: e.scalar_tensor_tensor(out=ogs[:, :], in0=osb_[:, :], scalar=stt[:, 1:2], in1=gz[:, tl, :], op0=ALU.mult, op1=ALU.mult),
                     r=[R_osb_, R_stt, R_gz], w=[R_ogs])
                P.dma("pool", og_d[t * 128:(t + 1) * 128, :], ogs[:, :], r=[R_ogs])
        P.finish()
    return nc


def run_mix_odd(hs_pad, I):
    nc = get_prog("mix_odd", build_mix_odd)
    w = I["w_in_odd"][0]
    cwv = I["dn_conv_w"][0]
    in_maps = []
    for h in range(NCORES):
        sl = lambda base: slice(base + h * 128, base + (h + 1) * 128)
        zc = np.zeros((D, 1), np.float32)
        wh = np.concatenate([w[:, sl(0)], w[:, sl(1024)], w[:, sl(2048)], w[:, sl(3072)], w[:, 4096 + h:4097 + h], zc, w[:, 4104 + h:4105 + h], zc], axis=1)
        cw = np.concatenate([cwv[:, sl(0)].T, cwv[:, sl(1024)].T, cwv[:, sl(2048)].T], axis=1)
        in_maps.append({"hs": hs_pad, "wh": np.ascontiguousarray(wh), "g_pre": I["pre_mix_norm"][1], "convw": np.ascontiguousarray(cw),
                        "a_log": np.ascontiguousarray(I["dn_a_log"][0, h:h + 1]), "dt_bias": np.ascontiguousarray(I["dn_dt_bias"][0, h:h + 1]),
                        "g_dn": I["dn_out_norm"][0]})
    res = run_bass_kernel_spmd(nc, in_maps, core_ids=list(range(NCORES)))
    return np.ascontiguousarray(np.concatenate([r["og"] for r in res.results], axis=1))


def kernel(**inputs):
    I = {k: np.ascontiguousarray(np.asarray(v, dtype=np.float32)) for k, v in inputs.items()}
    x = I["x"][0]
    hs0 = np.ascontiguousarray(np.concatenate([np.zeros((PADF, D), np.float32), I["meta_tokens"], x], axis=0))
    osb, ys = run_mix_even(hs0, I)
    W0 = {"wglu": I["s5_w_glu"][0], "bglu": I["s5_b_glu"][0],
          "gmerge": np.ascontiguousarray(np.concatenate([I["sb_out_norm"][0], I["s5_out_norm"][0]])),
          "wout": I["w_out_even"][0], "w1": I["mlp_w1"][0], "w2": I["mlp_w2"][0],
          "g_postmix": I["post_mix_norm"][0], "g_premlp": I["pre_mlp_norm"][0], "g_postmlp": I["post_mlp_norm"][0]}
    hs1 = run_post("even", hs0, (osb, ys), W0)
    og = run_mix_odd(hs1, I)
    W1 = {"wout": I["w_out_odd"][0], "w1": I["mlp_w1"][1], "w2": I["mlp_w2"][1],
          "g_postmix": I["post_mix_norm"][1], "g_premlp": I["pre_mlp_norm"][1], "g_postmlp": I["post_mlp_norm"][1]}
    out = run_post("odd", hs1, og, W1)
    return np.ascontiguousarray(out[128:].reshape(1, 16384, D).astype(np.float32))
```

```python
from contextlib import ExitStack
import numpy as np
import concourse.bass as bass
import concourse.mybir as mybir
from concourse.bass_utils import run_bass_kernel_spmd

F32 = mybir.dt.float32
BF16 = mybir.dt.bfloat16
ALU = mybir.AluOpType
AF = mybir.ActivationFunctionType

NCORES = 8
D = 1024
DFF = 4096
NT = 129
LP = NT * 128
PADF = 112
NMETA = 16
TPC = 17
EPS = 1e-6


class Res:
    __slots__ = ("name", "lw", "rd", "psum")

    def __init__(self, name="", psum=False):
        self.name = name
        self.lw = None
        self.rd = {}
        self.psum = psum


class Prog:
    ENG = ("pe", "act", "dve", "pool", "sp")
    NDMA = 6

    def __init__(self, nc, es):
        self.nc = nc
        self.es = es
        self.lists = {k: [] for k in self.ENG}
        self.count = {k: 0 for k in self.ENG}
        self.sem = {}
        for k in self.ENG:
            self.sem[k] = es.enter_context(nc.semaphore("s_" + k))
        self.dsem = {}
        self.dcount = {}
        for q in ("sp", "pool", "act"):
            self.dsem[q] = [es.enter_context(nc.semaphore(f"d_{q}{i}")) for i in range(self.NDMA)]
            self.dcount[q] = 0
        self.waited = {}
        self.eobj = {"pe": nc.tensor, "act": nc.scalar, "dve": nc.vector, "pool": nc.gpsimd, "sp": nc.sync}

    def sb(self, name, shape, dt):
        return self.es.enter_context(self.nc.sbuf_tensor(name, list(shape), dt))

    def ps(self, name, shape, dt):
        return self.es.enter_context(self.nc.psum_tensor(name, list(shape), dt))

    def _semobj(self, key):
        if isinstance(key, tuple):
            return self.dsem[key[0]][key[1]]
        return self.sem[key]

    def _wait(self, eng, key, val):
        if key == eng and eng == "pe":
            return
        k = (eng, key)
        if self.waited.get(k, 0) >= val:
            return
        self.waited[k] = val
        so = self._semobj(key)
        self.eobj[eng].wait_ge(so, val)

    def _deps(self, eng, r, w):
        for x in r:
            if x.lw is not None:
                self._wait(eng, *x.lw)
            if x.psum:
                for key, val in x.rd.items():
                    if key != eng:
                        self._wait(eng, key, val)
        for x in w:
            if x.lw is not None:
                self._wait(eng, *x.lw)
            for key, val in x.rd.items():
                self._wait(eng, key, val)

    def _mark(self, tok, r, w):
        key, val = tok
        for x in r:
            if x.rd.get(key, 0) < val:
                x.rd[key] = val
        for x in w:
            x.lw = tok
            x.rd = {}

    def op(self, eng, fn, r=(), w=()):
        self._deps(eng, r, w)
        self.count[eng] += 1
        seq = self.count[eng]
        so = self.sem[eng]
        fn(self.eobj[eng]).then_inc(so, 1)
        self._mark((eng, seq), r, w)

    def dma(self, q, out, in_, r=(), w=(), **kw):
        i = self.dcount[q]
        self.dcount[q] += 1
        slot = i % self.NDMA
        key = (q, slot)
        val = 16 * (i // self.NDMA + 1)
        if i >= self.NDMA:
            self._wait(q, key, val - 16)
        self._deps(q, r, w)
        so = self.dsem[q][slot]
        self.eobj[q].dma_start(out=out, in_=in_, **kw).then_inc(so, 16)
        self._mark((key, val), r, w)

    def barrier(self):
        for e in self.ENG:
            for o in self.ENG:
                if o != e and self.count[o] > 0:
                    self._wait(e, o, self.count[o])
            for q in ("sp", "pool", "act"):
                n = self.dcount[q]
                for slot in range(min(n, self.NDMA)):
                    last_i = ((n - 1 - slot) // self.NDMA) * self.NDMA + slot
                    self._wait(e, (q, slot), 16 * (last_i // self.NDMA + 1))

    def finish(self):
        for q in ("sp", "pool", "act"):
            n = self.dcount[q]
            for slot in range(min(n, self.NDMA)):
                last_i = ((n - 1 - slot) // self.NDMA) * self.NDMA + slot
                self._wait(q, (q, slot), 16 * (last_i // self.NDMA + 1))


class Ctx:
    def __init__(self, nc, es):
        self.P = Prog(nc, es)
        self.nc = nc
        P = self.P
        self.ident = P.sb("ident", [128, 128], BF16)
        self.R_ident = Res("ident")
        P.op("pool", lambda e: e.memset(self.ident[:, :], 1.0), w=[self.R_ident])
        P.op("pool", lambda e: e.affine_select(out=self.ident[:, :], in_=self.ident[:, :], pattern=[[-1, 128]],
                                                 compare_op=ALU.is_equal, fill=0.0, base=0, channel_multiplier=1),
             r=[self.R_ident], w=[self.R_ident])
        self.ones_bf = P.sb("ones_bf", [128, 128], BF16)
        self.R_ones = Res("ones")
        P.op("pool", lambda e: e.memset(self.ones_bf[:, :], 1.0), w=[self.R_ones])
        self.rr = 0

    def rstd(self, ss_ap, out_ap, R_ss, R_out, n, eng2="dve"):
        P = self.P
        P.op("dve", lambda e: e.tensor_scalar(out=out_ap, in0=ss_ap, scalar1=1.0 / n, scalar2=EPS,
                                               op0=ALU.mult, op1=ALU.add), r=[R_ss], w=[R_out])
        P.op("act", lambda e: e.activation(out=out_ap, in_=out_ap, func=AF.Sqrt), r=[R_out], w=[R_out])
        P.op("dve", lambda e: e.reciprocal(out=out_ap, in_=out_ap), r=[R_out], w=[R_out])

    def cast_eng(self):
        self.rr += 1
        return ("dve", "pool", "act")[self.rr % 3]

    def load_weight(self, dram, wb, R_wb, stage, R_stage, nk, ncols, gsc=None, R_g=None, col0=0, colw=None):
        P = self.P
        idx = 0
        for k in range(nk):
            for c0 in range(0, ncols, 1024):
                cw = min(1024, ncols - c0)
                st, Rs = stage[idx % len(stage)], R_stage[idx % len(stage)]
                idx += 1
                P.dma("sp", st[:, 0:cw], dram[k * 128:(k + 1) * 128, col0 + c0:col0 + c0 + cw], w=[Rs])
                eng = self.cast_eng()
                rs = [Rs] + ([R_g] if gsc is not None else [])
                if gsc is not None:
                    if eng == "act":
                        P.op("act", lambda e, st=st, k=k, c0=c0, cw=cw: e.activation(
                            out=wb[:, k, c0:c0 + cw], in_=st[:, 0:cw], func=AF.Copy, scale=gsc[:, k:k + 1]), r=rs, w=[R_wb])
                    else:
                        P.op(eng, lambda e, st=st, k=k, c0=c0, cw=cw: e.tensor_scalar(
                            out=wb[:, k, c0:c0 + cw], in0=st[:, 0:cw], scalar1=gsc[:, k:k + 1], scalar2=None,
                            op0=ALU.mult), r=rs, w=[R_wb])
                else:
                    if eng == "act":
                        P.op("act", lambda e, st=st, k=k, c0=c0, cw=cw: e.activation(
                            out=wb[:, k, c0:c0 + cw], in_=st[:, 0:cw], func=AF.Copy), r=rs, w=[R_wb])
                    else:
                        P.op(eng, lambda e, st=st, k=k, c0=c0, cw=cw: e.tensor_copy(
                            out=wb[:, k, c0:c0 + cw], in_=st[:, 0:cw]), r=rs, w=[R_wb])


def build_post(kind):
    even = kind == "even"
    nc = bass.Bass("TRN2", target_bir_lowering=False)
    dr = lambda n, s, k="ExternalInput": nc.dram_tensor(n, list(s), F32, kind=k).ap()
    rows = TPC * 128
    hs_d = dr("hs", [rows, D])
    if even:
        osb_d = dr("osb", [rows, 512])
        ys5_d = dr("ys5", [rows, 512])
        wglu_d = dr("wglu", [512, 512])
        bglu_d = dr("bglu", [512])
        gmerge_d = dr("gmerge", [D])
    else:
        og_d = dr("og", [rows, D])
    wout_d = dr("wout", [D, D])
    w1_d = dr("w1", [D, DFF])
    w2_d = dr("w2", [DFF, D])
    g1_d = dr("g_postmix", [D])
    g2_d = dr("g_premlp", [D])
    g3_d = dr("g_postmlp", [D])
    out_d = dr("out", [rows, D], "ExternalOutput")

    with ExitStack() as es:
        C = Ctx(nc, es)
        P = C.P
        w1b = P.sb("w1b", [128, 8, DFF], BF16)
        w2b = P.sb("w2b", [128, 32, D], BF16)
        woutb = P.sb("woutb", [128, 8, D], BF16)
        R_w1, R_w2, R_wout = Res("w1"), Res("w2"), Res("wout")
        stage = [P.sb("stage0", [128, 1024], F32), P.sb("stage1", [128, 1024], F32)]
        R_stage = [Res("st0"), Res("st1")]
        gsm = P.sb("gsm", [128, 3, 8], F32)
        R_gsm = Res("gsm")
        G1 = P.sb("G1", [128, D], F32)
        G3 = P.sb("G3", [128, D], F32)
        R_G = Res("G")
        hs = P.sb("hs_t", [128, D], F32)
        mg = P.sb("mg_t", [128, D], F32)
        xn = P.sb("xn_t", [128, D], BF16)
        xnT = P.sb("xnT_t", [128, 8, 128], BF16)
        h1T = P.sb("h1T_t", [128, 32, 128], BF16)
        rl = P.sb("rl_t", [128, 512], BF16)
        tmp = P.sb("tmp_t", [128, 512], F32)
        st = P.sb("stat", [128, 8], F32)
        R_hs, R_mg, R_xn, R_xnT, R_h1T, R_rl, R_tmp = (Res(n) for n in ("hs", "mg", "xn", "xnT", "h1T", "rl", "tmp"))
        R_st = [Res(f"st{i}") for i in range(8)]
        pT = P.ps("pT", [128, 8, 128], BF16)
        pA = P.ps("pA", [128, 512], F32)
        pB = P.ps("pB", [128, 512], F32)
        pH = [P.ps("pH0", [128, 4, 128], F32), P.ps("pH1", [128, 4, 128], F32)]
        R_pT, R_pA, R_pB = Res("pT", True), Res("pA", True), Res("pB", True)
        R_pH = [Res("pH0", True), Res("pH1", True)]
        junk = stage[0]
        R_junk = R_stage[0]

        P.dma("sp", G1[:, :], g1_d.partition_broadcast(128), w=[R_G])
        P.dma("sp", G3[:, :], g3_d.partition_broadcast(128), w=[R_G])
        P.dma("sp", gsm[:, 0, :], g2_d.rearrange("(k p) -> p k", p=128), w=[R_gsm], allow_slow_non_contiguous=True)
        if even:
            P.dma("sp", gsm[:, 1, :], gmerge_d.rearrange("(k p) -> p k", p=128), w=[R_gsm], allow_slow_non_contiguous=True)
            wglub = P.sb("wglub", [128, 4, 512], BF16)
            bglub = P.sb("bglub", [1, 512], BF16)
            R_wglu, R_bglu = Res("wglu"), Res("bglu")
            P.dma("sp", tmp[0:1, :], bglu_d.rearrange("(o n) -> o n", o=1), w=[R_tmp])
            P.op("dve", lambda e: e.tensor_copy(out=bglub[:, :], in_=tmp[0:1, :]), r=[R_tmp], w=[R_bglu])
            C.load_weight(wglu_d, wglub, R_wglu, stage, R_stage, 4, 512)
            C.load_weight(wout_d, woutb, R_wout, stage, R_stage, 8, D, gsc=gsm[:, 1, :], R_g=R_gsm)
        else:
            C.load_weight(wout_d, woutb, R_wout, stage, R_stage, 8, D)
        C.load_weight(w1_d, w1b, R_w1, stage, R_stage, 8, DFF, gsc=gsm[:, 0, :], R_g=R_gsm)
        C.load_weight(w2_d, w2b, R_w2, stage, R_stage, 32, D)

        def transpose8(nblk):
            for k in range(nblk):
                P.op("pe", lambda e, k=k: e.transpose(out=pT[:, k, :], in_=xn[:, k * 128:(k + 1) * 128], identity=C.ident[:, :]),
                     r=[R_xn, C.R_ident], w=[R_pT])
            P.op("dve", lambda e: e.tensor_copy(out=xnT[:, 0:nblk, :], in_=pT[:, 0:nblk, :]), r=[R_pT], w=[R_xnT])

        def norm_residual(G):
            P.op("act", lambda e: e.activation(out=junk[:, 0:512], in_=pA[:, :], func=AF.Square, accum_out=st[:, 0:1]),
                 r=[R_pA], w=[R_junk, R_st[0]])
            P.op("act", lambda e: e.activation(out=junk[:, 512:1024], in_=pB[:, :], func=AF.Square, accum_out=st[:, 1:2]),
                 r=[R_pB], w=[R_junk, R_st[1]])
            P.op("dve", lambda e: e.tensor_tensor(out=st[:, 0:1], in0=st[:, 0:1], in1=st[:, 1:2], op=ALU.add),
                 r=[R_st[0], R_st[1]], w=[R_st[0]])
            C.rstd(st[:, 0:1], st[:, 0:1], R_st[0], R_st[0], D)
            for h, (pp, Rp) in enumerate(((pA, R_pA), (pB, R_pB))):
                sl = slice(h * 512, (h + 1) * 512)
                P.op("dve", lambda e, pp=pp, sl=sl: e.scalar_tensor_tensor(out=tmp[:, :], in0=pp[:, :], scalar=st[:, 0:1], in1=G[:, sl],
                                                                          op0=ALU.mult, op1=ALU.mult),
                     r=[Rp, R_st[0], R_G], w=[R_tmp])
                P.op("pool", lambda e, sl=sl: e.tensor_tensor(out=hs[:, sl], in0=hs[:, sl], in1=tmp[:, :], op=ALU.add),
                     r=[R_hs, R_tmp], w=[R_hs])

        for t in range(TPC):
            rsl = slice(t * 128, (t + 1) * 128)
            P.dma("sp", hs[:, :], hs_d[rsl, :], w=[R_hs])
            if even:
                P.dma("sp", mg[:, 0:512], osb_d[rsl, :], w=[R_mg])
                P.dma("sp", mg[:, 512:1024], ys5_d[rsl, :], w=[R_mg])
                y = mg[:, 512:1024]
                P.op("act", lambda e: e.activation(out=tmp[:, :], in_=y, func=AF.Square), r=[R_mg], w=[R_tmp])
                P.op("dve", lambda e: e.tensor_scalar(out=tmp[:, :], in0=tmp[:, :], scalar1=0.044715, scalar2=1.0, op0=ALU.mult, op1=ALU.add),
                     r=[R_tmp], w=[R_tmp])
                P.op("dve", lambda e: e.tensor_tensor(out=tmp[:, :], in0=tmp[:, :], in1=y, op=ALU.mult), r=[R_tmp, R_mg], w=[R_tmp])
                P.op("act", lambda e: e.activation(out=tmp[:, :], in_=tmp[:, :], func=AF.Sigmoid, scale=1.5957691216057308), r=[R_tmp], w=[R_tmp])
                P.op("dve", lambda e: e.tensor_tensor(out=y, in0=tmp[:, :], in1=y, op=ALU.mult), r=[R_tmp, R_mg], w=[R_mg])
                P.op("act", lambda e: e.activation(out=xn[:, 0:512], in_=y, func=AF.Copy), r=[R_mg], w=[R_xn])
                transpose8(4)
                for k in range(4):
                    P.op("pe", lambda e, k=k: e.matmul(out=pA[:, :], lhsT=xnT[:, k, :], rhs=wglub[:, k, :], start=(k == 0), stop=False),
                         r=[R_xnT, R_wglu], w=[R_pA])
                P.op("pe", lambda e: e.matmul(out=pA[:, :], lhsT=C.ones_bf[0:1, :], rhs=bglub[0:1, :], start=False, stop=True),
                     r=[C.R_ones, R_bglu], w=[R_pA])
                P.op("act", lambda e: e.activation(out=tmp[:, :], in_=pA[:, :], func=AF.Sigmoid), r=[R_pA], w=[R_tmp])
                P.op("dve", lambda e: e.tensor_tensor(out=y, in0=tmp[:, :], in1=y, op=ALU.mult), r=[R_tmp, R_mg], w=[R_mg])
                for h in range(2):
                    sl = slice(h * 512, (h + 1) * 512)
                    P.op("act", lambda e, sl=sl, h=h: e.activation(out=junk[:, sl], in_=mg[:, sl], func=AF.Square, accum_out=st[:, 2 + h:3 + h]),
                         r=[R_mg], w=[R_junk, R_st[2 + h]])
                    C.rstd(st[:, 2 + h:3 + h], st[:, 2 + h:3 + h], R_st[2 + h], R_st[2 + h], 512)
                    P.op("act", lambda e, sl=sl, h=h: e.activation(out=xn[:, sl], in_=mg[:, sl], func=AF.Copy, scale=st[:, 2 + h:3 + h]),
                         r=[R_mg, R_st[2 + h]], w=[R_xn])
            else:
                P.dma("sp", mg[:, :], og_d[rsl, :], w=[R_mg])
                P.op("act", lambda e: e.activation(out=xn[:, :], in_=mg[:, :], func=AF.Copy), r=[R_mg], w=[R_xn])
            transpose8(8)
            for k in range(8):
                P.op("pe", lambda e, k=k: e.matmul(out=pA[:, :], lhsT=xnT[:, k, :], rhs=woutb[:, k, 0:512], start=(k == 0), stop=(k == 7)),
                     r=[R_xnT, R_wout], w=[R_pA])
            for k in range(8):
                P.op("pe", lambda e, k=k: e.matmul(out=pB[:, :], lhsT=xnT[:, k, :], rhs=woutb[:, k, 512:1024], start=(k == 0), stop=(k == 7)),
                     r=[R_xnT, R_wout], w=[R_pB])
            norm_residual(G1)
            P.op("act", lambda e: e.activation(out=junk[:, :], in_=hs[:, :], func=AF.Square, accum_out=st[:, 4:5]), r=[R_hs], w=[R_junk, R_st[4]])
            C.rstd(st[:, 4:5], st[:, 4:5], R_st[4], R_st[4], D)
            P.op("act", lambda e: e.activation(out=xn[:, :], in_=hs[:, :], func=AF.Copy, scale=st[:, 4:5]), r=[R_hs, R_st[4]], w=[R_xn])
            transpose8(8)
            for fg in range(8):
                ph, Rph = pH[fg % 2], R_pH[fg % 2]
                for j in range(4):
                    fb = fg * 4 + j
                    for k in range(8):
                        P.op("pe", lambda e, ph=ph, j=j, fb=fb, k=k: e.matmul(out=ph[:, j, :], lhsT=w1b[:, k, fb * 128:(fb + 1) * 128], rhs=xnT[:, k, :],
                                                                               start=(k == 0), stop=(k == 7)),
                             r=[R_w1, R_xnT], w=[Rph])
                P.op("act", lambda e, ph=ph: e.activation(out=rl[:, :], in_=ph[:, :, :].rearrange("p a b -> p (a b)"), func=AF.Relu), r=[Rph], w=[R_rl])
                P.op("pool", lambda e, fg=fg: e.tensor_tensor(out=h1T[:, fg * 4:(fg + 1) * 4, :].rearrange("p a b -> p (a b)"), in0=rl[:, :], in1=rl[:, :], op=ALU.mult),
                     r=[R_rl], w=[R_h1T])
            for fb in range(32):
                P.op("pe", lambda e, fb=fb: e.matmul(out=pA[:, :], lhsT=h1T[:, fb, :], rhs=w2b[:, fb, 0:512], start=(fb == 0), stop=(fb == 31)),
                     r=[R_h1T, R_w2], w=[R_pA])
            for fb in range(32):
                P.op("pe", lambda e, fb=fb: e.matmul(out=pB[:, :], lhsT=h1T[:, fb, :], rhs=w2b[:, fb, 512:1024], start=(fb == 0), stop=(fb == 31)),
                     r=[R_h1T, R_w2], w=[R_pB])
            norm_residual(G3)
            P.dma("pool", out_d[rsl, :], hs[:, :], r=[R_hs])
        P.finish()
    return nc


def tok_shard(a_pad, c):
    return np.ascontiguousarray(np.concatenate([a_pad[0:128], a_pad[128 * (1 + 16 * c):128 * (17 + 16 * c)]], axis=0))


def tok_unshard(outs):
    parts = [outs[0][0:128]] + [o[128:] for o in outs]
    return np.concatenate(parts, axis=0)


_CACHE = {}


def get_prog(name, builder, *args):
    if name not in _CACHE:
        _CACHE[name] = builder(*args)
    return _CACHE[name]


def run_post(kind, hs_pad, mix_in, W):
    nc = get_prog("post_" + kind, build_post, kind)
    in_maps = []
    for c in range(NCORES):
        m = {"hs": tok_shard(hs_pad, c)}
        if kind == "even":
            m["osb"] = tok_shard(mix_in[0], c)
            m["ys5"] = tok_shard(mix_in[1], c)
        else:
            m["og"] = tok_shard(mix_in, c)
        m.update(W)
        in_maps.append(m)
    res = run_bass_kernel_spmd(nc, in_maps, core_ids=list(range(NCORES)))
    return tok_unshard([r["out"] for r in res.results])


NG = 33


def grp_cols(gi):
    return (gi * 512, 512 if gi < 32 else 128)


def build_mix_even():
    nc = bass.Bass("TRN2", target_bir_lowering=False)
    dr = lambda n, s, k="ExternalInput": nc.dram_tensor(n, list(s), F32, kind=k).ap()
    hs_d = dr("hs", [LP, D])
    wh_d = dr("wh", [D, 384])
    g_d = dr("g_pre", [D])
    lre_d = dr("lam_re", [4, 64])
    lim_d = dr("lam_im", [4, 64])
    ldt_d = dr("log_dt", [4])
    bre_d = dr("b_re", [4, 64, 16])
    bim_d = dr("b_im", [4, 64, 16])
    cre_d = dr("c_re", [4, 16, 64])
    cim_d = dr("c_im", [4, 16, 64])
    dsk_d = dr("d_skip", [64])
    osb_d = dr("osbT", [64, LP], "ExternalOutput")
    ys_d = dr("ysT", [64, LP], "ExternalOutput")

    with ExitStack() as es:
        C = Ctx(nc, es)
        P = C.P
        QQ = P.sb("QQ", [128, LP], BF16)
        KZ = P.sb("KZ", [128, LP], BF16)
        KN = P.sb("KN", [128, LP], BF16)
        R_kinit = Res("kinit")
        P.op("pool", lambda e: e.memset(KZ[64:128, :], 0.0), w=[R_kinit])
        P.op("pool", lambda e: e.memset(KN[64:128, :], 0.0), w=[R_kinit])
        vA = P.sb("vA", [128, NT, 128], BF16)
        P.op("pool", lambda e: e.memset(vA[:, :, :].rearrange("p a b -> p (a b)"), 0.0), w=[R_kinit])
        R_qT = [Res(f"qT{g}") for g in range(NG)]
        R_kT = [Res(f"kT{g}") for g in range(NG)]
        R_v = [Res(f"v{g}") for g in range(NG)]
        esA = ExitStack()
        P.es = esA
        whb = P.sb("whb", [128, 8, 384], BF16)
        R_wh = Res("wh")
        gsm = P.sb("gsm", [128, 8], F32)
        R_gsm = Res("gsm")
        stage = [P.sb("stage0", [128, 1024], F32)]
        R_stage = [Res("st0")]
        banks = [P.ps(f"bk{i}", [128, 512], F32) for i in range(7)]
        R_bk = [Res(f"bk{i}", True) for i in range(7)]
        pT = P.ps("pT", [128, 8, 128], BF16)
        R_pT = Res("pT", True)

        P.dma("sp", gsm[:, :], g_d.rearrange("(k p) -> p k", p=128), w=[R_gsm], allow_slow_non_contiguous=True)
        C.load_weight(wh_d, whb, R_wh, stage, R_stage, 8, 384, gsc=gsm, R_g=R_gsm)

        sp_ = P.sb("s5p", [128, 2, 24], F32)
        R_sp = Res("s5p")
        LR, LI, DT, AR, AI, IR, II, T0, T1, T2, T3, CR, CI, NLR = range(14)
        spi = P.sb("s5pi", [128, 2], mybir.dt.int32)
        col = lambda c: sp_[:, :, c]
        for sb in range(2):
            P.dma("sp", sp_[:, sb, LR:LR + 1], lre_d[2 * sb:2 * sb + 2, :].rearrange("g (n o) -> (g n) o", o=1), w=[R_sp])
            P.dma("sp", sp_[:, sb, LI:LI + 1], lim_d[2 * sb:2 * sb + 2, :].rearrange("g (n o) -> (g n) o", o=1), w=[R_sp])
            for gl in range(2):
                P.dma("sp", sp_[gl * 64:(gl + 1) * 64, sb, DT:DT + 1],
                      ldt_d[2 * sb + gl:2 * sb + gl + 1].rearrange("(o n) -> o n", o=1).partition_broadcast(64), w=[R_sp])
        so = lambda eng, fn: P.op(eng, fn, r=[R_sp], w=[R_sp])
        so("act", lambda e: e.activation(out=col(DT), in_=col(DT), func=AF.Exp))
        so("dve", lambda e: e.tensor_scalar(out=col(LR), in0=col(LR), scalar1=-1e-4, scalar2=None, op0=ALU.min))
        so("dve", lambda e: e.tensor_tensor(out=col(T0), in0=col(LR), in1=col(DT), op=ALU.mult))
        so("dve", lambda e: e.tensor_scalar(out=col(NLR), in0=col(T0), scalar1=-1.0, scalar2=None, op0=ALU.mult))
        so("dve", lambda e: e.tensor_tensor(out=col(T1), in0=col(LI), in1=col(DT), op=ALU.mult))
        so("dve", lambda e: e.tensor_scalar(out=col(T2), in0=col(T1), scalar1=1.0 / (2 * np.pi), scalar2=None, op0=ALU.mult))
        P.op("dve", lambda e: e.tensor_copy(out=spi[:, :], in_=col(T2)), r=[R_sp], w=[R_sp])
        P.op("dve", lambda e: e.tensor_copy(out=col(T2), in_=spi[:, :]), r=[R_sp], w=[R_sp])
        so("dve", lambda e: e.scalar_tensor_tensor(out=col(T1), in0=col(T2), scalar=-2 * np.pi, in1=col(T1), op0=ALU.mult, op1=ALU.add))
        so("dve", lambda e: e.tensor_scalar(out=col(T1), in0=col(T1), scalar1=0.5, scalar2=None, op0=ALU.mult))
        so("dve", lambda e: e.tensor_scalar(out=col(T2), in0=col(T1), scalar1=np.pi / 2, scalar2=None, op0=ALU.add))
        so("act", lambda e: e.activation(out=col(T1), in_=col(T1), func=AF.Sin))
        so("act", lambda e: e.activation(out=col(T2), in_=col(T2), func=AF.Sin))
        so("act", lambda e: e.activation(out=col(T3), in_=col(T0), func=AF.Exp))
        so("dve", lambda e: e.tensor_tensor(out=col(AI), in0=col(T1), in1=col(T2), op=ALU.mult))
        so("dve", lambda e: e.tensor_scalar(out=col(AI), in0=col(AI), scalar1=2.0, scalar2=None, op0=ALU.mult))
        so("dve", lambda e: e.tensor_tensor(out=col(AR), in0=col(T1), in1=col(T1), op=ALU.mult))
        so("dve", lambda e: e.tensor_scalar(out=col(AR), in0=col(AR), scalar1=-2.0, scalar2=1.0, op0=ALU.mult, op1=ALU.add))
        so("act", lambda e: e.activation(out=col(T0), in_=col(NLR), func=AF.Exp))
        so("dve", lambda e: e.tensor_tensor(out=col(IR), in0=col(AR), in1=col(T0), op=ALU.mult))
        so("dve", lambda e: e.tensor_tensor(out=col(II), in0=col(AI), in1=col(T0), op=ALU.mult))
        so("dve", lambda e: e.tensor_scalar(out=col(II), in0=col(II), scalar1=-1.0, scalar2=None, op0=ALU.mult))
        so("dve", lambda e: e.tensor_tensor(out=col(AR), in0=col(AR), in1=col(T3), op=ALU.mult))
        so("dve", lambda e: e.tensor_tensor(out=col(AI), in0=col(AI), in1=col(T3), op=ALU.mult))
        so("dve", lambda e: e.tensor_tensor(out=col(T0), in0=col(LR), in1=col(LR), op=ALU.mult))
        so("dve", lambda e: e.tensor_tensor(out=col(T1), in0=col(LI), in1=col(LI), op=ALU.mult))
        so("dve", lambda e: e.tensor_tensor(out=col(T0), in0=col(T0), in1=col(T1), op=ALU.add))
        so("dve", lambda e: e.reciprocal(out=col(T0), in_=col(T0)))
        so("dve", lambda e: e.tensor_scalar(out=col(T1), in0=col(AR), scalar1=-1.0, scalar2=None, op0=ALU.add))
        so("dve", lambda e: e.tensor_tensor(out=col(T2), in0=col(T1), in1=col(LR), op=ALU.mult))
        so("dve", lambda e: e.tensor_tensor(out=col(T3), in0=col(AI), in1=col(LI), op=ALU.mult))
        so("dve", lambda e: e.tensor_tensor(out=col(T2), in0=col(T2), in1=col(T3), op=ALU.add))
        so("dve", lambda e: e.tensor_tensor(out=col(CR), in0=col(T2), in1=col(T0), op=ALU.mult))
        so("dve", lambda e: e.tensor_tensor(out=col(T2), in0=col(AI), in1=col(LR), op=ALU.mult))
        so("dve", lambda e: e.tensor_tensor(out=col(T3), in0=col(T1), in1=col(LI), op=ALU.mult))
        so("dve", lambda e: e.tensor_tensor(out=col(T2), in0=col(T2), in1=col(T3), op=ALU.subtract))
        so("dve", lambda e: e.tensor_tensor(out=col(CI), in0=col(T2), in1=col(T0), op=ALU.mult))

        TB = {n: P.sb("tb_" + n, [128, 2, 512], F32) for n in ("pr", "pi", "ir", "ii")}
        R_tb = Res("tb")
        ptmp = P.sb("ptmp", [128, 256], F32)
        R_ptmp = Res("ptmp")
        pw = P.sb("pw", [128, 2, 2, 2], F32)
        R_pw = Res("pw")
        pw2 = P.sb("pw2", [128, 4], F32)
        for sb in range(2):
            for wi, (tr, ti, cr_, ci_) in enumerate((("pr", "pi", AR, AI), ("ir", "ii", IR, II))):
                P.op("pool", lambda e, tr=tr, sb=sb: e.memset(TB[tr][:, sb, 0:1], 1.0), w=[R_tb])
                P.op("pool", lambda e, ti=ti, sb=sb: e.memset(TB[ti][:, sb, 0:1], 0.0), w=[R_tb])
                P.op("dve", lambda e, sb=sb, wi=wi, cr_=cr_: e.tensor_copy(out=pw[:, sb, wi, 0:1], in_=sp_[:, sb, cr_:cr_ + 1]), r=[R_sp], w=[R_pw])
                P.op("dve", lambda e, sb=sb, wi=wi, ci_=ci_: e.tensor_copy(out=pw[:, sb, wi, 1:2], in_=sp_[:, sb, ci_:ci_ + 1]), r=[R_sp], w=[R_pw])
                n = 1
                while n < 512:
                    cr = pw[:, sb, wi, 0:1]
                    ci = pw[:, sb, wi, 1:2]
                    src_r = TB[tr][:, sb, 0:n]
                    src_i = TB[ti][:, sb, 0:n]
                    dst_r = TB[tr][:, sb, n:2 * n]
                    dst_i = TB[ti][:, sb, n:2 * n]
                    tm = ptmp[:, 0:n]
                    P.op("dve", lambda e, tm=tm, src_i=src_i, ci=ci: e.tensor_scalar(out=tm, in0=src_i, scalar1=ci, scalar2=None, op0=ALU.mult),
                         r=[R_tb, R_pw], w=[R_ptmp])
                    P.op("dve", lambda e, dst_r=dst_r, src_r=src_r, cr=cr, tm=tm: e.scalar_tensor_tensor(out=dst_r, in0=src_r, scalar=cr, in1=tm, op0=ALU.mult, op1=ALU.subtract),
                         r=[R_tb, R_pw, R_ptmp], w=[R_tb])
                    P.op("dve", lambda e, tm=tm, src_i=src_i, cr=cr: e.tensor_scalar(out=tm, in0=src_i, scalar1=cr, scalar2=None, op0=ALU.mult),
                         r=[R_tb, R_pw], w=[R_ptmp])
                    P.op("dve", lambda e, dst_i=dst_i, src_r=src_r, ci=ci, tm=tm: e.scalar_tensor_tensor(out=dst_i, in0=src_r, scalar=ci, in1=tm, op0=ALU.mult, op1=ALU.add),
                         r=[R_tb, R_pw, R_ptmp], w=[R_tb])
                    n *= 2
                    if n < 512:
                        P.op("dve", lambda e, cr=cr, ci=ci: e.tensor_tensor(out=pw2[:, 0:1], in0=cr, in1=cr, op=ALU.mult), r=[R_pw], w=[R_ptmp])
                        P.op("dve", lambda e, cr=cr, ci=ci: e.tensor_tensor(out=pw2[:, 1:2], in0=ci, in1=ci, op=ALU.mult), r=[R_pw], w=[R_ptmp])
                        P.op("dve", lambda e, cr=cr, ci=ci: e.tensor_tensor(out=pw2[:, 2:3], in0=cr, in1=ci, op=ALU.mult), r=[R_pw], w=[R_ptmp])
                        P.op("dve", lambda e, cr=cr: e.tensor_tensor(out=cr, in0=pw2[:, 0:1], in1=pw2[:, 1:2], op=ALU.subtract), r=[R_ptmp], w=[R_pw])
                        P.op("dve", lambda e, ci=ci: e.tensor_scalar(out=ci, in0=pw2[:, 2:3], scalar1=2.0, scalar2=None, op0=ALU.mult), r=[R_ptmp], w=[R_pw])

        braw = P.sb("braw", [128, 2, 2, 16], F32)
        R_braw = Res("braw")
        Bfull = P.sb("Bfull", [128, 2, 2, 64], F32)
        R_Bfull = Res("Bfull")
        BbT = P.sb("BbT", [64, 2, 2, 128], F32)
        R_BbT = Res("BbT")
        CT = P.sb("CT", [128, 2, 2, 64], F32)
        R_CT = Res("CT")
        identf = P.sb("identf", [128, 128], F32)
        R_if = Res("identf")
        P.op("dve", lambda e: e.tensor_copy(out=identf[:, :], in_=C.ident[:, :]), r=[C.R_ident], w=[R_if])
        P.op("pool", lambda e: e.memset(Bfull[:, :, :, :].rearrange("p a b c -> p (a b c)"), 0.0), w=[R_Bfull])
        P.op("pool", lambda e: e.memset(CT[:, :, :, :].rearrange("p a b c -> p (a b c)"), 0.0), w=[R_CT])
        for sb in range(2):
            P.dma("sp", braw[:, sb, 0, :], bre_d[2 * sb:2 * sb + 2].rearrange("g n p -> (g n) p"), w=[R_braw])
            P.dma("sp", braw[:, sb, 1, :], bim_d[2 * sb:2 * sb + 2].rearrange("g n p -> (g n) p"), w=[R_braw])
            for gl in range(2):
                g = 2 * sb + gl
                ps_ = slice(gl * 64, (gl + 1) * 64)
                cs = slice(g * 16, (g + 1) * 16)
                P.dma("sp", CT[ps_, sb, 0, cs], cre_d[g].rearrange("p n -> n p"), w=[R_CT], allow_slow_non_contiguous=True)
                P.dma("sp", CT[ps_, sb, 1, cs], cim_d[g].rearrange("p n -> n p"), w=[R_CT], allow_slow_non_contiguous=True)
                crr = sp_[ps_, sb, CR:CR + 1]
                cii = sp_[ps_, sb, CI:CI + 1]
                tm = ptmp[ps_, 0:16]
                P.op("dve", lambda e, tm=tm, sb=sb, ps_=ps_, cii=cii: e.tensor_scalar(out=tm, in0=braw[ps_, sb, 1, :], scalar1=cii, scalar2=None, op0=ALU.mult),
                     r=[R_braw, R_sp], w=[R_ptmp])
                P.op("dve", lambda e, tm=tm, sb=sb, ps_=ps_, cs=cs, crr=crr: e.scalar_tensor_tensor(out=Bfull[ps_, sb, 0, cs], in0=braw[ps_, sb, 0, :], scalar=crr, in1=tm,
                                                                                              op0=ALU.mult, op1=ALU.subtract),
                     r=[R_braw, R_sp, R_ptmp], w=[R_Bfull])
                P.op("dve", lambda e, tm=tm, sb=sb, ps_=ps_, cii=cii: e.tensor_scalar(out=tm, in0=braw[ps_, sb, 0, :], scalar1=cii, scalar2=None, op0=ALU.mult),
                     r=[R_braw, R_sp], w=[R_ptmp])
                P.op("dve", lambda e, tm=tm, sb=sb, ps_=ps_, cs=cs, crr=crr: e.scalar_tensor_tensor(out=Bfull[ps_, sb, 1, cs], in0=braw[ps_, sb, 1, :], scalar=crr, in1=tm,
                                                                                              op0=ALU.mult, op1=ALU.add),
                     r=[R_braw, R_sp, R_ptmp], w=[R_Bfull])
            P.op("dve", lambda e, sb=sb: e.tensor_scalar(out=CT[:, sb, 1, :], in0=CT[:, sb, 1, :], scalar1=-1.0, scalar2=None, op0=ALU.mult), r=[R_CT], w=[R_CT])
            for ri in range(2):
                P.op("pe", lambda e, sb=sb, ri=ri: e.transpose(out=banks[0][0:64, 0:128], in_=Bfull[:, sb, ri, :], identity=identf[:, :]),
                     r=[R_Bfull, R_if], w=[R_bk[0]])
                P.op("dve", lambda e, sb=sb, ri=ri: e.tensor_copy(out=BbT[:, sb, ri, :], in_=banks[0][0:64, 0:128]), r=[R_bk[0]], w=[R_BbT])
        dsk = P.sb("dsk", [64, 1], F32)
        R_dsk = Res("dsk")
        P.dma("sp", dsk[:, :], dsk_d.rearrange("(n o) -> n o", o=1), w=[R_dsk])
        onesf = P.sb("onesf", [128, 512], F32)
        R_onesf = Res("onesf")
        P.op("pool", lambda e: e.memset(onesf[:, :], 1.0), w=[R_onesf])
        carry = P.sb("carry", [128, 2, 2], F32)
        R_carry = Res("carry")
        P.op("pool", lambda e: e.memset(carry[:, :, :].rearrange("p a b -> p (a b)"), 0.0), w=[R_carry])

        hsb = [P.sb("hsb0", [128, D], F32)] * 2
        R_hsb = [Res("hsb0")] * 2
        xn = P.sb("xn", [128, D], BF16)
        R_xn = Res("xn")
        xnT = P.sb("xnT", [128, 8, 512], BF16)
        R_xnT = Res("xnT")
        stt = P.sb("stt", [128, 2], F32)
        R_stt = [Res("stt0"), Res("stt1")]
        uT = P.sb("uT", [64, 512], F32)
        R_uT = Res("uT")
        S5 = {n: P.sb("s5_" + n, [128, 512], F32) for n in ("sre", "sim", "cre", "cim", "t1", "t2", "wre", "wim")}
        R_S5 = {n: Res("s5_" + n) for n in S5}
        S5["xre"], S5["xim"] = S5["sre"], S5["sim"]
        R_S5["xre"], R_S5["xim"] = R_S5["sre"], R_S5["sim"]
        ysb = P.sb("ysb", [64, 512], F32)
        R_ysb = Res("ysb")
        junk = stage[0]
        R_junk = R_stage[0]
        pV, pQ, pK, pU, pBr, pBi, pY = banks
        R_pV, R_pQ, R_pK, R_pU, R_pBr, R_pBi, R_pY = R_bk

        tile_i = 0
        for gi in range(NG):
            c0, ncol = grp_cols(gi)
            ntl = ncol // 128
            for tl in range(ntl):
                t = gi * 4 + tl
                hb, Rh = hsb[tile_i % 2], R_hsb[tile_i % 2]
                tile_i += 1
                P.dma("sp", hb[:, :], hs_d[t * 128:(t + 1) * 128, :], w=[Rh])
                P.op("act", lambda e, hb=hb: e.activation(out=junk[:, :], in_=hb[:, :], func=AF.Square, accum_out=stt[:, 0:1]), r=[Rh], w=[R_junk, R_stt[0]])
                C.rstd(stt[:, 0:1], stt[:, 0:1], R_stt[0], R_stt[0], D)
                P.op("act", lambda e, hb=hb: e.activation(out=xn[:, :], in_=hb[:, :], func=AF.Copy, scale=stt[:, 0:1]), r=[Rh, R_stt[0]], w=[R_xn])
                for k in range(8):
                    P.op("pe", lambda e, k=k: e.transpose(out=pT[:, k, :], in_=xn[:, k * 128:(k + 1) * 128], identity=C.ident[:, :]),
                         r=[R_xn, C.R_ident], w=[R_pT])
                P.op("dve", lambda e, tl=tl: e.tensor_copy(out=xnT[:, :, tl * 128:(tl + 1) * 128], in_=pT[:, :, :]), r=[R_pT], w=[R_xnT])
                for k in range(8):
                    P.op("pe", lambda e, k=k, tl=tl: e.matmul(out=pV[:, 0:64], lhsT=xnT[:, k, tl * 128:(tl + 1) * 128], rhs=whb[:, k, 256:320],
                                                               start=(k == 0), stop=(k == 7)), r=[R_xnT, R_wh], w=[R_pV])
                P.op("act", lambda e, t=t: e.activation(out=vA[:, t, 0:64], in_=pV[:, 0:64], func=AF.Copy), r=[R_pV, R_kinit], w=[R_v[gi]])
            cs = slice(c0, c0 + ncol)
            for (pp, Rp, wc, wn) in ((pQ, R_pQ, 0, 128), (pK, R_pK, 128, 128), (pU, R_pU, 320, 64)):
                for k in range(8):
                    P.op("pe", lambda e, pp=pp, k=k, wc=wc, wn=wn: e.matmul(out=pp[0:wn, 0:ncol], lhsT=whb[:, k, wc:wc + wn], rhs=xnT[:, k, 0:ncol],
                                                                             start=(k == 0), stop=(k == 7)), r=[R_xnT, R_wh], w=[Rp])
            P.op("act", lambda e, cs=cs: e.activation(out=QQ[:, cs], in_=pQ[:, 0:ncol], func=AF.Copy, scale=0.125), r=[R_pQ], w=[R_qT[gi]])
            P.op("dve", lambda e, cs=cs: e.tensor_copy(out=KZ[0:64, cs], in_=pK[0:64, 0:ncol]), r=[R_pK, R_kinit], w=[R_kT[gi]])
            P.op("dve", lambda e, cs=cs: e.tensor_scalar(out=KN[0:64, cs], in0=pK[0:64, 0:ncol], scalar1=-1.0, scalar2=None, op0=ALU.mult), r=[R_pK, R_kinit], w=[R_kT[gi]])
            P.op("act", lambda e: e.activation(out=uT[:, 0:ncol], in_=pU[0:64, 0:ncol], func=AF.Copy), r=[R_pU], w=[R_uT])
            for sb in range(2):
                nn = ncol
                P.op("pe", lambda e, sb=sb: e.matmul(out=pBr[:, 0:nn], lhsT=BbT[:, sb, 0, :], rhs=uT[:, 0:nn], start=True, stop=True), r=[R_BbT, R_uT], w=[R_pBr])
                P.op("pe", lambda e, sb=sb: e.matmul(out=pBi[:, 0:nn], lhsT=BbT[:, sb, 1, :], rhs=uT[:, 0:nn], start=True, stop=True), r=[R_BbT, R_uT], w=[R_pBi])
                A = lambda n: S5[n][:, 0:nn]
                P.op("act", lambda e: e.activation(out=A("sre"), in_=pBr[:, 0:nn], func=AF.Copy), r=[R_pBr], w=[R_S5["sre"]])
                P.op("act", lambda e: e.activation(out=A("sim"), in_=pBi[:, 0:nn], func=AF.Copy), r=[R_pBi], w=[R_S5["sim"]])
                ir_, ii_ = TB["ir"][:, sb, 0:nn], TB["ii"][:, sb, 0:nn]
                pr_, pi_ = TB["pr"][:, sb, 0:nn], TB["pi"][:, sb, 0:nn]
                P.op("dve", lambda e: e.tensor_tensor(out=A("t1"), in0=A("sim"), in1=ii_, op=ALU.mult), r=[R_S5["sim"], R_tb], w=[R_S5["t1"]])
                P.op("dve", lambda e: e.tensor_tensor(out=A("cre"), in0=A("sre"), in1=ir_, op=ALU.mult), r=[R_S5["sre"], R_tb], w=[R_S5["cre"]])
                P.op("dve", lambda e: e.tensor_tensor(out=A("cre"), in0=A("cre"), in1=A("t1"), op=ALU.subtract), r=[R_S5["cre"], R_S5["t1"]], w=[R_S5["cre"]])
                P.op("pool", lambda e: e.tensor_tensor(out=A("t2"), in0=A("sim"), in1=ir_, op=ALU.mult), r=[R_S5["sim"], R_tb], w=[R_S5["t2"]])
                P.op("pool", lambda e: e.tensor_tensor(out=A("cim"), in0=A("sre"), in1=ii_, op=ALU.mult), r=[R_S5["sre"], R_tb], w=[R_S5["cim"]])
                P.op("pool", lambda e: e.tensor_tensor(out=A("cim"), in0=A("cim"), in1=A("t2"), op=ALU.add), r=[R_S5["cim"], R_S5["t2"]], w=[R_S5["cim"]])
                P.op("dve", lambda e, sb=sb: e.tensor_tensor_scan(out=A("wre"), data0=onesf[:, 0:nn], data1=A("cre"), initial=carry[:, sb, 0:1], op0=ALU.mult, op1=ALU.add),
                     r=[R_onesf, R_S5["cre"], R_carry], w=[R_S5["wre"]])
                P.op("dve", lambda e, sb=sb: e.tensor_tensor_scan(out=A("wim"), data0=onesf[:, 0:nn], data1=A("cim"), initial=carry[:, sb, 1:2], op0=ALU.mult, op1=ALU.add),
                     r=[R_onesf, R_S5["cim"], R_carry], w=[R_S5["wim"]])
                P.op("dve", lambda e: e.tensor_tensor(out=A("t1"), in0=A("wim"), in1=pi_, op=ALU.mult), r=[R_S5["wim"], R_tb], w=[R_S5["t1"]])
                P.op("dve", lambda e: e.tensor_tensor(out=A("xre"), in0=A("wre"), in1=pr_, op=ALU.mult), r=[R_S5["wre"], R_tb], w=[R_S5["xre"]])
                P.op("dve", lambda e: e.tensor_tensor(out=A("xre"), in0=A("xre"), in1=A("t1"), op=ALU.subtract), r=[R_S5["xre"], R_S5["t1"]], w=[R_S5["xre"]])
                P.op("pool", lambda e: e.tensor_tensor(out=A("t2"), in0=A("wim"), in1=pr_, op=ALU.mult), r=[R_S5["wim"], R_tb], w=[R_S5["t2"]])
                P.op("pool", lambda e: e.tensor_tensor(out=A("xim"), in0=A("wre"), in1=pi_, op=ALU.mult), r=[R_S5["wre"], R_tb], w=[R_S5["xim"]])
                P.op("pool", lambda e: e.tensor_tensor(out=A("xim"), in0=A("xim"), in1=A("t2"), op=ALU.add), r=[R_S5["xim"], R_S5["t2"]], w=[R_S5["xim"]])
                xer, xei = S5["xre"][:, nn - 1:nn], S5["xim"][:, nn - 1:nn]
                ar_, ai_ = sp_[:, sb, AR:AR + 1], sp_[:, sb, AI:AI + 1]
                P.op("dve", lambda e: e.tensor_tensor(out=pw2[:, 0:1], in0=xei, in1=ai_, op=ALU.mult), r=[R_S5["xim"], R_sp], w=[R_ptmp])
                P.op("dve", lambda e: e.tensor_tensor(out=pw2[:, 1:2], in0=xei, in1=ar_, op=ALU.mult), r=[R_S5["xim"], R_sp], w=[R_ptmp])
                P.op("dve", lambda e, sb=sb: e.scalar_tensor_tensor(out=carry[:, sb, 0:1], in0=xer, scalar=ar_, in1=pw2[:, 0:1], op0=ALU.mult, op1=ALU.subtract),
                     r=[R_S5["xre"], R_sp, R_ptmp], w=[R_carry])
                P.op("dve", lambda e, sb=sb: e.scalar_tensor_tensor(out=carry[:, sb, 1:2], in0=xer, scalar=ai_, in1=pw2[:, 1:2], op0=ALU.mult, op1=ALU.add),
                     r=[R_S5["xre"], R_sp, R_ptmp], w=[R_carry])
                P.op("pe", lambda e, sb=sb: e.matmul(out=pY[0:64, 0:nn], lhsT=CT[:, sb, 0, :], rhs=A("xre"), start=(sb == 0), stop=False), r=[R_CT, R_S5["xre"]], w=[R_pY])
                P.op("pe", lambda e, sb=sb: e.matmul(out=pY[0:64, 0:nn], lhsT=CT[:, sb, 1, :], rhs=A("xim"), start=False, stop=(sb == 1)), r=[R_CT, R_S5["xim"]], w=[R_pY])
            P.op("dve", lambda e: e.scalar_tensor_tensor(out=ysb[:, 0:ncol], in0=uT[:, 0:ncol], scalar=dsk[:, 0:1], in1=pY[0:64, 0:ncol], op0=ALU.mult, op1=ALU.add),
                 r=[R_uT, R_dsk, R_pY], w=[R_ysb])
            P.dma("pool", ys_d[:, cs], ysb[:, 0:ncol], r=[R_ysb])

        P.barrier()
        esA.close()
        P.es = es
        onesf = P.sb("onesfB", [128, 1], F32)
        R_onesf = Res("onesfB")
        P.op("pool", lambda e: e.memset(onesf[:, :], 1.0), w=[R_onesf])
        Tincl = P.sb("Tincl", [128, 128], BF16)
        R_Ti = Res("Tincl")
        P.op("pool", lambda e: e.memset(Tincl[:, :], 1.0), w=[R_Ti])
        P.op("pool", lambda e: e.affine_select(out=Tincl[:, :], in_=Tincl[:, :], pattern=[[-1, 128]], compare_op=ALU.is_ge, fill=0.0, base=0, channel_multiplier=1),
             r=[R_Ti], w=[R_Ti])
        onesb512 = P.sb("onesb512", [128, 512], BF16)
        R_o512 = Res("o512")
        P.op("pool", lambda e: e.memset(onesb512[:, :], 1.0), w=[R_o512])
        masks = P.sb("masks", [128, 4, 512], BF16)
        R_masks = Res("masks")
        for i in range(4):
            P.op("pool", lambda e, i=i: e.affine_select(out=masks[:, i, :], in_=onesb512[:, :], pattern=[[1, 512]], compare_op=ALU.is_gt, fill=0.0,
                                                        base=-128 * i, channel_multiplier=-1), r=[R_o512], w=[R_masks])
        padcol = P.sb("padcol", [128, 1], F32)
        R_pad = Res("padcol")
        P.op("pool", lambda e: e.affine_select(out=padcol[:, :], in_=onesf[:, 0:1], pattern=[[0, 1]], compare_op=ALU.is_ge, fill=0.0, base=-PADF, channel_multiplier=1),
             r=[R_onesf], w=[R_pad])
        NB = 3
        eb = [P.sb(f"eb{i}", [128, 512], F32) for i in range(NB)]
        R_eb = [Res(f"eb{i}") for i in range(NB)]
        spb = [P.sb(f"spb{i}", [128, 512], BF16) for i in range(NB)]
        R_spb = [Res(f"spb{i}") for i in range(NB)]
        wb_ = [P.sb("wbb0", [128, 512], BF16), P.sb("wbb1", [128, 512], BF16)]
        R_wb = [Res("wbb0"), Res("wbb1")]
        S16 = [P.sb(f"S16_{i}", [128, 512], BF16) for i in range(4)]
        R_S16 = [Res(f"S16_{i}") for i in range(4)]
        osb_sb = P.sb("osb_sb", [64, 512], F32)
        R_osb = Res("osb_sb")
        pz = [banks[0], banks[1], banks[5]]
        R_pz = [R_bk[0], R_bk[1], R_bk[5]]
        pc = [banks[2], banks[3]]
        R_pc = [R_bk[2], R_bk[3]]
        po = banks[4]
        R_po = R_bk[4]
        iters = []
        for Q in range(NG):
            jmax = min(4 * Q + 3, NT - 1)
            for j in range(jmax, -1, -1):
                iters.append((Q, j, jmax))

        def maskop(buf, Rbuf, Q, j, nq):
            diag = j >= 4 * Q
            if not (diag or j == 0):
                return
            mk = masks[:, j - 4 * Q, 0:nq] if diag else onesb512[:, 0:nq]
            if j == 0:
                P.op("dve", lambda e: e.scalar_tensor_tensor(out=buf[:, 0:nq], in0=buf[:, 0:nq], scalar=padcol[:, 0:1], in1=mk,
                                                             op0=ALU.mult, op1=ALU.mult), r=[Rbuf, R_pad, R_masks, R_o512], w=[Rbuf])
            else:
                P.op("dve", lambda e: e.tensor_tensor(out=buf[:, 0:nq], in0=buf[:, 0:nq], in1=mk, op=ALU.mult), r=[Rbuf, R_masks], w=[Rbuf])

        def stageA(i):
            Q, j, jmax = iters[i]
            q0, nq = grp_cols(Q)
            qs = slice(q0, q0 + nq)
            ks = slice(j * 128, (j + 1) * 128)
            b = i % NB
            k = jmax - j
            P.op("pe", lambda e: e.matmul(out=pz[b][:, 0:nq], lhsT=KZ[:, ks], rhs=QQ[:, qs], start=True, stop=True),
                 r=[R_kT[j // 4], R_qT[Q]], w=[R_pz[b]])
            P.op("act", lambda e: e.activation(out=eb[b][:, 0:nq], in_=pz[b][:, 0:nq], func=AF.Exp), r=[R_pz[b]], w=[R_eb[b]])
            P.op("act", lambda e: e.activation(out=spb[b][:, 0:nq], in_=eb[b][:, 0:nq], func=AF.Ln, bias=1.0), r=[R_eb[b]], w=[R_spb[b]])
            maskop(spb[b], R_spb[b], Q, j, nq)
            if j > 0:
                if k == 0:
                    P.op("pool", lambda e: e.tensor_copy(out=S16[(k + 1) % 4][:, 0:nq], in_=spb[b][:, 0:nq]), r=[R_spb[b]], w=[R_S16[(k + 1) % 4]])
                else:
                    P.op("pool", lambda e: e.tensor_tensor(out=S16[(k + 1) % 4][:, 0:nq], in0=S16[k % 4][:, 0:nq], in1=spb[b][:, 0:nq], op=ALU.add),
                         r=[R_S16[k % 4], R_spb[b]], w=[R_S16[(k + 1) % 4]])

        def stageB1(i):
            Q, j, jmax = iters[i]
            q0, nq = grp_cols(Q)
            qs = slice(q0, q0 + nq)
            ks = slice(j * 128, (j + 1) * 128)
            b = i % NB
            c = i % 2
            k = jmax - j
            P.op("pe", lambda e: e.matmul(out=pc[c][:, 0:nq], lhsT=Tincl[:, :], rhs=spb[b][:, 0:nq], start=True, stop=False),
                 r=[R_Ti, R_spb[b]], w=[R_pc[c]])
            if k > 0:
                P.op("pe", lambda e: e.matmul(out=pc[c][:, 0:nq], lhsT=C.ones_bf[:, :], rhs=S16[k % 4][:, 0:nq], start=False, stop=False),
                     r=[C.R_ones, R_S16[k % 4]], w=[R_pc[c]])
            P.op("pe", lambda e: e.matmul(out=pc[c][:, 0:nq], lhsT=KN[:, ks], rhs=QQ[:, qs], start=False, stop=True),
                 r=[R_kT[j // 4], R_qT[Q]], w=[R_pc[c]])
            P.op("act", lambda e: e.activation(out=wb_[c][:, 0:nq], in_=pc[c][:, 0:nq], func=AF.Exp, scale=-1.0), r=[R_pc[c]], w=[R_wb[c]])
            maskop(wb_[c], R_wb[c], Q, j, nq)

        def stageB2(i):
            Q, j, jmax = iters[i]
            q0, nq = grp_cols(Q)
            qs = slice(q0, q0 + nq)
            c = i % 2
            P.op("pe", lambda e: e.matmul(out=po[:, 0:nq], lhsT=vA[:, j, :], rhs=wb_[c][:, 0:nq], start=(j == jmax), stop=(j == 0)),
                 r=[R_v[j // 4], R_wb[c]], w=[R_po])
            if j == 0:
                P.op("dve", lambda e: e.tensor_copy(out=osb_sb[:, 0:nq], in_=po[0:64, 0:nq]), r=[R_po], w=[R_osb])
                P.dma("pool", osb_d[:, qs], osb_sb[:, 0:nq], r=[R_osb])

        n_it = len(iters)
        LA = 2
        for i in range(min(LA, n_it)):
            stageA(i)
        for i in range(n_it):
            if i + LA < n_it:
                stageA(i + LA)
            stageB1(i)
            if i > 0:
                stageB2(i - 1)
        stageB2(n_it - 1)
        P.finish()
    return nc


def run_mix_even(hs_pad, I):
    nc = get_prog("mix_even", build_mix_even)
    w = I["w_in_even"][0]
    in_maps = []
    for h in range(NCORES):
        wq, wk = w[:, h * 64:(h + 1) * 64], w[:, 512 + h * 64:512 + (h + 1) * 64]
        wh = np.concatenate([wq, wq, wk, wk, w[:, 1024 + h * 64:1024 + (h + 1) * 64], w[:, 1536 + h * 64:1536 + (h + 1) * 64]], axis=1)
        gs = slice(4 * h, 4 * h + 4)
        in_maps.append({
            "hs": hs_pad, "wh": np.ascontiguousarray(wh), "g_pre": I["pre_mix_norm"][0],
            "lam_re": np.ascontiguousarray(I["s5_lambda_re"][0, gs]), "lam_im": np.ascontiguousarray(I["s5_lambda_im"][0, gs]),
            "log_dt": np.ascontiguousarray(I["s5_log_dt"][0, gs]),
            "b_re": np.ascontiguousarray(I["s5_b_re"][0, gs]), "b_im": np.ascontiguousarray(I["s5_b_im"][0, gs]),
            "c_re": np.ascontiguousarray(I["s5_c_re"][0, gs]), "c_im": np.ascontiguousarray(I["s5_c_im"][0, gs]),
            "d_skip": np.ascontiguousarray(I["s5_d"][0, h * 64:(h + 1) * 64]),
        })
    res = run_bass_kernel_spmd(nc, in_maps, core_ids=list(range(NCORES)))
    osb = np.concatenate([r["osbT"].T for r in res.results], axis=1)
    ys = np.concatenate([r["ysT"].T for r in res.results], axis=1)
    return np.ascontiguousarray(osb), np.ascontiguousarray(ys)


def build_mix_odd(ng_limit=NG):
    nc = bass.Bass("TRN2", target_bir_lowering=False)
    dr = lambda n, s, k="ExternalInput": nc.dram_tensor(n, list(s), F32, kind=k).ap()
    hs_d = dr("hs", [LP, D])
    wh_d = dr("wh", [D, 516])
    g_d = dr("g_pre", [D])
    cw_d = dr("convw", [128, 12])
    alog_d = dr("a_log", [1])
    dtb_d = dr("dt_bias", [1])
    gdn_d = dr("g_dn", [128])
    og_d = dr("og", [LP, 128], "ExternalOutput")

    with ExitStack() as es:
        C = Ctx(nc, es)
        P = C.P
        whb = P.sb("whb", [128, 8, 516], BF16)
        R_wh = Res("wh")
        gsm = P.sb("gsm", [128, 8], F32)
        R_gsm = Res("gsm")
        stage = [P.sb("stage0", [128, 1024], F32), P.sb("stage1", [128, 1024], F32)]
        R_stage = [Res("st0"), Res("st1")]
        bk = [P.ps(f"bk{i}", [128, 512], F32) for i in range(7)]
        R_bk = [Res(f"bk{i}", True) for i in range(7)]
        pT = P.ps("pT", [128, 8, 128], BF16)
        R_pT = Res("pT", True)
        P.dma("sp", gsm[:, :], g_d.rearrange("(k p) -> p k", p=128), w=[R_gsm], allow_slow_non_contiguous=True)
        C.load_weight(wh_d, whb, R_wh, stage, R_stage, 8, 516, gsc=gsm, R_g=R_gsm)
        cst = P.sb("cst", [128, 16], F32)
        R_cst = Res("cst")
        P.dma("sp", cst[:, 0:12], cw_d, w=[R_cst])
        P.dma("sp", cst[:, 12:13], dtb_d.rearrange("(o n) -> o n", o=1).partition_broadcast(128), w=[R_cst])
        P.dma("sp", cst[:, 13:14], alog_d.rearrange("(o n) -> o n", o=1).partition_broadcast(128), w=[R_cst])
        P.op("act", lambda e: e.activation(out=cst[:, 13:14], in_=cst[:, 13:14], func=AF.Exp), r=[R_cst], w=[R_cst])
        P.op("dve", lambda e: e.tensor_scalar(out=cst[:, 13:14], in0=cst[:, 13:14], scalar1=-1.0, scalar2=None, op0=ALU.mult), r=[R_cst], w=[R_cst])
        Gdn = P.sb("Gdn", [128, 128], F32)
        R_Gdn = Res("Gdn")
        P.dma("sp", Gdn[:, :], gdn_d.partition_broadcast(128), w=[R_Gdn])
        onesf = P.sb("onesf", [128, 128], F32)
        R_onesf = Res("onesf")
        P.op("pool", lambda e: e.memset(onesf[:, :], 1.0), w=[R_onesf])
        identf = P.sb("identf", [128, 128], F32)
        R_if = Res("identf")
        P.op("dve", lambda e: e.tensor_copy(out=identf[:, :], in_=C.ident[:, :]), r=[C.R_ident], w=[R_if])
        maskS = P.sb("maskS", [128, 128], F32)
        maskST = P.sb("maskST", [128, 128], F32)
        TriX = P.sb("TriX", [128, 130], F32)
        R_mk = Res("masks")
        P.op("pool", lambda e: e.affine_select(out=maskS[:, :], in_=onesf[:, :], pattern=[[-1, 128]], compare_op=ALU.is_gt, fill=0.0, base=0, channel_multiplier=1),
             r=[R_onesf], w=[R_mk])
        P.op("pool", lambda e: e.memset(maskS[64:128, 0:64], 0.0), r=[R_mk], w=[R_mk])
        P.op("pool", lambda e: e.affine_select(out=maskST[:, :], in_=onesf[:, :], pattern=[[1, 128]], compare_op=ALU.is_gt, fill=0.0, base=0, channel_multiplier=-1),
             r=[R_onesf], w=[R_mk])
        P.op("pool", lambda e: e.memset(maskST[0:64, 64:128], 0.0), r=[R_mk], w=[R_mk])
        P.op("pool", lambda e: e.affine_select(out=TriX[:, 0:128], in_=onesf[:, :], pattern=[[1, 128]], compare_op=ALU.is_ge, fill=0.0, base=0, channel_multiplier=-1),
             r=[R_onesf], w=[R_mk])
        P.op("pool", lambda e: e.memset(TriX[0:64, 64:128], 0.0), r=[R_mk], w=[R_mk])
        P.op("pool", lambda e: e.memset(TriX[:, 128:130], 0.0), r=[R_mk], w=[R_mk])
        P.op("pool", lambda e: e.memset(TriX[0:64, 128:129], 1.0), r=[R_mk], w=[R_mk])
        P.op("pool", lambda e: e.memset(TriX[64:128, 129:130], 1.0), r=[R_mk], w=[R_mk])
        maskIT = TriX
        mblk = P.sb("mblk", [128, 3, 128], F32)
        for bi, bs in enumerate((16, 32, 64)):
            nb_ = 128 // bs
            P.op("pool", lambda e: e.affine_select(out=mblk[:, bi, :], in_=onesf[:, :], pattern=[[-bs, nb_], [0, bs]], compare_op=ALU.is_ge, fill=0.0,
                                                   base=0, channel_multiplier=1), r=[R_onesf, R_mk], w=[R_mk])
            P.op("pool", lambda e: e.affine_select(out=mblk[:, bi, :], in_=mblk[:, bi, :], pattern=[[bs, nb_], [0, bs]], compare_op=ALU.is_ge, fill=0.0,
                                                   base=bs - 1, channel_multiplier=-1), r=[R_mk], w=[R_mk])
        P.op("pool", lambda e: e.tensor_tensor(out=mblk[:, 2, :], in0=mblk[:, 2, :], in1=mblk[:, 1, :], op=ALU.subtract), r=[R_mk], w=[R_mk])
        P.op("pool", lambda e: e.tensor_tensor(out=mblk[:, 1, :], in0=mblk[:, 1, :], in1=mblk[:, 0, :], op=ALU.subtract), r=[R_mk], w=[R_mk])

        hsb = [P.sb("hsb0", [128, D], F32), P.sb("hsb1", [128, D], F32)]
        R_hsb = [Res("hsb0"), Res("hsb1")]
        xn = P.sb("xn", [128, D], BF16)
        R_xn = Res("xn")
        xnT = P.sb("xnT", [128, 8, 512], BF16)
        R_xnT = Res("xnT")
        stt = P.sb("stt", [128, 4], F32)
        R_stt = Res("stt")
        raw = P.sb("raw", [128, 3, 515], F32)
        R_raw = [Res("rawq"), Res("rawk"), Res("rawv")]
        P.op("pool", lambda e: e.memset(raw[:, :, :].rearrange("p a b -> p (a b)"), 0.0), w=R_raw)
        acc = P.sb("acc", [128, 512], F32)
        R_acc = Res("acc")
        sil = P.sb("sil", [128, 3, 512], F32)
        R_sil = [Res("silq"), Res("silk"), Res("silv")]
        sq = P.sb("sq", [128, 512], F32)
        R_sq = Res("sq")
        rn = P.sb("rn", [128, 512], F32)
        R_rn = Res("rn")
        qnT = P.sb("qnT", [128, 512], BF16)
        knT2 = [P.sb("knT0", [128, 512], BF16), P.sb("knT1", [128, 512], BF16)]
        R_knT2 = [Res("knT0"), Res("knT1")]
        kbT = P.sb("kbT", [128, 512], BF16)
        vT = P.sb("vT", [128, 512], BF16)
        R_qnT, R_kbT, R_vT = Res("qnT"), Res("kbT"), Res("vT")
        brow = P.sb("brow", [1, 512], BF16)
        R_brow = Res("brow")
        NS = 8
        gz = P.sb("gz", [128, NS, 128], F32)
        R_gzs = [Res(f"gz{i}") for i in range(NS)]
        cols = P.sb("cols", [128, NS, 8], F32)
        R_cols = [Res(f"cols{i}") for i in range(NS)]
        egl = P.sb("egl", [128, NS, 2], F32)
        R_egls = [Res(f"egl{i}") for i in range(NS)]
        Gbc = P.sb("Gbc", [128, NS, 128], F32)
        R_Gbcs = [Res(f"Gbc{i}") for i in range(NS)]
        ktl = P.sb("ktl", [128, NS, 2, 128], BF16)
        ind = P.sb("ind", [128, 2], F32)
        R_ind = Res("ind")
        P.op("pool", lambda e: e.memset(ind[:, :], 0.0), w=[R_ind])
        P.op("pool", lambda e: e.memset(ind[0:64, 0:1], 1.0), r=[R_ind], w=[R_ind])
        P.op("pool", lambda e: e.memset(ind[64:128, 1:2], 1.0), r=[R_ind], w=[R_ind])
        osb_ = P.sb("o_sb", [128, 128], F32)
        R_osb_ = Res("o_sb")
        bv = P.sb("bv", [128, NS, 128], F32)
        R_ktls, R_bvs = [Res(f"ktl{i}") for i in range(NS)], [Res(f"bv{i}") for i in range(NS)]
        egb = P.sb("egb", [128, NS, 128], F32)
        R_egbs = [Res(f"egb{i}") for i in range(NS)]
        qg = P.sb("qg", [128, NS, 128], BF16)
        R_qgs = [Res(f"qg{i}") for i in range(NS)]
        Dm_ = P.sb("Dm", [128, NS, 128], F32)
        DTm_ = P.sb("DTm", [128, NS, 128], F32)
        DTI_ = P.sb("DTI", [128, NS, 128], F32)
        R_Dms, R_DTms, R_DTIs = ([Res(f"{n}{i}") for i in range(NS)] for n in ("Dm", "DTm", "DTI"))
        NW = 14
        Wk_ = P.sb("Wk", [128, 4, NW, 128], F32)
        R_Wks = [[Res(f"Wk{s_}_{i}") for i in range(NW)] for s_ in range(4)]
        ATb_ = P.sb("ATb", [128, NS, 128], BF16)
        R_ATbs = [Res(f"ATb{i}") for i in range(NS)]
        aiT_ = P.sb("aiT", [128, NS, 128], BF16)
        R_aiTs = [Res(f"aiT{i}") for i in range(NS)]
        rt = P.sb("rt", [128, 128], BF16)
        vn = P.sb("vn", [128, 128], BF16)
        R_rt, R_vn = Res("rt"), Res("vn")
        S32 = P.sb("S32", [128, 128], F32)
        Sbf = P.sb("Sbf", [128, 128], BF16)
        R_S32, R_Sbf = Res("S32"), Res("Sbf")
        P.op("pool", lambda e: e.memset(S32[:, :], 0.0), w=[R_S32])
        P.op("pool", lambda e: e.memset(Sbf[:, :], 0.0), w=[R_Sbf])
        ogs = P.sb("ogs", [128, 128], F32)
        R_ogs = Res("ogs")
        junk = stage[0]
        R_junk = R_stage[0]
        pQKV = [bk[0], bk[1], bk[2]]
        R_pQKV = [R_bk[0], R_bk[1], R_bk[2]]

        tile_i = 0
        pending_scan = None
        R_stt2 = Res("stt2")
        for gi in range(ng_limit):
            c0, ncol = grp_cols(gi)
            ntl = ncol // 128
            N = ncol
            for tl in range(ntl):
                t = gi * 4 + tl
                hb, Rh = hsb[tile_i % 2], R_hsb[tile_i % 2]
                tile_i += 1
                tc_ = slice(tl * 128, (tl + 1) * 128)
                P.dma("sp", hb[:, :], hs_d[t * 128:(t + 1) * 128, :], w=[Rh])
                P.op("act", lambda e: e.activation(out=junk[:, :], in_=hb[:, :], func=AF.Square, accum_out=stt[:, 0:1]), r=[Rh], w=[R_junk, R_stt])
                C.rstd(stt[:, 0:1], stt[:, 0:1], R_stt, R_stt, D)
                P.op("act", lambda e: e.activation(out=xn[:, :], in_=hb[:, :], func=AF.Copy, scale=stt[:, 0:1]), r=[Rh, R_stt], w=[R_xn])
                for k in range(8):
                    P.op("pe", lambda e: e.transpose(out=pT[:, k, :], in_=xn[:, k * 128:(k + 1) * 128], identity=C.ident[:, :]), r=[R_xn, C.R_ident], w=[R_pT])
                P.op("dve", lambda e: e.tensor_copy(out=xnT[:, :, tc_], in_=pT[:, :, :]), r=[R_pT], w=[R_xnT])
                for k in range(8):
                    P.op("pe", lambda e: e.matmul(out=bk[5][:, 0:132], lhsT=xnT[:, k, tc_], rhs=whb[:, k, 384:516], start=(k == 0), stop=(k == 7)),
                         r=[R_xnT, R_wh], w=[R_bk[5]])
                sl_ = (gi % 2) * 4 + tl
                P.op("act", lambda e: e.activation(out=gz[:, sl_, :], in_=bk[5][:, 0:128], func=AF.Silu), r=[R_bk[5]], w=[R_gzs[sl_]])
                P.op("pool", lambda e: e.tensor_tensor(out=gz[:, sl_, :], in0=gz[:, sl_, :], in1=Gdn[:, :], op=ALU.mult), r=[R_gzs[sl_], R_Gdn], w=[R_gzs[sl_]])
                cc = lambda i: cols[:, sl_, i:i + 1]
                Rc = R_cols[sl_]
                P.op("act", lambda e: e.activation(out=cc(1), in_=bk[5][:, 130:131], func=AF.Sigmoid), r=[R_bk[5]], w=[Rc])
                P.op("dve", lambda e: e.tensor_tensor(out=cc(7), in0=bk[5][:, 128:129], in1=cst[:, 12:13], op=ALU.add), r=[R_bk[5], R_cst], w=[Rc])
                P.op("dve", lambda e: e.tensor_scalar(out=cc(0), in0=cc(7), scalar1=-1.0, scalar2=None, op0=ALU.mult), r=[Rc], w=[Rc])
                P.op("dve", lambda e: e.tensor_tensor(out=cc(0), in0=cc(0), in1=cc(7), op=ALU.max), r=[Rc], w=[Rc])
                P.op("act", lambda e: e.activation(out=cc(0), in_=cc(0), func=AF.Exp, scale=-1.0), r=[Rc], w=[Rc])
                P.op("dve", lambda e: e.tensor_scalar(out=cc(6), in0=cc(0), scalar1=2.0, scalar2=None, op0=ALU.add), r=[Rc], w=[Rc])
                P.op("dve", lambda e: e.reciprocal(out=cc(6), in_=cc(6)), r=[Rc], w=[Rc])
                P.op("dve", lambda e: e.tensor_tensor(out=cc(6), in0=cc(6), in1=cc(0), op=ALU.mult), r=[Rc], w=[Rc])
                P.op("dve", lambda e: e.tensor_tensor(out=cc(0), in0=cc(6), in1=cc(6), op=ALU.mult), r=[Rc], w=[Rc])
                P.op("dve", lambda e: e.tensor_scalar(out=cc(5), in0=cc(0), scalar1=1.0 / 15, scalar2=1.0 / 13, op0=ALU.mult, op1=ALU.add), r=[Rc], w=[Rc])
                for cf in (1.0 / 11, 1.0 / 9, 1.0 / 7, 1.0 / 5, 1.0 / 3, 1.0):
                    P.op("dve", lambda e: e.tensor_tensor(out=cc(5), in0=cc(5), in1=cc(0), op=ALU.mult), r=[Rc], w=[Rc])
                    P.op("dve", lambda e: e.tensor_scalar(out=cc(5), in0=cc(5), scalar1=cf, scalar2=None, op0=ALU.add), r=[Rc], w=[Rc])
                P.op("dve", lambda e: e.tensor_tensor(out=cc(5), in0=cc(5), in1=cc(6), op=ALU.mult), r=[Rc], w=[Rc])
                P.op("dve", lambda e: e.tensor_scalar(out=cc(7), in0=cc(7), scalar1=0.0, scalar2=None, op0=ALU.max), r=[Rc], w=[Rc])
                P.op("dve", lambda e: e.scalar_tensor_tensor(out=cc(0), in0=cc(5), scalar=2.0, in1=cc(7), op0=ALU.mult, op1=ALU.add), r=[Rc], w=[Rc])
                P.op("dve", lambda e: e.tensor_tensor(out=cc(0), in0=cc(0), in1=cst[:, 13:14], op=ALU.mult), r=[Rc, R_cst], w=[Rc])
            for wi in range(3):
                for k in range(8):
                    P.op("pe", lambda e: e.matmul(out=pQKV[wi][:, 0:N], lhsT=whb[:, k, wi * 128:(wi + 1) * 128], rhs=xnT[:, k, 0:N], start=(k == 0), stop=(k == 7)),
                         r=[R_xnT, R_wh], w=[R_pQKV[wi]])
            for k in range(8):
                P.op("pe", lambda e: e.matmul(out=bk[3][0:1, 0:N], lhsT=whb[:, k, 514:515], rhs=xnT[:, k, 0:N], start=(k == 0), stop=(k == 7)),
                     r=[R_xnT, R_wh], w=[R_bk[3]])
            P.op("act", lambda e: e.activation(out=brow[:, 0:N], in_=bk[3][0:1, 0:N], func=AF.Sigmoid), r=[R_bk[3]], w=[R_brow])
            P.op("pe", lambda e: e.matmul(out=bk[4][:, 0:N], lhsT=C.ones_bf[0:1, :], rhs=brow[0:1, 0:N], start=True, stop=True), r=[C.R_ones, R_brow], w=[R_bk[4]])
            for wi in range(3):
                P.op("act", lambda e: e.activation(out=raw[:, wi, 3:3 + N], in_=pQKV[wi][:, 0:N], func=AF.Copy), r=[R_pQKV[wi]], w=[R_raw[wi]])
                eng = "dve" if wi != 1 else "pool"
                P.op(eng, lambda e: e.tensor_scalar(out=acc[:, 0:N], in0=raw[:, wi, 0:N], scalar1=cst[:, wi * 4:wi * 4 + 1], scalar2=None, op0=ALU.mult),
                     r=[R_raw[wi], R_cst], w=[R_acc])
                for j in range(1, 4):
                    P.op("dve", lambda e: e.scalar_tensor_tensor(out=acc[:, 0:N], in0=raw[:, wi, j:j + N], scalar=cst[:, wi * 4 + j:wi * 4 + j + 1], in1=acc[:, 0:N],
                                                                 op0=ALU.mult, op1=ALU.add), r=[R_raw[wi], R_cst, R_acc], w=[R_acc])
                P.op("act", lambda e: e.activation(out=sil[:, wi, 0:N], in_=acc[:, 0:N], func=AF.Silu), r=[R_acc], w=[R_sil[wi]])
                P.op("pool", lambda e: e.tensor_copy(out=raw[:, wi, 0:3], in_=raw[:, wi, N:N + 3]), r=[R_raw[wi]], w=[R_raw[wi]])
            knT = knT2[gi % 2]
            R_knT = R_knT2[gi % 2]
            for wi, (dst, Rd, sc) in enumerate(((qnT, R_qnT, 128 ** -0.5), (knT, R_knT, 1.0))):
                P.op("act", lambda e: e.activation(out=sq[:, 0:N], in_=sil[:, wi, 0:N], func=AF.Square), r=[R_sil[wi]], w=[R_sq])
                P.op("pe", lambda e: e.matmul(out=bk[6][:, 0:N], lhsT=onesf[:, :], rhs=sq[:, 0:N], start=True, stop=True), r=[R_onesf, R_sq], w=[R_bk[6]])
                P.op("dve", lambda e: e.tensor_scalar(out=rn[:, 0:N], in0=bk[6][:, 0:N], scalar1=EPS, scalar2=None, op0=ALU.add), r=[R_bk[6]], w=[R_rn])
                P.op("act", lambda e: e.activation(out=rn[:, 0:N], in_=rn[:, 0:N], func=AF.Sqrt), r=[R_rn], w=[R_rn])
                P.op("dve", lambda e: e.reciprocal(out=rn[:, 0:N], in_=rn[:, 0:N]), r=[R_rn], w=[R_rn])
                P.op("dve", lambda e: e.scalar_tensor_tensor(out=dst[:, 0:N], in0=sil[:, wi, 0:N], scalar=sc, in1=rn[:, 0:N], op0=ALU.mult, op1=ALU.mult),
                     r=[R_sil[wi], R_rn], w=[Rd])
            P.op("dve", lambda e: e.tensor_tensor(out=kbT[:, 0:N], in0=knT[:, 0:N], in1=bk[4][:, 0:N], op=ALU.mult), r=[R_knT, R_bk[4]], w=[R_kbT])
            P.op("pool", lambda e: e.tensor_copy(out=vT[:, 0:N], in_=sil[:, 2, 0:N]), r=[R_sil[2]], w=[R_vT])

            def pre(tl, gi=gi, knT=knT, R_knT=R_knT):
                sl_ = (gi % 2) * 4 + tl
                tc_ = slice(tl * 128, (tl + 1) * 128)
                cc = lambda i: cols[:, sl_, i:i + 1]
                Rc = R_cols[sl_]
                par = tl % 2
                bG, RG = (bk[6], R_bk[6]) if par == 0 else (bk[5], R_bk[5])
                bKK, RKK = (bk[0], R_bk[0]) if par == 0 else (bk[3], R_bk[3])
                bC, RC_ = (bk[1], R_bk[1]) if par == 0 else (bk[2], R_bk[2])
                bA, RA = bC[:, 0:256], RC_
                bB, RB = bC[:, 256:512], RC_
                pTk, pTv = pT[:, 2 * par, :], pT[:, 2 * par + 1, :]
                Gb, R_Gb = Gbc[:, sl_, :], R_Gbcs[sl_]
                Dm, DTm, DTI = Dm_[:, sl_, :], DTm_[:, sl_, :], DTI_[:, sl_, :]
                R_Dm, R_DTm, R_DTI = R_Dms[sl_], R_DTms[sl_], R_DTIs[sl_]
                R_Wk = R_Wks[tl]
                W = lambda i: Wk_[:, tl, i, :]
                P.op("dve", lambda e: e.tensor_scalar(out=Gb, in0=onesf[:, :], scalar1=cc(0), scalar2=None, op0=ALU.mult), r=[R_onesf, Rc], w=[R_Gb])
                P.op("pe", lambda e: e.matmul(out=bG[:, 0:130], lhsT=Gb, rhs=TriX[:, :], start=True, stop=True), r=[R_Gb, R_mk], w=[RG])
                P.op("pe", lambda e: e.matmul(out=bG[:, 256:258], lhsT=TriX[:, 0:128], rhs=cols[:, sl_, 0:2], start=True, stop=True), r=[R_mk, Rc], w=[RG])
                yield
                P.op("dve", lambda e: e.tensor_copy(out=cc(2), in_=bG[:, 256:257]), r=[RG], w=[Rc])
                P.op("dve", lambda e: e.tensor_scalar(out=cc(6), in0=bG[:, 256:257], scalar1=-1.0, scalar2=None, op0=ALU.mult), r=[RG], w=[Rc])
                P.op("dve", lambda e: e.tensor_copy(out=cols[0:64, sl_, 3:4], in_=bG[0:64, 128:129]), r=[RG], w=[Rc])
                P.op("dve", lambda e: e.tensor_copy(out=cols[64:128, sl_, 3:4], in_=bG[64:128, 129:130]), r=[RG], w=[Rc])
                P.op("dve", lambda e: e.tensor_tensor(out=cc(4), in0=cc(3), in1=cc(2), op=ALU.subtract), r=[Rc], w=[Rc])
                P.op("dve", lambda e: e.tensor_scalar(out=Dm, in0=bG[:, 0:128], scalar1=cc(2), scalar2=None, op0=ALU.subtract), r=[RG, Rc], w=[R_Dm])
                P.op("act", lambda e: e.activation(out=egl[:, sl_, :], in_=bG[:, 128:130], func=AF.Exp), r=[RG], w=[R_egls[sl_]])
                P.op("act", lambda e: e.activation(out=egb[:, sl_, :], in_=bG[:, 0:128], func=AF.Exp), r=[RG], w=[R_egbs[sl_]])
                yield
                P.op("act", lambda e: e.activation(out=cc(4), in_=cc(4), func=AF.Exp), r=[Rc], w=[Rc])
                P.op("act", lambda e: e.activation(out=cc(7), in_=cc(2), func=AF.Exp), r=[Rc], w=[Rc])
                P.op("dve", lambda e: e.tensor_scalar(out=DTm, in0=Dm, scalar1=0.0, scalar2=None, op0=ALU.min), r=[R_Dm], w=[R_DTm])
                P.op("dve", lambda e: e.tensor_scalar(out=Dm, in0=Dm, scalar1=0.0, scalar2=-1.0, op0=ALU.max, op1=ALU.mult), r=[R_Dm], w=[R_Dm])
                P.op("dve", lambda e: e.tensor_tensor(out=qg[:, sl_, :], in0=qnT[:, tc_], in1=egb[:, sl_, :], op=ALU.mult), r=[R_qnT, R_egbs[sl_]], w=[R_qgs[sl_]])
                P.op("pe", lambda e: e.transpose(out=pTk, in_=knT[:, tc_], identity=C.ident[:, :]), r=[R_knT, C.R_ident], w=[R_pT])
                P.op("pe", lambda e: e.transpose(out=pTv, in_=vT[:, tc_], identity=C.ident[:, :]), r=[R_vT, C.R_ident], w=[R_pT])
                P.op("pe", lambda e: e.matmul(out=bKK[:, 0:128], lhsT=kbT[:, tc_], rhs=knT[:, tc_], start=True, stop=True), r=[R_kbT, R_knT], w=[RKK])
                P.op("pe", lambda e: e.matmul(out=bKK[:, 128:256], lhsT=knT[:, tc_], rhs=kbT[:, tc_], start=True, stop=True), r=[R_kbT, R_knT], w=[RKK])
                P.op("pe", lambda e: e.matmul(out=bKK[:, 256:384], lhsT=knT[:, tc_], rhs=qnT[:, tc_], start=True, stop=True), r=[R_qnT, R_knT], w=[RKK])
                yield
                P.op("dve", lambda e: e.scalar_tensor_tensor(out=cc(5), in0=cc(7), scalar=-1.0, in1=cc(1), op0=ALU.mult, op1=ALU.mult), r=[Rc], w=[Rc])
                P.op("act", lambda e: e.activation(out=Dm, in_=Dm, func=AF.Exp), r=[R_Dm], w=[R_Dm])
                P.op("act", lambda e: e.activation(out=DTm, in_=DTm, func=AF.Exp), r=[R_DTm], w=[R_DTm])
                for ch in range(2):
                    P.op("dve", lambda e: e.tensor_scalar(out=ktl[:, sl_, ch, :], in0=pTk, scalar1=cc(4), scalar2=ind[:, ch:ch + 1], op0=ALU.mult, op1=ALU.mult),
                         r=[R_pT, Rc, R_ind], w=[R_ktls[sl_]])
                P.op("dve", lambda e: e.tensor_scalar(out=bv[:, sl_, :], in0=pTv, scalar1=cc(1), scalar2=None, op0=ALU.mult), r=[R_pT, Rc], w=[R_bvs[sl_]])
                yield
                P.op("pool", lambda e: e.tensor_tensor(out=Dm, in0=Dm, in1=maskS[:, :], op=ALU.mult), r=[R_Dm, R_mk], w=[R_Dm])
                P.op("pool", lambda e: e.tensor_tensor(out=DTI, in0=DTm, in1=maskIT[:, 0:128], op=ALU.mult), r=[R_DTm, R_mk], w=[R_DTI])
                P.op("pool", lambda e: e.tensor_tensor(out=DTm, in0=DTm, in1=maskST[:, :], op=ALU.mult), r=[R_DTm, R_mk], w=[R_DTm])
                yield
                (LF, LTF, L_, LT_, O32, O32T, O64, O64T, X_, XT_, L2_, L2T_, Y_, Y2_) = range(NW)
                P.op("dve", lambda e: e.tensor_tensor(out=W(LF), in0=bKK[:, 0:128], in1=Dm, op=ALU.mult), r=[RKK, R_Dm], w=[R_Wk[LF]])
                P.op("dve", lambda e: e.tensor_tensor(out=W(LTF), in0=bKK[:, 128:256], in1=DTm, op=ALU.mult), r=[RKK, R_DTm], w=[R_Wk[LTF]])
                P.op("dve", lambda e: e.tensor_tensor(out=aiT_[:, sl_, :], in0=bKK[:, 256:384], in1=DTI, op=ALU.mult), r=[RKK, R_DTI], w=[R_aiTs[sl_]])
                yield
                for (dst, src, mi) in ((L_, LF, 0), (LT_, LTF, 0), (O32, LF, 1), (O32T, LTF, 1), (O64, LF, 2), (O64T, LTF, 2)):
                    P.op("pool", lambda e: e.tensor_tensor(out=W(dst), in0=W(src), in1=mblk[:, mi, :], op=ALU.mult), r=[R_Wk[src], R_mk], w=[R_Wk[dst]])
                P.op("pool", lambda e: e.tensor_tensor(out=W(X_), in0=identf[:, :], in1=W(L_), op=ALU.subtract), r=[R_if, R_Wk[L_]], w=[R_Wk[X_]])
                P.op("pool", lambda e: e.tensor_tensor(out=W(XT_), in0=identf[:, :], in1=W(LT_), op=ALU.subtract), r=[R_if, R_Wk[LT_]], w=[R_Wk[XT_]])
                yield

                def mm(out_ap, Rout, li, ri):
                    P.op("pe", lambda e: e.matmul(out=out_ap, lhsT=W(li), rhs=W(ri), start=True, stop=True), r=[R_Wk[li], R_Wk[ri]], w=[Rout])

                cl, clt, nl, nlt = L_, LT_, L2_, L2T_
                for lvl in range(3):
                    last = lvl == 2
                    mm(bA[:, 0:128], RA, clt, cl)
                    if not last:
                        mm(bA[:, 128:256], RA, cl, clt)
                    yield
                    P.op("act", lambda e: e.activation(out=W(nl), in_=bA[:, 0:128], func=AF.Copy), r=[RA], w=[R_Wk[nl]])
                    if not last:
                        P.op("act", lambda e: e.activation(out=W(nlt), in_=bA[:, 128:256], func=AF.Copy), r=[RA], w=[R_Wk[nlt]])
                    yield
                    mm(bB[:, 0:128], RB, nl, XT_)
                    mm(bB[:, 128:256], RB, XT_, nl)
                    yield
                    P.op("dve", lambda e: e.tensor_tensor(out=W(XT_), in0=bB[:, 0:128], in1=W(XT_), op=ALU.add), r=[RB, R_Wk[XT_]], w=[R_Wk[XT_]])
                    P.op("dve", lambda e: e.tensor_tensor(out=W(X_), in0=bB[:, 128:256], in1=W(X_), op=ALU.add), r=[RB, R_Wk[X_]], w=[R_Wk[X_]])
                    yield
                    cl, clt, nl, nlt = nl, nlt, cl, clt
                mm(bA[:, 0:128], RA, O32T, X_)
                mm(bA[:, 128:256], RA, O32, XT_)
                yield
                P.op("act", lambda e: e.activation(out=W(Y_), in_=bA[:, 0:128], func=AF.Copy), r=[RA], w=[R_Wk[Y_]])
                P.op("act", lambda e: e.activation(out=W(Y2_), in_=bA[:, 128:256], func=AF.Copy), r=[RA], w=[R_Wk[Y2_]])
                yield
                mm(bB[:, 0:128], RB, XT_, Y_)
                mm(bB[:, 128:256], RB, X_, Y2_)
                yield
                P.op("dve", lambda e: e.tensor_tensor(out=W(X_), in0=W(X_), in1=bB[:, 0:128], op=ALU.subtract), r=[RB, R_Wk[X_]], w=[R_Wk[X_]])
                P.op("dve", lambda e: e.tensor_tensor(out=W(XT_), in0=W(XT_), in1=bB[:, 128:256], op=ALU.subtract), r=[RB, R_Wk[XT_]], w=[R_Wk[XT_]])
                yield
                mm(bA[:, 0:128], RA, O64, XT_)
                yield
                P.op("act", lambda e: e.activation(out=W(Y2_), in_=bA[:, 0:128], func=AF.Copy), r=[RA], w=[R_Wk[Y2_]])
                yield
                mm(bB[:, 0:128], RB, X_, Y2_)
                yield
                P.op("dve", lambda e: e.tensor_tensor(out=ATb_[:, sl_, :], in0=W(XT_), in1=bB[:, 0:128], op=ALU.subtract), r=[RB, R_Wk[XT_]], w=[R_ATbs[sl_]])
                yield

            def scan(gi, ntl, knT, R_knT):
                for tl in range(ntl):
                    t = gi * 4 + tl
                    sl_ = (gi % 2) * 4 + tl
                    tc_ = slice(tl * 128, (tl + 1) * 128)
                    Rc = R_cols[sl_]
                    AT, R_AT = ATb_[:, sl_, :], R_ATbs[sl_]
                    for ch in range(2):
                        ps_ = slice(ch * 64, (ch + 1) * 64)
                        pb, R_pb = bk[4], R_bk[4]
                        co = 256 * ch
                        P.op("pe", lambda e: e.matmul(out=pb[:, 0:128], lhsT=knT[:, tc_], rhs=Sbf[:, :], start=True, stop=True), r=[R_knT, R_Sbf], w=[R_pb])
                        yield
                        P.op("dve", lambda e: e.scalar_tensor_tensor(out=rt[:, :], in0=pb[:, 0:128], scalar=cols[:, sl_, 5:6], in1=bv[:, sl_, :], op0=ALU.mult, op1=ALU.add),
                             r=[R_pb, Rc, R_bvs[sl_]], w=[R_rt])
                        yield
                        P.op("pe", lambda e: e.matmul(out=pb[:, 128:256], lhsT=AT, rhs=rt[:, :], start=True, stop=True), r=[R_AT, R_rt], w=[R_pb])
                        yield
                        P.op("act", lambda e: e.activation(out=vn[:, :], in_=pb[:, 128:256], func=AF.Copy), r=[R_pb], w=[R_vn])
                        yield
                        P.op("pe", lambda e: e.matmul(out=pb[:, 256:384], lhsT=ktl[:, sl_, ch, :], rhs=vn[:, :], start=True, stop=True), r=[R_ktls[sl_], R_vn], w=[R_pb])
                        P.op("pe", lambda e: e.matmul(out=pb[:, 384:512], lhsT=qg[:, sl_, :], rhs=Sbf[:, :], start=True, stop=False), r=[R_qgs[sl_], R_Sbf], w=[R_pb])
                        P.op("pe", lambda e: e.matmul(out=pb[:, 384:512], lhsT=aiT_[:, sl_, :], rhs=vn[:, :], start=False, stop=True), r=[R_aiTs[sl_], R_vn], w=[R_pb])
                        yield
                        P.op("dve", lambda e: e.scalar_tensor_tensor(out=S32[:, :], in0=S32[:, :], scalar=egl[:, sl_, ch:ch + 1], in1=pb[:, 256:384], op0=ALU.mult, op1=ALU.add),
                             r=[R_S32, R_egls[sl_], R_pb], w=[R_S32])
                        P.op("dve", lambda e: e.tensor_copy(out=osb_[ps_, :], in_=pb[ps_, 384:512]), r=[R_pb], w=[R_osb_])
                        yield
                        P.op("act", lambda e: e.activation(out=Sbf[:, :], in_=S32[:, :], func=AF.Copy), r=[R_S32], w=[R_Sbf])
                        yield
                    P.op("act", lambda e: e.activation(out=junk[:, 0:128], in_=osb_[:, :], func=AF.Square, accum_out=stt[:, 1:2]), r=[R_osb_], w=[R_junk, R_stt2])
                    C.rstd(stt[:, 1:2], stt[:, 1:2], R_stt2, R_stt2, 128)
                    P.op("dve", lambda e: e.scalar_tensor_tensor(out=ogs[:, :], in0=osb_[:, :], scalar=stt[:, 1:2], in1=gz[:, sl_, :], op0=ALU.mult, op1=ALU.mult),
                         r=[R_osb_, R_stt2, R_gzs[sl_]], w=[R_ogs])
                    P.dma("pool", og_d[t * 128:(t + 1) * 128, :], ogs[:, :], r=[R_ogs])
                    yield

            for pair0 in range(0, ntl, 2):
                gens = [pre(tl) for tl in range(pair0, min(pair0 + 2, ntl))]
                last_pair = pair0 + 2 >= ntl
                while gens:
                    for g_ in list(gens):
                        try:
                            next(g_)
                        except StopIteration:
                            gens.remove(g_)
                    if pending_scan is not None:
                        try:
                            next(pending_scan)
                        except StopIteration:
                            pending_scan = None
                if last_pair and pending_scan is not None:
                    for _ in pending_scan:
                        pass
                    pending_scan = None
            pending_scan = scan(gi, ntl, knT, R_knT)
        for _ in pending_scan:
            pass
        P.finish()
    return nc


def run_mix_odd(hs_pad, I):
    nc = get_prog("mix_odd", build_mix_odd)
    w = I["w_in_odd"][0]
    cwv = I["dn_conv_w"][0]
    in_maps = []
    for h in range(NCORES):
        sl = lambda base: slice(base + h * 128, base + (h + 1) * 128)
        zc = np.zeros((D, 1), np.float32)
        wh = np.concatenate([w[:, sl(0)], w[:, sl(1024)], w[:, sl(2048)], w[:, sl(3072)], w[:, 4096 + h:4097 + h], zc, w[:, 4104 + h:4105 + h], zc], axis=1)
        cw = np.concatenate([cwv[:, sl(0)].T, cwv[:, sl(1024)].T, cwv[:, sl(2048)].T], axis=1)
        in_maps.append({"hs": hs_pad, "wh": np.ascontiguousarray(wh), "g_pre": I["pre_mix_norm"][1], "convw": np.ascontiguousarray(cw),
                        "a_log": np.ascontiguousarray(I["dn_a_log"][0, h:h + 1]), "dt_bias": np.ascontiguousarray(I["dn_dt_bias"][0, h:h + 1]),
                        "g_dn": I["dn_out_norm"][0]})
    res = run_bass_kernel_spmd(nc, in_maps, core_ids=list(range(NCORES)))
    return np.ascontiguousarray(np.concatenate([r["og"] for r in res.results], axis=1))


def kernel(**inputs):
    I = {k: np.ascontiguousarray(np.asarray(v, dtype=np.float32)) for k, v in inputs.items()}
    x = I["x"][0]
    hs0 = np.ascontiguousarray(np.concatenate([np.zeros((PADF, D), np.float32), I["meta_tokens"], x], axis=0))
    osb, ys = run_mix_even(hs0, I)
    W0 = {"wglu": I["s5_w_glu"][0], "bglu": I["s5_b_glu"][0],
          "gmerge": np.ascontiguousarray(np.concatenate([I["sb_out_norm"][0], I["s5_out_norm"][0]])),
          "wout": I["w_out_even"][0], "w1": I["mlp_w1"][0], "w2": I["mlp_w2"][0],
          "g_postmix": I["post_mix_norm"][0], "g_premlp": I["pre_mlp_norm"][0], "g_postmlp": I["post_mlp_norm"][0]}
    hs1 = run_post("even", hs0, (osb, ys), W0)
    og = run_mix_odd(hs1, I)
    W1 = {"wout": I["w_out_odd"][0], "w1": I["mlp_w1"][1], "w2": I["mlp_w2"][1],
          "g_postmix": I["post_mix_norm"][1], "g_premlp": I["pre_mlp_norm"][1], "g_postmlp": I["post_mlp_norm"][1]}
    out = run_post("odd", hs1, og, W1)
    return np.ascontiguousarray(out[128:].reshape(1, 16384, D).astype(np.float32))
```

```python
from contextlib import ExitStack
import numpy as np
import concourse.bass as bass
import concourse.mybir as mybir
from concourse.bass_utils import run_bass_kernel_spmd

F32 = mybir.dt.float32
BF16 = mybir.dt.bfloat16
ALU = mybir.AluOpType
AF = mybir.ActivationFunctionType

NCORES = 8
D = 1024
DFF = 4096
NT = 129
LP = NT * 128
PADF = 112
NMETA = 16
TPC = 17
EPS = 1e-6


class Res:
    __slots__ = ("name", "lw", "rd", "psum")

    def __init__(self, name="", psum=False):
        self.name = name
        self.lw = None
        self.rd = {}
        self.psum = psum


class Prog:
    ENG = ("pe", "act", "dve", "pool", "sp")
    NDMA = 6

    def __init__(self, nc, es):
        self.nc = nc
        self.es = es
        self.lists = {k: [] for k in self.ENG}
        self.count = {k: 0 for k in self.ENG}
        self.sem = {}
        for k in self.ENG:
            self.sem[k] = es.enter_context(nc.semaphore("s_" + k))
        self.dsem = {}
        self.dcount = {}
        for q in ("sp", "pool", "act"):
            self.dsem[q] = [es.enter_context(nc.semaphore(f"d_{q}{i}")) for i in range(self.NDMA)]
            self.dcount[q] = 0
        self.waited = {}
        self.eobj = {"pe": nc.tensor, "act": nc.scalar, "dve": nc.vector, "pool": nc.gpsimd, "sp": nc.sync}

    def sb(self, name, shape, dt):
        return self.es.enter_context(self.nc.sbuf_tensor(name, list(shape), dt))

    def ps(self, name, shape, dt):
        return self.es.enter_context(self.nc.psum_tensor(name, list(shape), dt))

    def _semobj(self, key):
        if isinstance(key, tuple):
            return self.dsem[key[0]][key[1]]
        return self.sem[key]

    def _wait(self, eng, key, val):
        if key == eng and eng == "pe":
            return
        k = (eng, key)
        if self.waited.get(k, 0) >= val:
            return
        self.waited[k] = val
        so = self._semobj(key)
        self.eobj[eng].wait_ge(so, val)

    def _deps(self, eng, r, w):
        for x in r:
            if x.lw is not None:
                self._wait(eng, *x.lw)
            if x.psum:
                for key, val in x.rd.items():
                    if key != eng:
                        self._wait(eng, key, val)
        for x in w:
            if x.lw is not None:
                self._wait(eng, *x.lw)
            for key, val in x.rd.items():
                self._wait(eng, key, val)

    def _mark(self, tok, r, w):
        key, val = tok
        for x in r:
            if x.rd.get(key, 0) < val:
                x.rd[key] = val
        for x in w:
            x.lw = tok
            x.rd = {}

    def op(self, eng, fn, r=(), w=()):
        self._deps(eng, r, w)
        self.count[eng] += 1
        seq = self.count[eng]
        so = self.sem[eng]
        fn(self.eobj[eng]).then_inc(so, 1)
        self._mark((eng, seq), r, w)

    def dma(self, q, out, in_, r=(), w=(), **kw):
        i = self.dcount[q]
        self.dcount[q] += 1
        slot = i % self.NDMA
        key = (q, slot)
        val = 16 * (i // self.NDMA + 1)
        if i >= self.NDMA:
            self._wait(q, key, val - 16)
        self._deps(q, r, w)
        so = self.dsem[q][slot]
        self.eobj[q].dma_start(out=out, in_=in_, **kw).then_inc(so, 16)
        self._mark((key, val), r, w)

    def barrier(self):
        for e in self.ENG:
            for o in self.ENG:
                if o != e and self.count[o] > 0:
                    self._wait(e, o, self.count[o])
            for q in ("sp", "pool", "act"):
                n = self.dcount[q]
                for slot in range(min(n, self.NDMA)):
                    last_i = ((n - 1 - slot) // self.NDMA) * self.NDMA + slot
                    self._wait(e, (q, slot), 16 * (last_i // self.NDMA + 1))

    def finish(self):
        for q in ("sp", "pool", "act"):
            n = self.dcount[q]
            for slot in range(min(n, self.NDMA)):
                last_i = ((n - 1 - slot) // self.NDMA) * self.NDMA + slot
                self._wait(q, (q, slot), 16 * (last_i // self.NDMA + 1))


class Ctx:
    def __init__(self, nc, es):
        self.P = Prog(nc, es)
        self.nc = nc
        P = self.P
        self.ident = P.sb("ident", [128, 128], BF16)
        self.R_ident = Res("ident")
        P.op("pool", lambda e: e.memset(self.ident[:, :], 1.0), w=[self.R_ident])
        P.op("pool", lambda e: e.affine_select(out=self.ident[:, :], in_=self.ident[:, :], pattern=[[-1, 128]],
                                                 compare_op=ALU.is_equal, fill=0.0, base=0, channel_multiplier=1),
             r=[self.R_ident], w=[self.R_ident])
        self.ones_bf = P.sb("ones_bf", [128, 128], BF16)
        self.R_ones = Res("ones")
        P.op("pool", lambda e: e.memset(self.ones_bf[:, :], 1.0), w=[self.R_ones])
        self.rr = 0

    def rstd(self, ss_ap, out_ap, R_ss, R_out, n, eng2="dve"):
        P = self.P
        P.op("dve", lambda e: e.tensor_scalar(out=out_ap, in0=ss_ap, scalar1=1.0 / n, scalar2=EPS,
                                               op0=ALU.mult, op1=ALU.add), r=[R_ss], w=[R_out])
        P.op("act", lambda e: e.activation(out=out_ap, in_=out_ap, func=AF.Sqrt), r=[R_out], w=[R_out])
        P.op("dve", lambda e: e.reciprocal(out=out_ap, in_=out_ap), r=[R_out], w=[R_out])

    def cast_eng(self):
        self.rr += 1
        return ("dve", "act")[self.rr % 2]

    def load_weight(self, dram, wb, R_wb, stage, R_stage, nk, ncols, gsc=None, R_g=None, col0=0, colw=None):
        P = self.P
        idx = 0
        for k in range(nk):
            for c0 in range(0, ncols, 1024):
                cw = min(1024, ncols - c0)
                st, Rs = stage[idx % len(stage)], R_stage[idx % len(stage)]
                idx += 1
                P.dma("sp", st[:, 0:cw], dram[k * 128:(k + 1) * 128, col0 + c0:col0 + c0 + cw], w=[Rs])
                eng = self.cast_eng()
                rs = [Rs] + ([R_g] if gsc is not None else [])
                if gsc is not None:
                    if eng == "act":
                        P.op("act", lambda e, st=st, k=k, c0=c0, cw=cw: e.activation(
                            out=wb[:, k, c0:c0 + cw], in_=st[:, 0:cw], func=AF.Copy, scale=gsc[:, k:k + 1]), r=rs, w=[R_wb])
                    else:
                        P.op(eng, lambda e, st=st, k=k, c0=c0, cw=cw: e.tensor_scalar(
                            out=wb[:, k, c0:c0 + cw], in0=st[:, 0:cw], scalar1=gsc[:, k:k + 1], scalar2=None,
                            op0=ALU.mult), r=rs, w=[R_wb])
                else:
                    if eng == "act":
                        P.op("act", lambda e, st=st, k=k, c0=c0, cw=cw: e.activation(
                            out=wb[:, k, c0:c0 + cw], in_=st[:, 0:cw], func=AF.Copy), r=rs, w=[R_wb])
                    else:
                        P.op(eng, lambda e, st=st, k=k, c0=c0, cw=cw: e.tensor_copy(
                            out=wb[:, k, c0:c0 + cw], in_=st[:, 0:cw]), r=rs, w=[R_wb])


def build_post(kind):
    even = kind == "even"
    nc = bass.Bass("TRN2", target_bir_lowering=False)
    dr = lambda n, s, k="ExternalInput": nc.dram_tensor(n, list(s), F32, kind=k).ap()
    rows = TPC * 128
    hs_d = dr("hs", [rows, D])
    if even:
        osb_d = dr("osb", [rows, 512])
        ys5_d = dr("ys5", [rows, 512])
        wglu_d = dr("wglu", [512, 512])
        bglu_d = dr("bglu", [512])
        gmerge_d = dr("gmerge", [D])
    else:
        og_d = dr("og", [rows, D])
    wout_d = dr("wout", [D, D])
    w1_d = dr("w1", [D, DFF])
    w2_d = dr("w2", [DFF, D])
    g1_d = dr("g_postmix", [D])
    g2_d = dr("g_premlp", [D])
    g3_d = dr("g_postmlp", [D])
    out_d = dr("out", [rows, D], "ExternalOutput")

    with ExitStack() as es:
        C = Ctx(nc, es)
        P = C.P
        w1b = P.sb("w1b", [128, 8, DFF], BF16)
        w2b = P.sb("w2b", [128, 32, D], BF16)
        woutb = P.sb("woutb", [128, 8, D], BF16)
        R_w1, R_w2, R_wout = Res("w1"), Res("w2"), Res("wout")
        stage = [P.sb(f"stage{i}", [128, 1024], F32) for i in range(4)]
        R_stage = [Res(f"st{i}") for i in range(4)]
        gsm = P.sb("gsm", [128, 3, 8], F32)
        R_gsm = Res("gsm")
        G1 = P.sb("G1", [128, D], F32)
        G3 = P.sb("G3", [128, D], F32)
        R_G = Res("G")
        hs = P.sb("hs_t", [128, D], F32)
        mg = P.sb("mg_t", [128, D], F32)
        xn = P.sb("xn_t", [128, D], BF16)
        xnT = P.sb("xnT_t", [128, 8, 128], BF16)
        h1T = P.sb("h1T_t", [128, 32, 128], BF16)
        rl = P.sb("rl_t", [128, 512], BF16)
        tmp = P.sb("tmp_t", [128, 512], F32)
        st = P.sb("stat", [128, 8], F32)
        R_hs, R_mg, R_xn, R_xnT, R_h1T, R_rl, R_tmp = (Res(n) for n in ("hs", "mg", "xn", "xnT", "h1T", "rl", "tmp"))
        R_st = [Res(f"st{i}") for i in range(8)]
        pT = P.ps("pT", [128, 8, 128], BF16)
        pA = P.ps("pA", [128, 512], F32)
        pB = P.ps("pB", [128, 512], F32)
        pH = [P.ps("pH0", [128, 4, 128], F32), P.ps("pH1", [128, 4, 128], F32)]
        R_pT, R_pA, R_pB = Res("pT", True), Res("pA", True), Res("pB", True)
        R_pH = [Res("pH0", True), Res("pH1", True)]
        junk = stage[0]
        R_junk = R_stage[0]

        P.dma("sp", G1[:, :], g1_d.partition_broadcast(128), w=[R_G])
        P.dma("sp", G3[:, :], g3_d.partition_broadcast(128), w=[R_G])
        P.dma("sp", gsm[:, 0, :], g2_d.rearrange("(k p) -> p k", p=128), w=[R_gsm], allow_slow_non_contiguous=True)
        if even:
            P.dma("sp", gsm[:, 1, :], gmerge_d.rearrange("(k p) -> p k", p=128), w=[R_gsm], allow_slow_non_contiguous=True)
            wglub = P.sb("wglub", [128, 4, 512], BF16)
            bglub = P.sb("bglub", [1, 512], BF16)
            R_wglu, R_bglu = Res("wglu"), Res("bglu")
            P.dma("sp", tmp[0:1, :], bglu_d.rearrange("(o n) -> o n", o=1), w=[R_tmp])
            P.op("dve", lambda e: e.tensor_copy(out=bglub[:, :], in_=tmp[0:1, :]), r=[R_tmp], w=[R_bglu])
            C.load_weight(wglu_d, wglub, R_wglu, stage, R_stage, 4, 512)
            C.load_weight(wout_d, woutb, R_wout, stage, R_stage, 8, D, gsc=gsm[:, 1, :], R_g=R_gsm)
        else:
            C.load_weight(wout_d, woutb, R_wout, stage, R_stage, 8, D)
        C.load_weight(w1_d, w1b, R_w1, stage, R_stage, 8, DFF, gsc=gsm[:, 0, :], R_g=R_gsm)
        C.load_weight(w2_d, w2b, R_w2, stage, R_stage, 32, D)

        def transpose8(nblk):
            for k in range(nblk):
                P.op("pe", lambda e, k=k: e.transpose(out=pT[:, k, :], in_=xn[:, k * 128:(k + 1) * 128], identity=C.ident[:, :]),
                     r=[R_xn, C.R_ident], w=[R_pT])
            P.op("dve", lambda e: e.tensor_copy(out=xnT[:, 0:nblk, :], in_=pT[:, 0:nblk, :]), r=[R_pT], w=[R_xnT])

        def norm_residual(G):
            P.op("act", lambda e: e.activation(out=junk[:, 0:512], in_=pA[:, :], func=AF.Square, accum_out=st[:, 0:1]),
                 r=[R_pA], w=[R_junk, R_st[0]])
            P.op("act", lambda e: e.activation(out=junk[:, 512:1024], in_=pB[:, :], func=AF.Square, accum_out=st[:, 1:2]),
                 r=[R_pB], w=[R_junk, R_st[1]])
            P.op("dve", lambda e: e.tensor_tensor(out=st[:, 0:1], in0=st[:, 0:1], in1=st[:, 1:2], op=ALU.add),
                 r=[R_st[0], R_st[1]], w=[R_st[0]])
            C.rstd(st[:, 0:1], st[:, 0:1], R_st[0], R_st[0], D)
            for h, (pp, Rp) in enumerate(((pA, R_pA), (pB, R_pB))):
                sl = slice(h * 512, (h + 1) * 512)
                P.op("dve", lambda e, pp=pp, sl=sl: e.scalar_tensor_tensor(out=tmp[:, :], in0=pp[:, :], scalar=st[:, 0:1], in1=G[:, sl],
                                                                          op0=ALU.mult, op1=ALU.mult),
                     r=[Rp, R_st[0], R_G], w=[R_tmp])
                P.op("dve", lambda e, sl=sl: e.tensor_tensor(out=hs[:, sl], in0=hs[:, sl], in1=tmp[:, :], op=ALU.add),
                     r=[R_hs, R_tmp], w=[R_hs])

        for t in range(TPC):
            rsl = slice(t * 128, (t + 1) * 128)
            P.dma("sp", hs[:, :], hs_d[rsl, :], w=[R_hs])
            if even:
                P.dma("sp", mg[:, 0:512], osb_d[rsl, :], w=[R_mg])
                P.dma("sp", mg[:, 512:1024], ys5_d[rsl, :], w=[R_mg])
                y = mg[:, 512:1024]
                P.op("act", lambda e: e.activation(out=tmp[:, :], in_=y, func=AF.Square), r=[R_mg], w=[R_tmp])
                P.op("dve", lambda e: e.tensor_scalar(out=tmp[:, :], in0=tmp[:, :], scalar1=0.044715, scalar2=1.0, op0=ALU.mult, op1=ALU.add),
                     r=[R_tmp], w=[R_tmp])
                P.op("dve", lambda e: e.tensor_tensor(out=tmp[:, :], in0=tmp[:, :], in1=y, op=ALU.mult), r=[R_tmp, R_mg], w=[R_tmp])
                P.op("act", lambda e: e.activation(out=tmp[:, :], in_=tmp[:, :], func=AF.Sigmoid, scale=1.5957691216057308), r=[R_tmp], w=[R_tmp])
                P.op("dve", lambda e: e.tensor_tensor(out=y, in0=tmp[:, :], in1=y, op=ALU.mult), r=[R_tmp, R_mg], w=[R_mg])
                P.op("act", lambda e: e.activation(out=xn[:, 0:512], in_=y, func=AF.Copy), r=[R_mg], w=[R_xn])
                transpose8(4)
                for k in range(4):
                    P.op("pe", lambda e, k=k: e.matmul(out=pA[:, :], lhsT=xnT[:, k, :], rhs=wglub[:, k, :], start=(k == 0), stop=False),
                         r=[R_xnT, R_wglu], w=[R_pA])
                P.op("pe", lambda e: e.matmul(out=pA[:, :], lhsT=C.ones_bf[0:1, :], rhs=bglub[0:1, :], start=False, stop=True),
                     r=[C.R_ones, R_bglu], w=[R_pA])
                P.op("act", lambda e: e.activation(out=tmp[:, :], in_=pA[:, :], func=AF.Sigmoid), r=[R_pA], w=[R_tmp])
                P.op("dve", lambda e: e.tensor_tensor(out=y, in0=tmp[:, :], in1=y, op=ALU.mult), r=[R_tmp, R_mg], w=[R_mg])
                for h in range(2):
                    sl = slice(h * 512, (h + 1) * 512)
                    P.op("act", lambda e, sl=sl, h=h: e.activation(out=junk[:, sl], in_=mg[:, sl], func=AF.Square, accum_out=st[:, 2 + h:3 + h]),
                         r=[R_mg], w=[R_junk, R_st[2 + h]])
                    C.rstd(st[:, 2 + h:3 + h], st[:, 2 + h:3 + h], R_st[2 + h], R_st[2 + h], 512)
                    P.op("act", lambda e, sl=sl, h=h: e.activation(out=xn[:, sl], in_=mg[:, sl], func=AF.Copy, scale=st[:, 2 + h:3 + h]),
                         r=[R_mg, R_st[2 + h]], w=[R_xn])
            else:
                P.dma("sp", mg[:, :], og_d[rsl, :], w=[R_mg])
                P.op("act", lambda e: e.activation(out=xn[:, :], in_=mg[:, :], func=AF.Copy), r=[R_mg], w=[R_xn])
            transpose8(8)
            for k in range(8):
                P.op("pe", lambda e, k=k: e.matmul(out=pA[:, :], lhsT=xnT[:, k, :], rhs=woutb[:, k, 0:512], start=(k == 0), stop=(k == 7)),
                     r=[R_xnT, R_wout], w=[R_pA])
            for k in range(8):
                P.op("pe", lambda e, k=k: e.matmul(out=pB[:, :], lhsT=xnT[:, k, :], rhs=woutb[:, k, 512:1024], start=(k == 0), stop=(k == 7)),
                     r=[R_xnT, R_wout], w=[R_pB])
            norm_residual(G1)
            P.op("act", lambda e: e.activation(out=junk[:, :], in_=hs[:, :], func=AF.Square, accum_out=st[:, 4:5]), r=[R_hs], w=[R_junk, R_st[4]])
            C.rstd(st[:, 4:5], st[:, 4:5], R_st[4], R_st[4], D)
            P.op("act", lambda e: e.activation(out=xn[:, :], in_=hs[:, :], func=AF.Copy, scale=st[:, 4:5]), r=[R_hs, R_st[4]], w=[R_xn])
            transpose8(8)
            for fg in range(8):
                ph, Rph = pH[fg % 2], R_pH[fg % 2]
                for j in range(4):
                    fb = fg * 4 + j
                    for k in range(8):
                        P.op("pe", lambda e, ph=ph, j=j, fb=fb, k=k: e.matmul(out=ph[:, j, :], lhsT=w1b[:, k, fb * 128:(fb + 1) * 128], rhs=xnT[:, k, :],
                                                                               start=(k == 0), stop=(k == 7)),
                             r=[R_w1, R_xnT], w=[Rph])
                P.op("act", lambda e, ph=ph: e.activation(out=rl[:, :], in_=ph[:, :, :].rearrange("p a b -> p (a b)"), func=AF.Relu), r=[Rph], w=[R_rl])
                P.op("dve", lambda e, fg=fg: e.tensor_tensor(out=h1T[:, fg * 4:(fg + 1) * 4, :].rearrange("p a b -> p (a b)"), in0=rl[:, :], in1=rl[:, :], op=ALU.mult),
                     r=[R_rl], w=[R_h1T])
            for fb in range(32):
                P.op("pe", lambda e, fb=fb: e.matmul(out=pA[:, :], lhsT=h1T[:, fb, :], rhs=w2b[:, fb, 0:512], start=(fb == 0), stop=(fb == 31)),
                     r=[R_h1T, R_w2], w=[R_pA])
            for fb in range(32):
                P.op("pe", lambda e, fb=fb: e.matmul(out=pB[:, :], lhsT=h1T[:, fb, :], rhs=w2b[:, fb, 512:1024], start=(fb == 0), stop=(fb == 31)),
                     r=[R_h1T, R_w2], w=[R_pB])
            norm_residual(G3)
            P.dma("pool", out_d[rsl, :], hs[:, :], r=[R_hs])
        P.finish()
    return nc


def tok_shard(a_pad, c):
    return np.ascontiguousarray(np.concatenate([a_pad[0:128], a_pad[128 * (1 + 16 * c):128 * (17 + 16 * c)]], axis=0))


def tok_unshard(outs):
    parts = [outs[0][0:128]] + [o[128:] for o in outs]
    return np.concatenate(parts, axis=0)


_CACHE = {}


def get_prog(name, builder, *args):
    if name not in _CACHE:
        _CACHE[name] = builder(*args)
    return _CACHE[name]


def run_post(kind, hs_pad, mix_in, W):
    nc = get_prog("post_" + kind, build_post, kind)
    in_maps = []
    for c in range(NCORES):
        m = {"hs": tok_shard(hs_pad, c)}
        if kind == "even":
            m["osb"] = tok_shard(mix_in[0], c)
            m["ys5"] = tok_shard(mix_in[1], c)
        else:
            m["og"] = tok_shard(mix_in, c)
        m.update(W)
        in_maps.append(m)
    res = run_bass_kernel_spmd(nc, in_maps, core_ids=list(range(NCORES)))
    return tok_unshard([r["out"] for r in res.results])


NG = 33


def grp_cols(gi):
    return (gi * 512, 512 if gi < 32 else 128)


def build_mix_even():
    nc = bass.Bass("TRN2", target_bir_lowering=False)
    dr = lambda n, s, k="ExternalInput": nc.dram_tensor(n, list(s), F32, kind=k).ap()
    hs_d = dr("hs", [LP, D])
    wh_d = dr("wh", [D, 384])
    g_d = dr("g_pre", [D])
    lre_d = dr("lam_re", [4, 64])
    lim_d = dr("lam_im", [4, 64])
    ldt_d = dr("log_dt", [4])
    bre_d = dr("b_re", [4, 64, 16])
    bim_d = dr("b_im", [4, 64, 16])
    cre_d = dr("c_re", [4, 16, 64])
    cim_d = dr("c_im", [4, 16, 64])
    dsk_d = dr("d_skip", [64])
    osb_d = dr("osbT", [64, LP], "ExternalOutput")
    ys_d = dr("ysT", [64, LP], "ExternalOutput")

    with ExitStack() as es:
        C = Ctx(nc, es)
        P = C.P
        QQ = P.sb("QQ", [128, LP], BF16)
        KZ = P.sb("KZ", [128, LP], BF16)
        KN = P.sb("KN", [128, LP], BF16)
        R_kinit = Res("kinit")
        P.op("pool", lambda e: e.memset(KZ[64:128, :], 0.0), w=[R_kinit])
        P.op("pool", lambda e: e.memset(KN[64:128, :], 0.0), w=[R_kinit])
        vA = P.sb("vA", [128, NT, 128], BF16)
        P.op("pool", lambda e: e.memset(vA[:, :, :].rearrange("p a b -> p (a b)"), 0.0), w=[R_kinit])
        R_qT = [Res(f"qT{g}") for g in range(NG)]
        R_kT = [Res(f"kT{g}") for g in range(NG)]
        R_v = [Res(f"v{g}") for g in range(NG)]
        esA = ExitStack()
        P.es = esA
        whb = P.sb("whb", [128, 8, 384], BF16)
        R_wh = Res("wh")
        gsm = P.sb("gsm", [128, 8], F32)
        R_gsm = Res("gsm")
        stage = [P.sb("stage0", [128, 1024], F32)]
        R_stage = [Res("st0")]
        banks = [P.ps(f"bk{i}", [128, 512], F32) for i in range(7)]
        R_bk = [Res(f"bk{i}", True) for i in range(7)]
        pT = P.ps("pT", [128, 8, 128], BF16)
        R_pT = Res("pT", True)

        P.dma("sp", gsm[:, :], g_d.rearrange("(k p) -> p k", p=128), w=[R_gsm], allow_slow_non_contiguous=True)
        C.load_weight(wh_d, whb, R_wh, stage, R_stage, 8, 384, gsc=gsm, R_g=R_gsm)

        sp_ = P.sb("s5p", [128, 2, 24], F32)
        R_sp = Res("s5p")
        LR, LI, DT, AR, AI, IR, II, T0, T1, T2, T3, CR, CI, NLR = range(14)
        spi = P.sb("s5pi", [128, 2], mybir.dt.int32)
        col = lambda c: sp_[:, :, c]
        for sb in range(2):
            P.dma("sp", sp_[:, sb, LR:LR + 1], lre_d[2 * sb:2 * sb + 2, :].rearrange("g (n o) -> (g n) o", o=1), w=[R_sp])
            P.dma("sp", sp_[:, sb, LI:LI + 1], lim_d[2 * sb:2 * sb + 2, :].rearrange("g (n o) -> (g n) o", o=1), w=[R_sp])
            for gl in range(2):
                P.dma("sp", sp_[gl * 64:(gl + 1) * 64, sb, DT:DT + 1],
                      ldt_d[2 * sb + gl:2 * sb + gl + 1].rearrange("(o n) -> o n", o=1).partition_broadcast(64), w=[R_sp])
        so = lambda eng, fn: P.op(eng, fn, r=[R_sp], w=[R_sp])
        so("act", lambda e: e.activation(out=col(DT), in_=col(DT), func=AF.Exp))
        so("dve", lambda e: e.tensor_scalar(out=col(LR), in0=col(LR), scalar1=-1e-4, scalar2=None, op0=ALU.min))
        so("dve", lambda e: e.tensor_tensor(out=col(T0), in0=col(LR), in1=col(DT), op=ALU.mult))
        so("dve", lambda e: e.tensor_scalar(out=col(NLR), in0=col(T0), scalar1=-1.0, scalar2=None, op0=ALU.mult))
        so("dve", lambda e: e.tensor_tensor(out=col(T1), in0=col(LI), in1=col(DT), op=ALU.mult))
        so("dve", lambda e: e.tensor_scalar(out=col(T2), in0=col(T1), scalar1=1.0 / (2 * np.pi), scalar2=None, op0=ALU.mult))
        P.op("dve", lambda e: e.tensor_copy(out=spi[:, :], in_=col(T2)), r=[R_sp], w=[R_sp])
        P.op("dve", lambda e: e.tensor_copy(out=col(T2), in_=spi[:, :]), r=[R_sp], w=[R_sp])
        so("dve", lambda e: e.scalar_tensor_tensor(out=col(T1), in0=col(T2), scalar=-2 * np.pi, in1=col(T1), op0=ALU.mult, op1=ALU.add))
        so("dve", lambda e: e.tensor_scalar(out=col(T1), in0=col(T1), scalar1=0.5, scalar2=None, op0=ALU.mult))
        so("dve", lambda e: e.tensor_scalar(out=col(T2), in0=col(T1), scalar1=np.pi / 2, scalar2=None, op0=ALU.add))
        so("act", lambda e: e.activation(out=col(T1), in_=col(T1), func=AF.Sin))
        so("act", lambda e: e.activation(out=col(T2), in_=col(T2), func=AF.Sin))
        so("act", lambda e: e.activation(out=col(T3), in_=col(T0), func=AF.Exp))
        so("dve", lambda e: e.tensor_tensor(out=col(AI), in0=col(T1), in1=col(T2), op=ALU.mult))
        so("dve", lambda e: e.tensor_scalar(out=col(AI), in0=col(AI), scalar1=2.0, scalar2=None, op0=ALU.mult))
        so("dve", lambda e: e.tensor_tensor(out=col(AR), in0=col(T1), in1=col(T1), op=ALU.mult))
        so("dve", lambda e: e.tensor_scalar(out=col(AR), in0=col(AR), scalar1=-2.0, scalar2=1.0, op0=ALU.mult, op1=ALU.add))
        so("act", lambda e: e.activation(out=col(T0), in_=col(NLR), func=AF.Exp))
        so("dve", lambda e: e.tensor_tensor(out=col(IR), in0=col(AR), in1=col(T0), op=ALU.mult))
        so("dve", lambda e: e.tensor_tensor(out=col(II), in0=col(AI), in1=col(T0), op=ALU.mult))
        so("dve", lambda e: e.tensor_scalar(out=col(II), in0=col(II), scalar1=-1.0, scalar2=None, op0=ALU.mult))
        so("dve", lambda e: e.tensor_tensor(out=col(AR), in0=col(AR), in1=col(T3), op=ALU.mult))
        so("dve", lambda e: e.tensor_tensor(out=col(AI), in0=col(AI), in1=col(T3), op=ALU.mult))
        so("dve", lambda e: e.tensor_tensor(out=col(T0), in0=col(LR), in1=col(LR), op=ALU.mult))
        so("dve", lambda e: e.tensor_tensor(out=col(T1), in0=col(LI), in1=col(LI), op=ALU.mult))
        so("dve", lambda e: e.tensor_tensor(out=col(T0), in0=col(T0), in1=col(T1), op=ALU.add))
        so("dve", lambda e: e.reciprocal(out=col(T0), in_=col(T0)))
        so("dve", lambda e: e.tensor_scalar(out=col(T1), in0=col(AR), scalar1=-1.0, scalar2=None, op0=ALU.add))
        so("dve", lambda e: e.tensor_tensor(out=col(T2), in0=col(T1), in1=col(LR), op=ALU.mult))
        so("dve", lambda e: e.tensor_tensor(out=col(T3), in0=col(AI), in1=col(LI), op=ALU.mult))
        so("dve", lambda e: e.tensor_tensor(out=col(T2), in0=col(T2), in1=col(T3), op=ALU.add))
        so("dve", lambda e: e.tensor_tensor(out=col(CR), in0=col(T2), in1=col(T0), op=ALU.mult))
        so("dve", lambda e: e.tensor_tensor(out=col(T2), in0=col(AI), in1=col(LR), op=ALU.mult))
        so("dve", lambda e: e.tensor_tensor(out=col(T3), in0=col(T1), in1=col(LI), op=ALU.mult))
        so("dve", lambda e: e.tensor_tensor(out=col(T2), in0=col(T2), in1=col(T3), op=ALU.subtract))
        so("dve", lambda e: e.tensor_tensor(out=col(CI), in0=col(T2), in1=col(T0), op=ALU.mult))

        TB = {n: P.sb("tb_" + n, [128, 2, 512], F32) for n in ("pr", "pi", "ir", "ii")}
        R_tb = Res("tb")
        ptmp = P.sb("ptmp", [128, 256], F32)
        R_ptmp = Res("ptmp")
        pw = P.sb("pw", [128, 2, 2, 2], F32)
        R_pw = Res("pw")
        pw2 = P.sb("pw2", [128, 4], F32)
        for sb in range(2):
            for wi, (tr, ti, cr_, ci_) in enumerate((("pr", "pi", AR, AI), ("ir", "ii", IR, II))):
                P.op("pool", lambda e, tr=tr, sb=sb: e.memset(TB[tr][:, sb, 0:1], 1.0), w=[R_tb])
                P.op("pool", lambda e, ti=ti, sb=sb: e.memset(TB[ti][:, sb, 0:1], 0.0), w=[R_tb])
                P.op("dve", lambda e, sb=sb, wi=wi, cr_=cr_: e.tensor_copy(out=pw[:, sb, wi, 0:1], in_=sp_[:, sb, cr_:cr_ + 1]), r=[R_sp], w=[R_pw])
                P.op("dve", lambda e, sb=sb, wi=wi, ci_=ci_: e.tensor_copy(out=pw[:, sb, wi, 1:2], in_=sp_[:, sb, ci_:ci_ + 1]), r=[R_sp], w=[R_pw])
                n = 1
                while n < 512:
                    cr = pw[:, sb, wi, 0:1]
                    ci = pw[:, sb, wi, 1:2]
                    src_r = TB[tr][:, sb, 0:n]
                    src_i = TB[ti][:, sb, 0:n]
                    dst_r = TB[tr][:, sb, n:2 * n]
                    dst_i = TB[ti][:, sb, n:2 * n]
                    tm = ptmp[:, 0:n]
                    P.op("dve", lambda e, tm=tm, src_i=src_i, ci=ci: e.tensor_scalar(out=tm, in0=src_i, scalar1=ci, scalar2=None, op0=ALU.mult),
                         r=[R_tb, R_pw], w=[R_ptmp])
                    P.op("dve", lambda e, dst_r=dst_r, src_r=src_r, cr=cr, tm=tm: e.scalar_tensor_tensor(out=dst_r, in0=src_r, scalar=cr, in1=tm, op0=ALU.mult, op1=ALU.subtract),
                         r=[R_tb, R_pw, R_ptmp], w=[R_tb])
                    P.op("dve", lambda e, tm=tm, src_i=src_i, cr=cr: e.tensor_scalar(out=tm, in0=src_i, scalar1=cr, scalar2=None, op0=ALU.mult),
                         r=[R_tb, R_pw], w=[R_ptmp])
                    P.op("dve", lambda e, dst_i=dst_i, src_r=src_r, ci=ci, tm=tm: e.scalar_tensor_tensor(out=dst_i, in0=src_r, scalar=ci, in1=tm, op0=ALU.mult, op1=ALU.add),
                         r=[R_tb, R_pw, R_ptmp], w=[R_tb])
                    n *= 2
                    if n < 512:
                        P.op("dve", lambda e, cr=cr, ci=ci: e.tensor_tensor(out=pw2[:, 0:1], in0=cr, in1=cr, op=ALU.mult), r=[R_pw], w=[R_ptmp])
                        P.op("dve", lambda e, cr=cr, ci=ci: e.tensor_tensor(out=pw2[:, 1:2], in0=ci, in1=ci, op=ALU.mult), r=[R_pw], w=[R_ptmp])
                        P.op("dve", lambda e, cr=cr, ci=ci: e.tensor_tensor(out=pw2[:, 2:3], in0=cr, in1=ci, op=ALU.mult), r=[R_pw], w=[R_ptmp])
                        P.op("dve", lambda e, cr=cr: e.tensor_tensor(out=cr, in0=pw2[:, 0:1], in1=pw2[:, 1:2], op=ALU.subtract), r=[R_ptmp], w=[R_pw])
                        P.op("dve", lambda e, ci=ci: e.tensor_scalar(out=ci, in0=pw2[:, 2:3], scalar1=2.0, scalar2=None, op0=ALU.mult), r=[R_ptmp], w=[R_pw])

        braw = P.sb("braw", [128, 2, 2, 16], F32)
        R_braw = Res("braw")
        Bfull = P.sb("Bfull", [128, 2, 2, 64], F32)
        R_Bfull = Res("Bfull")
        BbT = P.sb("BbT", [64, 2, 2, 128], F32)
        R_BbT = Res("BbT")
        CT = P.sb("CT", [128, 2, 2, 64], F32)
        R_CT = Res("CT")
        identf = P.sb("identf", [128, 128], F32)
        R_if = Res("identf")
        P.op("dve", lambda e: e.tensor_copy(out=identf[:, :], in_=C.ident[:, :]), r=[C.R_ident], w=[R_if])
        P.op("pool", lambda e: e.memset(Bfull[:, :, :, :].rearrange("p a b c -> p (a b c)"), 0.0), w=[R_Bfull])
        P.op("pool", lambda e: e.memset(CT[:, :, :, :].rearrange("p a b c -> p (a b c)"), 0.0), w=[R_CT])
        for sb in range(2):
            P.dma("sp", braw[:, sb, 0, :], bre_d[2 * sb:2 * sb + 2].rearrange("g n p -> (g n) p"), w=[R_braw])
            P.dma("sp", braw[:, sb, 1, :], bim_d[2 * sb:2 * sb + 2].rearrange("g n p -> (g n) p"), w=[R_braw])
            for gl in range(2):
                g = 2 * sb + gl
                ps_ = slice(gl * 64, (gl + 1) * 64)
                cs = slice(g * 16, (g + 1) * 16)
                P.dma("sp", CT[ps_, sb, 0, cs], cre_d[g].rearrange("p n -> n p"), w=[R_CT], allow_slow_non_contiguous=True)
                P.dma("sp", CT[ps_, sb, 1, cs], cim_d[g].rearrange("p n -> n p"), w=[R_CT], allow_slow_non_contiguous=True)
                crr = sp_[ps_, sb, CR:CR + 1]
                cii = sp_[ps_, sb, CI:CI + 1]
                tm = ptmp[ps_, 0:16]
                P.op("dve", lambda e, tm=tm, sb=sb, ps_=ps_, cii=cii: e.tensor_scalar(out=tm, in0=braw[ps_, sb, 1, :], scalar1=cii, scalar2=None, op0=ALU.mult),
                     r=[R_braw, R_sp], w=[R_ptmp])
                P.op("dve", lambda e, tm=tm, sb=sb, ps_=ps_, cs=cs, crr=crr: e.scalar_tensor_tensor(out=Bfull[ps_, sb, 0, cs], in0=braw[ps_, sb, 0, :], scalar=crr, in1=tm,
                                                                                              op0=ALU.mult, op1=ALU.subtract),
                     r=[R_braw, R_sp, R_ptmp], w=[R_Bfull])
                P.op("dve", lambda e, tm=tm, sb=sb, ps_=ps_, cii=cii: e.tensor_scalar(out=tm, in0=braw[ps_, sb, 0, :], scalar1=cii, scalar2=None, op0=ALU.mult),
                     r=[R_braw, R_sp], w=[R_ptmp])
                P.op("dve", lambda e, tm=tm, sb=sb, ps_=ps_, cs=cs, crr=crr: e.scalar_tensor_tensor(out=Bfull[ps_, sb, 1, cs], in0=braw[ps_, sb, 1, :], scalar=crr, in1=tm,
                                                                                              op0=ALU.mult, op1=ALU.add),
                     r=[R_braw, R_sp, R_ptmp], w=[R_Bfull])
            P.op("dve", lambda e, sb=sb: e.tensor_scalar(out=CT[:, sb, 1, :], in0=CT[:, sb, 1, :], scalar1=-1.0, scalar2=None, op0=ALU.mult), r=[R_CT], w=[R_CT])
            for ri in range(2):
                P.op("pe", lambda e, sb=sb, ri=ri: e.transpose(out=banks[0][0:64, 0:128], in_=Bfull[:, sb, ri, :], identity=identf[:, :]),
                     r=[R_Bfull, R_if], w=[R_bk[0]])
                P.op("dve", lambda e, sb=sb, ri=ri: e.tensor_copy(out=BbT[:, sb, ri, :], in_=banks[0][0:64, 0:128]), r=[R_bk[0]], w=[R_BbT])
        dsk = P.sb("dsk", [64, 1], F32)
        R_dsk = Res("dsk")
        P.dma("sp", dsk[:, :], dsk_d.rearrange("(n o) -> n o", o=1), w=[R_dsk])
        onesf = P.sb("onesf", [128, 512], F32)
        R_onesf = Res("onesf")
        P.op("pool", lambda e: e.memset(onesf[:, :], 1.0), w=[R_onesf])
        carry = P.sb("carry", [128, 2, 2], F32)
        R_carry = Res("carry")
        P.op("pool", lambda e: e.memset(carry[:, :, :].rearrange("p a b -> p (a b)"), 0.0), w=[R_carry])

        hsb = [P.sb("hsb0", [128, D], F32)] * 2
        R_hsb = [Res("hsb0")] * 2
        xn = P.sb("xn", [128, D], BF16)
        R_xn = Res("xn")
        xnT = P.sb("xnT", [128, 8, 512], BF16)
        R_xnT = Res("xnT")
        stt = P.sb("stt", [128, 2], F32)
        R_stt = [Res("stt0"), Res("stt1")]
        uT = P.sb("uT", [64, 512], F32)
        R_uT = Res("uT")
        S5 = {n: P.sb("s5_" + n, [128, 512], F32) for n in ("sre", "sim", "cre", "cim", "t1", "t2", "wre", "wim")}
        R_S5 = {n: Res("s5_" + n) for n in S5}
        S5["xre"], S5["xim"] = S5["sre"], S5["sim"]
        R_S5["xre"], R_S5["xim"] = R_S5["sre"], R_S5["sim"]
        ysb = P.sb("ysb", [64, 512], F32)
        R_ysb = Res("ysb")
        junk = stage[0]
        R_junk = R_stage[0]
        pV, pQ, pK, pU, pBr, pBi, pY = banks
        R_pV, R_pQ, R_pK, R_pU, R_pBr, R_pBi, R_pY = R_bk

        tile_i = 0
        for gi in range(NG):
            c0, ncol = grp_cols(gi)
            ntl = ncol // 128
            for tl in range(ntl):
                t = gi * 4 + tl
                hb, Rh = hsb[tile_i % 2], R_hsb[tile_i % 2]
                tile_i += 1
                P.dma("sp", hb[:, :], hs_d[t * 128:(t + 1) * 128, :], w=[Rh])
                P.op("act", lambda e, hb=hb: e.activation(out=junk[:, :], in_=hb[:, :], func=AF.Square, accum_out=stt[:, 0:1]), r=[Rh], w=[R_junk, R_stt[0]])
                C.rstd(stt[:, 0:1], stt[:, 0:1], R_stt[0], R_stt[0], D)
                P.op("act", lambda e, hb=hb: e.activation(out=xn[:, :], in_=hb[:, :], func=AF.Copy, scale=stt[:, 0:1]), r=[Rh, R_stt[0]], w=[R_xn])
                for k in range(8):
                    P.op("pe", lambda e, k=k: e.transpose(out=pT[:, k, :], in_=xn[:, k * 128:(k + 1) * 128], identity=C.ident[:, :]),
                         r=[R_xn, C.R_ident], w=[R_pT])
                P.op("dve", lambda e, tl=tl: e.tensor_copy(out=xnT[:, :, tl * 128:(tl + 1) * 128], in_=pT[:, :, :]), r=[R_pT], w=[R_xnT])
                for k in range(8):
                    P.op("pe", lambda e, k=k, tl=tl: e.matmul(out=pV[:, 0:64], lhsT=xnT[:, k, tl * 128:(tl + 1) * 128], rhs=whb[:, k, 256:320],
                                                               start=(k == 0), stop=(k == 7)), r=[R_xnT, R_wh], w=[R_pV])
                P.op("act", lambda e, t=t: e.activation(out=vA[:, t, 0:64], in_=pV[:, 0:64], func=AF.Copy), r=[R_pV, R_kinit], w=[R_v[gi]])
            cs = slice(c0, c0 + ncol)
            for (pp, Rp, wc, wn) in ((pQ, R_pQ, 0, 128), (pK, R_pK, 128, 128), (pU, R_pU, 320, 64)):
                for k in range(8):
                    P.op("pe", lambda e, pp=pp, k=k, wc=wc, wn=wn: e.matmul(out=pp[0:wn, 0:ncol], lhsT=whb[:, k, wc:wc + wn], rhs=xnT[:, k, 0:ncol],
                                                                             start=(k == 0), stop=(k == 7)), r=[R_xnT, R_wh], w=[Rp])
            P.op("act", lambda e, cs=cs: e.activation(out=QQ[:, cs], in_=pQ[:, 0:ncol], func=AF.Copy, scale=0.125), r=[R_pQ], w=[R_qT[gi]])
            P.op("dve", lambda e, cs=cs: e.tensor_copy(out=KZ[0:64, cs], in_=pK[0:64, 0:ncol]), r=[R_pK, R_kinit], w=[R_kT[gi]])
            P.op("dve", lambda e, cs=cs: e.tensor_scalar(out=KN[0:64, cs], in0=pK[0:64, 0:ncol], scalar1=-1.0, scalar2=None, op0=ALU.mult), r=[R_pK, R_kinit], w=[R_kT[gi]])
            P.op("act", lambda e: e.activation(out=uT[:, 0:ncol], in_=pU[0:64, 0:ncol], func=AF.Copy), r=[R_pU], w=[R_uT])
            for sb in range(2):
                nn = ncol
                P.op("pe", lambda e, sb=sb: e.matmul(out=pBr[:, 0:nn], lhsT=BbT[:, sb, 0, :], rhs=uT[:, 0:nn], start=True, stop=True), r=[R_BbT, R_uT], w=[R_pBr])
                P.op("pe", lambda e, sb=sb: e.matmul(out=pBi[:, 0:nn], lhsT=BbT[:, sb, 1, :], rhs=uT[:, 0:nn], start=True, stop=True), r=[R_BbT, R_uT], w=[R_pBi])
                A = lambda n: S5[n][:, 0:nn]
                P.op("act", lambda e: e.activation(out=A("sre"), in_=pBr[:, 0:nn], func=AF.Copy), r=[R_pBr], w=[R_S5["sre"]])
                P.op("act", lambda e: e.activation(out=A("sim"), in_=pBi[:, 0:nn], func=AF.Copy), r=[R_pBi], w=[R_S5["sim"]])
                ir_, ii_ = TB["ir"][:, sb, 0:nn], TB["ii"][:, sb, 0:nn]
                pr_, pi_ = TB["pr"][:, sb, 0:nn], TB["pi"][:, sb, 0:nn]
                P.op("dve", lambda e: e.tensor_tensor(out=A("t1"), in0=A("sim"), in1=ii_, op=ALU.mult), r=[R_S5["sim"], R_tb], w=[R_S5["t1"]])
                P.op("dve", lambda e: e.tensor_tensor(out=A("cre"), in0=A("sre"), in1=ir_, op=ALU.mult), r=[R_S5["sre"], R_tb], w=[R_S5["cre"]])
                P.op("dve", lambda e: e.tensor_tensor(out=A("cre"), in0=A("cre"), in1=A("t1"), op=ALU.subtract), r=[R_S5["cre"], R_S5["t1"]], w=[R_S5["cre"]])
                P.op("pool", lambda e: e.tensor_tensor(out=A("t2"), in0=A("sim"), in1=ir_, op=ALU.mult), r=[R_S5["sim"], R_tb], w=[R_S5["t2"]])
                P.op("pool", lambda e: e.tensor_tensor(out=A("cim"), in0=A("sre"), in1=ii_, op=ALU.mult), r=[R_S5["sre"], R_tb], w=[R_S5["cim"]])
                P.op("pool", lambda e: e.tensor_tensor(out=A("cim"), in0=A("cim"), in1=A("t2"), op=ALU.add), r=[R_S5["cim"], R_S5["t2"]], w=[R_S5["cim"]])
                P.op("dve", lambda e, sb=sb: e.tensor_tensor_scan(out=A("wre"), data0=onesf[:, 0:nn], data1=A("cre"), initial=carry[:, sb, 0:1], op0=ALU.mult, op1=ALU.add),
                     r=[R_onesf, R_S5["cre"], R_carry], w=[R_S5["wre"]])
                P.op("dve", lambda e, sb=sb: e.tensor_tensor_scan(out=A("wim"), data0=onesf[:, 0:nn], data1=A("cim"), initial=carry[:, sb, 1:2], op0=ALU.mult, op1=ALU.add),
                     r=[R_onesf, R_S5["cim"], R_carry], w=[R_S5["wim"]])
                P.op("dve", lambda e: e.tensor_tensor(out=A("t1"), in0=A("wim"), in1=pi_, op=ALU.mult), r=[R_S5["wim"], R_tb], w=[R_S5["t1"]])
                P.op("dve", lambda e: e.tensor_tensor(out=A("xre"), in0=A("wre"), in1=pr_, op=ALU.mult), r=[R_S5["wre"], R_tb], w=[R_S5["xre"]])
                P.op("dve", lambda e: e.tensor_tensor(out=A("xre"), in0=A("xre"), in1=A("t1"), op=ALU.subtract), r=[R_S5["xre"], R_S5["t1"]], w=[R_S5["xre"]])
                P.op("pool", lambda e: e.tensor_tensor(out=A("t2"), in0=A("wim"), in1=pr_, op=ALU.mult), r=[R_S5["wim"], R_tb], w=[R_S5["t2"]])
                P.op("pool", lambda e: e.tensor_tensor(out=A("xim"), in0=A("wre"), in1=pi_, op=ALU.mult), r=[R_S5["wre"], R_tb], w=[R_S5["xim"]])
                P.op("pool", lambda e: e.tensor_tensor(out=A("xim"), in0=A("xim"), in1=A("t2"), op=ALU.add), r=[R_S5["xim"], R_S5["t2"]], w=[R_S5["xim"]])
                xer, xei = S5["xre"][:, nn - 1:nn], S5["xim"][:, nn - 1:nn]
                ar_, ai_ = sp_[:, sb, AR:AR + 1], sp_[:, sb, AI:AI + 1]
                P.op("dve", lambda e: e.tensor_tensor(out=pw2[:, 0:1], in0=xei, in1=ai_, op=ALU.mult), r=[R_S5["xim"], R_sp], w=[R_ptmp])
                P.op("dve", lambda e: e.tensor_tensor(out=pw2[:, 1:2], in0=xei, in1=ar_, op=ALU.mult), r=[R_S5["xim"], R_sp], w=[R_ptmp])
                P.op("dve", lambda e, sb=sb: e.scalar_tensor_tensor(out=carry[:, sb, 0:1], in0=xer, scalar=ar_, in1=pw2[:, 0:1], op0=ALU.mult, op1=ALU.subtract),
                     r=[R_S5["xre"], R_sp, R_ptmp], w=[R_carry])
                P.op("dve", lambda e, sb=sb: e.scalar_tensor_tensor(out=carry[:, sb, 1:2], in0=xer, scalar=ai_, in1=pw2[:, 1:2], op0=ALU.mult, op1=ALU.add),
                     r=[R_S5["xre"], R_sp, R_ptmp], w=[R_carry])
                P.op("pe", lambda e, sb=sb: e.matmul(out=pY[0:64, 0:nn], lhsT=CT[:, sb, 0, :], rhs=A("xre"), start=(sb == 0), stop=False), r=[R_CT, R_S5["xre"]], w=[R_pY])
                P.op("pe", lambda e, sb=sb: e.matmul(out=pY[0:64, 0:nn], lhsT=CT[:, sb, 1, :], rhs=A("xim"), start=False, stop=(sb == 1)), r=[R_CT, R_S5["xim"]], w=[R_pY])
            P.op("dve", lambda e: e.scalar_tensor_tensor(out=ysb[:, 0:ncol], in0=uT[:, 0:ncol], scalar=dsk[:, 0:1], in1=pY[0:64, 0:ncol], op0=ALU.mult, op1=ALU.add),
                 r=[R_uT, R_dsk, R_pY], w=[R_ysb])
            P.dma("pool", ys_d[:, cs], ysb[:, 0:ncol], r=[R_ysb])

        P.barrier()
        esA.close()
        P.es = es
        onesf = P.sb("onesfB", [128, 1], F32)
        R_onesf = Res("onesfB")
        P.op("pool", lambda e: e.memset(onesf[:, :], 1.0), w=[R_onesf])
        Tincl = P.sb("Tincl", [128, 128], BF16)
        R_Ti = Res("Tincl")
        P.op("pool", lambda e: e.memset(Tincl[:, :], 1.0), w=[R_Ti])
        P.op("pool", lambda e: e.affine_select(out=Tincl[:, :], in_=Tincl[:, :], pattern=[[-1, 128]], compare_op=ALU.is_ge, fill=0.0, base=0, channel_multiplier=1),
             r=[R_Ti], w=[R_Ti])
        onesb512 = P.sb("onesb512", [128, 512], BF16)
        R_o512 = Res("o512")
        P.op("pool", lambda e: e.memset(onesb512[:, :], 1.0), w=[R_o512])
        masks = P.sb("masks", [128, 4, 512], BF16)
        R_masks = Res("masks")
        for i in range(4):
            P.op("pool", lambda e, i=i: e.affine_select(out=masks[:, i, :], in_=onesb512[:, :], pattern=[[1, 512]], compare_op=ALU.is_gt, fill=0.0,
                                                        base=-128 * i, channel_multiplier=-1), r=[R_o512], w=[R_masks])
        padcol = P.sb("padcol", [128, 1], F32)
        R_pad = Res("padcol")
        P.op("pool", lambda e: e.affine_select(out=padcol[:, :], in_=onesf[:, 0:1], pattern=[[0, 1]], compare_op=ALU.is_ge, fill=0.0, base=-PADF, channel_multiplier=1),
             r=[R_onesf], w=[R_pad])
        NB = 3
        eb = [P.sb(f"eb{i}", [128, 512], F32) for i in range(NB)]
        R_eb = [Res(f"eb{i}") for i in range(NB)]
        spb = [P.sb(f"spb{i}", [128, 512], BF16) for i in range(NB)]
        R_spb = [Res(f"spb{i}") for i in range(NB)]
        wb_ = [P.sb("wbb0", [128, 512], BF16), P.sb("wbb1", [128, 512], BF16)]
        R_wb = [Res("wbb0"), Res("wbb1")]
        S16 = [P.sb(f"S16_{i}", [128, 512], BF16) for i in range(4)]
        R_S16 = [Res(f"S16_{i}") for i in range(4)]
        osb_sb = P.sb("osb_sb", [64, 512], F32)
        R_osb = Res("osb_sb")
        pz = [banks[0], banks[1], banks[5]]
        R_pz = [R_bk[0], R_bk[1], R_bk[5]]
        pc = [banks[2], banks[3]]
        R_pc = [R_bk[2], R_bk[3]]
        po = banks[4]
        R_po = R_bk[4]
        iters = []
        for Q in range(NG):
            jmax = min(4 * Q + 3, NT - 1)
            for j in range(jmax, -1, -1):
                iters.append((Q, j, jmax))

        def maskop(buf, Rbuf, Q, j, nq):
            diag = j >= 4 * Q
            if not (diag or j == 0):
                return
            mk = masks[:, j - 4 * Q, 0:nq] if diag else onesb512[:, 0:nq]
            if j == 0:
                P.op("dve", lambda e: e.scalar_tensor_tensor(out=buf[:, 0:nq], in0=buf[:, 0:nq], scalar=padcol[:, 0:1], in1=mk,
                                                             op0=ALU.mult, op1=ALU.mult), r=[Rbuf, R_pad, R_masks, R_o512], w=[Rbuf])
            else:
                P.op("dve", lambda e: e.tensor_tensor(out=buf[:, 0:nq], in0=buf[:, 0:nq], in1=mk, op=ALU.mult), r=[Rbuf, R_masks], w=[Rbuf])

        def stageA(i):
            Q, j, jmax = iters[i]
            q0, nq = grp_cols(Q)
            qs = slice(q0, q0 + nq)
            ks = slice(j * 128, (j + 1) * 128)
            b = i % NB
            k = jmax - j
            P.op("pe", lambda e: e.matmul(out=pz[b][:, 0:nq], lhsT=KZ[:, ks], rhs=QQ[:, qs], start=True, stop=True),
                 r=[R_kT[j // 4], R_qT[Q]], w=[R_pz[b]])
            P.op("act", lambda e: e.activation(out=eb[b][:, 0:nq], in_=pz[b][:, 0:nq], func=AF.Exp), r=[R_pz[b]], w=[R_eb[b]])
            P.op("act", lambda e: e.activation(out=spb[b][:, 0:nq], in_=eb[b][:, 0:nq], func=AF.Ln, bias=1.0), r=[R_eb[b]], w=[R_spb[b]])
            maskop(spb[b], R_spb[b], Q, j, nq)
            if j > 0:
                if k == 0:
                    P.op("pool", lambda e: e.tensor_copy(out=S16[(k + 1) % 4][:, 0:nq], in_=spb[b][:, 0:nq]), r=[R_spb[b]], w=[R_S16[(k + 1) % 4]])
                else:
                    P.op("pool", lambda e: e.tensor_tensor(out=S16[(k + 1) % 4][:, 0:nq], in0=S16[k % 4][:, 0:nq], in1=spb[b][:, 0:nq], op=ALU.add),
                         r=[R_S16[k % 4], R_spb[b]], w=[R_S16[(k + 1) % 4]])

        def stageB1(i):
            Q, j, jmax = iters[i]
            q0, nq = grp_cols(Q)
            qs = slice(q0, q0 + nq)
            ks = slice(j * 128, (j + 1) * 128)
            b = i % NB
            c = i % 2
            k = jmax - j
            P.op("pe", lambda e: e.matmul(out=pc[c][:, 0:nq], lhsT=Tincl[:, :], rhs=spb[b][:, 0:nq], start=True, stop=False),
                 r=[R_Ti, R_spb[b]], w=[R_pc[c]])
            if k > 0:
                P.op("pe", lambda e: e.matmul(out=pc[c][:, 0:nq], lhsT=C.ones_bf[:, :], rhs=S16[k % 4][:, 0:nq], start=False, stop=False),
                     r=[C.R_ones, R_S16[k % 4]], w=[R_pc[c]])
            P.op("pe", lambda e: e.matmul(out=pc[c][:, 0:nq], lhsT=KN[:, ks], rhs=QQ[:, qs], start=False, stop=True),
                 r=[R_kT[j // 4], R_qT[Q]], w=[R_pc[c]])
            P.op("act", lambda e: e.activation(out=wb_[c][:, 0:nq], in_=pc[c][:, 0:nq], func=AF.Exp, scale=-1.0), r=[R_pc[c]], w=[R_wb[c]])
            maskop(wb_[c], R_wb[c], Q, j, nq)

        def stageB2(i):
            Q, j, jmax = iters[i]
            q0, nq = grp_cols(Q)
            qs = slice(q0, q0 + nq)
            c = i % 2
            P.op("pe", lambda e: e.matmul(out=po[:, 0:nq], lhsT=vA[:, j, :], rhs=wb_[c][:, 0:nq], start=(j == jmax), stop=(j == 0)),
                 r=[R_v[j // 4], R_wb[c]], w=[R_po])
            if j == 0:
                P.op("dve", lambda e: e.tensor_copy(out=osb_sb[:, 0:nq], in_=po[0:64, 0:nq]), r=[R_po], w=[R_osb])
                P.dma("pool", osb_d[:, qs], osb_sb[:, 0:nq], r=[R_osb])

        n_it = len(iters)
        LA = 2
        for i in range(min(LA, n_it)):
            stageA(i)
        for i in range(n_it):
            if i + LA < n_it:
                stageA(i + LA)
            stageB1(i)
            if i > 0:
                stageB2(i - 1)
        stageB2(n_it - 1)
        P.finish()
    return nc


def run_mix_even(hs_pad, I):
    nc = get_prog("mix_even", build_mix_even)
    w = I["w_in_even"][0]
    in_maps = []
    for h in range(NCORES):
        wq, wk = w[:, h * 64:(h + 1) * 64], w[:, 512 + h * 64:512 + (h + 1) * 64]
        wh = np.concatenate([wq, wq, wk, wk, w[:, 1024 + h * 64:1024 + (h + 1) * 64], w[:, 1536 + h * 64:1536 + (h + 1) * 64]], axis=1)
        gs = slice(4 * h, 4 * h + 4)
        in_maps.append({
            "hs": hs_pad, "wh": np.ascontiguousarray(wh), "g_pre": I["pre_mix_norm"][0],
            "lam_re": np.ascontiguousarray(I["s5_lambda_re"][0, gs]), "lam_im": np.ascontiguousarray(I["s5_lambda_im"][0, gs]),
            "log_dt": np.ascontiguousarray(I["s5_log_dt"][0, gs]),
            "b_re": np.ascontiguousarray(I["s5_b_re"][0, gs]), "b_im": np.ascontiguousarray(I["s5_b_im"][0, gs]),
            "c_re": np.ascontiguousarray(I["s5_c_re"][0, gs]), "c_im": np.ascontiguousarray(I["s5_c_im"][0, gs]),
            "d_skip": np.ascontiguousarray(I["s5_d"][0, h * 64:(h + 1) * 64]),
        })
    res = run_bass_kernel_spmd(nc, in_maps, core_ids=list(range(NCORES)))
    osb = np.concatenate([r["osbT"].T for r in res.results], axis=1)
    ys = np.concatenate([r["ysT"].T for r in res.results], axis=1)
    return np.ascontiguousarray(osb), np.ascontiguousarray(ys)


def build_mix_odd(ng_limit=NG):
    nc = bass.Bass("TRN2", target_bir_lowering=False)
    dr = lambda n, s, k="ExternalInput": nc.dram_tensor(n, list(s), F32, kind=k).ap()
    hs_d = dr("hs", [LP, D])
    wh_d = dr("wh", [D, 516])
    g_d = dr("g_pre", [D])
    cw_d = dr("convw", [128, 12])
    alog_d = dr("a_log", [1])
    dtb_d = dr("dt_bias", [1])
    gdn_d = dr("g_dn", [128])
    og_d = dr("og", [LP, 128], "ExternalOutput")

    with ExitStack() as es:
        C = Ctx(nc, es)
        P = C.P
        whb = P.sb("whb", [128, 8, 516], BF16)
        R_wh = Res("wh")
        gsm = P.sb("gsm", [128, 8], F32)
        R_gsm = Res("gsm")
        stage = [P.sb("stage0", [128, 1024], F32), P.sb("stage1", [128, 1024], F32)]
        R_stage = [Res("st0"), Res("st1")]
        bk = [P.ps(f"bk{i}", [128, 512], F32) for i in range(7)]
        R_bk = [Res(f"bk{i}", True) for i in range(7)]
        pT = P.ps("pT", [128, 8, 128], BF16)
        R_pT = Res("pT", True)
        P.dma("sp", gsm[:, :], g_d.rearrange("(k p) -> p k", p=128), w=[R_gsm], allow_slow_non_contiguous=True)
        C.load_weight(wh_d, whb, R_wh, stage, R_stage, 8, 516, gsc=gsm, R_g=R_gsm)
        cst = P.sb("cst", [128, 16], F32)
        R_cst = Res("cst")
        P.dma("sp", cst[:, 0:12], cw_d, w=[R_cst])
        P.dma("sp", cst[:, 12:13], dtb_d.rearrange("(o n) -> o n", o=1).partition_broadcast(128), w=[R_cst])
        P.dma("sp", cst[:, 13:14], alog_d.rearrange("(o n) -> o n", o=1).partition_broadcast(128), w=[R_cst])
        P.op("act", lambda e: e.activation(out=cst[:, 13:14], in_=cst[:, 13:14], func=AF.Exp), r=[R_cst], w=[R_cst])
        P.op("dve", lambda e: e.tensor_scalar(out=cst[:, 13:14], in0=cst[:, 13:14], scalar1=-1.0, scalar2=None, op0=ALU.mult), r=[R_cst], w=[R_cst])
        Gdn = P.sb("Gdn", [128, 128], F32)
        R_Gdn = Res("Gdn")
        P.dma("sp", Gdn[:, :], gdn_d.partition_broadcast(128), w=[R_Gdn])
        onesf = P.sb("onesf", [128, 128], F32)
        R_onesf = Res("onesf")
        P.op("pool", lambda e: e.memset(onesf[:, :], 1.0), w=[R_onesf])
        identf = P.sb("identf", [128, 128], F32)
        R_if = Res("identf")
        P.op("dve", lambda e: e.tensor_copy(out=identf[:, :], in_=C.ident[:, :]), r=[C.R_ident], w=[R_if])
        maskS = P.sb("maskS", [128, 128], F32)
        maskST = P.sb("maskST", [128, 128], F32)
        TriX = P.sb("TriX", [128, 130], F32)
        R_mk = Res("masks")
        P.op("pool", lambda e: e.affine_select(out=maskS[:, :], in_=onesf[:, :], pattern=[[-1, 128]], compare_op=ALU.is_gt, fill=0.0, base=0, channel_multiplier=1),
             r=[R_onesf], w=[R_mk])
        P.op("pool", lambda e: e.memset(maskS[64:128, 0:64], 0.0), r=[R_mk], w=[R_mk])
        P.op("pool", lambda e: e.affine_select(out=maskST[:, :], in_=onesf[:, :], pattern=[[1, 128]], compare_op=ALU.is_gt, fill=0.0, base=0, channel_multiplier=-1),
             r=[R_onesf], w=[R_mk])
        P.op("pool", lambda e: e.memset(maskST[0:64, 64:128], 0.0), r=[R_mk], w=[R_mk])
        P.op("pool", lambda e: e.affine_select(out=TriX[:, 0:128], in_=onesf[:, :], pattern=[[1, 128]], compare_op=ALU.is_ge, fill=0.0, base=0, channel_multiplier=-1),
             r=[R_onesf], w=[R_mk])
        P.op("pool", lambda e: e.memset(TriX[0:64, 64:128], 0.0), r=[R_mk], w=[R_mk])
        P.op("pool", lambda e: e.memset(TriX[:, 128:130], 0.0), r=[R_mk], w=[R_mk])
        P.op("pool", lambda e: e.memset(TriX[0:64, 128:129], 1.0), r=[R_mk], w=[R_mk])
        P.op("pool", lambda e: e.memset(TriX[64:128, 129:130], 1.0), r=[R_mk], w=[R_mk])
        maskIT = TriX
        mblk = P.sb("mblk", [128, 3, 128], F32)
        for bi, bs in enumerate((16, 32, 64)):
            nb_ = 128 // bs
            P.op("pool", lambda e: e.affine_select(out=mblk[:, bi, :], in_=onesf[:, :], pattern=[[-bs, nb_], [0, bs]], compare_op=ALU.is_ge, fill=0.0,
                                                   base=0, channel_multiplier=1), r=[R_onesf, R_mk], w=[R_mk])
            P.op("pool", lambda e: e.affine_select(out=mblk[:, bi, :], in_=mblk[:, bi, :], pattern=[[bs, nb_], [0, bs]], compare_op=ALU.is_ge, fill=0.0,
                                                   base=bs - 1, channel_multiplier=-1), r=[R_mk], w=[R_mk])
        P.op("pool", lambda e: e.tensor_tensor(out=mblk[:, 2, :], in0=mblk[:, 2, :], in1=mblk[:, 1, :], op=ALU.subtract), r=[R_mk], w=[R_mk])
        P.op("pool", lambda e: e.tensor_tensor(out=mblk[:, 1, :], in0=mblk[:, 1, :], in1=mblk[:, 0, :], op=ALU.subtract), r=[R_mk], w=[R_mk])

        hsb = [P.sb("hsb0", [128, D], F32), P.sb("hsb1", [128, D], F32)]
        R_hsb = [Res("hsb0"), Res("hsb1")]
        xn = P.sb("xn", [128, D], BF16)
        R_xn = Res("xn")
        xnT = P.sb("xnT", [128, 8, 512], BF16)
        R_xnT = Res("xnT")
        stt = P.sb("stt", [128, 4], F32)
        R_stt = Res("stt")
        raw = P.sb("raw", [128, 3, 515], F32)
        R_raw = [Res("rawq"), Res("rawk"), Res("rawv")]
        P.op("pool", lambda e: e.memset(raw[:, :, :].rearrange("p a b -> p (a b)"), 0.0), w=R_raw)
        acc = P.sb("acc", [128, 512], F32)
        R_acc = Res("acc")
        sil = P.sb("sil", [128, 3, 512], F32)
        R_sil = [Res("silq"), Res("silk"), Res("silv")]
        sq = P.sb("sq", [128, 512], F32)
        R_sq = Res("sq")
        rn = P.sb("rn", [128, 512], F32)
        R_rn = Res("rn")
        qnT = P.sb("qnT", [128, 512], BF16)
        knT2 = [P.sb("knT0", [128, 512], BF16), P.sb("knT1", [128, 512], BF16)]
        R_knT2 = [Res("knT0"), Res("knT1")]
        kbT = P.sb("kbT", [128, 512], BF16)
        vT = P.sb("vT", [128, 512], BF16)
        R_qnT, R_kbT, R_vT = Res("qnT"), Res("kbT"), Res("vT")
        brow = P.sb("brow", [1, 512], BF16)
        R_brow = Res("brow")
        NS = 8
        gz = P.sb("gz", [128, NS, 128], F32)
        R_gzs = [Res(f"gz{i}") for i in range(NS)]
        cols = P.sb("cols", [128, NS, 8], F32)
        R_cols = [Res(f"cols{i}") for i in range(NS)]
        egl = P.sb("egl", [128, NS, 2], F32)
        R_egls = [Res(f"egl{i}") for i in range(NS)]
        Gbc = P.sb("Gbc", [128, NS, 128], F32)
        R_Gbcs = [Res(f"Gbc{i}") for i in range(NS)]
        ktl = P.sb("ktl", [128, NS, 2, 128], BF16)
        ind = P.sb("ind", [128, 2], F32)
        R_ind = Res("ind")
        P.op("pool", lambda e: e.memset(ind[:, :], 0.0), w=[R_ind])
        P.op("pool", lambda e: e.memset(ind[0:64, 0:1], 1.0), r=[R_ind], w=[R_ind])
        P.op("pool", lambda e: e.memset(ind[64:128, 1:2], 1.0), r=[R_ind], w=[R_ind])
        osb_ = P.sb("o_sb", [128, 128], F32)
        R_osb_ = Res("o_sb")
        bv = P.sb("bv", [128, NS, 128], BF16)
        ktok = P.sb("ktok", [128, NS, 128], BF16)
        nUT = P.sb("nUT", [128, NS, 128], BF16)
        W1n = P.sb("W1n", [128, NS, 128], BF16)
        V1s = P.sb("V1s", [128, NS, 128], BF16)
        nDT = P.sb("nDT", [128, NS, 2, 128], BF16)
        PTs = P.sb("PTs", [128, NS, 128], BF16)
        Rsb = P.sb("Rsb", [128, NS, 128], F32)
        Nsb = P.sb("Nsb", [128, NS, 2, 128], F32)
        R_x = {n: [Res(f"{n}{i}") for i in range(NS)] for n in ("ktok", "nUT", "W1n", "V1s", "nDT", "PTs", "Rsb", "Nsb")}
        Tt = P.sb("Tt", [128, 128], F32)
        R_Tt = Res("Tt")
        R_ktls, R_bvs = [Res(f"ktl{i}") for i in range(NS)], [Res(f"bv{i}") for i in range(NS)]
        egb = P.sb("egb", [128, NS, 128], F32)
        R_egbs = [Res(f"egb{i}") for i in range(NS)]
        qg = P.sb("qg", [128, NS, 128], BF16)
        R_qgs = [Res(f"qg{i}") for i in range(NS)]
        Dm_ = P.sb("Dm", [128, NS, 128], F32)
        DTm_ = P.sb("DTm", [128, NS, 128], F32)
        DTI_ = P.sb("DTI", [128, NS, 128], F32)
        R_Dms, R_DTms, R_DTIs = ([Res(f"{n}{i}") for i in range(NS)] for n in ("Dm", "DTm", "DTI"))
        NW = 14
        Wk_ = P.sb("Wk", [128, 4, NW, 128], F32)
        R_Wks = [[Res(f"Wk{s_}_{i}") for i in range(NW)] for s_ in range(4)]
        ATb_ = P.sb("ATb", [128, NS, 128], BF16)
        R_ATbs = [Res(f"ATb{i}") for i in range(NS)]
        aiT_ = P.sb("aiT", [128, NS, 128], BF16)
        R_aiTs = [Res(f"aiT{i}") for i in range(NS)]
        rt = P.sb("rt", [128, 128], BF16)
        vn = P.sb("vn", [128, 128], BF16)
        R_rt, R_vn = Res("rt"), Res("vn")
        S32 = P.sb("S32", [128, 128], F32)
        Sbf = P.sb("Sbf", [128, 128], BF16)
        R_S32, R_Sbf = Res("S32"), Res("Sbf")
        P.op("pool", lambda e: e.memset(S32[:, :], 0.0), w=[R_S32])
        P.op("pool", lambda e: e.memset(Sbf[:, :], 0.0), w=[R_Sbf])
        ogs = P.sb("ogs", [128, 128], F32)
        R_ogs = Res("ogs")
        junk = stage[0]
        R_junk = R_stage[0]
        pQKV = [bk[0], bk[1], bk[2]]
        R_pQKV = [R_bk[0], R_bk[1], R_bk[2]]

        tile_i = 0
        pending_scan = None
        R_stt2 = Res("stt2")
        for gi in range(ng_limit):
            c0, ncol = grp_cols(gi)
            ntl = ncol // 128
            N = ncol
            for tl in range(ntl):
                t = gi * 4 + tl
                hb, Rh = hsb[tile_i % 2], R_hsb[tile_i % 2]
                tile_i += 1
                tc_ = slice(tl * 128, (tl + 1) * 128)
                P.dma("sp", hb[:, :], hs_d[t * 128:(t + 1) * 128, :], w=[Rh])
                P.op("act", lambda e: e.activation(out=junk[:, :], in_=hb[:, :], func=AF.Square, accum_out=stt[:, 0:1]), r=[Rh], w=[R_junk, R_stt])
                C.rstd(stt[:, 0:1], stt[:, 0:1], R_stt, R_stt, D)
                P.op("act", lambda e: e.activation(out=xn[:, :], in_=hb[:, :], func=AF.Copy, scale=stt[:, 0:1]), r=[Rh, R_stt], w=[R_xn])
                for k in range(8):
                    P.op("pe", lambda e: e.transpose(out=pT[:, k, :], in_=xn[:, k * 128:(k + 1) * 128], identity=C.ident[:, :]), r=[R_xn, C.R_ident], w=[R_pT])
                P.op("dve", lambda e: e.tensor_copy(out=xnT[:, :, tc_], in_=pT[:, :, :]), r=[R_pT], w=[R_xnT])
                for k in range(8):
                    P.op("pe", lambda e: e.matmul(out=bk[5][:, 0:132], lhsT=xnT[:, k, tc_], rhs=whb[:, k, 384:516], start=(k == 0), stop=(k == 7)),
                         r=[R_xnT, R_wh], w=[R_bk[5]])
                sl_ = (gi % 2) * 4 + tl
                P.op("act", lambda e: e.activation(out=gz[:, sl_, :], in_=bk[5][:, 0:128], func=AF.Silu), r=[R_bk[5]], w=[R_gzs[sl_]])
                P.op("pool", lambda e: e.tensor_tensor(out=gz[:, sl_, :], in0=gz[:, sl_, :], in1=Gdn[:, :], op=ALU.mult), r=[R_gzs[sl_], R_Gdn], w=[R_gzs[sl_]])
                cc = lambda i: cols[:, sl_, i:i + 1]
                Rc = R_cols[sl_]
                P.op("act", lambda e: e.activation(out=cc(1), in_=bk[5][:, 130:131], func=AF.Sigmoid), r=[R_bk[5]], w=[Rc])
                P.op("dve", lambda e: e.tensor_tensor(out=cc(7), in0=bk[5][:, 128:129], in1=cst[:, 12:13], op=ALU.add), r=[R_bk[5], R_cst], w=[Rc])
                P.op("dve", lambda e: e.tensor_scalar(out=cc(0), in0=cc(7), scalar1=-1.0, scalar2=None, op0=ALU.mult), r=[Rc], w=[Rc])
                P.op("dve", lambda e: e.tensor_tensor(out=cc(0), in0=cc(0), in1=cc(7), op=ALU.max), r=[Rc], w=[Rc])
                P.op("act", lambda e: e.activation(out=cc(0), in_=cc(0), func=AF.Exp, scale=-1.0), r=[Rc], w=[Rc])
                P.op("dve", lambda e: e.tensor_scalar(out=cc(6), in0=cc(0), scalar1=2.0, scalar2=None, op0=ALU.add), r=[Rc], w=[Rc])
                P.op("dve", lambda e: e.reciprocal(out=cc(6), in_=cc(6)), r=[Rc], w=[Rc])
                P.op("dve", lambda e: e.tensor_tensor(out=cc(6), in0=cc(6), in1=cc(0), op=ALU.mult), r=[Rc], w=[Rc])
                P.op("dve", lambda e: e.tensor_tensor(out=cc(0), in0=cc(6), in1=cc(6), op=ALU.mult), r=[Rc], w=[Rc])
                P.op("dve", lambda e: e.tensor_scalar(out=cc(5), in0=cc(0), scalar1=1.0 / 15, scalar2=1.0 / 13, op0=ALU.mult, op1=ALU.add), r=[Rc], w=[Rc])
                for cf in (1.0 / 11, 1.0 / 9, 1.0 / 7, 1.0 / 5, 1.0 / 3, 1.0):
                    P.op("dve", lambda e: e.tensor_tensor(out=cc(5), in0=cc(5), in1=cc(0), op=ALU.mult), r=[Rc], w=[Rc])
                    P.op("dve", lambda e: e.tensor_scalar(out=cc(5), in0=cc(5), scalar1=cf, scalar2=None, op0=ALU.add), r=[Rc], w=[Rc])
                P.op("dve", lambda e: e.tensor_tensor(out=cc(5), in0=cc(5), in1=cc(6), op=ALU.mult), r=[Rc], w=[Rc])
                P.op("dve", lambda e: e.tensor_scalar(out=cc(7), in0=cc(7), scalar1=0.0, scalar2=None, op0=ALU.max), r=[Rc], w=[Rc])
                P.op("dve", lambda e: e.scalar_tensor_tensor(out=cc(0), in0=cc(5), scalar=2.0, in1=cc(7), op0=ALU.mult, op1=ALU.add), r=[Rc], w=[Rc])
                P.op("dve", lambda e: e.tensor_tensor(out=cc(0), in0=cc(0), in1=cst[:, 13:14], op=ALU.mult), r=[Rc, R_cst], w=[Rc])
            for wi in range(3):
                for k in range(8):
                    P.op("pe", lambda e: e.matmul(out=pQKV[wi][:, 0:N], lhsT=whb[:, k, wi * 128:(wi + 1) * 128], rhs=xnT[:, k, 0:N], start=(k == 0), stop=(k == 7)),
                         r=[R_xnT, R_wh], w=[R_pQKV[wi]])
            for k in range(8):
                P.op("pe", lambda e: e.matmul(out=bk[3][0:1, 0:N], lhsT=whb[:, k, 514:515], rhs=xnT[:, k, 0:N], start=(k == 0), stop=(k == 7)),
                     r=[R_xnT, R_wh], w=[R_bk[3]])
            P.op("act", lambda e: e.activation(out=brow[:, 0:N], in_=bk[3][0:1, 0:N], func=AF.Sigmoid), r=[R_bk[3]], w=[R_brow])
            P.op("pe", lambda e: e.matmul(out=bk[4][:, 0:N], lhsT=C.ones_bf[0:1, :], rhs=brow[0:1, 0:N], start=True, stop=True), r=[C.R_ones, R_brow], w=[R_bk[4]])
            for wi in range(3):
                P.op("act", lambda e: e.activation(out=raw[:, wi, 3:3 + N], in_=pQKV[wi][:, 0:N], func=AF.Copy), r=[R_pQKV[wi]], w=[R_raw[wi]])
                eng = "dve" if wi != 1 else "pool"
                P.op(eng, lambda e: e.tensor_scalar(out=acc[:, 0:N], in0=raw[:, wi, 0:N], scalar1=cst[:, wi * 4:wi * 4 + 1], scalar2=None, op0=ALU.mult),
                     r=[R_raw[wi], R_cst], w=[R_acc])
                for j in range(1, 4):
                    P.op("dve", lambda e: e.scalar_tensor_tensor(out=acc[:, 0:N], in0=raw[:, wi, j:j + N], scalar=cst[:, wi * 4 + j:wi * 4 + j + 1], in1=acc[:, 0:N],
                                                                 op0=ALU.mult, op1=ALU.add), r=[R_raw[wi], R_cst, R_acc], w=[R_acc])
                P.op("act", lambda e: e.activation(out=sil[:, wi, 0:N], in_=acc[:, 0:N], func=AF.Silu), r=[R_acc], w=[R_sil[wi]])
                P.op("pool", lambda e: e.tensor_copy(out=raw[:, wi, 0:3], in_=raw[:, wi, N:N + 3]), r=[R_raw[wi]], w=[R_raw[wi]])
            knT = knT2[gi % 2]
            R_knT = R_knT2[gi % 2]
            for wi, (dst, Rd, sc) in enumerate(((qnT, R_qnT, 128 ** -0.5), (knT, R_knT, 1.0))):
                P.op("act", lambda e: e.activation(out=sq[:, 0:N], in_=sil[:, wi, 0:N], func=AF.Square), r=[R_sil[wi]], w=[R_sq])
                P.op("pe", lambda e: e.matmul(out=bk[6][:, 0:N], lhsT=onesf[:, :], rhs=sq[:, 0:N], start=True, stop=True), r=[R_onesf, R_sq], w=[R_bk[6]])
                P.op("dve", lambda e: e.tensor_scalar(out=rn[:, 0:N], in0=bk[6][:, 0:N], scalar1=EPS, scalar2=None, op0=ALU.add), r=[R_bk[6]], w=[R_rn])
                P.op("act", lambda e: e.activation(out=rn[:, 0:N], in_=rn[:, 0:N], func=AF.Sqrt), r=[R_rn], w=[R_rn])
                P.op("dve", lambda e: e.reciprocal(out=rn[:, 0:N], in_=rn[:, 0:N]), r=[R_rn], w=[R_rn])
                P.op("dve", lambda e: e.scalar_tensor_tensor(out=dst[:, 0:N], in0=sil[:, wi, 0:N], scalar=sc, in1=rn[:, 0:N], op0=ALU.mult, op1=ALU.mult),
                     r=[R_sil[wi], R_rn], w=[Rd])
            P.op("dve", lambda e: e.tensor_tensor(out=kbT[:, 0:N], in0=knT[:, 0:N], in1=bk[4][:, 0:N], op=ALU.mult), r=[R_knT, R_bk[4]], w=[R_kbT])
            P.op("pool", lambda e: e.tensor_copy(out=vT[:, 0:N], in_=sil[:, 2, 0:N]), r=[R_sil[2]], w=[R_vT])

            def pre(tl, gi=gi, knT=knT, R_knT=R_knT):
                sl_ = (gi % 2) * 4 + tl
                tc_ = slice(tl * 128, (tl + 1) * 128)
                cc = lambda i: cols[:, sl_, i:i + 1]
                Rc = R_cols[sl_]
                bC, RC_ = bk[tl], R_bk[tl]
                bG, RG = bC, RC_
                bKK, RKK = bC, RC_
                bA, RA = bC[:, 0:256], RC_
                bB, RB = bC[:, 256:512], RC_
                pTk, pTv = pT[:, 2 * tl, :], pT[:, 2 * tl + 1, :]
                Gb, R_Gb = Gbc[:, sl_, :], R_Gbcs[sl_]
                Dm, DTm, DTI = Dm_[:, sl_, :], DTm_[:, sl_, :], DTI_[:, sl_, :]
                R_Dm, R_DTm, R_DTI = R_Dms[sl_], R_DTms[sl_], R_DTIs[sl_]
                R_Wk = R_Wks[tl]
                W = lambda i: Wk_[:, tl, i, :]
                P.op("dve", lambda e: e.tensor_scalar(out=Gb, in0=onesf[:, :], scalar1=cc(0), scalar2=None, op0=ALU.mult), r=[R_onesf, Rc], w=[R_Gb])
                P.op("pe", lambda e: e.matmul(out=bG[:, 0:130], lhsT=Gb, rhs=TriX[:, :], start=True, stop=True), r=[R_Gb, R_mk], w=[RG])
                P.op("pe", lambda e: e.matmul(out=bG[:, 256:258], lhsT=TriX[:, 0:128], rhs=cols[:, sl_, 0:2], start=True, stop=True), r=[R_mk, Rc], w=[RG])
                yield
                P.op("dve", lambda e: e.tensor_copy(out=cc(2), in_=bG[:, 256:257]), r=[RG], w=[Rc])
                P.op("dve", lambda e: e.tensor_scalar(out=cc(6), in0=bG[:, 256:257], scalar1=-1.0, scalar2=None, op0=ALU.mult), r=[RG], w=[Rc])
                P.op("dve", lambda e: e.tensor_copy(out=cols[0:64, sl_, 3:4], in_=bG[0:64, 128:129]), r=[RG], w=[Rc])
                P.op("dve", lambda e: e.tensor_copy(out=cols[64:128, sl_, 3:4], in_=bG[64:128, 129:130]), r=[RG], w=[Rc])
                P.op("dve", lambda e: e.tensor_tensor(out=cc(4), in0=cc(3), in1=cc(2), op=ALU.subtract), r=[Rc], w=[Rc])
                P.op("dve", lambda e: e.tensor_scalar(out=Dm, in0=bG[:, 0:128], scalar1=cc(2), scalar2=None, op0=ALU.subtract), r=[RG, Rc], w=[R_Dm])
                P.op("act", lambda e: e.activation(out=egl[:, sl_, :], in_=bG[:, 128:130], func=AF.Exp), r=[RG], w=[R_egls[sl_]])
                P.op("act", lambda e: e.activation(out=egb[:, sl_, :], in_=bG[:, 0:128], func=AF.Exp), r=[RG], w=[R_egbs[sl_]])
                yield
                P.op("act", lambda e: e.activation(out=cc(4), in_=cc(4), func=AF.Exp), r=[Rc], w=[Rc])
                P.op("act", lambda e: e.activation(out=cc(7), in_=cc(2), func=AF.Exp), r=[Rc], w=[Rc])
                P.op("dve", lambda e: e.tensor_scalar(out=DTm, in0=Dm, scalar1=0.0, scalar2=None, op0=ALU.min), r=[R_Dm], w=[R_DTm])
                P.op("dve", lambda e: e.tensor_scalar(out=Dm, in0=Dm, scalar1=0.0, scalar2=-1.0, op0=ALU.max, op1=ALU.mult), r=[R_Dm], w=[R_Dm])
                P.op("dve", lambda e: e.tensor_tensor(out=qg[:, sl_, :], in0=qnT[:, tc_], in1=egb[:, sl_, :], op=ALU.mult), r=[R_qnT, R_egbs[sl_]], w=[R_qgs[sl_]])
                P.op("pe", lambda e: e.transpose(out=pTk, in_=knT[:, tc_], identity=C.ident[:, :]), r=[R_knT, C.R_ident], w=[R_pT])
                P.op("pe", lambda e: e.transpose(out=pTv, in_=vT[:, tc_], identity=C.ident[:, :]), r=[R_vT, C.R_ident], w=[R_pT])
                P.op("pe", lambda e: e.matmul(out=bKK[:, 0:128], lhsT=kbT[:, tc_], rhs=knT[:, tc_], start=True, stop=True), r=[R_kbT, R_knT], w=[RKK])
                P.op("pe", lambda e: e.matmul(out=bKK[:, 128:256], lhsT=knT[:, tc_], rhs=kbT[:, tc_], start=True, stop=True), r=[R_kbT, R_knT], w=[RKK])
                P.op("pe", lambda e: e.matmul(out=bKK[:, 256:384], lhsT=knT[:, tc_], rhs=qnT[:, tc_], start=True, stop=True), r=[R_qnT, R_knT], w=[RKK])
                yield
                P.op("dve", lambda e: e.scalar_tensor_tensor(out=cc(5), in0=cc(7), scalar=-1.0, in1=cc(1), op0=ALU.mult, op1=ALU.mult), r=[Rc], w=[Rc])
                P.op("act", lambda e: e.activation(out=Dm, in_=Dm, func=AF.Exp), r=[R_Dm], w=[R_Dm])
                P.op("act", lambda e: e.activation(out=DTm, in_=DTm, func=AF.Exp), r=[R_DTm], w=[R_DTm])
                for ch in range(2):
                    P.op("dve", lambda e: e.tensor_scalar(out=ktl[:, sl_, ch, :], in0=pTk, scalar1=cc(4), scalar2=ind[:, ch:ch + 1], op0=ALU.mult, op1=ALU.mult),
                         r=[R_pT, Rc, R_ind], w=[R_ktls[sl_]])
                P.op("dve", lambda e: e.tensor_scalar(out=bv[:, sl_, :], in0=pTv, scalar1=cc(1), scalar2=None, op0=ALU.mult), r=[R_pT, Rc], w=[R_bvs[sl_]])
                P.op("dve", lambda e: e.tensor_copy(out=ktok[:, sl_, :], in_=pTk), r=[R_pT], w=[R_x["ktok"][sl_]])
                yield
                P.op("pool", lambda e: e.tensor_tensor(out=Dm, in0=Dm, in1=maskS[:, :], op=ALU.mult), r=[R_Dm, R_mk], w=[R_Dm])
                P.op("pool", lambda e: e.tensor_tensor(out=DTI, in0=DTm, in1=maskIT[:, 0:128], op=ALU.mult), r=[R_DTm, R_mk], w=[R_DTI])
                P.op("pool", lambda e: e.tensor_tensor(out=DTm, in0=DTm, in1=maskST[:, :], op=ALU.mult), r=[R_DTm, R_mk], w=[R_DTm])
                yield
                (LF, LTF, L_, LT_, O32, O32T, O64, O64T, X_, XT_, L2_, L2T_, Y_, Y2_) = range(NW)
                P.op("dve", lambda e: e.tensor_tensor(out=W(LF), in0=bKK[:, 0:128], in1=Dm, op=ALU.mult), r=[RKK, R_Dm], w=[R_Wk[LF]])
                P.op("dve", lambda e: e.tensor_tensor(out=W(LTF), in0=bKK[:, 128:256], in1=DTm, op=ALU.mult), r=[RKK, R_DTm], w=[R_Wk[LTF]])
                P.op("dve", lambda e: e.tensor_tensor(out=aiT_[:, sl_, :], in0=bKK[:, 256:384], in1=DTI, op=ALU.mult), r=[RKK, R_DTI], w=[R_aiTs[sl_]])
                yield
                for (dst, src, mi) in ((L_, LF, 0), (LT_, LTF, 0), (O32, LF, 1), (O32T, LTF, 1), (O64, LF, 2), (O64T, LTF, 2)):
                    P.op("pool", lambda e: e.tensor_tensor(out=W(dst), in0=W(src), in1=mblk[:, mi, :], op=ALU.mult), r=[R_Wk[src], R_mk], w=[R_Wk[dst]])
                P.op("pool", lambda e: e.tensor_tensor(out=W(X_), in0=identf[:, :], in1=W(L_), op=ALU.subtract), r=[R_if, R_Wk[L_]], w=[R_Wk[X_]])
                P.op("pool", lambda e: e.tensor_tensor(out=W(XT_), in0=identf[:, :], in1=W(LT_), op=ALU.subtract), r=[R_if, R_Wk[LT_]], w=[R_Wk[XT_]])
                yield

                def mm(out_ap, Rout, li, ri):
                    P.op("pe", lambda e: e.matmul(out=out_ap, lhsT=W(li), rhs=W(ri), start=True, stop=True), r=[R_Wk[li], R_Wk[ri]], w=[Rout])

                cl, clt, nl, nlt = L_, LT_, L2_, L2T_
                for lvl in range(3):
                    last = lvl == 2
                    mm(bA[:, 0:128], RA, clt, cl)
                    if not last:
                        mm(bA[:, 128:256], RA, cl, clt)
                    yield
                    P.op("act", lambda e: e.activation(out=W(nl), in_=bA[:, 0:128], func=AF.Copy), r=[RA], w=[R_Wk[nl]])
                    if not last:
                        P.op("act", lambda e: e.activation(out=W(nlt), in_=bA[:, 128:256], func=AF.Copy), r=[RA], w=[R_Wk[nlt]])
                    yield
                    mm(bB[:, 0:128], RB, nl, XT_)
                    mm(bB[:, 128:256], RB, XT_, nl)
                    yield
                    P.op("dve", lambda e: e.tensor_tensor(out=W(XT_), in0=bB[:, 0:128], in1=W(XT_), op=ALU.add), r=[RB, R_Wk[XT_]], w=[R_Wk[XT_]])
                    P.op("dve", lambda e: e.tensor_tensor(out=W(X_), in0=bB[:, 128:256], in1=W(X_), op=ALU.add), r=[RB, R_Wk[X_]], w=[R_Wk[X_]])
                    yield
                    cl, clt, nl, nlt = nl, nlt, cl, clt
                mm(bA[:, 0:128], RA, O32T, X_)
                mm(bA[:, 128:256], RA, O32, XT_)
                yield
                P.op("act", lambda e: e.activation(out=W(Y_), in_=bA[:, 0:128], func=AF.Copy), r=[RA], w=[R_Wk[Y_]])
                P.op("act", lambda e: e.activation(out=W(Y2_), in_=bA[:, 128:256], func=AF.Copy), r=[RA], w=[R_Wk[Y2_]])
                yield
                mm(bB[:, 0:128], RB, XT_, Y_)
                mm(bB[:, 128:256], RB, X_, Y2_)
                yield
                P.op("dve", lambda e: e.tensor_tensor(out=W(X_), in0=W(X_), in1=bB[:, 0:128], op=ALU.subtract), r=[RB, R_Wk[X_]], w=[R_Wk[X_]])
                P.op("dve", lambda e: e.tensor_tensor(out=W(XT_), in0=W(XT_), in1=bB[:, 128:256], op=ALU.subtract), r=[RB, R_Wk[XT_]], w=[R_Wk[XT_]])
                yield
                mm(bA[:, 0:128], RA, O64, XT_)
                yield
                P.op("act", lambda e: e.activation(out=W(Y2_), in_=bA[:, 0:128], func=AF.Copy), r=[RA], w=[R_Wk[Y2_]])
                yield
                mm(bB[:, 0:128], RB, X_, Y2_)
                yield
                P.op("dve", lambda e: e.tensor_tensor(out=ATb_[:, sl_, :], in0=W(XT_), in1=bB[:, 0:128], op=ALU.subtract), r=[RB, R_Wk[XT_]], w=[R_ATbs[sl_]])
                yield
                AT, R_AT = ATb_[:, sl_, :], R_ATbs[sl_]
                P.op("dve", lambda e: e.tensor_scalar(out=nUT[:, sl_, :], in0=AT, scalar1=cc(5), scalar2=None, op0=ALU.mult), r=[R_AT, Rc], w=[R_x["nUT"][sl_]])
                yield
                P.op("pe", lambda e: e.matmul(out=bA[:, 0:128], lhsT=nUT[:, sl_, :], rhs=ktok[:, sl_, :], start=True, stop=True),
                     r=[R_x["nUT"][sl_], R_x["ktok"][sl_]], w=[RA])
                P.op("pe", lambda e: e.matmul(out=bA[:, 128:256], lhsT=AT, rhs=bv[:, sl_, :], start=True, stop=True), r=[R_AT, R_bvs[sl_]], w=[RA])
                yield
                P.op("act", lambda e: e.activation(out=W1n[:, sl_, :], in_=bA[:, 0:128], func=AF.Copy), r=[RA], w=[R_x["W1n"][sl_]])
                P.op("dve", lambda e: e.tensor_copy(out=V1s[:, sl_, :], in_=bA[:, 128:256]), r=[RA], w=[R_x["V1s"][sl_]])
                yield
                for ch in range(2):
                    P.op("pe", lambda e: e.matmul(out=bB[:, ch * 128:(ch + 1) * 128], lhsT=W1n[:, sl_, :], rhs=ktl[:, sl_, ch, :], start=True, stop=True),
                         r=[R_x["W1n"][sl_], R_ktls[sl_]], w=[RB])
                P.op("pe", lambda e: e.matmul(out=bA[:, 0:128], lhsT=W1n[:, sl_, :], rhs=aiT_[:, sl_, :], start=True, stop=True),
                     r=[R_x["W1n"][sl_], R_aiTs[sl_]], w=[RA])
                P.op("pe", lambda e: e.matmul(out=bA[:, 128:256], lhsT=aiT_[:, sl_, :], rhs=V1s[:, sl_, :], start=True, stop=True),
                     r=[R_aiTs[sl_], R_x["V1s"][sl_]], w=[RA])
                yield
                P.op("act", lambda e: e.activation(out=nDT[:, sl_, :, :].rearrange("p a b -> p (a b)"), in_=bB[:, 0:256], func=AF.Copy), r=[RB], w=[R_x["nDT"][sl_]])
                P.op("dve", lambda e: e.tensor_tensor(out=PTs[:, sl_, :], in0=bA[:, 0:128], in1=qg[:, sl_, :], op=ALU.add), r=[RA, R_qgs[sl_]], w=[R_x["PTs"][sl_]])
                P.op("act", lambda e: e.activation(out=Rsb[:, sl_, :], in_=bA[:, 128:256], func=AF.Copy), r=[RA], w=[R_x["Rsb"][sl_]])
                yield
                for ch in range(2):
                    P.op("pe", lambda e: e.matmul(out=bB[:, ch * 128:(ch + 1) * 128], lhsT=ktl[:, sl_, ch, :], rhs=V1s[:, sl_, :], start=True, stop=True),
                         r=[R_ktls[sl_], R_x["V1s"][sl_]], w=[RB])
                yield
                P.op("dve", lambda e: e.tensor_copy(out=Nsb[:, sl_, :, :].rearrange("p a b -> p (a b)"), in_=bB[:, 0:256]), r=[RB], w=[R_x["Nsb"][sl_]])
                yield

            def scan(gi, ntl, knT, R_knT):
                pb, R_pb = bk[4], R_bk[4]
                for tl in range(ntl):
                    t = gi * 4 + tl
                    sl_ = (gi % 2) * 4 + tl
                    for ch in range(2):
                        P.op("dve", lambda e: e.scalar_tensor_tensor(out=Tt[:, :], in0=S32[:, :], scalar=egl[:, sl_, ch:ch + 1], in1=Nsb[:, sl_, ch, :], op0=ALU.mult, op1=ALU.add),
                             r=[R_S32, R_egls[sl_], R_x["Nsb"][sl_]], w=[R_Tt])
                        P.op("pe", lambda e: e.matmul(out=pb[:, ch * 128:(ch + 1) * 128], lhsT=nDT[:, sl_, ch, :], rhs=Sbf[:, :], start=True, stop=True),
                             r=[R_x["nDT"][sl_], R_Sbf], w=[R_pb])
                        P.op("pe", lambda e: e.matmul(out=pb[:, 256 + ch * 128:256 + (ch + 1) * 128], lhsT=PTs[:, sl_, :], rhs=Sbf[:, :], start=True, stop=True),
                             r=[R_x["PTs"][sl_], R_Sbf], w=[R_pb])
                        yield
                        P.op("dve", lambda e: e.tensor_tensor(out=Sbf[:, :], in0=pb[:, ch * 128:(ch + 1) * 128], in1=Tt[:, :], op=ALU.add), r=[R_pb, R_Tt], w=[R_Sbf])
                        P.op("dve", lambda e: e.tensor_tensor(out=S32[:, :], in0=pb[:, ch * 128:(ch + 1) * 128], in1=Tt[:, :], op=ALU.add), r=[R_pb, R_Tt], w=[R_S32])
                        yield
                    P.op("dve", lambda e: e.tensor_tensor(out=osb_[0:64, :], in0=pb[0:64, 256:384], in1=Rsb[0:64, sl_, :], op=ALU.add), r=[R_pb, R_x["Rsb"][sl_]], w=[R_osb_])
                    P.op("dve", lambda e: e.tensor_tensor(out=osb_[64:128, :], in0=pb[64:128, 384:512], in1=Rsb[64:128, sl_, :], op=ALU.add), r=[R_pb, R_x["Rsb"][sl_]], w=[R_osb_])
                    P.op("act", lambda e: e.activation(out=junk[:, 0:128], in_=osb_[:, :], func=AF.Square, accum_out=stt[:, 1:2]), r=[R_osb_], w=[R_junk, R_stt2])
                    C.rstd(stt[:, 1:2], stt[:, 1:2], R_stt2, R_stt2, 128)
                    P.op("dve", lambda e: e.scalar_tensor_tensor(out=ogs[:, :], in0=osb_[:, :], scalar=stt[:, 1:2], in1=gz[:, sl_, :], op0=ALU.mult, op1=ALU.mult),
                         r=[R_osb_, R_stt2, R_gzs[sl_]], w=[R_ogs])
                    P.dma("pool", og_d[t * 128:(t + 1) * 128, :], ogs[:, :], r=[R_ogs])
                    yield

            for pair0 in range(0, ntl, 4):
                gens = [pre(tl) for tl in range(pair0, min(pair0 + 4, ntl))]
                last_pair = pair0 + 4 >= ntl
                while gens:
                    for g_ in list(gens):
                        try:
                            next(g_)
                        except StopIteration:
                            gens.remove(g_)
                    if pending_scan is not None:
                        try:
                            next(pending_scan)
                        except StopIteration:
                            pending_scan = None
                if last_pair and pending_scan is not None:
                    for _ in pending_scan:
                        pass
                    pending_scan = None
            pending_scan = scan(gi, ntl, knT, R_knT)
        for _ in pending_scan:
            pass
        P.finish()
    return nc


def run_mix_odd(hs_pad, I):
    nc = get_prog("mix_odd", build_mix_odd)
    w = I["w_in_odd"][0]
    cwv = I["dn_conv_w"][0]
    in_maps = []
    for h in range(NCORES):
        sl = lambda base: slice(base + h * 128, base + (h + 1) * 128)
        zc = np.zeros((D, 1), np.float32)
        wh = np.concatenate([w[:, sl(0)], w[:, sl(1024)], w[:, sl(2048)], w[:, sl(3072)], w[:, 4096 + h:4097 + h], zc, w[:, 4104 + h:4105 + h], zc], axis=1)
        cw = np.concatenate([cwv[:, sl(0)].T, cwv[:, sl(1024)].T, cwv[:, sl(2048)].T], axis=1)
        in_maps.append({"hs": hs_pad, "wh": np.ascontiguousarray(wh), "g_pre": I["pre_mix_norm"][1], "convw": np.ascontiguousarray(cw),
                        "a_log": np.ascontiguousarray(I["dn_a_log"][0, h:h + 1]), "dt_bias": np.ascontiguousarray(I["dn_dt_bias"][0, h:h + 1]),
                        "g_dn": I["dn_out_norm"][0]})
    res = run_bass_kernel_spmd(nc, in_maps, core_ids=list(range(NCORES)))
    return np.ascontiguousarray(np.concatenate([r["og"] for r in res.results], axis=1))


def kernel(**inputs):
    I = {k: np.ascontiguousarray(np.asarray(v, dtype=np.float32)) for k, v in inputs.items()}
    x = I["x"][0]
    hs0 = np.ascontiguousarray(np.concatenate([np.zeros((PADF, D), np.float32), I["meta_tokens"], x], axis=0))
    osb, ys = run_mix_even(hs0, I)
    W0 = {"wglu": I["s5_w_glu"][0], "bglu": I["s5_b_glu"][0],
          "gmerge": np.ascontiguousarray(np.concatenate([I["sb_out_norm"][0], I["s5_out_norm"][0]])),
          "wout": I["w_out_even"][0], "w1": I["mlp_w1"][0], "w2": I["mlp_w2"][0],
          "g_postmix": I["post_mix_norm"][0], "g_premlp": I["pre_mlp_norm"][0], "g_postmlp": I["post_mlp_norm"][0]}
    hs1 = run_post("even", hs0, (osb, ys), W0)
    og = run_mix_odd(hs1, I)
    W1 = {"wout": I["w_out_odd"][0], "w1": I["mlp_w1"][1], "w2": I["mlp_w2"][1],
          "g_postmix": I["post_mix_norm"][1], "g_premlp": I["pre_mlp_norm"][1], "g_postmlp": I["post_mlp_norm"][1]}
    out = run_post("odd", hs1, og, W1)
    return np.ascontiguousarray(out[128:].reshape(1, 16384, D).astype(np.float32))
```

```python
from contextlib import ExitStack
import numpy as np
import concourse.bass as bass
import concourse.mybir as mybir
from concourse.bass_utils import run_bass_kernel_spmd

F32 = mybir.dt.float32
BF16 = mybir.dt.bfloat16
ALU = mybir.AluOpType
AF = mybir.ActivationFunctionType

NCORES = 8
D = 1024
DFF = 4096
NT = 129
LP = NT * 128
PADF = 112
NMETA = 16
TPC = 17
EPS = 1e-6


class Res:
    __slots__ = ("name", "lw", "rd", "psum")

    def __init__(self, name="", psum=False):
        self.name = name
        self.lw = None
        self.rd = {}
        self.psum = psum


class Prog:
    ENG = ("pe", "act", "dve", "pool", "sp")
    NDMA = 6

    def __init__(self, nc, es):
        self.nc = nc
        self.es = es
        self.lists = {k: [] for k in self.ENG}
        self.count = {k: 0 for k in self.ENG}
        self.sem = {}
        for k in self.ENG:
            self.sem[k] = es.enter_context(nc.semaphore("s_" + k))
        self.dsem = {}
        self.dcount = {}
        for q in ("sp", "pool", "act"):
            self.dsem[q] = [es.enter_context(nc.semaphore(f"d_{q}{i}")) for i in range(self.NDMA)]
            self.dcount[q] = 0
        self.waited = {}
        self.eobj = {"pe": nc.tensor, "act": nc.scalar, "dve": nc.vector, "pool": nc.gpsimd, "sp": nc.sync}

    def sb(self, name, shape, dt):
        return self.es.enter_context(self.nc.sbuf_tensor(name, list(shape), dt))

    def ps(self, name, shape, dt):
        return self.es.enter_context(self.nc.psum_tensor(name, list(shape), dt))

    def _semobj(self, key):
        if isinstance(key, tuple):
            return self.dsem[key[0]][key[1]]
        return self.sem[key]

    def _wait(self, eng, key, val):
        if key == eng and eng == "pe":
            return
        k = (eng, key)
        if self.waited.get(k, 0) >= val:
            return
        self.waited[k] = val
        so = self._semobj(key)
        self.eobj[eng].wait_ge(so, val)

    def _deps(self, eng, r, w):
        for x in r:
            if x.lw is not None:
                self._wait(eng, *x.lw)
            if x.psum:
                for key, val in x.rd.items():
                    if key != eng:
                        self._wait(eng, key, val)
        for x in w:
            if x.lw is not None:
                self._wait(eng, *x.lw)
            for key, val in x.rd.items():
                self._wait(eng, key, val)

    def _mark(self, tok, r, w):
        key, val = tok
        for x in r:
            if x.rd.get(key, 0) < val:
                x.rd[key] = val
        for x in w:
            x.lw = tok
            x.rd = {}

    def op(self, eng, fn, r=(), w=()):
        self._deps(eng, r, w)
        self.count[eng] += 1
        seq = self.count[eng]
        so = self.sem[eng]
        fn(self.eobj[eng]).then_inc(so, 1)
        self._mark((eng, seq), r, w)

    def dma(self, q, out, in_, r=(), w=(), **kw):
        i = self.dcount[q]
        self.dcount[q] += 1
        slot = i % self.NDMA
        key = (q, slot)
        val = 16 * (i // self.NDMA + 1)
        if i >= self.NDMA:
            self._wait(q, key, val - 16)
        self._deps(q, r, w)
        so = self.dsem[q][slot]
        self.eobj[q].dma_start(out=out, in_=in_, **kw).then_inc(so, 16)
        self._mark((key, val), r, w)

    def barrier(self):
        for e in self.ENG:
            for o in self.ENG:
                if o != e and self.count[o] > 0:
                    self._wait(e, o, self.count[o])
            for q in ("sp", "pool", "act"):
                n = self.dcount[q]
                for slot in range(min(n, self.NDMA)):
                    last_i = ((n - 1 - slot) // self.NDMA) * self.NDMA + slot
                    self._wait(e, (q, slot), 16 * (last_i // self.NDMA + 1))

    def finish(self):
        for q in ("sp", "pool", "act"):
            n = self.dcount[q]
            for slot in range(min(n, self.NDMA)):
                last_i = ((n - 1 - slot) // self.NDMA) * self.NDMA + slot
                self._wait(q, (q, slot), 16 * (last_i // self.NDMA + 1))


class Ctx:
    def __init__(self, nc, es):
        self.P = Prog(nc, es)
        self.nc = nc
        P = self.P
        self.ident = P.sb("ident", [128, 128], BF16)
        self.R_ident = Res("ident")
        P.op("pool", lambda e: e.memset(self.ident[:, :], 1.0), w=[self.R_ident])
        P.op("pool", lambda e: e.affine_select(out=self.ident[:, :], in_=self.ident[:, :], pattern=[[-1, 128]],
                                                 compare_op=ALU.is_equal, fill=0.0, base=0, channel_multiplier=1),
             r=[self.R_ident], w=[self.R_ident])
        self.ones_bf = P.sb("ones_bf", [128, 128], BF16)
        self.R_ones = Res("ones")
        P.op("pool", lambda e: e.memset(self.ones_bf[:, :], 1.0), w=[self.R_ones])
        self.rr = 0

    def rstd(self, ss_ap, out_ap, R_ss, R_out, n, eng2="dve"):
        P = self.P
        P.op("dve", lambda e: e.tensor_scalar(out=out_ap, in0=ss_ap, scalar1=1.0 / n, scalar2=EPS,
                                               op0=ALU.mult, op1=ALU.add), r=[R_ss], w=[R_out])
        P.op("act", lambda e: e.activation(out=out_ap, in_=out_ap, func=AF.Sqrt), r=[R_out], w=[R_out])
        P.op("dve", lambda e: e.reciprocal(out=out_ap, in_=out_ap), r=[R_out], w=[R_out])

    def cast_eng(self):
        self.rr += 1
        return ("dve", "act")[self.rr % 2]

    def load_weight(self, dram, wb, R_wb, stage, R_stage, nk, ncols, gsc=None, R_g=None, col0=0, colw=None):
        P = self.P
        idx = 0
        for k in range(nk):
            for c0 in range(0, ncols, 1024):
                cw = min(1024, ncols - c0)
                st, Rs = stage[idx % len(stage)], R_stage[idx % len(stage)]
                idx += 1
                P.dma("sp", st[:, 0:cw], dram[k * 128:(k + 1) * 128, col0 + c0:col0 + c0 + cw], w=[Rs])
                eng = self.cast_eng()
                rs = [Rs] + ([R_g] if gsc is not None else [])
                if gsc is not None:
                    if eng == "act":
                        P.op("act", lambda e, st=st, k=k, c0=c0, cw=cw: e.activation(
                            out=wb[:, k, c0:c0 + cw], in_=st[:, 0:cw], func=AF.Copy, scale=gsc[:, k:k + 1]), r=rs, w=[R_wb])
                    else:
                        P.op(eng, lambda e, st=st, k=k, c0=c0, cw=cw: e.tensor_scalar(
                            out=wb[:, k, c0:c0 + cw], in0=st[:, 0:cw], scalar1=gsc[:, k:k + 1], scalar2=None,
                            op0=ALU.mult), r=rs, w=[R_wb])
                else:
                    if eng == "act":
                        P.op("act", lambda e, st=st, k=k, c0=c0, cw=cw: e.activation(
                            out=wb[:, k, c0:c0 + cw], in_=st[:, 0:cw], func=AF.Copy), r=rs, w=[R_wb])
                    else:
                        P.op(eng, lambda e, st=st, k=k, c0=c0, cw=cw: e.tensor_copy(
                            out=wb[:, k, c0:c0 + cw], in_=st[:, 0:cw]), r=rs, w=[R_wb])


def build_post(kind):
    even = kind == "even"
    nc = bass.Bass("TRN2", target_bir_lowering=False)
    dr = lambda n, s, k="ExternalInput": nc.dram_tensor(n, list(s), F32, kind=k).ap()
    rows = TPC * 128
    hs_d = dr("hs", [rows, D])
    if even:
        osb_d = dr("osb", [rows, 512])
        ys5_d = dr("ys5", [rows, 512])
        wglu_d = dr("wglu", [512, 512])
        bglu_d = dr("bglu", [512])
        gmerge_d = dr("gmerge", [D])
    else:
        og_d = dr("og", [rows, D])
    wout_d = dr("wout", [D, D])
    w1_d = dr("w1", [D, DFF])
    w2_d = dr("w2", [DFF, D])
    g1_d = dr("g_postmix", [D])
    g2_d = dr("g_premlp", [D])
    g3_d = dr("g_postmlp", [D])
    out_d = dr("out", [rows, D], "ExternalOutput")

    with ExitStack() as es:
        C = Ctx(nc, es)
        P = C.P
        w1b = P.sb("w1b", [128, 8, DFF], BF16)
        w2b = P.sb("w2b", [128, 32, D], BF16)
        woutb = P.sb("woutb", [128, 8, D], BF16)
        R_w1, R_w2, R_wout = Res("w1"), Res("w2"), Res("wout")
        stage = [P.sb(f"stage{i}", [128, 1024], F32) for i in range(4)]
        R_stage = [Res(f"st{i}") for i in range(4)]
        gsm = P.sb("gsm", [128, 3, 8], F32)
        R_gsm = Res("gsm")
        G1 = P.sb("G1", [128, D], F32)
        G3 = P.sb("G3", [128, D], F32)
        R_G = Res("G")
        hs = P.sb("hs_t", [128, D], F32)
        mg = P.sb("mg_t", [128, D], F32)
        xn = P.sb("xn_t", [128, D], BF16)
        xnT = P.sb("xnT_t", [128, 8, 128], BF16)
        h1T = P.sb("h1T_t", [128, 32, 128], BF16)
        rl = P.sb("rl_t", [128, 512], BF16)
        tmp = P.sb("tmp_t", [128, 512], F32)
        st = P.sb("stat", [128, 8], F32)
        R_hs, R_mg, R_xn, R_xnT, R_h1T, R_rl, R_tmp = (Res(n) for n in ("hs", "mg", "xn", "xnT", "h1T", "rl", "tmp"))
        R_st = [Res(f"st{i}") for i in range(8)]
        pT = P.ps("pT", [128, 8, 128], BF16)
        pA = P.ps("pA", [128, 512], F32)
        pB = P.ps("pB", [128, 512], F32)
        pH = [P.ps("pH0", [128, 4, 128], F32), P.ps("pH1", [128, 4, 128], F32)]
        R_pT, R_pA, R_pB = Res("pT", True), Res("pA", True), Res("pB", True)
        R_pH = [Res("pH0", True), Res("pH1", True)]
        junk = stage[0]
        R_junk = R_stage[0]

        P.dma("sp", G1[:, :], g1_d.partition_broadcast(128), w=[R_G])
        P.dma("sp", G3[:, :], g3_d.partition_broadcast(128), w=[R_G])
        P.dma("sp", gsm[:, 0, :], g2_d.rearrange("(k p) -> p k", p=128), w=[R_gsm], allow_slow_non_contiguous=True)
        if even:
            P.dma("sp", gsm[:, 1, :], gmerge_d.rearrange("(k p) -> p k", p=128), w=[R_gsm], allow_slow_non_contiguous=True)
            wglub = P.sb("wglub", [128, 4, 512], BF16)
            bglub = P.sb("bglub", [1, 512], BF16)
            R_wglu, R_bglu = Res("wglu"), Res("bglu")
            P.dma("sp", tmp[0:1, :], bglu_d.rearrange("(o n) -> o n", o=1), w=[R_tmp])
            P.op("dve", lambda e: e.tensor_copy(out=bglub[:, :], in_=tmp[0:1, :]), r=[R_tmp], w=[R_bglu])
            C.load_weight(wglu_d, wglub, R_wglu, stage, R_stage, 4, 512)
            C.load_weight(wout_d, woutb, R_wout, stage, R_stage, 8, D, gsc=gsm[:, 1, :], R_g=R_gsm)
        else:
            C.load_weight(wout_d, woutb, R_wout, stage, R_stage, 8, D)
        C.load_weight(w1_d, w1b, R_w1, stage, R_stage, 8, DFF, gsc=gsm[:, 0, :], R_g=R_gsm)
        C.load_weight(w2_d, w2b, R_w2, stage, R_stage, 32, D)

        def transpose8(nblk):
            for k in range(nblk):
                P.op("pe", lambda e, k=k: e.transpose(out=pT[:, k, :], in_=xn[:, k * 128:(k + 1) * 128], identity=C.ident[:, :]),
                     r=[R_xn, C.R_ident], w=[R_pT])
            P.op("dve", lambda e: e.tensor_copy(out=xnT[:, 0:nblk, :], in_=pT[:, 0:nblk, :]), r=[R_pT], w=[R_xnT])

        def norm_residual(G):
            P.op("act", lambda e: e.activation(out=junk[:, 0:512], in_=pA[:, :], func=AF.Square, accum_out=st[:, 0:1]),
                 r=[R_pA], w=[R_junk, R_st[0]])
            P.op("act", lambda e: e.activation(out=junk[:, 512:1024], in_=pB[:, :], func=AF.Square, accum_out=st[:, 1:2]),
                 r=[R_pB], w=[R_junk, R_st[1]])
            P.op("dve", lambda e: e.tensor_tensor(out=st[:, 0:1], in0=st[:, 0:1], in1=st[:, 1:2], op=ALU.add),
                 r=[R_st[0], R_st[1]], w=[R_st[0]])
            C.rstd(st[:, 0:1], st[:, 0:1], R_st[0], R_st[0], D)
            for h, (pp, Rp) in enumerate(((pA, R_pA), (pB, R_pB))):
                sl = slice(h * 512, (h + 1) * 512)
                P.op("dve", lambda e, pp=pp, sl=sl: e.scalar_tensor_tensor(out=tmp[:, :], in0=pp[:, :], scalar=st[:, 0:1], in1=G[:, sl],
                                                                          op0=ALU.mult, op1=ALU.mult),
                     r=[Rp, R_st[0], R_G], w=[R_tmp])
                P.op("dve", lambda e, sl=sl: e.tensor_tensor(out=hs[:, sl], in0=hs[:, sl], in1=tmp[:, :], op=ALU.add),
                     r=[R_hs, R_tmp], w=[R_hs])

        for t in range(TPC):
            rsl = slice(t * 128, (t + 1) * 128)
            P.dma("sp", hs[:, :], hs_d[rsl, :], w=[R_hs])
            if even:
                P.dma("sp", mg[:, 0:512], osb_d[rsl, :], w=[R_mg])
                P.dma("sp", mg[:, 512:1024], ys5_d[rsl, :], w=[R_mg])
                y = mg[:, 512:1024]
                P.op("act", lambda e: e.activation(out=tmp[:, :], in_=y, func=AF.Square), r=[R_mg], w=[R_tmp])
                P.op("dve", lambda e: e.tensor_scalar(out=tmp[:, :], in0=tmp[:, :], scalar1=0.044715, scalar2=1.0, op0=ALU.mult, op1=ALU.add),
                     r=[R_tmp], w=[R_tmp])
                P.op("dve", lambda e: e.tensor_tensor(out=tmp[:, :], in0=tmp[:, :], in1=y, op=ALU.mult), r=[R_tmp, R_mg], w=[R_tmp])
                P.op("act", lambda e: e.activation(out=tmp[:, :], in_=tmp[:, :], func=AF.Sigmoid, scale=1.5957691216057308), r=[R_tmp], w=[R_tmp])
                P.op("dve", lambda e: e.tensor_tensor(out=y, in0=tmp[:, :], in1=y, op=ALU.mult), r=[R_tmp, R_mg], w=[R_mg])
                P.op("act", lambda e: e.activation(out=xn[:, 0:512], in_=y, func=AF.Copy), r=[R_mg], w=[R_xn])
                transpose8(4)
                for k in range(4):
                    P.op("pe", lambda e, k=k: e.matmul(out=pA[:, :], lhsT=xnT[:, k, :], rhs=wglub[:, k, :], start=(k == 0), stop=False),
                         r=[R_xnT, R_wglu], w=[R_pA])
                P.op("pe", lambda e: e.matmul(out=pA[:, :], lhsT=C.ones_bf[0:1, :], rhs=bglub[0:1, :], start=False, stop=True),
                     r=[C.R_ones, R_bglu], w=[R_pA])
                P.op("act", lambda e: e.activation(out=tmp[:, :], in_=pA[:, :], func=AF.Sigmoid), r=[R_pA], w=[R_tmp])
                P.op("dve", lambda e: e.tensor_tensor(out=y, in0=tmp[:, :], in1=y, op=ALU.mult), r=[R_tmp, R_mg], w=[R_mg])
                for h in range(2):
                    sl = slice(h * 512, (h + 1) * 512)
                    P.op("act", lambda e, sl=sl, h=h: e.activation(out=junk[:, sl], in_=mg[:, sl], func=AF.Square, accum_out=st[:, 2 + h:3 + h]),
                         r=[R_mg], w=[R_junk, R_st[2 + h]])
                    C.rstd(st[:, 2 + h:3 + h], st[:, 2 + h:3 + h], R_st[2 + h], R_st[2 + h], 512)
                    P.op("act", lambda e, sl=sl, h=h: e.activation(out=xn[:, sl], in_=mg[:, sl], func=AF.Copy, scale=st[:, 2 + h:3 + h]),
                         r=[R_mg, R_st[2 + h]], w=[R_xn])
            else:
                P.dma("sp", mg[:, :], og_d[rsl, :], w=[R_mg])
                P.op("act", lambda e: e.activation(out=xn[:, :], in_=mg[:, :], func=AF.Copy), r=[R_mg], w=[R_xn])
            transpose8(8)
            for k in range(8):
                P.op("pe", lambda e, k=k: e.matmul(out=pA[:, :], lhsT=xnT[:, k, :], rhs=woutb[:, k, 0:512], start=(k == 0), stop=(k == 7)),
                     r=[R_xnT, R_wout], w=[R_pA])
            for k in range(8):
                P.op("pe", lambda e, k=k: e.matmul(out=pB[:, :], lhsT=xnT[:, k, :], rhs=woutb[:, k, 512:1024], start=(k == 0), stop=(k == 7)),
                     r=[R_xnT, R_wout], w=[R_pB])
            norm_residual(G1)
            P.op("act", lambda e: e.activation(out=junk[:, :], in_=hs[:, :], func=AF.Square, accum_out=st[:, 4:5]), r=[R_hs], w=[R_junk, R_st[4]])
            C.rstd(st[:, 4:5], st[:, 4:5], R_st[4], R_st[4], D)
            P.op("act", lambda e: e.activation(out=xn[:, :], in_=hs[:, :], func=AF.Copy, scale=st[:, 4:5]), r=[R_hs, R_st[4]], w=[R_xn])
            transpose8(8)
            for fg in range(8):
                ph, Rph = pH[fg % 2], R_pH[fg % 2]
                for j in range(4):
                    fb = fg * 4 + j
                    for k in range(8):
                        P.op("pe", lambda e, ph=ph, j=j, fb=fb, k=k: e.matmul(out=ph[:, j, :], lhsT=w1b[:, k, fb * 128:(fb + 1) * 128], rhs=xnT[:, k, :],
                                                                               start=(k == 0), stop=(k == 7)),
                             r=[R_w1, R_xnT], w=[Rph])
                P.op("act", lambda e, ph=ph: e.activation(out=rl[:, :], in_=ph[:, :, :].rearrange("p a b -> p (a b)"), func=AF.Relu), r=[Rph], w=[R_rl])
                P.op("dve", lambda e, fg=fg: e.tensor_tensor(out=h1T[:, fg * 4:(fg + 1) * 4, :].rearrange("p a b -> p (a b)"), in0=rl[:, :], in1=rl[:, :], op=ALU.mult),
                     r=[R_rl], w=[R_h1T])
            for fb in range(32):
                P.op("pe", lambda e, fb=fb: e.matmul(out=pA[:, :], lhsT=h1T[:, fb, :], rhs=w2b[:, fb, 0:512], start=(fb == 0), stop=(fb == 31)),
                     r=[R_h1T, R_w2], w=[R_pA])
            for fb in range(32):
                P.op("pe", lambda e, fb=fb: e.matmul(out=pB[:, :], lhsT=h1T[:, fb, :], rhs=w2b[:, fb, 512:1024], start=(fb == 0), stop=(fb == 31)),
                     r=[R_h1T, R_w2], w=[R_pB])
            norm_residual(G3)
            P.dma("pool", out_d[rsl, :], hs[:, :], r=[R_hs])
        P.finish()
    return nc


def tok_shard(a_pad, c):
    return np.ascontiguousarray(np.concatenate([a_pad[0:128], a_pad[128 * (1 + 16 * c):128 * (17 + 16 * c)]], axis=0))


def tok_unshard(outs):
    parts = [outs[0][0:128]] + [o[128:] for o in outs]
    return np.concatenate(parts, axis=0)


_CACHE = {}


def get_prog(name, builder, *args):
    if name not in _CACHE:
        _CACHE[name] = builder(*args)
    return _CACHE[name]


def run_post(kind, hs_pad, mix_in, W):
    nc = get_prog("post_" + kind, build_post, kind)
    in_maps = []
    for c in range(NCORES):
        m = {"hs": tok_shard(hs_pad, c)}
        if kind == "even":
            m["osb"] = tok_shard(mix_in[0], c)
            m["ys5"] = tok_shard(mix_in[1], c)
        else:
            m["og"] = tok_shard(mix_in, c)
        m.update(W)
        in_maps.append(m)
    res = run_bass_kernel_spmd(nc, in_maps, core_ids=list(range(NCORES)))
    return tok_unshard([r["out"] for r in res.results])


NG = 33


def grp_cols(gi):
    return (gi * 512, 512 if gi < 32 else 128)


def build_mix_even():
    nc = bass.Bass("TRN2", target_bir_lowering=False)
    dr = lambda n, s, k="ExternalInput": nc.dram_tensor(n, list(s), F32, kind=k).ap()
    hs_d = dr("hs", [LP, D])
    wh_d = dr("wh", [D, 384])
    g_d = dr("g_pre", [D])
    lre_d = dr("lam_re", [4, 64])
    lim_d = dr("lam_im", [4, 64])
    ldt_d = dr("log_dt", [4])
    bre_d = dr("b_re", [4, 64, 16])
    bim_d = dr("b_im", [4, 64, 16])
    cre_d = dr("c_re", [4, 16, 64])
    cim_d = dr("c_im", [4, 16, 64])
    dsk_d = dr("d_skip", [64])
    osb_d = dr("osbT", [64, LP], "ExternalOutput")
    ys_d = dr("ysT", [64, LP], "ExternalOutput")

    with ExitStack() as es:
        C = Ctx(nc, es)
        P = C.P
        QQ = P.sb("QQ", [128, LP], BF16)
        KZ = P.sb("KZ", [128, LP], BF16)
        KN = P.sb("KN", [128, LP], BF16)
        R_kinit = Res("kinit")
        P.op("pool", lambda e: e.memset(KZ[64:128, :], 0.0), w=[R_kinit])
        P.op("pool", lambda e: e.memset(KN[64:128, :], 0.0), w=[R_kinit])
        vA = P.sb("vA", [128, NT, 128], BF16)
        P.op("pool", lambda e: e.memset(vA[:, :, :].rearrange("p a b -> p (a b)"), 0.0), w=[R_kinit])
        R_qT = [Res(f"qT{g}") for g in range(NG)]
        R_kT = [Res(f"kT{g}") for g in range(NG)]
        R_v = [Res(f"v{g}") for g in range(NG)]
        esA = ExitStack()
        P.es = esA
        whb = P.sb("whb", [128, 8, 384], BF16)
        R_wh = Res("wh")
        gsm = P.sb("gsm", [128, 8], F32)
        R_gsm = Res("gsm")
        stage = [P.sb("stage0", [128, 1024], F32)]
        R_stage = [Res("st0")]
        banks = [P.ps(f"bk{i}", [128, 512], F32) for i in range(7)]
        R_bk = [Res(f"bk{i}", True) for i in range(7)]
        pT = P.ps("pT", [128, 8, 128], BF16)
        R_pT = Res("pT", True)

        P.dma("sp", gsm[:, :], g_d.rearrange("(k p) -> p k", p=128), w=[R_gsm], allow_slow_non_contiguous=True)
        C.load_weight(wh_d, whb, R_wh, stage, R_stage, 8, 384, gsc=gsm, R_g=R_gsm)

        sp_ = P.sb("s5p", [128, 2, 24], F32)
        R_sp = Res("s5p")
        LR, LI, DT, AR, AI, IR, II, T0, T1, T2, T3, CR, CI, NLR = range(14)
        spi = P.sb("s5pi", [128, 2], mybir.dt.int32)
        col = lambda c: sp_[:, :, c]
        for sb in range(2):
            P.dma("sp", sp_[:, sb, LR:LR + 1], lre_d[2 * sb:2 * sb + 2, :].rearrange("g (n o) -> (g n) o", o=1), w=[R_sp])
            P.dma("sp", sp_[:, sb, LI:LI + 1], lim_d[2 * sb:2 * sb + 2, :].rearrange("g (n o) -> (g n) o", o=1), w=[R_sp])
            for gl in range(2):
                P.dma("sp", sp_[gl * 64:(gl + 1) * 64, sb, DT:DT + 1],
                      ldt_d[2 * sb + gl:2 * sb + gl + 1].rearrange("(o n) -> o n", o=1).partition_broadcast(64), w=[R_sp])
        so = lambda eng, fn: P.op(eng, fn, r=[R_sp], w=[R_sp])
        so("act", lambda e: e.activation(out=col(DT), in_=col(DT), func=AF.Exp))
        so("dve", lambda e: e.tensor_scalar(out=col(LR), in0=col(LR), scalar1=-1e-4, scalar2=None, op0=ALU.min))
        so("dve", lambda e: e.tensor_tensor(out=col(T0), in0=col(LR), in1=col(DT), op=ALU.mult))
        so("dve", lambda e: e.tensor_scalar(out=col(NLR), in0=col(T0), scalar1=-1.0, scalar2=None, op0=ALU.mult))
        so("dve", lambda e: e.tensor_tensor(out=col(T1), in0=col(LI), in1=col(DT), op=ALU.mult))
        so("dve", lambda e: e.tensor_scalar(out=col(T2), in0=col(T1), scalar1=1.0 / (2 * np.pi), scalar2=None, op0=ALU.mult))
        P.op("dve", lambda e: e.tensor_copy(out=spi[:, :], in_=col(T2)), r=[R_sp], w=[R_sp])
        P.op("dve", lambda e: e.tensor_copy(out=col(T2), in_=spi[:, :]), r=[R_sp], w=[R_sp])
        so("dve", lambda e: e.scalar_tensor_tensor(out=col(T1), in0=col(T2), scalar=-2 * np.pi, in1=col(T1), op0=ALU.mult, op1=ALU.add))
        so("dve", lambda e: e.tensor_scalar(out=col(T1), in0=col(T1), scalar1=0.5, scalar2=None, op0=ALU.mult))
        so("dve", lambda e: e.tensor_scalar(out=col(T2), in0=col(T1), scalar1=np.pi / 2, scalar2=None, op0=ALU.add))
        so("act", lambda e: e.activation(out=col(T1), in_=col(T1), func=AF.Sin))
        so("act", lambda e: e.activation(out=col(T2), in_=col(T2), func=AF.Sin))
        so("act", lambda e: e.activation(out=col(T3), in_=col(T0), func=AF.Exp))
        so("dve", lambda e: e.tensor_tensor(out=col(AI), in0=col(T1), in1=col(T2), op=ALU.mult))
        so("dve", lambda e: e.tensor_scalar(out=col(AI), in0=col(AI), scalar1=2.0, scalar2=None, op0=ALU.mult))
        so("dve", lambda e: e.tensor_tensor(out=col(AR), in0=col(T1), in1=col(T1), op=ALU.mult))
        so("dve", lambda e: e.tensor_scalar(out=col(AR), in0=col(AR), scalar1=-2.0, scalar2=1.0, op0=ALU.mult, op1=ALU.add))
        so("act", lambda e: e.activation(out=col(T0), in_=col(NLR), func=AF.Exp))
        so("dve", lambda e: e.tensor_tensor(out=col(IR), in0=col(AR), in1=col(T0), op=ALU.mult))
        so("dve", lambda e: e.tensor_tensor(out=col(II), in0=col(AI), in1=col(T0), op=ALU.mult))
        so("dve", lambda e: e.tensor_scalar(out=col(II), in0=col(II), scalar1=-1.0, scalar2=None, op0=ALU.mult))
        so("dve", lambda e: e.tensor_tensor(out=col(AR), in0=col(AR), in1=col(T3), op=ALU.mult))
        so("dve", lambda e: e.tensor_tensor(out=col(AI), in0=col(AI), in1=col(T3), op=ALU.mult))
        so("dve", lambda e: e.tensor_tensor(out=col(T0), in0=col(LR), in1=col(LR), op=ALU.mult))
        so("dve", lambda e: e.tensor_tensor(out=col(T1), in0=col(LI), in1=col(LI), op=ALU.mult))
        so("dve", lambda e: e.tensor_tensor(out=col(T0), in0=col(T0), in1=col(T1), op=ALU.add))
        so("dve", lambda e: e.reciprocal(out=col(T0), in_=col(T0)))
        so("dve", lambda e: e.tensor_scalar(out=col(T1), in0=col(AR), scalar1=-1.0, scalar2=None, op0=ALU.add))
        so("dve", lambda e: e.tensor_tensor(out=col(T2), in0=col(T1), in1=col(LR), op=ALU.mult))
        so("dve", lambda e: e.tensor_tensor(out=col(T3), in0=col(AI), in1=col(LI), op=ALU.mult))
        so("dve", lambda e: e.tensor_tensor(out=col(T2), in0=col(T2), in1=col(T3), op=ALU.add))
        so("dve", lambda e: e.tensor_tensor(out=col(CR), in0=col(T2), in1=col(T0), op=ALU.mult))
        so("dve", lambda e: e.tensor_tensor(out=col(T2), in0=col(AI), in1=col(LR), op=ALU.mult))
        so("dve", lambda e: e.tensor_tensor(out=col(T3), in0=col(T1), in1=col(LI), op=ALU.mult))
        so("dve", lambda e: e.tensor_tensor(out=col(T2), in0=col(T2), in1=col(T3), op=ALU.subtract))
        so("dve", lambda e: e.tensor_tensor(out=col(CI), in0=col(T2), in1=col(T0), op=ALU.mult))

        TB = {n: P.sb("tb_" + n, [128, 2, 512], F32) for n in ("pr", "pi", "ir", "ii")}
        R_tb = Res("tb")
        ptmp = P.sb("ptmp", [128, 256], F32)
        R_ptmp = Res("ptmp")
        pw = P.sb("pw", [128, 2, 2, 2], F32)
        R_pw = Res("pw")
        pw2 = P.sb("pw2", [128, 4], F32)
        for sb in range(2):
            for wi, (tr, ti, cr_, ci_) in enumerate((("pr", "pi", AR, AI), ("ir", "ii", IR, II))):
                P.op("pool", lambda e, tr=tr, sb=sb: e.memset(TB[tr][:, sb, 0:1], 1.0), w=[R_tb])
                P.op("pool", lambda e, ti=ti, sb=sb: e.memset(TB[ti][:, sb, 0:1], 0.0), w=[R_tb])
                P.op("dve", lambda e, sb=sb, wi=wi, cr_=cr_: e.tensor_copy(out=pw[:, sb, wi, 0:1], in_=sp_[:, sb, cr_:cr_ + 1]), r=[R_sp], w=[R_pw])
                P.op("dve", lambda e, sb=sb, wi=wi, ci_=ci_: e.tensor_copy(out=pw[:, sb, wi, 1:2], in_=sp_[:, sb, ci_:ci_ + 1]), r=[R_sp], w=[R_pw])
                n = 1
                while n < 512:
                    cr = pw[:, sb, wi, 0:1]
                    ci = pw[:, sb, wi, 1:2]
                    src_r = TB[tr][:, sb, 0:n]
                    src_i = TB[ti][:, sb, 0:n]
                    dst_r = TB[tr][:, sb, n:2 * n]
                    dst_i = TB[ti][:, sb, n:2 * n]
                    tm = ptmp[:, 0:n]
                    P.op("dve", lambda e, tm=tm, src_i=src_i, ci=ci: e.tensor_scalar(out=tm, in0=src_i, scalar1=ci, scalar2=None, op0=ALU.mult),
                         r=[R_tb, R_pw], w=[R_ptmp])
                    P.op("dve", lambda e, dst_r=dst_r, src_r=src_r, cr=cr, tm=tm: e.scalar_tensor_tensor(out=dst_r, in0=src_r, scalar=cr, in1=tm, op0=ALU.mult, op1=ALU.subtract),
                         r=[R_tb, R_pw, R_ptmp], w=[R_tb])
                    P.op("dve", lambda e, tm=tm, src_i=src_i, cr=cr: e.tensor_scalar(out=tm, in0=src_i, scalar1=cr, scalar2=None, op0=ALU.mult),
                         r=[R_tb, R_pw], w=[R_ptmp])
                    P.op("dve", lambda e, dst_i=dst_i, src_r=src_r, ci=ci, tm=tm: e.scalar_tensor_tensor(out=dst_i, in0=src_r, scalar=ci, in1=tm, op0=ALU.mult, op1=ALU.add),
                         r=[R_tb, R_pw, R_ptmp], w=[R_tb])
                    n *= 2
                    if n < 512:
                        P.op("dve", lambda e, cr=cr, ci=ci: e.tensor_tensor(out=pw2[:, 0:1], in0=cr, in1=cr, op=ALU.mult), r=[R_pw], w=[R_ptmp])
                        P.op("dve", lambda e, cr=cr, ci=ci: e.tensor_tensor(out=pw2[:, 1:2], in0=ci, in1=ci, op=ALU.mult), r=[R_pw], w=[R_ptmp])
                        P.op("dve", lambda e, cr=cr, ci=ci: e.tensor_tensor(out=pw2[:, 2:3], in0=cr, in1=ci, op=ALU.mult), r=[R_pw], w=[R_ptmp])
                        P.op("dve", lambda e, cr=cr: e.tensor_tensor(out=cr, in0=pw2[:, 0:1], in1=pw2[:, 1:2], op=ALU.subtract), r=[R_ptmp], w=[R_pw])
                        P.op("dve", lambda e, ci=ci: e.tensor_scalar(out=ci, in0=pw2[:, 2:3], scalar1=2.0, scalar2=None, op0=ALU.mult), r=[R_ptmp], w=[R_pw])

        braw = P.sb("braw", [128, 2, 2, 16], F32)
        R_braw = Res("braw")
        Bfull = P.sb("Bfull", [128, 2, 2, 64], F32)
        R_Bfull = Res("Bfull")
        BbT = P.sb("BbT", [64, 2, 2, 128], F32)
        R_BbT = Res("BbT")
        CT = P.sb("CT", [128, 2, 2, 64], F32)
        R_CT = Res("CT")
        identf = P.sb("identf", [128, 128], F32)
        R_if = Res("identf")
        P.op("dve", lambda e: e.tensor_copy(out=identf[:, :], in_=C.ident[:, :]), r=[C.R_ident], w=[R_if])
        P.op("pool", lambda e: e.memset(Bfull[:, :, :, :].rearrange("p a b c -> p (a b c)"), 0.0), w=[R_Bfull])
        P.op("pool", lambda e: e.memset(CT[:, :, :, :].rearrange("p a b c -> p (a b c)"), 0.0), w=[R_CT])
        for sb in range(2):
            P.dma("sp", braw[:, sb, 0, :], bre_d[2 * sb:2 * sb + 2].rearrange("g n p -> (g n) p"), w=[R_braw])
            P.dma("sp", braw[:, sb, 1, :], bim_d[2 * sb:2 * sb + 2].rearrange("g n p -> (g n) p"), w=[R_braw])
            for gl in range(2):
                g = 2 * sb + gl
                ps_ = slice(gl * 64, (gl + 1) * 64)
                cs = slice(g * 16, (g + 1) * 16)
                P.dma("sp", CT[ps_, sb, 0, cs], cre_d[g].rearrange("p n -> n p"), w=[R_CT], allow_slow_non_contiguous=True)
                P.dma("sp", CT[ps_, sb, 1, cs], cim_d[g].rearrange("p n -> n p"), w=[R_CT], allow_slow_non_contiguous=True)
                crr = sp_[ps_, sb, CR:CR + 1]
                cii = sp_[ps_, sb, CI:CI + 1]
                tm = ptmp[ps_, 0:16]
                P.op("dve", lambda e, tm=tm, sb=sb, ps_=ps_, cii=cii: e.tensor_scalar(out=tm, in0=braw[ps_, sb, 1, :], scalar1=cii, scalar2=None, op0=ALU.mult),
                     r=[R_braw, R_sp], w=[R_ptmp])
                P.op("dve", lambda e, tm=tm, sb=sb, ps_=ps_, cs=cs, crr=crr: e.scalar_tensor_tensor(out=Bfull[ps_, sb, 0, cs], in0=braw[ps_, sb, 0, :], scalar=crr, in1=tm,
                                                                                              op0=ALU.mult, op1=ALU.subtract),
                     r=[R_braw, R_sp, R_ptmp], w=[R_Bfull])
                P.op("dve", lambda e, tm=tm, sb=sb, ps_=ps_, cii=cii: e.tensor_scalar(out=tm, in0=braw[ps_, sb, 0, :], scalar1=cii, scalar2=None, op0=ALU.mult),
                     r=[R_braw, R_sp], w=[R_ptmp])
                P.op("dve", lambda e, tm=tm, sb=sb, ps_=ps_, cs=cs, crr=crr: e.scalar_tensor_tensor(out=Bfull[ps_, sb, 1, cs], in0=braw[ps_, sb, 1, :], scalar=crr, in1=tm,
                                                                                              op0=ALU.mult, op1=ALU.add),
                     r=[R_braw, R_sp, R_ptmp], w=[R_Bfull])
            P.op("dve", lambda e, sb=sb: e.tensor_scalar(out=CT[:, sb, 1, :], in0=CT[:, sb, 1, :], scalar1=-1.0, scalar2=None, op0=ALU.mult), r=[R_CT], w=[R_CT])
            for ri in range(2):
                P.op("pe", lambda e, sb=sb, ri=ri: e.transpose(out=banks[0][0:64, 0:128], in_=Bfull[:, sb, ri, :], identity=identf[:, :]),
                     r=[R_Bfull, R_if], w=[R_bk[0]])
                P.op("dve", lambda e, sb=sb, ri=ri: e.tensor_copy(out=BbT[:, sb, ri, :], in_=banks[0][0:64, 0:128]), r=[R_bk[0]], w=[R_BbT])
        dsk = P.sb("dsk", [64, 1], F32)
        R_dsk = Res("dsk")
        P.dma("sp", dsk[:, :], dsk_d.rearrange("(n o) -> n o", o=1), w=[R_dsk])
        onesf = P.sb("onesf", [128, 512], F32)
        R_onesf = Res("onesf")
        P.op("pool", lambda e: e.memset(onesf[:, :], 1.0), w=[R_onesf])
        carry = P.sb("carry", [128, 2, 2], F32)
        R_carry = Res("carry")
        P.op("pool", lambda e: e.memset(carry[:, :, :].rearrange("p a b -> p (a b)"), 0.0), w=[R_carry])

        hsb = [P.sb("hsb0", [128, D], F32)] * 2
        R_hsb = [Res("hsb0")] * 2
        xn = P.sb("xn", [128, D], BF16)
        R_xn = Res("xn")
        xnT = P.sb("xnT", [128, 8, 512], BF16)
        R_xnT = Res("xnT")
        stt = P.sb("stt", [128, 2], F32)
        R_stt = [Res("stt0"), Res("stt1")]
        uT = P.sb("uT", [64, 512], F32)
        R_uT = Res("uT")
        S5 = {n: P.sb("s5_" + n, [128, 512], F32) for n in ("sre", "sim", "cre", "cim", "t1", "t2", "wre", "wim")}
        R_S5 = {n: Res("s5_" + n) for n in S5}
        S5["xre"], S5["xim"] = S5["sre"], S5["sim"]
        R_S5["xre"], R_S5["xim"] = R_S5["sre"], R_S5["sim"]
        ysb = P.sb("ysb", [64, 512], F32)
        R_ysb = Res("ysb")
        junk = stage[0]
        R_junk = R_stage[0]
        pV, pQ, pK, pU, pBr, pBi, pY = banks
        R_pV, R_pQ, R_pK, R_pU, R_pBr, R_pBi, R_pY = R_bk

        tile_i = 0
        for gi in range(NG):
            c0, ncol = grp_cols(gi)
            ntl = ncol // 128
            for tl in range(ntl):
                t = gi * 4 + tl
                hb, Rh = hsb[tile_i % 2], R_hsb[tile_i % 2]
                tile_i += 1
                P.dma("sp", hb[:, :], hs_d[t * 128:(t + 1) * 128, :], w=[Rh])
                P.op("act", lambda e, hb=hb: e.activation(out=junk[:, :], in_=hb[:, :], func=AF.Square, accum_out=stt[:, 0:1]), r=[Rh], w=[R_junk, R_stt[0]])
                C.rstd(stt[:, 0:1], stt[:, 0:1], R_stt[0], R_stt[0], D)
                P.op("act", lambda e, hb=hb: e.activation(out=xn[:, :], in_=hb[:, :], func=AF.Copy, scale=stt[:, 0:1]), r=[Rh, R_stt[0]], w=[R_xn])
                for k in range(8):
                    P.op("pe", lambda e, k=k: e.transpose(out=pT[:, k, :], in_=xn[:, k * 128:(k + 1) * 128], identity=C.ident[:, :]),
                         r=[R_xn, C.R_ident], w=[R_pT])
                P.op("dve", lambda e, tl=tl: e.tensor_copy(out=xnT[:, :, tl * 128:(tl + 1) * 128], in_=pT[:, :, :]), r=[R_pT], w=[R_xnT])
                for k in range(8):
                    P.op("pe", lambda e, k=k, tl=tl: e.matmul(out=pV[:, 0:64], lhsT=xnT[:, k, tl * 128:(tl + 1) * 128], rhs=whb[:, k, 256:320],
                                                               start=(k == 0), stop=(k == 7)), r=[R_xnT, R_wh], w=[R_pV])
                P.op("act", lambda e, t=t: e.activation(out=vA[:, t, 0:64], in_=pV[:, 0:64], func=AF.Copy), r=[R_pV, R_kinit], w=[R_v[gi]])
            cs = slice(c0, c0 + ncol)
            for (pp, Rp, wc, wn) in ((pQ, R_pQ, 0, 128), (pK, R_pK, 128, 128), (pU, R_pU, 320, 64)):
                for k in range(8):
                    P.op("pe", lambda e, pp=pp, k=k, wc=wc, wn=wn: e.matmul(out=pp[0:wn, 0:ncol], lhsT=whb[:, k, wc:wc + wn], rhs=xnT[:, k, 0:ncol],
                                                                             start=(k == 0), stop=(k == 7)), r=[R_xnT, R_wh], w=[Rp])
            P.op("act", lambda e, cs=cs: e.activation(out=QQ[:, cs], in_=pQ[:, 0:ncol], func=AF.Copy, scale=0.125), r=[R_pQ], w=[R_qT[gi]])
            P.op("dve", lambda e, cs=cs: e.tensor_copy(out=KZ[0:64, cs], in_=pK[0:64, 0:ncol]), r=[R_pK, R_kinit], w=[R_kT[gi]])
            P.op("dve", lambda e, cs=cs: e.tensor_scalar(out=KN[0:64, cs], in0=pK[0:64, 0:ncol], scalar1=-1.0, scalar2=None, op0=ALU.mult), r=[R_pK, R_kinit], w=[R_kT[gi]])
            P.op("act", lambda e: e.activation(out=uT[:, 0:ncol], in_=pU[0:64, 0:ncol], func=AF.Copy), r=[R_pU], w=[R_uT])
            for sb in range(2):
                nn = ncol
                P.op("pe", lambda e, sb=sb: e.matmul(out=pBr[:, 0:nn], lhsT=BbT[:, sb, 0, :], rhs=uT[:, 0:nn], start=True, stop=True), r=[R_BbT, R_uT], w=[R_pBr])
                P.op("pe", lambda e, sb=sb: e.matmul(out=pBi[:, 0:nn], lhsT=BbT[:, sb, 1, :], rhs=uT[:, 0:nn], start=True, stop=True), r=[R_BbT, R_uT], w=[R_pBi])
                A = lambda n: S5[n][:, 0:nn]
                P.op("act", lambda e: e.activation(out=A("sre"), in_=pBr[:, 0:nn], func=AF.Copy), r=[R_pBr], w=[R_S5["sre"]])
                P.op("act", lambda e: e.activation(out=A("sim"), in_=pBi[:, 0:nn], func=AF.Copy), r=[R_pBi], w=[R_S5["sim"]])
                ir_, ii_ = TB["ir"][:, sb, 0:nn], TB["ii"][:, sb, 0:nn]
                pr_, pi_ = TB["pr"][:, sb, 0:nn], TB["pi"][:, sb, 0:nn]
                P.op("dve", lambda e: e.tensor_tensor(out=A("t1"), in0=A("sim"), in1=ii_, op=ALU.mult), r=[R_S5["sim"], R_tb], w=[R_S5["t1"]])
                P.op("dve", lambda e: e.tensor_tensor(out=A("cre"), in0=A("sre"), in1=ir_, op=ALU.mult), r=[R_S5["sre"], R_tb], w=[R_S5["cre"]])
                P.op("dve", lambda e: e.tensor_tensor(out=A("cre"), in0=A("cre"), in1=A("t1"), op=ALU.subtract), r=[R_S5["cre"], R_S5["t1"]], w=[R_S5["cre"]])
                P.op("pool", lambda e: e.tensor_tensor(out=A("t2"), in0=A("sim"), in1=ir_, op=ALU.mult), r=[R_S5["sim"], R_tb], w=[R_S5["t2"]])
                P.op("pool", lambda e: e.tensor_tensor(out=A("cim"), in0=A("sre"), in1=ii_, op=ALU.mult), r=[R_S5["sre"], R_tb], w=[R_S5["cim"]])
                P.op("dve", lambda e: e.tensor_tensor(out=A("cim"), in0=A("cim"), in1=A("t2"), op=ALU.add), r=[R_S5["cim"], R_S5["t2"]], w=[R_S5["cim"]])
                P.op("dve", lambda e, sb=sb: e.tensor_tensor_scan(out=A("wre"), data0=onesf[:, 0:nn], data1=A("cre"), initial=carry[:, sb, 0:1], op0=ALU.mult, op1=ALU.add),
                     r=[R_onesf, R_S5["cre"], R_carry], w=[R_S5["wre"]])
                P.op("dve", lambda e, sb=sb: e.tensor_tensor_scan(out=A("wim"), data0=onesf[:, 0:nn], data1=A("cim"), initial=carry[:, sb, 1:2], op0=ALU.mult, op1=ALU.add),
                     r=[R_onesf, R_S5["cim"], R_carry], w=[R_S5["wim"]])
                P.op("dve", lambda e: e.tensor_tensor(out=A("t1"), in0=A("wim"), in1=pi_, op=ALU.mult), r=[R_S5["wim"], R_tb], w=[R_S5["t1"]])
                P.op("dve", lambda e: e.tensor_tensor(out=A("xre"), in0=A("wre"), in1=pr_, op=ALU.mult), r=[R_S5["wre"], R_tb], w=[R_S5["xre"]])
                P.op("dve", lambda e: e.tensor_tensor(out=A("xre"), in0=A("xre"), in1=A("t1"), op=ALU.subtract), r=[R_S5["xre"], R_S5["t1"]], w=[R_S5["xre"]])
                P.op("pool", lambda e: e.tensor_tensor(out=A("t2"), in0=A("wim"), in1=pr_, op=ALU.mult), r=[R_S5["wim"], R_tb], w=[R_S5["t2"]])
                P.op("pool", lambda e: e.tensor_tensor(out=A("xim"), in0=A("wre"), in1=pi_, op=ALU.mult), r=[R_S5["wre"], R_tb], w=[R_S5["xim"]])
                P.op("dve", lambda e: e.tensor_tensor(out=A("xim"), in0=A("xim"), in1=A("t2"), op=ALU.add), r=[R_S5["xim"], R_S5["t2"]], w=[R_S5["xim"]])
                xer, xei = S5["xre"][:, nn - 1:nn], S5["xim"][:, nn - 1:nn]
                ar_, ai_ = sp_[:, sb, AR:AR + 1], sp_[:, sb, AI:AI + 1]
                P.op("dve", lambda e: e.tensor_tensor(out=pw2[:, 0:1], in0=xei, in1=ai_, op=ALU.mult), r=[R_S5["xim"], R_sp], w=[R_ptmp])
                P.op("dve", lambda e: e.tensor_tensor(out=pw2[:, 1:2], in0=xei, in1=ar_, op=ALU.mult), r=[R_S5["xim"], R_sp], w=[R_ptmp])
                P.op("dve", lambda e, sb=sb: e.scalar_tensor_tensor(out=carry[:, sb, 0:1], in0=xer, scalar=ar_, in1=pw2[:, 0:1], op0=ALU.mult, op1=ALU.subtract),
                     r=[R_S5["xre"], R_sp, R_ptmp], w=[R_carry])
                P.op("dve", lambda e, sb=sb: e.scalar_tensor_tensor(out=carry[:, sb, 1:2], in0=xer, scalar=ai_, in1=pw2[:, 1:2], op0=ALU.mult, op1=ALU.add),
                     r=[R_S5["xre"], R_sp, R_ptmp], w=[R_carry])
                P.op("pe", lambda e, sb=sb: e.matmul(out=pY[0:64, 0:nn], lhsT=CT[:, sb, 0, :], rhs=A("xre"), start=(sb == 0), stop=False), r=[R_CT, R_S5["xre"]], w=[R_pY])
                P.op("pe", lambda e, sb=sb: e.matmul(out=pY[0:64, 0:nn], lhsT=CT[:, sb, 1, :], rhs=A("xim"), start=False, stop=(sb == 1)), r=[R_CT, R_S5["xim"]], w=[R_pY])
            P.op("dve", lambda e: e.scalar_tensor_tensor(out=ysb[:, 0:ncol], in0=uT[:, 0:ncol], scalar=dsk[:, 0:1], in1=pY[0:64, 0:ncol], op0=ALU.mult, op1=ALU.add),
                 r=[R_uT, R_dsk, R_pY], w=[R_ysb])
            P.dma("pool", ys_d[:, cs], ysb[:, 0:ncol], r=[R_ysb])

        P.barrier()
        esA.close()
        P.es = es
        onesf = P.sb("onesfB", [128, 1], F32)
        R_onesf = Res("onesfB")
        P.op("pool", lambda e: e.memset(onesf[:, :], 1.0), w=[R_onesf])
        Tincl = P.sb("Tincl", [128, 128], BF16)
        R_Ti = Res("Tincl")
        P.op("pool", lambda e: e.memset(Tincl[:, :], 1.0), w=[R_Ti])
        P.op("pool", lambda e: e.affine_select(out=Tincl[:, :], in_=Tincl[:, :], pattern=[[-1, 128]], compare_op=ALU.is_ge, fill=0.0, base=0, channel_multiplier=1),
             r=[R_Ti], w=[R_Ti])
        onesb512 = P.sb("onesb512", [128, 512], BF16)
        R_o512 = Res("o512")
        P.op("pool", lambda e: e.memset(onesb512[:, :], 1.0), w=[R_o512])
        masks = P.sb("masks", [128, 4, 512], BF16)
        R_masks = Res("masks")
        for i in range(4):
            P.op("pool", lambda e, i=i: e.affine_select(out=masks[:, i, :], in_=onesb512[:, :], pattern=[[1, 512]], compare_op=ALU.is_gt, fill=0.0,
                                                        base=-128 * i, channel_multiplier=-1), r=[R_o512], w=[R_masks])
        padcol = P.sb("padcol", [128, 1], F32)
        R_pad = Res("padcol")
        P.op("pool", lambda e: e.affine_select(out=padcol[:, :], in_=onesf[:, 0:1], pattern=[[0, 1]], compare_op=ALU.is_ge, fill=0.0, base=-PADF, channel_multiplier=1),
             r=[R_onesf], w=[R_pad])
        NB = 3
        eb = [P.sb(f"eb{i}", [128, 512], F32) for i in range(NB)]
        R_eb = [Res(f"eb{i}") for i in range(NB)]
        spb = [P.sb(f"spb{i}", [128, 512], BF16) for i in range(NB)]
        R_spb = [Res(f"spb{i}") for i in range(NB)]
        wb_ = [P.sb("wbb0", [128, 512], BF16), P.sb("wbb1", [128, 512], BF16)]
        R_wb = [Res("wbb0"), Res("wbb1")]
        S16 = [P.sb(f"S16_{i}", [128, 512], BF16) for i in range(4)]
        R_S16 = [Res(f"S16_{i}") for i in range(4)]
        osb_sb = P.sb("osb_sb", [64, 512], F32)
        R_osb = Res("osb_sb")
        pz = [banks[0], banks[1], banks[5]]
        R_pz = [R_bk[0], R_bk[1], R_bk[5]]
        pc = [banks[2], banks[3]]
        R_pc = [R_bk[2], R_bk[3]]
        po = banks[4]
        R_po = R_bk[4]
        iters = []
        for Q in range(NG):
            jmax = min(4 * Q + 3, NT - 1)
            for j in range(jmax, -1, -1):
                iters.append((Q, j, jmax))

        def maskop(buf, Rbuf, Q, j, nq):
            diag = j >= 4 * Q
            if not (diag or j == 0):
                return
            mk = masks[:, j - 4 * Q, 0:nq] if diag else onesb512[:, 0:nq]
            if j == 0:
                P.op("dve", lambda e: e.scalar_tensor_tensor(out=buf[:, 0:nq], in0=buf[:, 0:nq], scalar=padcol[:, 0:1], in1=mk,
                                                             op0=ALU.mult, op1=ALU.mult), r=[Rbuf, R_pad, R_masks, R_o512], w=[Rbuf])
            else:
                P.op("dve", lambda e: e.tensor_tensor(out=buf[:, 0:nq], in0=buf[:, 0:nq], in1=mk, op=ALU.mult), r=[Rbuf, R_masks], w=[Rbuf])

        def stageA(i):
            Q, j, jmax = iters[i]
            q0, nq = grp_cols(Q)
            qs = slice(q0, q0 + nq)
            ks = slice(j * 128, (j + 1) * 128)
            b = i % NB
            k = jmax - j
            P.op("pe", lambda e: e.matmul(out=pz[b][:, 0:nq], lhsT=KZ[:, ks], rhs=QQ[:, qs], start=True, stop=True),
                 r=[R_kT[j // 4], R_qT[Q]], w=[R_pz[b]])
            P.op("act", lambda e: e.activation(out=eb[b][:, 0:nq], in_=pz[b][:, 0:nq], func=AF.Exp), r=[R_pz[b]], w=[R_eb[b]])
            P.op("act", lambda e: e.activation(out=spb[b][:, 0:nq], in_=eb[b][:, 0:nq], func=AF.Ln, bias=1.0), r=[R_eb[b]], w=[R_spb[b]])
            maskop(spb[b], R_spb[b], Q, j, nq)
            if j > 0:
                if k == 0:
                    P.op("pool", lambda e: e.tensor_copy(out=S16[(k + 1) % 4][:, 0:nq], in_=spb[b][:, 0:nq]), r=[R_spb[b]], w=[R_S16[(k + 1) % 4]])
                else:
                    P.op("pool", lambda e: e.tensor_tensor(out=S16[(k + 1) % 4][:, 0:nq], in0=S16[k % 4][:, 0:nq], in1=spb[b][:, 0:nq], op=ALU.add),
                         r=[R_S16[k % 4], R_spb[b]], w=[R_S16[(k + 1) % 4]])

        def stageB1(i):
            Q, j, jmax = iters[i]
            q0, nq = grp_cols(Q)
            qs = slice(q0, q0 + nq)
            ks = slice(j * 128, (j + 1) * 128)
            b = i % NB
            c = i % 2
            k = jmax - j
            P.op("pe", lambda e: e.matmul(out=pc[c][:, 0:nq], lhsT=Tincl[:, :], rhs=spb[b][:, 0:nq], start=True, stop=False),
                 r=[R_Ti, R_spb[b]], w=[R_pc[c]])
            if k > 0:
                P.op("pe", lambda e: e.matmul(out=pc[c][:, 0:nq], lhsT=C.ones_bf[:, :], rhs=S16[k % 4][:, 0:nq], start=False, stop=False),
                     r=[C.R_ones, R_S16[k % 4]], w=[R_pc[c]])
            P.op("pe", lambda e: e.matmul(out=pc[c][:, 0:nq], lhsT=KN[:, ks], rhs=QQ[:, qs], start=False, stop=True),
                 r=[R_kT[j // 4], R_qT[Q]], w=[R_pc[c]])
            P.op("act", lambda e: e.activation(out=wb_[c][:, 0:nq], in_=pc[c][:, 0:nq], func=AF.Exp, scale=-1.0), r=[R_pc[c]], w=[R_wb[c]])
            maskop(wb_[c], R_wb[c], Q, j, nq)

        def stageB2(i):
            Q, j, jmax = iters[i]
            q0, nq = grp_cols(Q)
            qs = slice(q0, q0 + nq)
            c = i % 2
            P.op("pe", lambda e: e.matmul(out=po[:, 0:nq], lhsT=vA[:, j, :], rhs=wb_[c][:, 0:nq], start=(j == jmax), stop=(j == 0)),
                 r=[R_v[j // 4], R_wb[c]], w=[R_po])
            if j == 0:
                P.op("dve", lambda e: e.tensor_copy(out=osb_sb[:, 0:nq], in_=po[0:64, 0:nq]), r=[R_po], w=[R_osb])
                P.dma("pool", osb_d[:, qs], osb_sb[:, 0:nq], r=[R_osb])

        n_it = len(iters)
        LA = 2
        for i in range(min(LA, n_it)):
            stageA(i)
        for i in range(n_it):
            if i + LA < n_it:
                stageA(i + LA)
            stageB1(i)
            if i > 0:
                stageB2(i - 1)
        stageB2(n_it - 1)
        P.finish()
    return nc


def run_mix_even(hs_pad, I):
    nc = get_prog("mix_even", build_mix_even)
    w = I["w_in_even"][0]
    in_maps = []
    for h in range(NCORES):
        wq, wk = w[:, h * 64:(h + 1) * 64], w[:, 512 + h * 64:512 + (h + 1) * 64]
        wh = np.concatenate([wq, wq, wk, wk, w[:, 1024 + h * 64:1024 + (h + 1) * 64], w[:, 1536 + h * 64:1536 + (h + 1) * 64]], axis=1)
        gs = slice(4 * h, 4 * h + 4)
        in_maps.append({
            "hs": hs_pad, "wh": np.ascontiguousarray(wh), "g_pre": I["pre_mix_norm"][0],
            "lam_re": np.ascontiguousarray(I["s5_lambda_re"][0, gs]), "lam_im": np.ascontiguousarray(I["s5_lambda_im"][0, gs]),
            "log_dt": np.ascontiguousarray(I["s5_log_dt"][0, gs]),
            "b_re": np.ascontiguousarray(I["s5_b_re"][0, gs]), "b_im": np.ascontiguousarray(I["s5_b_im"][0, gs]),
            "c_re": np.ascontiguousarray(I["s5_c_re"][0, gs]), "c_im": np.ascontiguousarray(I["s5_c_im"][0, gs]),
            "d_skip": np.ascontiguousarray(I["s5_d"][0, h * 64:(h + 1) * 64]),
        })
    res = run_bass_kernel_spmd(nc, in_maps, core_ids=list(range(NCORES)))
    osb = np.concatenate([r["osbT"].T for r in res.results], axis=1)
    ys = np.concatenate([r["ysT"].T for r in res.results], axis=1)
    return np.ascontiguousarray(osb), np.ascontiguousarray(ys)


def build_mix_odd(ng_limit=NG):
    nc = bass.Bass("TRN2", target_bir_lowering=False)
    dr = lambda n, s, k="ExternalInput": nc.dram_tensor(n, list(s), F32, kind=k).ap()
    hs_d = dr("hs", [LP, D])
    wh_d = dr("wh", [D, 516])
    g_d = dr("g_pre", [D])
    cw_d = dr("convw", [128, 12])
    alog_d = dr("a_log", [1])
    dtb_d = dr("dt_bias", [1])
    gdn_d = dr("g_dn", [128])
    og_d = dr("og", [LP, 128], "ExternalOutput")

    with ExitStack() as es:
        C = Ctx(nc, es)
        P = C.P
        whb = P.sb("whb", [128, 8, 516], BF16)
        R_wh = Res("wh")
        gsm = P.sb("gsm", [128, 8], F32)
        R_gsm = Res("gsm")
        stage = [P.sb("stage0", [128, 1024], F32), P.sb("stage1", [128, 1024], F32)]
        R_stage = [Res("st0"), Res("st1")]
        bk = [P.ps(f"bk{i}", [128, 512], F32) for i in range(7)]
        R_bk = [Res(f"bk{i}", True) for i in range(7)]
        pT = P.ps("pT", [128, 8, 128], BF16)
        R_pT = Res("pT", True)
        P.dma("sp", gsm[:, :], g_d.rearrange("(k p) -> p k", p=128), w=[R_gsm], allow_slow_non_contiguous=True)
        C.load_weight(wh_d, whb, R_wh, stage, R_stage, 8, 516, gsc=gsm, R_g=R_gsm)
        cst = P.sb("cst", [128, 16], F32)
        R_cst = Res("cst")
        P.dma("sp", cst[:, 0:12], cw_d, w=[R_cst])
        P.dma("sp", cst[:, 12:13], dtb_d.rearrange("(o n) -> o n", o=1).partition_broadcast(128), w=[R_cst])
        P.dma("sp", cst[:, 13:14], alog_d.rearrange("(o n) -> o n", o=1).partition_broadcast(128), w=[R_cst])
        P.op("act", lambda e: e.activation(out=cst[:, 13:14], in_=cst[:, 13:14], func=AF.Exp), r=[R_cst], w=[R_cst])
        P.op("dve", lambda e: e.tensor_scalar(out=cst[:, 13:14], in0=cst[:, 13:14], scalar1=-1.0, scalar2=None, op0=ALU.mult), r=[R_cst], w=[R_cst])
        Gdn = P.sb("Gdn", [128, 128], F32)
        R_Gdn = Res("Gdn")
        P.dma("sp", Gdn[:, :], gdn_d.partition_broadcast(128), w=[R_Gdn])
        onesf = P.sb("onesf", [128, 128], F32)
        R_onesf = Res("onesf")
        P.op("pool", lambda e: e.memset(onesf[:, :], 1.0), w=[R_onesf])
        identf = P.sb("identf", [128, 128], F32)
        R_if = Res("identf")
        P.op("dve", lambda e: e.tensor_copy(out=identf[:, :], in_=C.ident[:, :]), r=[C.R_ident], w=[R_if])
        maskS = P.sb("maskS", [128, 128], F32)
        maskST = P.sb("maskST", [128, 128], F32)
        TriX = P.sb("TriX", [128, 130], F32)
        R_mk = Res("masks")
        P.op("pool", lambda e: e.affine_select(out=maskS[:, :], in_=onesf[:, :], pattern=[[-1, 128]], compare_op=ALU.is_gt, fill=0.0, base=0, channel_multiplier=1),
             r=[R_onesf], w=[R_mk])
        P.op("pool", lambda e: e.memset(maskS[64:128, 0:64], 0.0), r=[R_mk], w=[R_mk])
        P.op("pool", lambda e: e.affine_select(out=maskST[:, :], in_=onesf[:, :], pattern=[[1, 128]], compare_op=ALU.is_gt, fill=0.0, base=0, channel_multiplier=-1),
             r=[R_onesf], w=[R_mk])
        P.op("pool", lambda e: e.memset(maskST[0:64, 64:128], 0.0), r=[R_mk], w=[R_mk])
        P.op("pool", lambda e: e.affine_select(out=TriX[:, 0:128], in_=onesf[:, :], pattern=[[1, 128]], compare_op=ALU.is_ge, fill=0.0, base=0, channel_multiplier=-1),
             r=[R_onesf], w=[R_mk])
        P.op("pool", lambda e: e.memset(TriX[0:64, 64:128], 0.0), r=[R_mk], w=[R_mk])
        P.op("pool", lambda e: e.memset(TriX[:, 128:130], 0.0), r=[R_mk], w=[R_mk])
        P.op("pool", lambda e: e.memset(TriX[0:64, 128:129], 1.0), r=[R_mk], w=[R_mk])
        P.op("pool", lambda e: e.memset(TriX[64:128, 129:130], 1.0), r=[R_mk], w=[R_mk])
        maskIT = TriX
        mblk = P.sb("mblk", [128, 3, 128], F32)
        for bi, bs in enumerate((16, 32, 64)):
            nb_ = 128 // bs
            P.op("pool", lambda e: e.affine_select(out=mblk[:, bi, :], in_=onesf[:, :], pattern=[[-bs, nb_], [0, bs]], compare_op=ALU.is_ge, fill=0.0,
                                                   base=0, channel_multiplier=1), r=[R_onesf, R_mk], w=[R_mk])
            P.op("pool", lambda e: e.affine_select(out=mblk[:, bi, :], in_=mblk[:, bi, :], pattern=[[bs, nb_], [0, bs]], compare_op=ALU.is_ge, fill=0.0,
                                                   base=bs - 1, channel_multiplier=-1), r=[R_mk], w=[R_mk])
        P.op("pool", lambda e: e.tensor_tensor(out=mblk[:, 2, :], in0=mblk[:, 2, :], in1=mblk[:, 1, :], op=ALU.subtract), r=[R_mk], w=[R_mk])
        P.op("pool", lambda e: e.tensor_tensor(out=mblk[:, 1, :], in0=mblk[:, 1, :], in1=mblk[:, 0, :], op=ALU.subtract), r=[R_mk], w=[R_mk])

        hsb = [P.sb("hsb0", [128, D], F32), P.sb("hsb1", [128, D], F32)]
        R_hsb = [Res("hsb0"), Res("hsb1")]
        xn = P.sb("xn", [128, D], BF16)
        R_xn = Res("xn")
        xnT = P.sb("xnT", [128, 8, 512], BF16)
        R_xnT = Res("xnT")
        stt = P.sb("stt", [128, 4], F32)
        R_stt = Res("stt")
        raw = P.sb("raw", [128, 3, 515], F32)
        R_raw = [Res("rawq"), Res("rawk"), Res("rawv")]
        P.op("pool", lambda e: e.memset(raw[:, :, :].rearrange("p a b -> p (a b)"), 0.0), w=R_raw)
        acc = P.sb("acc", [128, 512], F32)
        R_acc = Res("acc")
        sil = P.sb("sil", [128, 3, 512], F32)
        R_sil = [Res("silq"), Res("silk"), Res("silv")]
        sq = P.sb("sq", [128, 512], F32)
        R_sq = Res("sq")
        rn = P.sb("rn", [128, 512], F32)
        R_rn = Res("rn")
        qnT = P.sb("qnT", [128, 512], BF16)
        knT2 = [P.sb("knT0", [128, 512], BF16), P.sb("knT1", [128, 512], BF16)]
        R_knT2 = [Res("knT0"), Res("knT1")]
        kbT = P.sb("kbT", [128, 512], BF16)
        vT = P.sb("vT", [128, 512], BF16)
        R_qnT, R_kbT, R_vT = Res("qnT"), Res("kbT"), Res("vT")
        brow = P.sb("brow", [1, 512], BF16)
        R_brow = Res("brow")
        NS = 8
        gz = P.sb("gz", [128, NS, 128], F32)
        R_gzs = [Res(f"gz{i}") for i in range(NS)]
        cols = P.sb("cols", [128, NS, 8], F32)
        R_cols = [Res(f"cols{i}") for i in range(NS)]
        egl = P.sb("egl", [128, NS, 2], F32)
        R_egls = [Res(f"egl{i}") for i in range(NS)]
        Gbc = P.sb("Gbc", [128, NS, 128], F32)
        R_Gbcs = [Res(f"Gbc{i}") for i in range(NS)]
        ktl = P.sb("ktl", [128, NS, 2, 128], BF16)
        ind = P.sb("ind", [128, 2], F32)
        R_ind = Res("ind")
        P.op("pool", lambda e: e.memset(ind[:, :], 0.0), w=[R_ind])
        P.op("pool", lambda e: e.memset(ind[0:64, 0:1], 1.0), r=[R_ind], w=[R_ind])
        P.op("pool", lambda e: e.memset(ind[64:128, 1:2], 1.0), r=[R_ind], w=[R_ind])
        osb_ = P.sb("o_sb", [128, 128], F32)
        R_osb_ = Res("o_sb")
        bv = P.sb("bv", [128, NS, 128], BF16)
        ktok = P.sb("ktok", [128, NS, 128], BF16)
        nUT = P.sb("nUT", [128, NS, 128], BF16)
        W1n = P.sb("W1n", [128, NS, 128], BF16)
        V1s = P.sb("V1s", [128, NS, 128], BF16)
        nDT = P.sb("nDT", [128, NS, 2, 128], BF16)
        PTs = P.sb("PTs", [128, NS, 128], BF16)
        Rsb = P.sb("Rsb", [128, NS, 128], F32)
        Nsb = P.sb("Nsb", [128, NS, 2, 128], F32)
        R_x = {n: [Res(f"{n}{i}") for i in range(NS)] for n in ("ktok", "nUT", "W1n", "V1s", "nDT", "PTs", "Rsb", "Nsb")}
        Tt = P.sb("Tt", [128, 128], F32)
        R_Tt = Res("Tt")
        R_ktls, R_bvs = [Res(f"ktl{i}") for i in range(NS)], [Res(f"bv{i}") for i in range(NS)]
        egb = P.sb("egb", [128, NS, 128], F32)
        R_egbs = [Res(f"egb{i}") for i in range(NS)]
        qg = P.sb("qg", [128, NS, 128], BF16)
        R_qgs = [Res(f"qg{i}") for i in range(NS)]
        Dm_ = P.sb("Dm", [128, NS, 128], F32)
        DTm_ = P.sb("DTm", [128, NS, 128], F32)
        DTI_ = P.sb("DTI", [128, NS, 128], F32)
        R_Dms, R_DTms, R_DTIs = ([Res(f"{n}{i}") for i in range(NS)] for n in ("Dm", "DTm", "DTI"))
        NW = 14
        Wk_ = P.sb("Wk", [128, 4, NW, 128], F32)
        R_Wks = [[Res(f"Wk{s_}_{i}") for i in range(NW)] for s_ in range(4)]
        ATb_ = P.sb("ATb", [128, NS, 128], BF16)
        R_ATbs = [Res(f"ATb{i}") for i in range(NS)]
        aiT_ = P.sb("aiT", [128, NS, 128], BF16)
        R_aiTs = [Res(f"aiT{i}") for i in range(NS)]
        rt = P.sb("rt", [128, 128], BF16)
        vn = P.sb("vn", [128, 128], BF16)
        R_rt, R_vn = Res("rt"), Res("vn")
        S32 = P.sb("S32", [128, 128], F32)
        Sbf = P.sb("Sbf", [128, 128], BF16)
        R_S32, R_Sbf = Res("S32"), Res("Sbf")
        P.op("pool", lambda e: e.memset(S32[:, :], 0.0), w=[R_S32])
        P.op("pool", lambda e: e.memset(Sbf[:, :], 0.0), w=[R_Sbf])
        ogs = P.sb("ogs", [128, 128], F32)
        R_ogs = Res("ogs")
        junk = stage[0]
        R_junk = R_stage[0]
        pQKV = [bk[0], bk[1], bk[2]]
        R_pQKV = [R_bk[0], R_bk[1], R_bk[2]]

        tile_i = 0
        pending_scan = None
        R_stt2 = Res("stt2")
        for gi in range(ng_limit):
            c0, ncol = grp_cols(gi)
            ntl = ncol // 128
            N = ncol
            for tl in range(ntl):
                t = gi * 4 + tl
                hb, Rh = hsb[tile_i % 2], R_hsb[tile_i % 2]
                tile_i += 1
                tc_ = slice(tl * 128, (tl + 1) * 128)
                P.dma("sp", hb[:, :], hs_d[t * 128:(t + 1) * 128, :], w=[Rh])
                P.op("act", lambda e: e.activation(out=junk[:, :], in_=hb[:, :], func=AF.Square, accum_out=stt[:, 0:1]), r=[Rh], w=[R_junk, R_stt])
                C.rstd(stt[:, 0:1], stt[:, 0:1], R_stt, R_stt, D)
                P.op("act", lambda e: e.activation(out=xn[:, :], in_=hb[:, :], func=AF.Copy, scale=stt[:, 0:1]), r=[Rh, R_stt], w=[R_xn])
                for k in range(8):
                    P.op("pe", lambda e: e.transpose(out=pT[:, k, :], in_=xn[:, k * 128:(k + 1) * 128], identity=C.ident[:, :]), r=[R_xn, C.R_ident], w=[R_pT])
                P.op("dve", lambda e: e.tensor_copy(out=xnT[:, :, tc_], in_=pT[:, :, :]), r=[R_pT], w=[R_xnT])
                for k in range(8):
                    P.op("pe", lambda e: e.matmul(out=bk[5][:, 0:132], lhsT=xnT[:, k, tc_], rhs=whb[:, k, 384:516], start=(k == 0), stop=(k == 7)),
                         r=[R_xnT, R_wh], w=[R_bk[5]])
                sl_ = (gi % 2) * 4 + tl
                P.op("act", lambda e: e.activation(out=gz[:, sl_, :], in_=bk[5][:, 0:128], func=AF.Silu), r=[R_bk[5]], w=[R_gzs[sl_]])
                P.op("pool", lambda e: e.tensor_tensor(out=gz[:, sl_, :], in0=gz[:, sl_, :], in1=Gdn[:, :], op=ALU.mult), r=[R_gzs[sl_], R_Gdn], w=[R_gzs[sl_]])
                cc = lambda i: cols[:, sl_, i:i + 1]
                Rc = R_cols[sl_]
                P.op("act", lambda e: e.activation(out=cc(1), in_=bk[5][:, 130:131], func=AF.Sigmoid), r=[R_bk[5]], w=[Rc])
                P.op("dve", lambda e: e.tensor_tensor(out=cc(7), in0=bk[5][:, 128:129], in1=cst[:, 12:13], op=ALU.add), r=[R_bk[5], R_cst], w=[Rc])
            s0_ = (gi % 2) * 4
            c4 = lambda i: cols[:, s0_:s0_ + ntl, i]
            Rcs = R_cols[s0_:s0_ + ntl]
            g4 = lambda fn: P.op("dve", fn, r=Rcs + [R_cst], w=Rcs)
            g4(lambda e: e.tensor_scalar(out=c4(0), in0=c4(7), scalar1=-1.0, scalar2=None, op0=ALU.mult))
            g4(lambda e: e.tensor_tensor(out=c4(0), in0=c4(0), in1=c4(7), op=ALU.max))
            P.op("act", lambda e: e.activation(out=c4(0), in_=c4(0), func=AF.Exp, scale=-1.0), r=Rcs, w=Rcs)
            g4(lambda e: e.tensor_scalar(out=c4(6), in0=c4(0), scalar1=2.0, scalar2=None, op0=ALU.add))
            g4(lambda e: e.reciprocal(out=c4(6), in_=c4(6)))
            g4(lambda e: e.tensor_tensor(out=c4(6), in0=c4(6), in1=c4(0), op=ALU.mult))
            g4(lambda e: e.tensor_tensor(out=c4(0), in0=c4(6), in1=c4(6), op=ALU.mult))
            g4(lambda e: e.tensor_scalar(out=c4(5), in0=c4(0), scalar1=1.0 / 15, scalar2=1.0 / 13, op0=ALU.mult, op1=ALU.add))
            for cf in (1.0 / 11, 1.0 / 9, 1.0 / 7, 1.0 / 5, 1.0 / 3, 1.0):
                g4(lambda e: e.tensor_tensor(out=c4(5), in0=c4(5), in1=c4(0), op=ALU.mult))
                g4(lambda e: e.tensor_scalar(out=c4(5), in0=c4(5), scalar1=cf, scalar2=None, op0=ALU.add))
            g4(lambda e: e.tensor_tensor(out=c4(5), in0=c4(5), in1=c4(6), op=ALU.mult))
            g4(lambda e: e.tensor_scalar(out=c4(7), in0=c4(7), scalar1=0.0, scalar2=None, op0=ALU.max))
            g4(lambda e: e.scalar_tensor_tensor(out=c4(0), in0=c4(5), scalar=2.0, in1=c4(7), op0=ALU.mult, op1=ALU.add))
            g4(lambda e: e.tensor_scalar(out=c4(0), in0=c4(0), scalar1=cst[:, 13:14], scalar2=None, op0=ALU.mult))
            for wi in range(3):
                for k in range(8):
                    P.op("pe", lambda e: e.matmul(out=pQKV[wi][:, 0:N], lhsT=whb[:, k, wi * 128:(wi + 1) * 128], rhs=xnT[:, k, 0:N], start=(k == 0), stop=(k == 7)),
                         r=[R_xnT, R_wh], w=[R_pQKV[wi]])
            for k in range(8):
                P.op("pe", lambda e: e.matmul(out=bk[3][0:1, 0:N], lhsT=whb[:, k, 514:515], rhs=xnT[:, k, 0:N], start=(k == 0), stop=(k == 7)),
                     r=[R_xnT, R_wh], w=[R_bk[3]])
            P.op("act", lambda e: e.activation(out=brow[:, 0:N], in_=bk[3][0:1, 0:N], func=AF.Sigmoid), r=[R_bk[3]], w=[R_brow])
            P.op("pe", lambda e: e.matmul(out=bk[4][:, 0:N], lhsT=C.ones_bf[0:1, :], rhs=brow[0:1, 0:N], start=True, stop=True), r=[C.R_ones, R_brow], w=[R_bk[4]])
            for wi in range(3):
                P.op("act", lambda e: e.activation(out=raw[:, wi, 3:3 + N], in_=pQKV[wi][:, 0:N], func=AF.Copy), r=[R_pQKV[wi]], w=[R_raw[wi]])
                eng = "dve" if wi != 1 else "pool"
                P.op(eng, lambda e: e.tensor_scalar(out=acc[:, 0:N], in0=raw[:, wi, 0:N], scalar1=cst[:, wi * 4:wi * 4 + 1], scalar2=None, op0=ALU.mult),
                     r=[R_raw[wi], R_cst], w=[R_acc])
                for j in range(1, 4):
                    P.op("dve", lambda e: e.scalar_tensor_tensor(out=acc[:, 0:N], in0=raw[:, wi, j:j + N], scalar=cst[:, wi * 4 + j:wi * 4 + j + 1], in1=acc[:, 0:N],
                                                                 op0=ALU.mult, op1=ALU.add), r=[R_raw[wi], R_cst, R_acc], w=[R_acc])
                P.op("act", lambda e: e.activation(out=sil[:, wi, 0:N], in_=acc[:, 0:N], func=AF.Silu), r=[R_acc], w=[R_sil[wi]])
                P.op("pool", lambda e: e.tensor_copy(out=raw[:, wi, 0:3], in_=raw[:, wi, N:N + 3]), r=[R_raw[wi]], w=[R_raw[wi]])
            knT = knT2[gi % 2]
            R_knT = R_knT2[gi % 2]
            for wi, (dst, Rd, sc) in enumerate(((qnT, R_qnT, 128 ** -0.5), (knT, R_knT, 1.0))):
                P.op("act", lambda e: e.activation(out=sq[:, 0:N], in_=sil[:, wi, 0:N], func=AF.Square), r=[R_sil[wi]], w=[R_sq])
                P.op("pe", lambda e: e.matmul(out=bk[6][:, 0:N], lhsT=onesf[:, :], rhs=sq[:, 0:N], start=True, stop=True), r=[R_onesf, R_sq], w=[R_bk[6]])
                P.op("dve", lambda e: e.tensor_scalar(out=rn[:, 0:N], in0=bk[6][:, 0:N], scalar1=EPS, scalar2=None, op0=ALU.add), r=[R_bk[6]], w=[R_rn])
                P.op("act", lambda e: e.activation(out=rn[:, 0:N], in_=rn[:, 0:N], func=AF.Sqrt), r=[R_rn], w=[R_rn])
                P.op("dve", lambda e: e.reciprocal(out=rn[:, 0:N], in_=rn[:, 0:N]), r=[R_rn], w=[R_rn])
                P.op("dve", lambda e: e.scalar_tensor_tensor(out=dst[:, 0:N], in0=sil[:, wi, 0:N], scalar=sc, in1=rn[:, 0:N], op0=ALU.mult, op1=ALU.mult),
                     r=[R_sil[wi], R_rn], w=[Rd])
            P.op("dve", lambda e: e.tensor_tensor(out=kbT[:, 0:N], in0=knT[:, 0:N], in1=bk[4][:, 0:N], op=ALU.mult), r=[R_knT, R_bk[4]], w=[R_kbT])
            P.op("pool", lambda e: e.tensor_copy(out=vT[:, 0:N], in_=sil[:, 2, 0:N]), r=[R_sil[2]], w=[R_vT])

            def pre(tl, gi=gi, knT=knT, R_knT=R_knT):
                sl_ = (gi % 2) * 4 + tl
                tc_ = slice(tl * 128, (tl + 1) * 128)
                cc = lambda i: cols[:, sl_, i:i + 1]
                Rc = R_cols[sl_]
                bC, RC_ = bk[tl], R_bk[tl]
                bG, RG = bC, RC_
                bKK, RKK = bC, RC_
                bA, RA = bC[:, 0:256], RC_
                bB, RB = bC[:, 256:512], RC_
                pTk, pTv = pT[:, 2 * tl, :], pT[:, 2 * tl + 1, :]
                Gb, R_Gb = Gbc[:, sl_, :], R_Gbcs[sl_]
                Dm, DTm, DTI = Dm_[:, sl_, :], DTm_[:, sl_, :], DTI_[:, sl_, :]
                R_Dm, R_DTm, R_DTI = R_Dms[sl_], R_DTms[sl_], R_DTIs[sl_]
                R_Wk = R_Wks[tl]
                W = lambda i: Wk_[:, tl, i, :]
                P.op("pool", lambda e: e.tensor_scalar(out=Gb, in0=onesf[:, :], scalar1=cc(0), scalar2=None, op0=ALU.mult), r=[R_onesf, Rc], w=[R_Gb])
                P.op("pe", lambda e: e.matmul(out=bG[:, 0:130], lhsT=Gb, rhs=TriX[:, :], start=True, stop=True), r=[R_Gb, R_mk], w=[RG])
                P.op("pe", lambda e: e.matmul(out=bG[:, 256:258], lhsT=TriX[:, 0:128], rhs=cols[:, sl_, 0:2], start=True, stop=True), r=[R_mk, Rc], w=[RG])
                yield
                P.op("dve", lambda e: e.tensor_copy(out=cc(2), in_=bG[:, 256:257]), r=[RG], w=[Rc])
                P.op("dve", lambda e: e.tensor_scalar(out=cc(6), in0=bG[:, 256:257], scalar1=-1.0, scalar2=None, op0=ALU.mult), r=[RG], w=[Rc])
                P.op("dve", lambda e: e.tensor_copy(out=cols[0:64, sl_, 3:4], in_=bG[0:64, 128:129]), r=[RG], w=[Rc])
                P.op("dve", lambda e: e.tensor_copy(out=cols[64:128, sl_, 3:4], in_=bG[64:128, 129:130]), r=[RG], w=[Rc])
                P.op("dve", lambda e: e.tensor_tensor(out=cc(4), in0=cc(3), in1=cc(2), op=ALU.subtract), r=[Rc], w=[Rc])
                P.op("dve", lambda e: e.tensor_scalar(out=Dm, in0=bG[:, 0:128], scalar1=cc(2), scalar2=None, op0=ALU.subtract), r=[RG, Rc], w=[R_Dm])
                P.op("act", lambda e: e.activation(out=egl[:, sl_, :], in_=bG[:, 128:130], func=AF.Exp), r=[RG], w=[R_egls[sl_]])
                P.op("act", lambda e: e.activation(out=egb[:, sl_, :], in_=bG[:, 0:128], func=AF.Exp), r=[RG], w=[R_egbs[sl_]])
                yield
                P.op("act", lambda e: e.activation(out=cc(4), in_=cc(4), func=AF.Exp), r=[Rc], w=[Rc])
                P.op("act", lambda e: e.activation(out=cc(7), in_=cc(2), func=AF.Exp), r=[Rc], w=[Rc])
                P.op("pool", lambda e: e.tensor_scalar(out=DTm, in0=Dm, scalar1=0.0, scalar2=None, op0=ALU.min), r=[R_Dm], w=[R_DTm])
                P.op("pool", lambda e: e.tensor_scalar(out=Dm, in0=Dm, scalar1=0.0, scalar2=-1.0, op0=ALU.max, op1=ALU.mult), r=[R_Dm], w=[R_Dm])
                P.op("pool", lambda e: e.tensor_tensor(out=qg[:, sl_, :], in0=qnT[:, tc_], in1=egb[:, sl_, :], op=ALU.mult), r=[R_qnT, R_egbs[sl_]], w=[R_qgs[sl_]])
                P.op("pe", lambda e: e.transpose(out=pTk, in_=knT[:, tc_], identity=C.ident[:, :]), r=[R_knT, C.R_ident], w=[R_pT])
                P.op("pe", lambda e: e.transpose(out=pTv, in_=vT[:, tc_], identity=C.ident[:, :]), r=[R_vT, C.R_ident], w=[R_pT])
                P.op("pe", lambda e: e.matmul(out=bKK[:, 0:128], lhsT=kbT[:, tc_], rhs=knT[:, tc_], start=True, stop=True), r=[R_kbT, R_knT], w=[RKK])
                P.op("pe", lambda e: e.matmul(out=bKK[:, 128:256], lhsT=knT[:, tc_], rhs=kbT[:, tc_], start=True, stop=True), r=[R_kbT, R_knT], w=[RKK])
                P.op("pe", lambda e: e.matmul(out=bKK[:, 256:384], lhsT=knT[:, tc_], rhs=qnT[:, tc_], start=True, stop=True), r=[R_qnT, R_knT], w=[RKK])
                yield
                P.op("dve", lambda e: e.scalar_tensor_tensor(out=cc(5), in0=cc(7), scalar=-1.0, in1=cc(1), op0=ALU.mult, op1=ALU.mult), r=[Rc], w=[Rc])
                P.op("act", lambda e: e.activation(out=Dm, in_=Dm, func=AF.Exp), r=[R_Dm], w=[R_Dm])
                P.op("act", lambda e: e.activation(out=DTm, in_=DTm, func=AF.Exp), r=[R_DTm], w=[R_DTm])
                for ch in range(2):
                    P.op("dve", lambda e: e.tensor_scalar(out=ktl[:, sl_, ch, :], in0=pTk, scalar1=cc(4), scalar2=ind[:, ch:ch + 1], op0=ALU.mult, op1=ALU.mult),
                         r=[R_pT, Rc, R_ind], w=[R_ktls[sl_]])
                P.op("dve", lambda e: e.tensor_scalar(out=bv[:, sl_, :], in0=pTv, scalar1=cc(1), scalar2=None, op0=ALU.mult), r=[R_pT, Rc], w=[R_bvs[sl_]])
                P.op("dve", lambda e: e.tensor_copy(out=ktok[:, sl_, :], in_=pTk), r=[R_pT], w=[R_x["ktok"][sl_]])
                yield
                P.op("pool", lambda e: e.tensor_tensor(out=Dm, in0=Dm, in1=maskS[:, :], op=ALU.mult), r=[R_Dm, R_mk], w=[R_Dm])
                P.op("pool", lambda e: e.tensor_tensor(out=DTI, in0=DTm, in1=maskIT[:, 0:128], op=ALU.mult), r=[R_DTm, R_mk], w=[R_DTI])
                P.op("pool", lambda e: e.tensor_tensor(out=DTm, in0=DTm, in1=maskST[:, :], op=ALU.mult), r=[R_DTm, R_mk], w=[R_DTm])
                yield
                (LF, LTF, L_, LT_, O32, O32T, O64, O64T, X_, XT_, L2_, L2T_, Y_, Y2_) = range(NW)
                P.op("dve", lambda e: e.tensor_tensor(out=W(LF), in0=bKK[:, 0:128], in1=Dm, op=ALU.mult), r=[RKK, R_Dm], w=[R_Wk[LF]])
                P.op("dve", lambda e: e.tensor_tensor(out=W(LTF), in0=bKK[:, 128:256], in1=DTm, op=ALU.mult), r=[RKK, R_DTm], w=[R_Wk[LTF]])
                P.op("dve", lambda e: e.tensor_tensor(out=aiT_[:, sl_, :], in0=bKK[:, 256:384], in1=DTI, op=ALU.mult), r=[RKK, R_DTI], w=[R_aiTs[sl_]])
                yield
                for (dst, src, mi) in ((L_, LF, 0), (LT_, LTF, 0), (O32, LF, 1), (O32T, LTF, 1), (O64, LF, 2), (O64T, LTF, 2)):
                    P.op("pool", lambda e: e.tensor_tensor(out=W(dst), in0=W(src), in1=mblk[:, mi, :], op=ALU.mult), r=[R_Wk[src], R_mk], w=[R_Wk[dst]])
                P.op("pool", lambda e: e.tensor_tensor(out=W(X_), in0=identf[:, :], in1=W(L_), op=ALU.subtract), r=[R_if, R_Wk[L_]], w=[R_Wk[X_]])
                P.op("pool", lambda e: e.tensor_tensor(out=W(XT_), in0=identf[:, :], in1=W(LT_), op=ALU.subtract), r=[R_if, R_Wk[LT_]], w=[R_Wk[XT_]])
                yield

                def mm(out_ap, Rout, li, ri):
                    P.op("pe", lambda e: e.matmul(out=out_ap, lhsT=W(li), rhs=W(ri), start=True, stop=True), r=[R_Wk[li], R_Wk[ri]], w=[Rout])

                cl, clt, nl, nlt = L_, LT_, L2_, L2T_
                for lvl in range(3):
                    last = lvl == 2
                    mm(bA[:, 0:128], RA, clt, cl)
                    if not last:
                        mm(bA[:, 128:256], RA, cl, clt)
                    yield
                    P.op("act", lambda e: e.activation(out=W(nl), in_=bA[:, 0:128], func=AF.Copy), r=[RA], w=[R_Wk[nl]])
                    if not last:
                        P.op("act", lambda e: e.activation(out=W(nlt), in_=bA[:, 128:256], func=AF.Copy), r=[RA], w=[R_Wk[nlt]])
                    yield
                    mm(bB[:, 0:128], RB, nl, XT_)
                    mm(bB[:, 128:256], RB, XT_, nl)
                    yield
                    P.op("dve", lambda e: e.tensor_tensor(out=W(XT_), in0=bB[:, 0:128], in1=W(XT_), op=ALU.add), r=[RB, R_Wk[XT_]], w=[R_Wk[XT_]])
                    P.op("dve", lambda e: e.tensor_tensor(out=W(X_), in0=bB[:, 128:256], in1=W(X_), op=ALU.add), r=[RB, R_Wk[X_]], w=[R_Wk[X_]])
                    yield
                    cl, clt, nl, nlt = nl, nlt, cl, clt
                mm(bA[:, 0:128], RA, O32T, X_)
                mm(bA[:, 128:256], RA, O32, XT_)
                yield
                P.op("act", lambda e: e.activation(out=W(Y_), in_=bA[:, 0:128], func=AF.Copy), r=[RA], w=[R_Wk[Y_]])
                P.op("act", lambda e: e.activation(out=W(Y2_), in_=bA[:, 128:256], func=AF.Copy), r=[RA], w=[R_Wk[Y2_]])
                yield
                mm(bB[:, 0:128], RB, XT_, Y_)
                mm(bB[:, 128:256], RB, X_, Y2_)
                yield
                P.op("dve", lambda e: e.tensor_tensor(out=W(X_), in0=W(X_), in1=bB[:, 0:128], op=ALU.subtract), r=[RB, R_Wk[X_]], w=[R_Wk[X_]])
                P.op("dve", lambda e: e.tensor_tensor(out=W(XT_), in0=W(XT_), in1=bB[:, 128:256], op=ALU.subtract), r=[RB, R_Wk[XT_]], w=[R_Wk[XT_]])
                yield
                mm(bA[:, 0:128], RA, O64, XT_)
                yield
                P.op("act", lambda e: e.activation(out=W(Y2_), in_=bA[:, 0:128], func=AF.Copy), r=[RA], w=[R_Wk[Y2_]])
                yield
                mm(bB[:, 0:128], RB, X_, Y2_)
                yield
                P.op("dve", lambda e: e.tensor_tensor(out=ATb_[:, sl_, :], in0=W(XT_), in1=bB[:, 0:128], op=ALU.subtract), r=[RB, R_Wk[XT_]], w=[R_ATbs[sl_]])
                yield
                AT, R_AT = ATb_[:, sl_, :], R_ATbs[sl_]
                P.op("pool", lambda e: e.tensor_scalar(out=nUT[:, sl_, :], in0=AT, scalar1=cc(5), scalar2=None, op0=ALU.mult), r=[R_AT, Rc], w=[R_x["nUT"][sl_]])
                yield
                P.op("pe", lambda e: e.matmul(out=bA[:, 0:128], lhsT=nUT[:, sl_, :], rhs=ktok[:, sl_, :], start=True, stop=True),
                     r=[R_x["nUT"][sl_], R_x["ktok"][sl_]], w=[RA])
                P.op("pe", lambda e: e.matmul(out=bA[:, 128:256], lhsT=AT, rhs=bv[:, sl_, :], start=True, stop=True), r=[R_AT, R_bvs[sl_]], w=[RA])
                yield
                P.op("act", lambda e: e.activation(out=W1n[:, sl_, :], in_=bA[:, 0:128], func=AF.Copy), r=[RA], w=[R_x["W1n"][sl_]])
                P.op("dve", lambda e: e.tensor_copy(out=V1s[:, sl_, :], in_=bA[:, 128:256]), r=[RA], w=[R_x["V1s"][sl_]])
                yield
                for ch in range(2):
                    P.op("pe", lambda e: e.matmul(out=bB[:, ch * 128:(ch + 1) * 128], lhsT=W1n[:, sl_, :], rhs=ktl[:, sl_, ch, :], start=True, stop=True),
                         r=[R_x["W1n"][sl_], R_ktls[sl_]], w=[RB])
                P.op("pe", lambda e: e.matmul(out=bA[:, 0:128], lhsT=W1n[:, sl_, :], rhs=aiT_[:, sl_, :], start=True, stop=True),
                     r=[R_x["W1n"][sl_], R_aiTs[sl_]], w=[RA])
                P.op("pe", lambda e: e.matmul(out=bA[:, 128:256], lhsT=aiT_[:, sl_, :], rhs=V1s[:, sl_, :], start=True, stop=True),
                     r=[R_aiTs[sl_], R_x["V1s"][sl_]], w=[RA])
                yield
                P.op("act", lambda e: e.activation(out=nDT[:, sl_, :, :].rearrange("p a b -> p (a b)"), in_=bB[:, 0:256], func=AF.Copy), r=[RB], w=[R_x["nDT"][sl_]])
                P.op("dve", lambda e: e.tensor_tensor(out=PTs[:, sl_, :], in0=bA[:, 0:128], in1=qg[:, sl_, :], op=ALU.add), r=[RA, R_qgs[sl_]], w=[R_x["PTs"][sl_]])
                P.op("act", lambda e: e.activation(out=Rsb[:, sl_, :], in_=bA[:, 128:256], func=AF.Copy), r=[RA], w=[R_x["Rsb"][sl_]])
                yield
                for ch in range(2):
                    P.op("pe", lambda e: e.matmul(out=bB[:, ch * 128:(ch + 1) * 128], lhsT=ktl[:, sl_, ch, :], rhs=V1s[:, sl_, :], start=True, stop=True),
                         r=[R_ktls[sl_], R_x["V1s"][sl_]], w=[RB])
                yield
                P.op("dve", lambda e: e.tensor_copy(out=Nsb[:, sl_, :, :].rearrange("p a b -> p (a b)"), in_=bB[:, 0:256]), r=[RB], w=[R_x["Nsb"][sl_]])
                yield

            def scan(gi, ntl, knT, R_knT):
                pb, R_pb = bk[4], R_bk[4]
                for tl in range(ntl):
                    t = gi * 4 + tl
                    sl_ = (gi % 2) * 4 + tl
                    for ch in range(2):
                        P.op("dve", lambda e: e.scalar_tensor_tensor(out=Tt[:, :], in0=S32[:, :], scalar=egl[:, sl_, ch:ch + 1], in1=Nsb[:, sl_, ch, :], op0=ALU.mult, op1=ALU.add),
                             r=[R_S32, R_egls[sl_], R_x["Nsb"][sl_]], w=[R_Tt])
                        P.op("pe", lambda e: e.matmul(out=pb[:, ch * 128:(ch + 1) * 128], lhsT=nDT[:, sl_, ch, :], rhs=Sbf[:, :], start=True, stop=True),
                             r=[R_x["nDT"][sl_], R_Sbf], w=[R_pb])
                        P.op("pe", lambda e: e.matmul(out=pb[:, 256 + ch * 128:256 + (ch + 1) * 128], lhsT=PTs[:, sl_, :], rhs=Sbf[:, :], start=True, stop=True),
                             r=[R_x["PTs"][sl_], R_Sbf], w=[R_pb])
                        yield
                        P.op("dve", lambda e: e.tensor_tensor(out=Sbf[:, :], in0=pb[:, ch * 128:(ch + 1) * 128], in1=Tt[:, :], op=ALU.add), r=[R_pb, R_Tt], w=[R_Sbf])
                        P.op("dve", lambda e: e.tensor_tensor(out=S32[:, :], in0=pb[:, ch * 128:(ch + 1) * 128], in1=Tt[:, :], op=ALU.add), r=[R_pb, R_Tt], w=[R_S32])
                        yield
                    P.op("dve", lambda e: e.tensor_tensor(out=osb_[0:64, :], in0=pb[0:64, 256:384], in1=Rsb[0:64, sl_, :], op=ALU.add), r=[R_pb, R_x["Rsb"][sl_]], w=[R_osb_])
                    P.op("dve", lambda e: e.tensor_tensor(out=osb_[64:128, :], in0=pb[64:128, 384:512], in1=Rsb[64:128, sl_, :], op=ALU.add), r=[R_pb, R_x["Rsb"][sl_]], w=[R_osb_])
                    P.op("act", lambda e: e.activation(out=junk[:, 0:128], in_=osb_[:, :], func=AF.Square, accum_out=stt[:, 1:2]), r=[R_osb_], w=[R_junk, R_stt2])
                    C.rstd(stt[:, 1:2], stt[:, 1:2], R_stt2, R_stt2, 128)
                    P.op("dve", lambda e: e.scalar_tensor_tensor(out=ogs[:, :], in0=osb_[:, :], scalar=stt[:, 1:2], in1=gz[:, sl_, :], op0=ALU.mult, op1=ALU.mult),
                         r=[R_osb_, R_stt2, R_gzs[sl_]], w=[R_ogs])
                    P.dma("pool", og_d[t * 128:(t + 1) * 128, :], ogs[:, :], r=[R_ogs])
                    yield

            for pair0 in range(0, ntl, 4):
                gens = [pre(tl) for tl in range(pair0, min(pair0 + 4, ntl))]
                last_pair = pair0 + 4 >= ntl
                while gens:
                    for g_ in list(gens):
                        try:
                            next(g_)
                        except StopIteration:
                            gens.remove(g_)
                    if pending_scan is not None:
                        try:
                            next(pending_scan)
                        except StopIteration:
                            pending_scan = None
                if last_pair and pending_scan is not None:
                    for _ in pending_scan:
                        pass
                    pending_scan = None
            pending_scan = scan(gi, ntl, knT, R_knT)
        for _ in pending_scan:
            pass
        P.finish()
    return nc


def run_mix_odd(hs_pad, I):
    nc = get_prog("mix_odd", build_mix_odd)
    w = I["w_in_odd"][0]
    cwv = I["dn_conv_w"][0]
    in_maps = []
    for h in range(NCORES):
        sl = lambda base: slice(base + h * 128, base + (h + 1) * 128)
        zc = np.zeros((D, 1), np.float32)
        wh = np.concatenate([w[:, sl(0)], w[:, sl(1024)], w[:, sl(2048)], w[:, sl(3072)], w[:, 4096 + h:4097 + h], zc, w[:, 4104 + h:4105 + h], zc], axis=1)
        cw = np.concatenate([cwv[:, sl(0)].T, cwv[:, sl(1024)].T, cwv[:, sl(2048)].T], axis=1)
        in_maps.append({"hs": hs_pad, "wh": np.ascontiguousarray(wh), "g_pre": I["pre_mix_norm"][1], "convw": np.ascontiguousarray(cw),
                        "a_log": np.ascontiguousarray(I["dn_a_log"][0, h:h + 1]), "dt_bias": np.ascontiguousarray(I["dn_dt_bias"][0, h:h + 1]),
                        "g_dn": I["dn_out_norm"][0]})
    res = run_bass_kernel_spmd(nc, in_maps, core_ids=list(range(NCORES)))
    return np.ascontiguousarray(np.concatenate([r["og"] for r in res.results], axis=1))


def kernel(**inputs):
    I = {k: np.ascontiguousarray(np.asarray(v, dtype=np.float32)) for k, v in inputs.items()}
    x = I["x"][0]
    hs0 = np.ascontiguousarray(np.concatenate([np.zeros((PADF, D), np.float32), I["meta_tokens"], x], axis=0))
    osb, ys = run_mix_even(hs0, I)
    W0 = {"wglu": I["s5_w_glu"][0], "bglu": I["s5_b_glu"][0],
          "gmerge": np.ascontiguousarray(np.concatenate([I["sb_out_norm"][0], I["s5_out_norm"][0]])),
          "wout": I["w_out_even"][0], "w1": I["mlp_w1"][0], "w2": I["mlp_w2"][0],
          "g_postmix": I["post_mix_norm"][0], "g_premlp": I["pre_mlp_norm"][0], "g_postmlp": I["post_mlp_norm"][0]}
    hs1 = run_post("even", hs0, (osb, ys), W0)
    og = run_mix_odd(hs1, I)
    W1 = {"wout": I["w_out_odd"][0], "w1": I["mlp_w1"][1], "w2": I["mlp_w2"][1],
          "g_postmix": I["post_mix_norm"][1], "g_premlp": I["pre_mlp_norm"][1], "g_postmlp": I["post_mlp_norm"][1]}
    out = run_post("odd", hs1, og, W1)
    return np.ascontiguousarray(out[128:].reshape(1, 16384, D).astype(np.float32))
```

```python
from contextlib import ExitStack
import numpy as np
import concourse.bass as bass
import concourse.mybir as mybir
from concourse.bass_utils import run_bass_kernel_spmd

F32 = mybir.dt.float32
BF16 = mybir.dt.bfloat16
ALU = mybir.AluOpType
AF = mybir.ActivationFunctionType

NCORES = 8
D = 1024
DFF = 4096
NT = 129
LP = NT * 128
PADF = 112
NMETA = 16
TPC = 17
EPS = 1e-6


class Res:
    __slots__ = ("name", "lw", "rd", "psum")

    def __init__(self, name="", psum=False):
        self.name = name
        self.lw = None
        self.rd = {}
        self.psum = psum


class Prog:
    ENG = ("pe", "act", "dve", "pool", "sp")
    NDMA = 6

    def __init__(self, nc, es):
        self.nc = nc
        self.es = es
        self.lists = {k: [] for k in self.ENG}
        self.count = {k: 0 for k in self.ENG}
        self.sem = {}
        for k in self.ENG:
            self.sem[k] = es.enter_context(nc.semaphore("s_" + k))
        self.dsem = {}
        self.dcount = {}
        for q in ("sp", "pool", "act"):
            self.dsem[q] = [es.enter_context(nc.semaphore(f"d_{q}{i}")) for i in range(self.NDMA)]
            self.dcount[q] = 0
        self.waited = {}
        self.eobj = {"pe": nc.tensor, "act": nc.scalar, "dve": nc.vector, "pool": nc.gpsimd, "sp": nc.sync}

    def sb(self, name, shape, dt):
        return self.es.enter_context(self.nc.sbuf_tensor(name, list(shape), dt))

    def ps(self, name, shape, dt):
        return self.es.enter_context(self.nc.psum_tensor(name, list(shape), dt))

    def _semobj(self, key):
        if isinstance(key, tuple):
            return self.dsem[key[0]][key[1]]
        return self.sem[key]

    def _wait(self, eng, key, val):
        if key == eng and eng == "pe":
            return
        k = (eng, key)
        if self.waited.get(k, 0) >= val:
            return
        self.waited[k] = val
        so = self._semobj(key)
        self.eobj[eng].wait_ge(so, val)

    def _deps(self, eng, r, w):
        for x in r:
            if x.lw is not None:
                self._wait(eng, *x.lw)
            if x.psum:
                for key, val in x.rd.items():
                    if key != eng:
                        self._wait(eng, key, val)
        for x in w:
            if x.lw is not None:
                self._wait(eng, *x.lw)
            for key, val in x.rd.items():
                self._wait(eng, key, val)

    def _mark(self, tok, r, w):
        key, val = tok
        for x in r:
            if x.rd.get(key, 0) < val:
                x.rd[key] = val
        for x in w:
            x.lw = tok
            x.rd = {}

    def op(self, eng, fn, r=(), w=()):
        self._deps(eng, r, w)
        self.count[eng] += 1
        seq = self.count[eng]
        so = self.sem[eng]
        fn(self.eobj[eng]).then_inc(so, 1)
        self._mark((eng, seq), r, w)

    def dma(self, q, out, in_, r=(), w=(), **kw):
        i = self.dcount[q]
        self.dcount[q] += 1
        slot = i % self.NDMA
        key = (q, slot)
        val = 16 * (i // self.NDMA + 1)
        if i >= self.NDMA:
            self._wait(q, key, val - 16)
        self._deps(q, r, w)
        so = self.dsem[q][slot]
        self.eobj[q].dma_start(out=out, in_=in_, **kw).then_inc(so, 16)
        self._mark((key, val), r, w)

    def barrier(self):
        for e in self.ENG:
            for o in self.ENG:
                if o != e and self.count[o] > 0:
                    self._wait(e, o, self.count[o])
            for q in ("sp", "pool", "act"):
                n = self.dcount[q]
                for slot in range(min(n, self.NDMA)):
                    last_i = ((n - 1 - slot) // self.NDMA) * self.NDMA + slot
                    self._wait(e, (q, slot), 16 * (last_i // self.NDMA + 1))

    def finish(self):
        for q in ("sp", "pool", "act"):
            n = self.dcount[q]
            for slot in range(min(n, self.NDMA)):
                last_i = ((n - 1 - slot) // self.NDMA) * self.NDMA + slot
                self._wait(q, (q, slot), 16 * (last_i // self.NDMA + 1))


class Ctx:
    def __init__(self, nc, es):
        self.P = Prog(nc, es)
        self.nc = nc
        P = self.P
        self.ident = P.sb("ident", [128, 128], BF16)
        self.R_ident = Res("ident")
        P.op("pool", lambda e: e.memset(self.ident[:, :], 1.0), w=[self.R_ident])
        P.op("pool", lambda e: e.affine_select(out=self.ident[:, :], in_=self.ident[:, :], pattern=[[-1, 128]],
                                                 compare_op=ALU.is_equal, fill=0.0, base=0, channel_multiplier=1),
             r=[self.R_ident], w=[self.R_ident])
        self.ones_bf = P.sb("ones_bf", [128, 128], BF16)
        self.R_ones = Res("ones")
        P.op("pool", lambda e: e.memset(self.ones_bf[:, :], 1.0), w=[self.R_ones])
        self.rr = 0

    def rstd(self, ss_ap, out_ap, R_ss, R_out, n, eng2="dve"):
        P = self.P
        P.op("dve", lambda e: e.tensor_scalar(out=out_ap, in0=ss_ap, scalar1=1.0 / n, scalar2=EPS,
                                               op0=ALU.mult, op1=ALU.add), r=[R_ss], w=[R_out])
        P.op("act", lambda e: e.activation(out=out_ap, in_=out_ap, func=AF.Sqrt), r=[R_out], w=[R_out])
        P.op("dve", lambda e: e.reciprocal(out=out_ap, in_=out_ap), r=[R_out], w=[R_out])

    def cast_eng(self):
        self.rr += 1
        return ("dve", "act")[self.rr % 2]

    def load_weight(self, dram, wb, R_wb, stage, R_stage, nk, ncols, gsc=None, R_g=None, col0=0, colw=None):
        P = self.P
        idx = 0
        for k in range(nk):
            for c0 in range(0, ncols, 1024):
                cw = min(1024, ncols - c0)
                st, Rs = stage[idx % len(stage)], R_stage[idx % len(stage)]
                idx += 1
                P.dma("sp", st[:, 0:cw], dram[k * 128:(k + 1) * 128, col0 + c0:col0 + c0 + cw], w=[Rs])
                eng = self.cast_eng()
                rs = [Rs] + ([R_g] if gsc is not None else [])
                if gsc is not None:
                    if eng == "act":
                        P.op("act", lambda e, st=st, k=k, c0=c0, cw=cw: e.activation(
                            out=wb[:, k, c0:c0 + cw], in_=st[:, 0:cw], func=AF.Copy, scale=gsc[:, k:k + 1]), r=rs, w=[R_wb])
                    else:
                        P.op(eng, lambda e, st=st, k=k, c0=c0, cw=cw: e.tensor_scalar(
                            out=wb[:, k, c0:c0 + cw], in0=st[:, 0:cw], scalar1=gsc[:, k:k + 1], scalar2=None,
                            op0=ALU.mult), r=rs, w=[R_wb])
                else:
                    if eng == "act":
                        P.op("act", lambda e, st=st, k=k, c0=c0, cw=cw: e.activation(
                            out=wb[:, k, c0:c0 + cw], in_=st[:, 0:cw], func=AF.Copy), r=rs, w=[R_wb])
                    else:
                        P.op(eng, lambda e, st=st, k=k, c0=c0, cw=cw: e.tensor_copy(
                            out=wb[:, k, c0:c0 + cw], in_=st[:, 0:cw]), r=rs, w=[R_wb])


def build_post(kind):
    even = kind == "even"
    nc = bass.Bass("TRN2", target_bir_lowering=False)
    dr = lambda n, s, k="ExternalInput": nc.dram_tensor(n, list(s), F32, kind=k).ap()
    rows = TPC * 128
    hs_d = dr("hs", [rows, D])
    if even:
        osb_d = dr("osb", [rows, 512])
        ys5_d = dr("ys5", [rows, 512])
        wglu_d = dr("wglu", [512, 512])
        bglu_d = dr("bglu", [512])
        gmerge_d = dr("gmerge", [D])
    else:
        og_d = dr("og", [rows, D])
    wout_d = dr("wout", [D, D])
    w1_d = dr("w1", [D, DFF])
    w2_d = dr("w2", [DFF, D])
    g1_d = dr("g_postmix", [D])
    g2_d = dr("g_premlp", [D])
    g3_d = dr("g_postmlp", [D])
    out_d = dr("out", [rows, D], "ExternalOutput")

    with ExitStack() as es:
        C = Ctx(nc, es)
        P = C.P
        w1b = P.sb("w1b", [128, 8, DFF], BF16)
        w2b = P.sb("w2b", [128, 32, D], BF16)
        woutb = P.sb("woutb", [128, 8, D], BF16)
        R_w1, R_w2, R_wout = Res("w1"), Res("w2"), Res("wout")
        stage = [P.sb(f"stage{i}", [128, 1024], F32) for i in range(4)]
        R_stage = [Res(f"st{i}") for i in range(4)]
        gsm = P.sb("gsm", [128, 3, 8], F32)
        R_gsm = Res("gsm")
        G1 = P.sb("G1", [128, D], F32)
        G3 = P.sb("G3", [128, D], F32)
        R_G = Res("G")
        hs = P.sb("hs_t", [128, D], F32)
        mg = P.sb("mg_t", [128, D], F32)
        xn = P.sb("xn_t", [128, D], BF16)
        xnT = P.sb("xnT_t", [128, 8, 128], BF16)
        h1T = P.sb("h1T_t", [128, 32, 128], BF16)
        rl = P.sb("rl_t", [128, 512], BF16)
        tmp = P.sb("tmp_t", [128, 512], F32)
        st = P.sb("stat", [128, 8], F32)
        R_hs, R_mg, R_xn, R_xnT, R_h1T, R_rl, R_tmp = (Res(n) for n in ("hs", "mg", "xn", "xnT", "h1T", "rl", "tmp"))
        R_st = [Res(f"st{i}") for i in range(8)]
        pT = P.ps("pT", [128, 8, 128], BF16)
        pA = P.ps("pA", [128, 512], F32)
        pB = P.ps("pB", [128, 512], F32)
        pH = [P.ps("pH0", [128, 4, 128], F32), P.ps("pH1", [128, 4, 128], F32)]
        R_pT, R_pA, R_pB = Res("pT", True), Res("pA", True), Res("pB", True)
        R_pH = [Res("pH0", True), Res("pH1", True)]
        junk = stage[0]
        R_junk = R_stage[0]

        P.dma("sp", G1[:, :], g1_d.partition_broadcast(128), w=[R_G])
        P.dma("sp", G3[:, :], g3_d.partition_broadcast(128), w=[R_G])
        P.dma("sp", gsm[:, 0, :], g2_d.rearrange("(k p) -> p k", p=128), w=[R_gsm], allow_slow_non_contiguous=True)
        if even:
            P.dma("sp", gsm[:, 1, :], gmerge_d.rearrange("(k p) -> p k", p=128), w=[R_gsm], allow_slow_non_contiguous=True)
            wglub = P.sb("wglub", [128, 4, 512], BF16)
            bglub = P.sb("bglub", [1, 512], BF16)
            R_wglu, R_bglu = Res("wglu"), Res("bglu")
            P.dma("sp", tmp[0:1, :], bglu_d.rearrange("(o n) -> o n", o=1), w=[R_tmp])
            P.op("dve", lambda e: e.tensor_copy(out=bglub[:, :], in_=tmp[0:1, :]), r=[R_tmp], w=[R_bglu])
            C.load_weight(wglu_d, wglub, R_wglu, stage, R_stage, 4, 512)
            C.load_weight(wout_d, woutb, R_wout, stage, R_stage, 8, D, gsc=gsm[:, 1, :], R_g=R_gsm)
        else:
            C.load_weight(wout_d, woutb, R_wout, stage, R_stage, 8, D)
        C.load_weight(w1_d, w1b, R_w1, stage, R_stage, 8, DFF, gsc=gsm[:, 0, :], R_g=R_gsm)
        C.load_weight(w2_d, w2b, R_w2, stage, R_stage, 32, D)

        def transpose8(nblk):
            for k in range(nblk):
                P.op("pe", lambda e, k=k: e.transpose(out=pT[:, k, :], in_=xn[:, k * 128:(k + 1) * 128], identity=C.ident[:, :]),
                     r=[R_xn, C.R_ident], w=[R_pT])
            P.op("dve", lambda e: e.tensor_copy(out=xnT[:, 0:nblk, :], in_=pT[:, 0:nblk, :]), r=[R_pT], w=[R_xnT])

        def norm_residual(G):
            P.op("act", lambda e: e.activation(out=junk[:, 0:512], in_=pA[:, :], func=AF.Square, accum_out=st[:, 0:1]),
                 r=[R_pA], w=[R_junk, R_st[0]])
            P.op("act", lambda e: e.activation(out=junk[:, 512:1024], in_=pB[:, :], func=AF.Square, accum_out=st[:, 1:2]),
                 r=[R_pB], w=[R_junk, R_st[1]])
            P.op("dve", lambda e: e.tensor_tensor(out=st[:, 0:1], in0=st[:, 0:1], in1=st[:, 1:2], op=ALU.add),
                 r=[R_st[0], R_st[1]], w=[R_st[0]])
            C.rstd(st[:, 0:1], st[:, 0:1], R_st[0], R_st[0], D)
            for h, (pp, Rp) in enumerate(((pA, R_pA), (pB, R_pB))):
                sl = slice(h * 512, (h + 1) * 512)
                P.op("dve", lambda e, pp=pp, sl=sl: e.scalar_tensor_tensor(out=tmp[:, :], in0=pp[:, :], scalar=st[:, 0:1], in1=G[:, sl],
                                                                          op0=ALU.mult, op1=ALU.mult),
                     r=[Rp, R_st[0], R_G], w=[R_tmp])
                P.op("dve", lambda e, sl=sl: e.tensor_tensor(out=hs[:, sl], in0=hs[:, sl], in1=tmp[:, :], op=ALU.add),
                     r=[R_hs, R_tmp], w=[R_hs])

        for t in range(TPC):
            rsl = slice(t * 128, (t + 1) * 128)
            P.dma("sp", hs[:, :], hs_d[rsl, :], w=[R_hs])
            if even:
                P.dma("sp", mg[:, 0:512], osb_d[rsl, :], w=[R_mg])
                P.dma("sp", mg[:, 512:1024], ys5_d[rsl, :], w=[R_mg])
                y = mg[:, 512:1024]
                P.op("act", lambda e: e.activation(out=tmp[:, :], in_=y, func=AF.Square), r=[R_mg], w=[R_tmp])
                P.op("dve", lambda e: e.tensor_scalar(out=tmp[:, :], in0=tmp[:, :], scalar1=0.044715, scalar2=1.0, op0=ALU.mult, op1=ALU.add),
                     r=[R_tmp], w=[R_tmp])
                P.op("dve", lambda e: e.tensor_tensor(out=tmp[:, :], in0=tmp[:, :], in1=y, op=ALU.mult), r=[R_tmp, R_mg], w=[R_tmp])
                P.op("act", lambda e: e.activation(out=tmp[:, :], in_=tmp[:, :], func=AF.Sigmoid, scale=1.5957691216057308), r=[R_tmp], w=[R_tmp])
                P.op("dve", lambda e: e.tensor_tensor(out=y, in0=tmp[:, :], in1=y, op=ALU.mult), r=[R_tmp, R_mg], w=[R_mg])
                P.op("act", lambda e: e.activation(out=xn[:, 0:512], in_=y, func=AF.Copy), r=[R_mg], w=[R_xn])
                transpose8(4)
                for k in range(4):
                    P.op("pe", lambda e, k=k: e.matmul(out=pA[:, :], lhsT=xnT[:, k, :], rhs=wglub[:, k, :], start=(k == 0), stop=False),
                         r=[R_xnT, R_wglu], w=[R_pA])
                P.op("pe", lambda e: e.matmul(out=pA[:, :], lhsT=C.ones_bf[0:1, :], rhs=bglub[0:1, :], start=False, stop=True),
                     r=[C.R_ones, R_bglu], w=[R_pA])
                P.op("act", lambda e: e.activation(out=tmp[:, :], in_=pA[:, :], func=AF.Sigmoid), r=[R_pA], w=[R_tmp])
                P.op("dve", lambda e: e.tensor_tensor(out=y, in0=tmp[:, :], in1=y, op=ALU.mult), r=[R_tmp, R_mg], w=[R_mg])
                for h in range(2):
                    sl = slice(h * 512, (h + 1) * 512)
                    P.op("act", lambda e, sl=sl, h=h: e.activation(out=junk[:, sl], in_=mg[:, sl], func=AF.Square, accum_out=st[:, 2 + h:3 + h]),
                         r=[R_mg], w=[R_junk, R_st[2 + h]])
                    C.rstd(st[:, 2 + h:3 + h], st[:, 2 + h:3 + h], R_st[2 + h], R_st[2 + h], 512)
                    P.op("act", lambda e, sl=sl, h=h: e.activation(out=xn[:, sl], in_=mg[:, sl], func=AF.Copy, scale=st[:, 2 + h:3 + h]),
                         r=[R_mg, R_st[2 + h]], w=[R_xn])
            else:
                P.dma("sp", mg[:, :], og_d[rsl, :], w=[R_mg])
                P.op("act", lambda e: e.activation(out=xn[:, :], in_=mg[:, :], func=AF.Copy), r=[R_mg], w=[R_xn])
            transpose8(8)
            for k in range(8):
                P.op("pe", lambda e, k=k: e.matmul(out=pA[:, :], lhsT=xnT[:, k, :], rhs=woutb[:, k, 0:512], start=(k == 0), stop=(k == 7)),
                     r=[R_xnT, R_wout], w=[R_pA])
            for k in range(8):
                P.op("pe", lambda e, k=k: e.matmul(out=pB[:, :], lhsT=xnT[:, k, :], rhs=woutb[:, k, 512:1024], start=(k == 0), stop=(k == 7)),
                     r=[R_xnT, R_wout], w=[R_pB])
            norm_residual(G1)
            P.op("act", lambda e: e.activation(out=junk[:, :], in_=hs[:, :], func=AF.Square, accum_out=st[:, 4:5]), r=[R_hs], w=[R_junk, R_st[4]])
            C.rstd(st[:, 4:5], st[:, 4:5], R_st[4], R_st[4], D)
            P.op("act", lambda e: e.activation(out=xn[:, :], in_=hs[:, :], func=AF.Copy, scale=st[:, 4:5]), r=[R_hs, R_st[4]], w=[R_xn])
            transpose8(8)
            for fg in range(8):
                ph, Rph = pH[fg % 2], R_pH[fg % 2]
                for j in range(4):
                    fb = fg * 4 + j
                    for k in range(8):
                        P.op("pe", lambda e, ph=ph, j=j, fb=fb, k=k: e.matmul(out=ph[:, j, :], lhsT=w1b[:, k, fb * 128:(fb + 1) * 128], rhs=xnT[:, k, :],
                                                                               start=(k == 0), stop=(k == 7)),
                             r=[R_w1, R_xnT], w=[Rph])
                P.op("act", lambda e, ph=ph: e.activation(out=rl[:, :], in_=ph[:, :, :].rearrange("p a b -> p (a b)"), func=AF.Relu), r=[Rph], w=[R_rl])
                P.op("dve", lambda e, fg=fg: e.tensor_tensor(out=h1T[:, fg * 4:(fg + 1) * 4, :].rearrange("p a b -> p (a b)"), in0=rl[:, :], in1=rl[:, :], op=ALU.mult),
                     r=[R_rl], w=[R_h1T])
            for fb in range(32):
                P.op("pe", lambda e, fb=fb: e.matmul(out=pA[:, :], lhsT=h1T[:, fb, :], rhs=w2b[:, fb, 0:512], start=(fb == 0), stop=(fb == 31)),
                     r=[R_h1T, R_w2], w=[R_pA])
            for fb in range(32):
                P.op("pe", lambda e, fb=fb: e.matmul(out=pB[:, :], lhsT=h1T[:, fb, :], rhs=w2b[:, fb, 512:1024], start=(fb == 0), stop=(fb == 31)),
                     r=[R_h1T, R_w2], w=[R_pB])
            norm_residual(G3)
            P.dma("pool", out_d[rsl, :], hs[:, :], r=[R_hs])
        P.finish()
    return nc


def tok_shard(a_pad, c):
    return np.ascontiguousarray(np.concatenate([a_pad[0:128], a_pad[128 * (1 + 16 * c):128 * (17 + 16 * c)]], axis=0))


def tok_unshard(outs):
    parts = [outs[0][0:128]] + [o[128:] for o in outs]
    return np.concatenate(parts, axis=0)


_CACHE = {}


def get_prog(name, builder, *args):
    if name not in _CACHE:
        _CACHE[name] = builder(*args)
    return _CACHE[name]


def run_post(kind, hs_pad, mix_in, W):
    nc = get_prog("post_" + kind, build_post, kind)
    in_maps = []
    for c in range(NCORES):
        m = {"hs": tok_shard(hs_pad, c)}
        if kind == "even":
            m["osb"] = tok_shard(mix_in[0], c)
            m["ys5"] = tok_shard(mix_in[1], c)
        else:
            m["og"] = tok_shard(mix_in, c)
        m.update(W)
        in_maps.append(m)
    res = run_bass_kernel_spmd(nc, in_maps, core_ids=list(range(NCORES)))
    return tok_unshard([r["out"] for r in res.results])


NG = 33


def grp_cols(gi):
    return (gi * 512, 512 if gi < 32 else 128)


def build_mix_even():
    nc = bass.Bass("TRN2", target_bir_lowering=False)
    dr = lambda n, s, k="ExternalInput": nc.dram_tensor(n, list(s), F32, kind=k).ap()
    hs_d = dr("hs", [LP, D])
    wh_d = dr("wh", [D, 384])
    g_d = dr("g_pre", [D])
    lre_d = dr("lam_re", [4, 64])
    lim_d = dr("lam_im", [4, 64])
    ldt_d = dr("log_dt", [4])
    bre_d = dr("b_re", [4, 64, 16])
    bim_d = dr("b_im", [4, 64, 16])
    cre_d = dr("c_re", [4, 16, 64])
    cim_d = dr("c_im", [4, 16, 64])
    dsk_d = dr("d_skip", [64])
    osb_d = dr("osbT", [64, LP], "ExternalOutput")
    ys_d = dr("ysT", [64, LP], "ExternalOutput")

    with ExitStack() as es:
        C = Ctx(nc, es)
        P = C.P
        QQ = P.sb("QQ", [128, LP], BF16)
        KZ = P.sb("KZ", [128, LP], BF16)
        KN = P.sb("KN", [128, LP], BF16)
        R_kinit = Res("kinit")
        P.op("pool", lambda e: e.memset(KZ[64:128, :], 0.0), w=[R_kinit])
        P.op("pool", lambda e: e.memset(KN[64:128, :], 0.0), w=[R_kinit])
        vA = P.sb("vA", [128, NT, 128], BF16)
        P.op("pool", lambda e: e.memset(vA[:, :, :].rearrange("p a b -> p (a b)"), 0.0), w=[R_kinit])
        R_qT = [Res(f"qT{g}") for g in range(NG)]
        R_kT = [Res(f"kT{g}") for g in range(NG)]
        R_v = [Res(f"v{g}") for g in range(NG)]
        esA = ExitStack()
        P.es = esA
        whb = P.sb("whb", [128, 8, 384], BF16)
        R_wh = Res("wh")
        gsm = P.sb("gsm", [128, 8], F32)
        R_gsm = Res("gsm")
        stage = [P.sb("stage0", [128, 1024], F32)]
        R_stage = [Res("st0")]
        banks = [P.ps(f"bk{i}", [128, 512], F32) for i in range(7)]
        R_bk = [Res(f"bk{i}", True) for i in range(7)]
        pT = P.ps("pT", [128, 8, 128], BF16)
        R_pT = Res("pT", True)

        P.dma("sp", gsm[:, :], g_d.rearrange("(k p) -> p k", p=128), w=[R_gsm], allow_slow_non_contiguous=True)
        C.load_weight(wh_d, whb, R_wh, stage, R_stage, 8, 384, gsc=gsm, R_g=R_gsm)

        sp_ = P.sb("s5p", [128, 2, 24], F32)
        R_sp = Res("s5p")
        LR, LI, DT, AR, AI, IR, II, T0, T1, T2, T3, CR, CI, NLR = range(14)
        spi = P.sb("s5pi", [128, 2], mybir.dt.int32)
        col = lambda c: sp_[:, :, c]
        for sb in range(2):
            P.dma("sp", sp_[:, sb, LR:LR + 1], lre_d[2 * sb:2 * sb + 2, :].rearrange("g (n o) -> (g n) o", o=1), w=[R_sp])
            P.dma("sp", sp_[:, sb, LI:LI + 1], lim_d[2 * sb:2 * sb + 2, :].rearrange("g (n o) -> (g n) o", o=1), w=[R_sp])
            for gl in range(2):
                P.dma("sp", sp_[gl * 64:(gl + 1) * 64, sb, DT:DT + 1],
                      ldt_d[2 * sb + gl:2 * sb + gl + 1].rearrange("(o n) -> o n", o=1).partition_broadcast(64), w=[R_sp])
        so = lambda eng, fn: P.op(eng, fn, r=[R_sp], w=[R_sp])
        so("act", lambda e: e.activation(out=col(DT), in_=col(DT), func=AF.Exp))
        so("dve", lambda e: e.tensor_scalar(out=col(LR), in0=col(LR), scalar1=-1e-4, scalar2=None, op0=ALU.min))
        so("dve", lambda e: e.tensor_tensor(out=col(T0), in0=col(LR), in1=col(DT), op=ALU.mult))
        so("dve", lambda e: e.tensor_scalar(out=col(NLR), in0=col(T0), scalar1=-1.0, scalar2=None, op0=ALU.mult))
        so("dve", lambda e: e.tensor_tensor(out=col(T1), in0=col(LI), in1=col(DT), op=ALU.mult))
        so("dve", lambda e: e.tensor_scalar(out=col(T2), in0=col(T1), scalar1=1.0 / (2 * np.pi), scalar2=None, op0=ALU.mult))
        P.op("dve", lambda e: e.tensor_copy(out=spi[:, :], in_=col(T2)), r=[R_sp], w=[R_sp])
        P.op("dve", lambda e: e.tensor_copy(out=col(T2), in_=spi[:, :]), r=[R_sp], w=[R_sp])
        so("dve", lambda e: e.scalar_tensor_tensor(out=col(T1), in0=col(T2), scalar=-2 * np.pi, in1=col(T1), op0=ALU.mult, op1=ALU.add))
        so("dve", lambda e: e.tensor_scalar(out=col(T1), in0=col(T1), scalar1=0.5, scalar2=None, op0=ALU.mult))
        so("dve", lambda e: e.tensor_scalar(out=col(T2), in0=col(T1), scalar1=np.pi / 2, scalar2=None, op0=ALU.add))
        so("act", lambda e: e.activation(out=col(T1), in_=col(T1), func=AF.Sin))
        so("act", lambda e: e.activation(out=col(T2), in_=col(T2), func=AF.Sin))
        so("act", lambda e: e.activation(out=col(T3), in_=col(T0), func=AF.Exp))
        so("dve", lambda e: e.tensor_tensor(out=col(AI), in0=col(T1), in1=col(T2), op=ALU.mult))
        so("dve", lambda e: e.tensor_scalar(out=col(AI), in0=col(AI), scalar1=2.0, scalar2=None, op0=ALU.mult))
        so("dve", lambda e: e.tensor_tensor(out=col(AR), in0=col(T1), in1=col(T1), op=ALU.mult))
        so("dve", lambda e: e.tensor_scalar(out=col(AR), in0=col(AR), scalar1=-2.0, scalar2=1.0, op0=ALU.mult, op1=ALU.add))
        so("act", lambda e: e.activation(out=col(T0), in_=col(NLR), func=AF.Exp))
        so("dve", lambda e: e.tensor_tensor(out=col(IR), in0=col(AR), in1=col(T0), op=ALU.mult))
        so("dve", lambda e: e.tensor_tensor(out=col(II), in0=col(AI), in1=col(T0), op=ALU.mult))
        so("dve", lambda e: e.tensor_scalar(out=col(II), in0=col(II), scalar1=-1.0, scalar2=None, op0=ALU.mult))
        so("dve", lambda e: e.tensor_tensor(out=col(AR), in0=col(AR), in1=col(T3), op=ALU.mult))
        so("dve", lambda e: e.tensor_tensor(out=col(AI), in0=col(AI), in1=col(T3), op=ALU.mult))
        so("dve", lambda e: e.tensor_tensor(out=col(T0), in0=col(LR), in1=col(LR), op=ALU.mult))
        so("dve", lambda e: e.tensor_tensor(out=col(T1), in0=col(LI), in1=col(LI), op=ALU.mult))
        so("dve", lambda e: e.tensor_tensor(out=col(T0), in0=col(T0), in1=col(T1), op=ALU.add))
        so("dve", lambda e: e.reciprocal(out=col(T0), in_=col(T0)))
        so("dve", lambda e: e.tensor_scalar(out=col(T1), in0=col(AR), scalar1=-1.0, scalar2=None, op0=ALU.add))
        so("dve", lambda e: e.tensor_tensor(out=col(T2), in0=col(T1), in1=col(LR), op=ALU.mult))
        so("dve", lambda e: e.tensor_tensor(out=col(T3), in0=col(AI), in1=col(LI), op=ALU.mult))
        so("dve", lambda e: e.tensor_tensor(out=col(T2), in0=col(T2), in1=col(T3), op=ALU.add))
        so("dve", lambda e: e.tensor_tensor(out=col(CR), in0=col(T2), in1=col(T0), op=ALU.mult))
        so("dve", lambda e: e.tensor_tensor(out=col(T2), in0=col(AI), in1=col(LR), op=ALU.mult))
        so("dve", lambda e: e.tensor_tensor(out=col(T3), in0=col(T1), in1=col(LI), op=ALU.mult))
        so("dve", lambda e: e.tensor_tensor(out=col(T2), in0=col(T2), in1=col(T3), op=ALU.subtract))
        so("dve", lambda e: e.tensor_tensor(out=col(CI), in0=col(T2), in1=col(T0), op=ALU.mult))

        TB = {n: P.sb("tb_" + n, [128, 2, 512], F32) for n in ("pr", "pi", "ir", "ii")}
        R_tb = Res("tb")
        ptmp = P.sb("ptmp", [128, 256], F32)
        R_ptmp = Res("ptmp")
        pw = P.sb("pw", [128, 2, 2, 2], F32)
        R_pw = Res("pw")
        pw2 = P.sb("pw2", [128, 4], F32)
        for sb in range(2):
            for wi, (tr, ti, cr_, ci_) in enumerate((("pr", "pi", AR, AI), ("ir", "ii", IR, II))):
                P.op("pool", lambda e, tr=tr, sb=sb: e.memset(TB[tr][:, sb, 0:1], 1.0), w=[R_tb])
                P.op("pool", lambda e, ti=ti, sb=sb: e.memset(TB[ti][:, sb, 0:1], 0.0), w=[R_tb])
                P.op("dve", lambda e, sb=sb, wi=wi, cr_=cr_: e.tensor_copy(out=pw[:, sb, wi, 0:1], in_=sp_[:, sb, cr_:cr_ + 1]), r=[R_sp], w=[R_pw])
                P.op("dve", lambda e, sb=sb, wi=wi, ci_=ci_: e.tensor_copy(out=pw[:, sb, wi, 1:2], in_=sp_[:, sb, ci_:ci_ + 1]), r=[R_sp], w=[R_pw])
                n = 1
                while n < 512:
                    cr = pw[:, sb, wi, 0:1]
                    ci = pw[:, sb, wi, 1:2]
                    src_r = TB[tr][:, sb, 0:n]
                    src_i = TB[ti][:, sb, 0:n]
                    dst_r = TB[tr][:, sb, n:2 * n]
                    dst_i = TB[ti][:, sb, n:2 * n]
                    tm = ptmp[:, 0:n]
                    P.op("dve", lambda e, tm=tm, src_i=src_i, ci=ci: e.tensor_scalar(out=tm, in0=src_i, scalar1=ci, scalar2=None, op0=ALU.mult),
                         r=[R_tb, R_pw], w=[R_ptmp])
                    P.op("dve", lambda e, dst_r=dst_r, src_r=src_r, cr=cr, tm=tm: e.scalar_tensor_tensor(out=dst_r, in0=src_r, scalar=cr, in1=tm, op0=ALU.mult, op1=ALU.subtract),
                         r=[R_tb, R_pw, R_ptmp], w=[R_tb])
                    P.op("dve", lambda e, tm=tm, src_i=src_i, cr=cr: e.tensor_scalar(out=tm, in0=src_i, scalar1=cr, scalar2=None, op0=ALU.mult),
                         r=[R_tb, R_pw], w=[R_ptmp])
                    P.op("dve", lambda e, dst_i=dst_i, src_r=src_r, ci=ci, tm=tm: e.scalar_tensor_tensor(out=dst_i, in0=src_r, scalar=ci, in1=tm, op0=ALU.mult, op1=ALU.add),
                         r=[R_tb, R_pw, R_ptmp], w=[R_tb])
                    n *= 2
                    if n < 512:
                        P.op("dve", lambda e, cr=cr, ci=ci: e.tensor_tensor(out=pw2[:, 0:1], in0=cr, in1=cr, op=ALU.mult), r=[R_pw], w=[R_ptmp])
                        P.op("dve", lambda e, cr=cr, ci=ci: e.tensor_tensor(out=pw2[:, 1:2], in0=ci, in1=ci, op=ALU.mult), r=[R_pw], w=[R_ptmp])
                        P.op("dve", lambda e, cr=cr, ci=ci: e.tensor_tensor(out=pw2[:, 2:3], in0=cr, in1=ci, op=ALU.mult), r=[R_pw], w=[R_ptmp])
                        P.op("dve", lambda e, cr=cr: e.tensor_tensor(out=cr, in0=pw2[:, 0:1], in1=pw2[:, 1:2], op=ALU.subtract), r=[R_ptmp], w=[R_pw])
                        P.op("dve", lambda e, ci=ci: e.tensor_scalar(out=ci, in0=pw2[:, 2:3], scalar1=2.0, scalar2=None, op0=ALU.mult), r=[R_ptmp], w=[R_pw])

        braw = P.sb("braw", [128, 2, 2, 16], F32)
        R_braw = Res("braw")
        Bfull = P.sb("Bfull", [128, 2, 2, 64], F32)
        R_Bfull = Res("Bfull")
        BbT = P.sb("BbT", [64, 2, 2, 128], F32)
        R_BbT = Res("BbT")
        CT = P.sb("CT", [128, 2, 2, 64], F32)
        R_CT = Res("CT")
        identf = P.sb("identf", [128, 128], F32)
        R_if = Res("identf")
        P.op("dve", lambda e: e.tensor_copy(out=identf[:, :], in_=C.ident[:, :]), r=[C.R_ident], w=[R_if])
        P.op("pool", lambda e: e.memset(Bfull[:, :, :, :].rearrange("p a b c -> p (a b c)"), 0.0), w=[R_Bfull])
        P.op("pool", lambda e: e.memset(CT[:, :, :, :].rearrange("p a b c -> p (a b c)"), 0.0), w=[R_CT])
        for sb in range(2):
            P.dma("sp", braw[:, sb, 0, :], bre_d[2 * sb:2 * sb + 2].rearrange("g n p -> (g n) p"), w=[R_braw])
            P.dma("sp", braw[:, sb, 1, :], bim_d[2 * sb:2 * sb + 2].rearrange("g n p -> (g n) p"), w=[R_braw])
            for gl in range(2):
                g = 2 * sb + gl
                ps_ = slice(gl * 64, (gl + 1) * 64)
                cs = slice(g * 16, (g + 1) * 16)
                P.dma("sp", CT[ps_, sb, 0, cs], cre_d[g].rearrange("p n -> n p"), w=[R_CT], allow_slow_non_contiguous=True)
                P.dma("sp", CT[ps_, sb, 1, cs], cim_d[g].rearrange("p n -> n p"), w=[R_CT], allow_slow_non_contiguous=True)
                crr = sp_[ps_, sb, CR:CR + 1]
                cii = sp_[ps_, sb, CI:CI + 1]
                tm = ptmp[ps_, 0:16]
                P.op("dve", lambda e, tm=tm, sb=sb, ps_=ps_, cii=cii: e.tensor_scalar(out=tm, in0=braw[ps_, sb, 1, :], scalar1=cii, scalar2=None, op0=ALU.mult),
                     r=[R_braw, R_sp], w=[R_ptmp])
                P.op("dve", lambda e, tm=tm, sb=sb, ps_=ps_, cs=cs, crr=crr: e.scalar_tensor_tensor(out=Bfull[ps_, sb, 0, cs], in0=braw[ps_, sb, 0, :], scalar=crr, in1=tm,
                                                                                              op0=ALU.mult, op1=ALU.subtract),
                     r=[R_braw, R_sp, R_ptmp], w=[R_Bfull])
                P.op("dve", lambda e, tm=tm, sb=sb, ps_=ps_, cii=cii: e.tensor_scalar(out=tm, in0=braw[ps_, sb, 0, :], scalar1=cii, scalar2=None, op0=ALU.mult),
                     r=[R_braw, R_sp], w=[R_ptmp])
                P.op("dve", lambda e, tm=tm, sb=sb, ps_=ps_, cs=cs, crr=crr: e.scalar_tensor_tensor(out=Bfull[ps_, sb, 1, cs], in0=braw[ps_, sb, 1, :], scalar=crr, in1=tm,
                                                                                              op0=ALU.mult, op1=ALU.add),
                     r=[R_braw, R_sp, R_ptmp], w=[R_Bfull])
            P.op("dve", lambda e, sb=sb: e.tensor_scalar(out=CT[:, sb, 1, :], in0=CT[:, sb, 1, :], scalar1=-1.0, scalar2=None, op0=ALU.mult), r=[R_CT], w=[R_CT])
            for ri in range(2):
                P.op("pe", lambda e, sb=sb, ri=ri: e.transpose(out=banks[0][0:64, 0:128], in_=Bfull[:, sb, ri, :], identity=identf[:, :]),
                     r=[R_Bfull, R_if], w=[R_bk[0]])
                P.op("dve", lambda e, sb=sb, ri=ri: e.tensor_copy(out=BbT[:, sb, ri, :], in_=banks[0][0:64, 0:128]), r=[R_bk[0]], w=[R_BbT])
        dsk = P.sb("dsk", [64, 1], F32)
        R_dsk = Res("dsk")
        P.dma("sp", dsk[:, :], dsk_d.rearrange("(n o) -> n o", o=1), w=[R_dsk])
        onesf = P.sb("onesf", [128, 512], F32)
        R_onesf = Res("onesf")
        P.op("pool", lambda e: e.memset(onesf[:, :], 1.0), w=[R_onesf])
        carry = P.sb("carry", [128, 2, 2], F32)
        R_carry = Res("carry")
        P.op("pool", lambda e: e.memset(carry[:, :, :].rearrange("p a b -> p (a b)"), 0.0), w=[R_carry])

        hsb = [P.sb("hsb0", [128, D], F32)] * 2
        R_hsb = [Res("hsb0")] * 2
        xn = P.sb("xn", [128, D], BF16)
        R_xn = Res("xn")
        xnT = P.sb("xnT", [128, 8, 512], BF16)
        R_xnT = Res("xnT")
        stt = P.sb("stt", [128, 2], F32)
        R_stt = [Res("stt0"), Res("stt1")]
        uT = P.sb("uT", [64, 512], F32)
        R_uT = Res("uT")
        S5 = {n: P.sb("s5_" + n, [128, 512], F32) for n in ("sre", "sim", "cre", "cim", "t1", "t2", "wre", "wim")}
        R_S5 = {n: Res("s5_" + n) for n in S5}
        S5["xre"], S5["xim"] = S5["sre"], S5["sim"]
        R_S5["xre"], R_S5["xim"] = R_S5["sre"], R_S5["sim"]
        ysb = P.sb("ysb", [64, 512], F32)
        R_ysb = Res("ysb")
        junk = stage[0]
        R_junk = R_stage[0]
        pV, pQ, pK, pU, pBr, pBi, pY = banks
        R_pV, R_pQ, R_pK, R_pU, R_pBr, R_pBi, R_pY = R_bk

        tile_i = 0
        for gi in range(NG):
            c0, ncol = grp_cols(gi)
            ntl = ncol // 128
            for tl in range(ntl):
                t = gi * 4 + tl
                hb, Rh = hsb[tile_i % 2], R_hsb[tile_i % 2]
                tile_i += 1
                P.dma("sp", hb[:, :], hs_d[t * 128:(t + 1) * 128, :], w=[Rh])
                P.op("act", lambda e, hb=hb: e.activation(out=junk[:, :], in_=hb[:, :], func=AF.Square, accum_out=stt[:, 0:1]), r=[Rh], w=[R_junk, R_stt[0]])
                C.rstd(stt[:, 0:1], stt[:, 0:1], R_stt[0], R_stt[0], D)
                P.op("act", lambda e, hb=hb: e.activation(out=xn[:, :], in_=hb[:, :], func=AF.Copy, scale=stt[:, 0:1]), r=[Rh, R_stt[0]], w=[R_xn])
                for k in range(8):
                    P.op("pe", lambda e, k=k: e.transpose(out=pT[:, k, :], in_=xn[:, k * 128:(k + 1) * 128], identity=C.ident[:, :]),
                         r=[R_xn, C.R_ident], w=[R_pT])
                P.op("dve", lambda e, tl=tl: e.tensor_copy(out=xnT[:, :, tl * 128:(tl + 1) * 128], in_=pT[:, :, :]), r=[R_pT], w=[R_xnT])
                for k in range(8):
                    P.op("pe", lambda e, k=k, tl=tl: e.matmul(out=pV[:, 0:64], lhsT=xnT[:, k, tl * 128:(tl + 1) * 128], rhs=whb[:, k, 256:320],
                                                               start=(k == 0), stop=(k == 7)), r=[R_xnT, R_wh], w=[R_pV])
                P.op("act", lambda e, t=t: e.activation(out=vA[:, t, 0:64], in_=pV[:, 0:64], func=AF.Copy), r=[R_pV, R_kinit], w=[R_v[gi]])
            cs = slice(c0, c0 + ncol)
            for (pp, Rp, wc, wn) in ((pQ, R_pQ, 0, 128), (pK, R_pK, 128, 128), (pU, R_pU, 320, 64)):
                for k in range(8):
                    P.op("pe", lambda e, pp=pp, k=k, wc=wc, wn=wn: e.matmul(out=pp[0:wn, 0:ncol], lhsT=whb[:, k, wc:wc + wn], rhs=xnT[:, k, 0:ncol],
                                                                             start=(k == 0), stop=(k == 7)), r=[R_xnT, R_wh], w=[Rp])
            P.op("act", lambda e, cs=cs: e.activation(out=QQ[:, cs], in_=pQ[:, 0:ncol], func=AF.Copy, scale=0.125), r=[R_pQ], w=[R_qT[gi]])
            P.op("dve", lambda e, cs=cs: e.tensor_copy(out=KZ[0:64, cs], in_=pK[0:64, 0:ncol]), r=[R_pK, R_kinit], w=[R_kT[gi]])
            P.op("dve", lambda e, cs=cs: e.tensor_scalar(out=KN[0:64, cs], in0=pK[0:64, 0:ncol], scalar1=-1.0, scalar2=None, op0=ALU.mult), r=[R_pK, R_kinit], w=[R_kT[gi]])
            P.op("act", lambda e: e.activation(out=uT[:, 0:ncol], in_=pU[0:64, 0:ncol], func=AF.Copy), r=[R_pU], w=[R_uT])
            for sb in range(2):
                nn = ncol
                P.op("pe", lambda e, sb=sb: e.matmul(out=pBr[:, 0:nn], lhsT=BbT[:, sb, 0, :], rhs=uT[:, 0:nn], start=True, stop=True), r=[R_BbT, R_uT], w=[R_pBr])
                P.op("pe", lambda e, sb=sb: e.matmul(out=pBi[:, 0:nn], lhsT=BbT[:, sb, 1, :], rhs=uT[:, 0:nn], start=True, stop=True), r=[R_BbT, R_uT], w=[R_pBi])
                A = lambda n: S5[n][:, 0:nn]
                P.op("act", lambda e: e.activation(out=A("sre"), in_=pBr[:, 0:nn], func=AF.Copy), r=[R_pBr], w=[R_S5["sre"]])
                P.op("act", lambda e: e.activation(out=A("sim"), in_=pBi[:, 0:nn], func=AF.Copy), r=[R_pBi], w=[R_S5["sim"]])
                ir_, ii_ = TB["ir"][:, sb, 0:nn], TB["ii"][:, sb, 0:nn]
                pr_, pi_ = TB["pr"][:, sb, 0:nn], TB["pi"][:, sb, 0:nn]
                P.op("dve", lambda e: e.tensor_tensor(out=A("t1"), in0=A("sim"), in1=ii_, op=ALU.mult), r=[R_S5["sim"], R_tb], w=[R_S5["t1"]])
                P.op("dve", lambda e: e.tensor_tensor(out=A("cre"), in0=A("sre"), in1=ir_, op=ALU.mult), r=[R_S5["sre"], R_tb], w=[R_S5["cre"]])
                P.op("dve", lambda e: e.tensor_tensor(out=A("cre"), in0=A("cre"), in1=A("t1"), op=ALU.subtract), r=[R_S5["cre"], R_S5["t1"]], w=[R_S5["cre"]])
                P.op("pool", lambda e: e.tensor_tensor(out=A("t2"), in0=A("sim"), in1=ir_, op=ALU.mult), r=[R_S5["sim"], R_tb], w=[R_S5["t2"]])
                P.op("pool", lambda e: e.tensor_tensor(out=A("cim"), in0=A("sre"), in1=ii_, op=ALU.mult), r=[R_S5["sre"], R_tb], w=[R_S5["cim"]])
                P.op("pool", lambda e: e.tensor_tensor(out=A("cim"), in0=A("cim"), in1=A("t2"), op=ALU.add), r=[R_S5["cim"], R_S5["t2"]], w=[R_S5["cim"]])
                P.op("dve", lambda e, sb=sb: e.tensor_tensor_scan(out=A("wre"), data0=onesf[:, 0:nn], data1=A("cre"), initial=carry[:, sb, 0:1], op0=ALU.mult, op1=ALU.add),
                     r=[R_onesf, R_S5["cre"], R_carry], w=[R_S5["wre"]])
                P.op("dve", lambda e, sb=sb: e.tensor_tensor_scan(out=A("wim"), data0=onesf[:, 0:nn], data1=A("cim"), initial=carry[:, sb, 1:2], op0=ALU.mult, op1=ALU.add),
                     r=[R_onesf, R_S5["cim"], R_carry], w=[R_S5["wim"]])
                P.op("dve", lambda e: e.tensor_tensor(out=A("t1"), in0=A("wim"), in1=pi_, op=ALU.mult), r=[R_S5["wim"], R_tb], w=[R_S5["t1"]])
                P.op("dve", lambda e: e.tensor_tensor(out=A("xre"), in0=A("wre"), in1=pr_, op=ALU.mult), r=[R_S5["wre"], R_tb], w=[R_S5["xre"]])
                P.op("dve", lambda e: e.tensor_tensor(out=A("xre"), in0=A("xre"), in1=A("t1"), op=ALU.subtract), r=[R_S5["xre"], R_S5["t1"]], w=[R_S5["xre"]])
                P.op("pool", lambda e: e.tensor_tensor(out=A("t2"), in0=A("wim"), in1=pr_, op=ALU.mult), r=[R_S5["wim"], R_tb], w=[R_S5["t2"]])
                P.op("pool", lambda e: e.tensor_tensor(out=A("xim"), in0=A("wre"), in1=pi_, op=ALU.mult), r=[R_S5["wre"], R_tb], w=[R_S5["xim"]])
                P.op("pool", lambda e: e.tensor_tensor(out=A("xim"), in0=A("xim"), in1=A("t2"), op=ALU.add), r=[R_S5["xim"], R_S5["t2"]], w=[R_S5["xim"]])
                xer, xei = S5["xre"][:, nn - 1:nn], S5["xim"][:, nn - 1:nn]
                ar_, ai_ = sp_[:, sb, AR:AR + 1], sp_[:, sb, AI:AI + 1]
                P.op("dve", lambda e: e.tensor_tensor(out=pw2[:, 0:1], in0=xei, in1=ai_, op=ALU.mult), r=[R_S5["xim"], R_sp], w=[R_ptmp])
                P.op("dve", lambda e: e.tensor_tensor(out=pw2[:, 1:2], in0=xei, in1=ar_, op=ALU.mult), r=[R_S5["xim"], R_sp], w=[R_ptmp])
                P.op("dve", lambda e, sb=sb: e.scalar_tensor_tensor(out=carry[:, sb, 0:1], in0=xer, scalar=ar_, in1=pw2[:, 0:1], op0=ALU.mult, op1=ALU.subtract),
                     r=[R_S5["xre"], R_sp, R_ptmp], w=[R_carry])
                P.op("dve", lambda e, sb=sb: e.scalar_tensor_tensor(out=carry[:, sb, 1:2], in0=xer, scalar=ai_, in1=pw2[:, 1:2], op0=ALU.mult, op1=ALU.add),
                     r=[R_S5["xre"], R_sp, R_ptmp], w=[R_carry])
                P.op("pe", lambda e, sb=sb: e.matmul(out=pY[0:64, 0:nn], lhsT=CT[:, sb, 0, :], rhs=A("xre"), start=(sb == 0), stop=False), r=[R_CT, R_S5["xre"]], w=[R_pY])
                P.op("pe", lambda e, sb=sb: e.matmul(out=pY[0:64, 0:nn], lhsT=CT[:, sb, 1, :], rhs=A("xim"), start=False, stop=(sb == 1)), r=[R_CT, R_S5["xim"]], w=[R_pY])
            P.op("dve", lambda e: e.scalar_tensor_tensor(out=ysb[:, 0:ncol], in0=uT[:, 0:ncol], scalar=dsk[:, 0:1], in1=pY[0:64, 0:ncol], op0=ALU.mult, op1=ALU.add),
                 r=[R_uT, R_dsk, R_pY], w=[R_ysb])
            P.dma("pool", ys_d[:, cs], ysb[:, 0:ncol], r=[R_ysb])

        P.barrier()
        esA.close()
        P.es = es
        onesf = P.sb("onesfB", [128, 1], F32)
        R_onesf = Res("onesfB")
        P.op("pool", lambda e: e.memset(onesf[:, :], 1.0), w=[R_onesf])
        Tincl = P.sb("Tincl", [128, 128], BF16)
        R_Ti = Res("Tincl")
        P.op("pool", lambda e: e.memset(Tincl[:, :], 1.0), w=[R_Ti])
        P.op("pool", lambda e: e.affine_select(out=Tincl[:, :], in_=Tincl[:, :], pattern=[[-1, 128]], compare_op=ALU.is_ge, fill=0.0, base=0, channel_multiplier=1),
             r=[R_Ti], w=[R_Ti])
        onesb512 = P.sb("onesb512", [128, 512], BF16)
        R_o512 = Res("o512")
        P.op("pool", lambda e: e.memset(onesb512[:, :], 1.0), w=[R_o512])
        masks = P.sb("masks", [128, 4, 512], BF16)
        R_masks = Res("masks")
        for i in range(4):
            P.op("pool", lambda e, i=i: e.affine_select(out=masks[:, i, :], in_=onesb512[:, :], pattern=[[1, 512]], compare_op=ALU.is_gt, fill=0.0,
                                                        base=-128 * i, channel_multiplier=-1), r=[R_o512], w=[R_masks])
        padcol = P.sb("padcol", [128, 1], F32)
        R_pad = Res("padcol")
        P.op("pool", lambda e: e.affine_select(out=padcol[:, :], in_=onesf[:, 0:1], pattern=[[0, 1]], compare_op=ALU.is_ge, fill=0.0, base=-PADF, channel_multiplier=1),
             r=[R_onesf], w=[R_pad])
        NB = 3
        eb = [P.sb(f"eb{i}", [128, 512], F32) for i in range(NB)]
        R_eb = [Res(f"eb{i}") for i in range(NB)]
        spb = [P.sb(f"spb{i}", [128, 512], BF16) for i in range(NB)]
        R_spb = [Res(f"spb{i}") for i in range(NB)]
        wb_ = [P.sb("wbb0", [128, 512], BF16), P.sb("wbb1", [128, 512], BF16)]
        R_wb = [Res("wbb0"), Res("wbb1")]
        S16 = [P.sb(f"S16_{i}", [128, 512], BF16) for i in range(4)]
        R_S16 = [Res(f"S16_{i}") for i in range(4)]
        osb_sb = P.sb("osb_sb", [64, 512], F32)
        R_osb = Res("osb_sb")
        pz = [banks[0], banks[1], banks[5]]
        R_pz = [R_bk[0], R_bk[1], R_bk[5]]
        pc = [banks[2], banks[3]]
        R_pc = [R_bk[2], R_bk[3]]
        po = banks[4]
        R_po = R_bk[4]
        iters = []
        for Q in range(NG):
            jmax = min(4 * Q + 3, NT - 1)
            for j in range(jmax, -1, -1):
                iters.append((Q, j, jmax))

        def maskop(buf, Rbuf, Q, j, nq):
            diag = j >= 4 * Q
            if not (diag or j == 0):
                return
            mk = masks[:, j - 4 * Q, 0:nq] if diag else onesb512[:, 0:nq]
            if j == 0:
                P.op("dve", lambda e: e.scalar_tensor_tensor(out=buf[:, 0:nq], in0=buf[:, 0:nq], scalar=padcol[:, 0:1], in1=mk,
                                                             op0=ALU.mult, op1=ALU.mult), r=[Rbuf, R_pad, R_masks, R_o512], w=[Rbuf])
            else:
                P.op("dve", lambda e: e.tensor_tensor(out=buf[:, 0:nq], in0=buf[:, 0:nq], in1=mk, op=ALU.mult), r=[Rbuf, R_masks], w=[Rbuf])

        def stageA(i):
            Q, j, jmax = iters[i]
            q0, nq = grp_cols(Q)
            qs = slice(q0, q0 + nq)
            ks = slice(j * 128, (j + 1) * 128)
            b = i % NB
            k = jmax - j
            P.op("pe", lambda e: e.matmul(out=pz[b][:, 0:nq], lhsT=KZ[:, ks], rhs=QQ[:, qs], start=True, stop=True),
                 r=[R_kT[j // 4], R_qT[Q]], w=[R_pz[b]])
            P.op("act", lambda e: e.activation(out=eb[b][:, 0:nq], in_=pz[b][:, 0:nq], func=AF.Exp), r=[R_pz[b]], w=[R_eb[b]])
            P.op("act", lambda e: e.activation(out=spb[b][:, 0:nq], in_=eb[b][:, 0:nq], func=AF.Ln, bias=1.0), r=[R_eb[b]], w=[R_spb[b]])
            maskop(spb[b], R_spb[b], Q, j, nq)
            if j > 0:
                if k == 0:
                    P.op("pool", lambda e: e.tensor_copy(out=S16[(k + 1) % 4][:, 0:nq], in_=spb[b][:, 0:nq]), r=[R_spb[b]], w=[R_S16[(k + 1) % 4]])
                else:
                    P.op("pool", lambda e: e.tensor_tensor(out=S16[(k + 1) % 4][:, 0:nq], in0=S16[k % 4][:, 0:nq], in1=spb[b][:, 0:nq], op=ALU.add),
                         r=[R_S16[k % 4], R_spb[b]], w=[R_S16[(k + 1) % 4]])

        def stageB1(i):
            Q, j, jmax = iters[i]
            q0, nq = grp_cols(Q)
            qs = slice(q0, q0 + nq)
            ks = slice(j * 128, (j + 1) * 128)
            b = i % NB
            c = i % 2
            k = jmax - j
            P.op("pe", lambda e: e.matmul(out=pc[c][:, 0:nq], lhsT=Tincl[:, :], rhs=spb[b][:, 0:nq], start=True, stop=False),
                 r=[R_Ti, R_spb[b]], w=[R_pc[c]])
            if k > 0:
                P.op("pe", lambda e: e.matmul(out=pc[c][:, 0:nq], lhsT=C.ones_bf[:, :], rhs=S16[k % 4][:, 0:nq], start=False, stop=False),
                     r=[C.R_ones, R_S16[k % 4]], w=[R_pc[c]])
            P.op("pe", lambda e: e.matmul(out=pc[c][:, 0:nq], lhsT=KN[:, ks], rhs=QQ[:, qs], start=False, stop=True),
                 r=[R_kT[j // 4], R_qT[Q]], w=[R_pc[c]])
            P.op("act", lambda e: e.activation(out=wb_[c][:, 0:nq], in_=pc[c][:, 0:nq], func=AF.Exp, scale=-1.0), r=[R_pc[c]], w=[R_wb[c]])
            maskop(wb_[c], R_wb[c], Q, j, nq)

        def stageB2(i):
            Q, j, jmax = iters[i]
            q0, nq = grp_cols(Q)
            qs = slice(q0, q0 + nq)
            c = i % 2
            P.op("pe", lambda e: e.matmul(out=po[:, 0:nq], lhsT=vA[:, j, :], rhs=wb_[c][:, 0:nq], start=(j == jmax), stop=(j == 0)),
                 r=[R_v[j // 4], R_wb[c]], w=[R_po])
            if j == 0:
                P.op("dve", lambda e: e.tensor_copy(out=osb_sb[:, 0:nq], in_=po[0:64, 0:nq]), r=[R_po], w=[R_osb])
                P.dma("pool", osb_d[:, qs], osb_sb[:, 0:nq], r=[R_osb])

        n_it = len(iters)
        LA = 2
        for i in range(min(LA, n_it)):
            stageA(i)
        for i in range(n_it):
            if i + LA < n_it:
                stageA(i + LA)
            stageB1(i)
            if i > 0:
                stageB2(i - 1)
        stageB2(n_it - 1)
        P.finish()
    return nc


def run_mix_even(hs_pad, I):
    nc = get_prog("mix_even", build_mix_even)
    w = I["w_in_even"][0]
    in_maps = []
    for h in range(NCORES):
        wq, wk = w[:, h * 64:(h + 1) * 64], w[:, 512 + h * 64:512 + (h + 1) * 64]
        wh = np.concatenate([wq, wq, wk, wk, w[:, 1024 + h * 64:1024 + (h + 1) * 64], w[:, 1536 + h * 64:1536 + (h + 1) * 64]], axis=1)
        gs = slice(4 * h, 4 * h + 4)
        in_maps.append({
            "hs": hs_pad, "wh": np.ascontiguousarray(wh), "g_pre": I["pre_mix_norm"][0],
            "lam_re": np.ascontiguousarray(I["s5_lambda_re"][0, gs]), "lam_im": np.ascontiguousarray(I["s5_lambda_im"][0, gs]),
            "log_dt": np.ascontiguousarray(I["s5_log_dt"][0, gs]),
            "b_re": np.ascontiguousarray(I["s5_b_re"][0, gs]), "b_im": np.ascontiguousarray(I["s5_b_im"][0, gs]),
            "c_re": np.ascontiguousarray(I["s5_c_re"][0, gs]), "c_im": np.ascontiguousarray(I["s5_c_im"][0, gs]),
            "d_skip": np.ascontiguousarray(I["s5_d"][0, h * 64:(h + 1) * 64]),
        })
    res = run_bass_kernel_spmd(nc, in_maps, core_ids=list(range(NCORES)))
    osb = np.concatenate([r["osbT"].T for r in res.results], axis=1)
    ys = np.concatenate([r["ysT"].T for r in res.results], axis=1)
    return np.ascontiguousarray(osb), np.ascontiguousarray(ys)


def build_mix_odd(ng_limit=NG):
    nc = bass.Bass("TRN2", target_bir_lowering=False)
    dr = lambda n, s, k="ExternalInput": nc.dram_tensor(n, list(s), F32, kind=k).ap()
    hs_d = dr("hs", [LP, D])
    wh_d = dr("wh", [D, 516])
    g_d = dr("g_pre", [D])
    cw_d = dr("convw", [128, 12])
    alog_d = dr("a_log", [1])
    dtb_d = dr("dt_bias", [1])
    gdn_d = dr("g_dn", [128])
    og_d = dr("og", [LP, 128], "ExternalOutput")

    with ExitStack() as es:
        C = Ctx(nc, es)
        P = C.P
        whb = P.sb("whb", [128, 8, 516], BF16)
        R_wh = Res("wh")
        gsm = P.sb("gsm", [128, 8], F32)
        R_gsm = Res("gsm")
        stage = [P.sb("stage0", [128, 1024], F32), P.sb("stage1", [128, 1024], F32)]
        R_stage = [Res("st0"), Res("st1")]
        bk = [P.ps(f"bk{i}", [128, 512], F32) for i in range(7)]
        R_bk = [Res(f"bk{i}", True) for i in range(7)]
        pT = P.ps("pT", [128, 8, 128], BF16)
        R_pT = Res("pT", True)
        P.dma("sp", gsm[:, :], g_d.rearrange("(k p) -> p k", p=128), w=[R_gsm], allow_slow_non_contiguous=True)
        C.load_weight(wh_d, whb, R_wh, stage, R_stage, 8, 516, gsc=gsm, R_g=R_gsm)
        cst = P.sb("cst", [128, 16], F32)
        R_cst = Res("cst")
        P.dma("sp", cst[:, 0:12], cw_d, w=[R_cst])
        P.dma("sp", cst[:, 12:13], dtb_d.rearrange("(o n) -> o n", o=1).partition_broadcast(128), w=[R_cst])
        P.dma("sp", cst[:, 13:14], alog_d.rearrange("(o n) -> o n", o=1).partition_broadcast(128), w=[R_cst])
        P.op("act", lambda e: e.activation(out=cst[:, 13:14], in_=cst[:, 13:14], func=AF.Exp), r=[R_cst], w=[R_cst])
        P.op("dve", lambda e: e.tensor_scalar(out=cst[:, 13:14], in0=cst[:, 13:14], scalar1=-1.0, scalar2=None, op0=ALU.mult), r=[R_cst], w=[R_cst])
        Gdn = P.sb("Gdn", [128, 128], F32)
        R_Gdn = Res("Gdn")
        P.dma("sp", Gdn[:, :], gdn_d.partition_broadcast(128), w=[R_Gdn])
        onesf = P.sb("onesf", [128, 128], F32)
        R_onesf = Res("onesf")
        P.op("pool", lambda e: e.memset(onesf[:, :], 1.0), w=[R_onesf])
        identf = P.sb("identf", [128, 128], F32)
        R_if = Res("identf")
        P.op("dve", lambda e: e.tensor_copy(out=identf[:, :], in_=C.ident[:, :]), r=[C.R_ident], w=[R_if])
        maskS = P.sb("maskS", [128, 128], F32)
        maskST = P.sb("maskST", [128, 128], F32)
        TriX = P.sb("TriX", [128, 130], F32)
        R_mk = Res("masks")
        P.op("pool", lambda e: e.affine_select(out=maskS[:, :], in_=onesf[:, :], pattern=[[-1, 128]], compare_op=ALU.is_gt, fill=0.0, base=0, channel_multiplier=1),
             r=[R_onesf], w=[R_mk])
        P.op("pool", lambda e: e.memset(maskS[64:128, 0:64], 0.0), r=[R_mk], w=[R_mk])
        P.op("pool", lambda e: e.affine_select(out=maskST[:, :], in_=onesf[:, :], pattern=[[1, 128]], compare_op=ALU.is_gt, fill=0.0, base=0, channel_multiplier=-1),
             r=[R_onesf], w=[R_mk])
        P.op("pool", lambda e: e.memset(maskST[0:64, 64:128], 0.0), r=[R_mk], w=[R_mk])
        P.op("pool", lambda e: e.affine_select(out=TriX[:, 0:128], in_=onesf[:, :], pattern=[[1, 128]], compare_op=ALU.is_ge, fill=0.0, base=0, channel_multiplier=-1),
             r=[R_onesf], w=[R_mk])
        P.op("pool", lambda e: e.memset(TriX[0:64, 64:128], 0.0), r=[R_mk], w=[R_mk])
        P.op("pool", lambda e: e.memset(TriX[:, 128:130], 0.0), r=[R_mk], w=[R_mk])
        P.op("pool", lambda e: e.memset(TriX[0:64, 128:129], 1.0), r=[R_mk], w=[R_mk])
        P.op("pool", lambda e: e.memset(TriX[64:128, 129:130], 1.0), r=[R_mk], w=[R_mk])
        maskIT = TriX
        mblk = P.sb("mblk", [128, 3, 128], F32)
        for bi, bs in enumerate((16, 32, 64)):
            nb_ = 128 // bs
            P.op("pool", lambda e: e.affine_select(out=mblk[:, bi, :], in_=onesf[:, :], pattern=[[-bs, nb_], [0, bs]], compare_op=ALU.is_ge, fill=0.0,
                                                   base=0, channel_multiplier=1), r=[R_onesf, R_mk], w=[R_mk])
            P.op("pool", lambda e: e.affine_select(out=mblk[:, bi, :], in_=mblk[:, bi, :], pattern=[[bs, nb_], [0, bs]], compare_op=ALU.is_ge, fill=0.0,
                                                   base=bs - 1, channel_multiplier=-1), r=[R_mk], w=[R_mk])
        P.op("pool", lambda e: e.tensor_tensor(out=mblk[:, 2, :], in0=mblk[:, 2, :], in1=mblk[:, 1, :], op=ALU.subtract), r=[R_mk], w=[R_mk])
        P.op("pool", lambda e: e.tensor_tensor(out=mblk[:, 1, :], in0=mblk[:, 1, :], in1=mblk[:, 0, :], op=ALU.subtract), r=[R_mk], w=[R_mk])

        hsb = [P.sb("hsb0", [128, D], F32), P.sb("hsb1", [128, D], F32)]
        R_hsb = [Res("hsb0"), Res("hsb1")]
        xn = P.sb("xn", [128, D], BF16)
        R_xn = Res("xn")
        xnT = P.sb("xnT", [128, 8, 512], BF16)
        R_xnT = Res("xnT")
        stt = P.sb("stt", [128, 4], F32)
        R_stt = Res("stt")
        raw = P.sb("raw", [128, 3, 515], F32)
        R_raw = [Res("rawq"), Res("rawk"), Res("rawv")]
        P.op("pool", lambda e: e.memset(raw[:, :, :].rearrange("p a b -> p (a b)"), 0.0), w=R_raw)
        acc = P.sb("acc", [128, 512], F32)
        R_acc = Res("acc")
        sil = P.sb("sil", [128, 3, 512], F32)
        R_sil = [Res("silq"), Res("silk"), Res("silv")]
        sq = P.sb("sq", [128, 512], F32)
        R_sq = Res("sq")
        rn = P.sb("rn", [128, 512], F32)
        R_rn = Res("rn")
        qnT = P.sb("qnT", [128, 512], BF16)
        knT2 = [P.sb("knT0", [128, 512], BF16), P.sb("knT1", [128, 512], BF16)]
        R_knT2 = [Res("knT0"), Res("knT1")]
        kbT = P.sb("kbT", [128, 512], BF16)
        vT = P.sb("vT", [128, 512], BF16)
        R_qnT, R_kbT, R_vT = Res("qnT"), Res("kbT"), Res("vT")
        brow = P.sb("brow", [1, 512], BF16)
        R_brow = Res("brow")
        NS = 8
        gz = P.sb("gz", [128, NS, 128], F32)
        R_gzs = [Res(f"gz{i}") for i in range(NS)]
        cols = P.sb("cols", [128, NS, 8], F32)
        R_cols = [Res(f"cols{i}") for i in range(NS)]
        egl = P.sb("egl", [128, NS, 2], F32)
        R_egls = [Res(f"egl{i}") for i in range(NS)]
        Gbc = P.sb("Gbc", [128, NS, 128], F32)
        R_Gbcs = [Res(f"Gbc{i}") for i in range(NS)]
        ktl = P.sb("ktl", [128, NS, 2, 128], BF16)
        ind = P.sb("ind", [128, 2], F32)
        R_ind = Res("ind")
        P.op("pool", lambda e: e.memset(ind[:, :], 0.0), w=[R_ind])
        P.op("pool", lambda e: e.memset(ind[0:64, 0:1], 1.0), r=[R_ind], w=[R_ind])
        P.op("pool", lambda e: e.memset(ind[64:128, 1:2], 1.0), r=[R_ind], w=[R_ind])
        osb_ = P.sb("o_sb", [128, 128], F32)
        R_osb_ = Res("o_sb")
        bv = P.sb("bv", [128, NS, 128], BF16)
        ktok = P.sb("ktok", [128, NS, 128], BF16)
        nUT = P.sb("nUT", [128, NS, 128], BF16)
        W1n = P.sb("W1n", [128, NS, 128], BF16)
        V1s = P.sb("V1s", [128, NS, 128], BF16)
        nDT = P.sb("nDT", [128, NS, 2, 128], BF16)
        PTs = P.sb("PTs", [128, NS, 128], BF16)
        Rsb = P.sb("Rsb", [128, NS, 128], F32)
        Nsb = P.sb("Nsb", [128, NS, 2, 128], F32)
        R_x = {n: [Res(f"{n}{i}") for i in range(NS)] for n in ("ktok", "nUT", "W1n", "V1s", "nDT", "PTs", "Rsb", "Nsb")}
        Tt = P.sb("Tt", [128, 128], F32)
        R_Tt = Res("Tt")
        R_ktls, R_bvs = [Res(f"ktl{i}") for i in range(NS)], [Res(f"bv{i}") for i in range(NS)]
        egb = P.sb("egb", [128, NS, 128], F32)
        R_egbs = [Res(f"egb{i}") for i in range(NS)]
        qg = P.sb("qg", [128, NS, 128], BF16)
        R_qgs = [Res(f"qg{i}") for i in range(NS)]
        Dm_ = P.sb("Dm", [128, NS, 128], F32)
        DTm_ = P.sb("DTm", [128, NS, 128], F32)
        DTI_ = P.sb("DTI", [128, NS, 128], F32)
        R_Dms, R_DTms, R_DTIs = ([Res(f"{n}{i}") for i in range(NS)] for n in ("Dm", "DTm", "DTI"))
        NW = 14
        Wk_ = P.sb("Wk", [128, 4, NW, 128], F32)
        R_Wks = [[Res(f"Wk{s_}_{i}") for i in range(NW)] for s_ in range(4)]
        ATb_ = P.sb("ATb", [128, NS, 128], BF16)
        R_ATbs = [Res(f"ATb{i}") for i in range(NS)]
        aiT_ = P.sb("aiT", [128, NS, 128], BF16)
        R_aiTs = [Res(f"aiT{i}") for i in range(NS)]
        rt = P.sb("rt", [128, 128], BF16)
        vn = P.sb("vn", [128, 128], BF16)
        R_rt, R_vn = Res("rt"), Res("vn")
        S32 = P.sb("S32", [128, 128], F32)
        Sbf = P.sb("Sbf", [128, 128], BF16)
        R_S32, R_Sbf = Res("S32"), Res("Sbf")
        P.op("pool", lambda e: e.memset(S32[:, :], 0.0), w=[R_S32])
        P.op("pool", lambda e: e.memset(Sbf[:, :], 0.0), w=[R_Sbf])
        ogs = P.sb("ogs", [128, 128], F32)
        R_ogs = Res("ogs")
        junk = stage[0]
        R_junk = R_stage[0]
        pQKV = [bk[0], bk[1], bk[2]]
        R_pQKV = [R_bk[0], R_bk[1], R_bk[2]]

        tile_i = 0
        pending_scan = None
        R_stt2 = Res("stt2")
        for gi in range(ng_limit):
            c0, ncol = grp_cols(gi)
            ntl = ncol // 128
            N = ncol
            for tl in range(ntl):
                t = gi * 4 + tl
                hb, Rh = hsb[tile_i % 2], R_hsb[tile_i % 2]
                tile_i += 1
                tc_ = slice(tl * 128, (tl + 1) * 128)
                P.dma("sp", hb[:, :], hs_d[t * 128:(t + 1) * 128, :], w=[Rh])
                P.op("act", lambda e: e.activation(out=junk[:, :], in_=hb[:, :], func=AF.Square, accum_out=stt[:, 0:1]), r=[Rh], w=[R_junk, R_stt])
                C.rstd(stt[:, 0:1], stt[:, 0:1], R_stt, R_stt, D)
                P.op("act", lambda e: e.activation(out=xn[:, :], in_=hb[:, :], func=AF.Copy, scale=stt[:, 0:1]), r=[Rh, R_stt], w=[R_xn])
                for k in range(8):
                    P.op("pe", lambda e: e.transpose(out=pT[:, k, :], in_=xn[:, k * 128:(k + 1) * 128], identity=C.ident[:, :]), r=[R_xn, C.R_ident], w=[R_pT])
                P.op("dve", lambda e: e.tensor_copy(out=xnT[:, :, tc_], in_=pT[:, :, :]), r=[R_pT], w=[R_xnT])
                for k in range(8):
                    P.op("pe", lambda e: e.matmul(out=bk[5][:, 0:132], lhsT=xnT[:, k, tc_], rhs=whb[:, k, 384:516], start=(k == 0), stop=(k == 7)),
                         r=[R_xnT, R_wh], w=[R_bk[5]])
                sl_ = (gi % 2) * 4 + tl
                P.op("act", lambda e: e.activation(out=gz[:, sl_, :], in_=bk[5][:, 0:128], func=AF.Silu), r=[R_bk[5]], w=[R_gzs[sl_]])
                P.op("pool", lambda e: e.tensor_tensor(out=gz[:, sl_, :], in0=gz[:, sl_, :], in1=Gdn[:, :], op=ALU.mult), r=[R_gzs[sl_], R_Gdn], w=[R_gzs[sl_]])
                cc = lambda i: cols[:, sl_, i:i + 1]
                Rc = R_cols[sl_]
                P.op("act", lambda e: e.activation(out=cc(1), in_=bk[5][:, 130:131], func=AF.Sigmoid), r=[R_bk[5]], w=[Rc])
                P.op("dve", lambda e: e.tensor_tensor(out=cc(7), in0=bk[5][:, 128:129], in1=cst[:, 12:13], op=ALU.add), r=[R_bk[5], R_cst], w=[Rc])
            s0_ = (gi % 2) * 4
            c4 = lambda i: cols[:, s0_:s0_ + ntl, i]
            Rcs = R_cols[s0_:s0_ + ntl]
            g4 = lambda fn: P.op("dve", fn, r=Rcs + [R_cst], w=Rcs)
            g4(lambda e: e.tensor_scalar(out=c4(0), in0=c4(7), scalar1=-1.0, scalar2=None, op0=ALU.mult))
            g4(lambda e: e.tensor_tensor(out=c4(0), in0=c4(0), in1=c4(7), op=ALU.max))
            P.op("act", lambda e: e.activation(out=c4(0), in_=c4(0), func=AF.Exp, scale=-1.0), r=Rcs, w=Rcs)
            g4(lambda e: e.tensor_scalar(out=c4(6), in0=c4(0), scalar1=2.0, scalar2=None, op0=ALU.add))
            g4(lambda e: e.reciprocal(out=c4(6), in_=c4(6)))
            g4(lambda e: e.tensor_tensor(out=c4(6), in0=c4(6), in1=c4(0), op=ALU.mult))
            g4(lambda e: e.tensor_tensor(out=c4(0), in0=c4(6), in1=c4(6), op=ALU.mult))
            g4(lambda e: e.tensor_scalar(out=c4(5), in0=c4(0), scalar1=1.0 / 15, scalar2=1.0 / 13, op0=ALU.mult, op1=ALU.add))
            for cf in (1.0 / 11, 1.0 / 9, 1.0 / 7, 1.0 / 5, 1.0 / 3, 1.0):
                g4(lambda e: e.tensor_tensor(out=c4(5), in0=c4(5), in1=c4(0), op=ALU.mult))
                g4(lambda e: e.tensor_scalar(out=c4(5), in0=c4(5), scalar1=cf, scalar2=None, op0=ALU.add))
            g4(lambda e: e.tensor_tensor(out=c4(5), in0=c4(5), in1=c4(6), op=ALU.mult))
            g4(lambda e: e.tensor_scalar(out=c4(7), in0=c4(7), scalar1=0.0, scalar2=None, op0=ALU.max))
            g4(lambda e: e.scalar_tensor_tensor(out=c4(0), in0=c4(5), scalar=2.0, in1=c4(7), op0=ALU.mult, op1=ALU.add))
            g4(lambda e: e.tensor_scalar(out=c4(0), in0=c4(0), scalar1=cst[:, 13:14], scalar2=None, op0=ALU.mult))
            for wi in range(3):
                for k in range(8):
                    P.op("pe", lambda e: e.matmul(out=pQKV[wi][:, 0:N], lhsT=whb[:, k, wi * 128:(wi + 1) * 128], rhs=xnT[:, k, 0:N], start=(k == 0), stop=(k == 7)),
                         r=[R_xnT, R_wh], w=[R_pQKV[wi]])
            for k in range(8):
                P.op("pe", lambda e: e.matmul(out=bk[3][0:1, 0:N], lhsT=whb[:, k, 514:515], rhs=xnT[:, k, 0:N], start=(k == 0), stop=(k == 7)),
                     r=[R_xnT, R_wh], w=[R_bk[3]])
            P.op("act", lambda e: e.activation(out=brow[:, 0:N], in_=bk[3][0:1, 0:N], func=AF.Sigmoid), r=[R_bk[3]], w=[R_brow])
            P.op("pe", lambda e: e.matmul(out=bk[4][:, 0:N], lhsT=C.ones_bf[0:1, :], rhs=brow[0:1, 0:N], start=True, stop=True), r=[C.R_ones, R_brow], w=[R_bk[4]])
            for wi in range(3):
                P.op("act", lambda e: e.activation(out=raw[:, wi, 3:3 + N], in_=pQKV[wi][:, 0:N], func=AF.Copy), r=[R_pQKV[wi]], w=[R_raw[wi]])
                eng = "dve" if wi != 1 else "pool"
                P.op(eng, lambda e: e.tensor_scalar(out=acc[:, 0:N], in0=raw[:, wi, 0:N], scalar1=cst[:, wi * 4:wi * 4 + 1], scalar2=None, op0=ALU.mult),
                     r=[R_raw[wi], R_cst], w=[R_acc])
                for j in range(1, 4):
                    P.op("dve", lambda e: e.scalar_tensor_tensor(out=acc[:, 0:N], in0=raw[:, wi, j:j + N], scalar=cst[:, wi * 4 + j:wi * 4 + j + 1], in1=acc[:, 0:N],
                                                                 op0=ALU.mult, op1=ALU.add), r=[R_raw[wi], R_cst, R_acc], w=[R_acc])
                P.op("act", lambda e: e.activation(out=sil[:, wi, 0:N], in_=acc[:, 0:N], func=AF.Silu), r=[R_acc], w=[R_sil[wi]])
                P.op("pool", lambda e: e.tensor_copy(out=raw[:, wi, 0:3], in_=raw[:, wi, N:N + 3]), r=[R_raw[wi]], w=[R_raw[wi]])
            knT = knT2[gi % 2]
            R_knT = R_knT2[gi % 2]
            for wi, (dst, Rd, sc) in enumerate(((qnT, R_qnT, 128 ** -0.5), (knT, R_knT, 1.0))):
                P.op("act", lambda e: e.activation(out=sq[:, 0:N], in_=sil[:, wi, 0:N], func=AF.Square), r=[R_sil[wi]], w=[R_sq])
                P.op("pe", lambda e: e.matmul(out=bk[6][:, 0:N], lhsT=onesf[:, :], rhs=sq[:, 0:N], start=True, stop=True), r=[R_onesf, R_sq], w=[R_bk[6]])
                P.op("dve", lambda e: e.tensor_scalar(out=rn[:, 0:N], in0=bk[6][:, 0:N], scalar1=EPS, scalar2=None, op0=ALU.add), r=[R_bk[6]], w=[R_rn])
                P.op("act", lambda e: e.activation(out=rn[:, 0:N], in_=rn[:, 0:N], func=AF.Sqrt), r=[R_rn], w=[R_rn])
                P.op("dve", lambda e: e.reciprocal(out=rn[:, 0:N], in_=rn[:, 0:N]), r=[R_rn], w=[R_rn])
                P.op("dve", lambda e: e.scalar_tensor_tensor(out=dst[:, 0:N], in0=sil[:, wi, 0:N], scalar=sc, in1=rn[:, 0:N], op0=ALU.mult, op1=ALU.mult),
                     r=[R_sil[wi], R_rn], w=[Rd])
            P.op("dve", lambda e: e.tensor_tensor(out=kbT[:, 0:N], in0=knT[:, 0:N], in1=bk[4][:, 0:N], op=ALU.mult), r=[R_knT, R_bk[4]], w=[R_kbT])
            P.op("pool", lambda e: e.tensor_copy(out=vT[:, 0:N], in_=sil[:, 2, 0:N]), r=[R_sil[2]], w=[R_vT])

            def pre(tl, gi=gi, knT=knT, R_knT=R_knT):
                sl_ = (gi % 2) * 4 + tl
                tc_ = slice(tl * 128, (tl + 1) * 128)
                cc = lambda i: cols[:, sl_, i:i + 1]
                Rc = R_cols[sl_]
                bC, RC_ = bk[tl], R_bk[tl]
                bG, RG = bC, RC_
                bKK, RKK = bC, RC_
                bA, RA = bC[:, 0:256], RC_
                bB, RB = bC[:, 256:512], RC_
                pTk, pTv = pT[:, 2 * tl, :], pT[:, 2 * tl + 1, :]
                Gb, R_Gb = Gbc[:, sl_, :], R_Gbcs[sl_]
                Dm, DTm, DTI = Dm_[:, sl_, :], DTm_[:, sl_, :], DTI_[:, sl_, :]
                R_Dm, R_DTm, R_DTI = R_Dms[sl_], R_DTms[sl_], R_DTIs[sl_]
                R_Wk = R_Wks[tl]
                W = lambda i: Wk_[:, tl, i, :]
                P.op("dve", lambda e: e.tensor_scalar(out=Gb, in0=onesf[:, :], scalar1=cc(0), scalar2=None, op0=ALU.mult), r=[R_onesf, Rc], w=[R_Gb])
                P.op("pe", lambda e: e.matmul(out=bG[:, 0:130], lhsT=Gb, rhs=TriX[:, :], start=True, stop=True), r=[R_Gb, R_mk], w=[RG])
                P.op("pe", lambda e: e.matmul(out=bG[:, 256:258], lhsT=TriX[:, 0:128], rhs=cols[:, sl_, 0:2], start=True, stop=True), r=[R_mk, Rc], w=[RG])
                yield
                P.op("dve", lambda e: e.tensor_copy(out=cc(2), in_=bG[:, 256:257]), r=[RG], w=[Rc])
                P.op("dve", lambda e: e.tensor_scalar(out=cc(6), in0=bG[:, 256:257], scalar1=-1.0, scalar2=None, op0=ALU.mult), r=[RG], w=[Rc])
                P.op("dve", lambda e: e.tensor_copy(out=cols[0:64, sl_, 3:4], in_=bG[0:64, 128:129]), r=[RG], w=[Rc])
                P.op("dve", lambda e: e.tensor_copy(out=cols[64:128, sl_, 3:4], in_=bG[64:128, 129:130]), r=[RG], w=[Rc])
                P.op("dve", lambda e: e.tensor_tensor(out=cc(4), in0=cc(3), in1=cc(2), op=ALU.subtract), r=[Rc], w=[Rc])
                P.op("dve", lambda e: e.tensor_scalar(out=Dm, in0=bG[:, 0:128], scalar1=cc(2), scalar2=None, op0=ALU.subtract), r=[RG, Rc], w=[R_Dm])
                P.op("act", lambda e: e.activation(out=egl[:, sl_, :], in_=bG[:, 128:130], func=AF.Exp), r=[RG], w=[R_egls[sl_]])
                P.op("act", lambda e: e.activation(out=egb[:, sl_, :], in_=bG[:, 0:128], func=AF.Exp), r=[RG], w=[R_egbs[sl_]])
                yield
                P.op("act", lambda e: e.activation(out=cc(4), in_=cc(4), func=AF.Exp), r=[Rc], w=[Rc])
                P.op("act", lambda e: e.activation(out=cc(7), in_=cc(2), func=AF.Exp), r=[Rc], w=[Rc])
                P.op("dve", lambda e: e.tensor_scalar(out=DTm, in0=Dm, scalar1=0.0, scalar2=None, op0=ALU.min), r=[R_Dm], w=[R_DTm])
                P.op("dve", lambda e: e.tensor_scalar(out=Dm, in0=Dm, scalar1=0.0, scalar2=-1.0, op0=ALU.max, op1=ALU.mult), r=[R_Dm], w=[R_Dm])
                P.op("dve", lambda e: e.tensor_tensor(out=qg[:, sl_, :], in0=qnT[:, tc_], in1=egb[:, sl_, :], op=ALU.mult), r=[R_qnT, R_egbs[sl_]], w=[R_qgs[sl_]])
                P.op("pe", lambda e: e.transpose(out=pTk, in_=knT[:, tc_], identity=C.ident[:, :]), r=[R_knT, C.R_ident], w=[R_pT])
                P.op("pe", lambda e: e.transpose(out=pTv, in_=vT[:, tc_], identity=C.ident[:, :]), r=[R_vT, C.R_ident], w=[R_pT])
                P.op("pe", lambda e: e.matmul(out=bKK[:, 0:128], lhsT=kbT[:, tc_], rhs=knT[:, tc_], start=True, stop=True), r=[R_kbT, R_knT], w=[RKK])
                P.op("pe", lambda e: e.matmul(out=bKK[:, 128:256], lhsT=knT[:, tc_], rhs=kbT[:, tc_], start=True, stop=True), r=[R_kbT, R_knT], w=[RKK])
                P.op("pe", lambda e: e.matmul(out=bKK[:, 256:384], lhsT=knT[:, tc_], rhs=qnT[:, tc_], start=True, stop=True), r=[R_qnT, R_knT], w=[RKK])
                yield
                P.op("dve", lambda e: e.scalar_tensor_tensor(out=cc(5), in0=cc(7), scalar=-1.0, in1=cc(1), op0=ALU.mult, op1=ALU.mult), r=[Rc], w=[Rc])
                P.op("act", lambda e: e.activation(out=Dm, in_=Dm, func=AF.Exp), r=[R_Dm], w=[R_Dm])
                P.op("act", lambda e: e.activation(out=DTm, in_=DTm, func=AF.Exp), r=[R_DTm], w=[R_DTm])
                for ch in range(2):
                    P.op("dve", lambda e: e.tensor_scalar(out=ktl[:, sl_, ch, :], in0=pTk, scalar1=cc(4), scalar2=ind[:, ch:ch + 1], op0=ALU.mult, op1=ALU.mult),
                         r=[R_pT, Rc, R_ind], w=[R_ktls[sl_]])
                P.op("dve", lambda e: e.tensor_scalar(out=bv[:, sl_, :], in0=pTv, scalar1=cc(1), scalar2=None, op0=ALU.mult), r=[R_pT, Rc], w=[R_bvs[sl_]])
                P.op("dve", lambda e: e.tensor_copy(out=ktok[:, sl_, :], in_=pTk), r=[R_pT], w=[R_x["ktok"][sl_]])
                yield
                P.op("pool", lambda e: e.tensor_tensor(out=Dm, in0=Dm, in1=maskS[:, :], op=ALU.mult), r=[R_Dm, R_mk], w=[R_Dm])
                P.op("pool", lambda e: e.tensor_tensor(out=DTI, in0=DTm, in1=maskIT[:, 0:128], op=ALU.mult), r=[R_DTm, R_mk], w=[R_DTI])
                P.op("pool", lambda e: e.tensor_tensor(out=DTm, in0=DTm, in1=maskST[:, :], op=ALU.mult), r=[R_DTm, R_mk], w=[R_DTm])
                yield
                (LF, LTF, L_, LT_, O32, O32T, O64, O64T, X_, XT_, L2_, L2T_, Y_, Y2_) = range(NW)
                P.op("dve", lambda e: e.tensor_tensor(out=W(LF), in0=bKK[:, 0:128], in1=Dm, op=ALU.mult), r=[RKK, R_Dm], w=[R_Wk[LF]])
                P.op("dve", lambda e: e.tensor_tensor(out=W(LTF), in0=bKK[:, 128:256], in1=DTm, op=ALU.mult), r=[RKK, R_DTm], w=[R_Wk[LTF]])
                P.op("dve", lambda e: e.tensor_tensor(out=aiT_[:, sl_, :], in0=bKK[:, 256:384], in1=DTI, op=ALU.mult), r=[RKK, R_DTI], w=[R_aiTs[sl_]])
                yield
                for (dst, src, mi) in ((L_, LF, 0), (LT_, LTF, 0), (O32, LF, 1), (O32T, LTF, 1), (O64, LF, 2), (O64T, LTF, 2)):
                    P.op("pool", lambda e: e.tensor_tensor(out=W(dst), in0=W(src), in1=mblk[:, mi, :], op=ALU.mult), r=[R_Wk[src], R_mk], w=[R_Wk[dst]])
                P.op("pool", lambda e: e.tensor_tensor(out=W(X_), in0=identf[:, :], in1=W(L_), op=ALU.subtract), r=[R_if, R_Wk[L_]], w=[R_Wk[X_]])
                P.op("pool", lambda e: e.tensor_tensor(out=W(XT_), in0=identf[:, :], in1=W(LT_), op=ALU.subtract), r=[R_if, R_Wk[LT_]], w=[R_Wk[XT_]])
                yield

                def mm(out_ap, Rout, li, ri):
                    P.op("pe", lambda e: e.matmul(out=out_ap, lhsT=W(li), rhs=W(ri), start=True, stop=True), r=[R_Wk[li], R_Wk[ri]], w=[Rout])

                cl, clt, nl, nlt = L_, LT_, L2_, L2T_
                for lvl in range(3):
                    last = lvl == 2
                    mm(bA[:, 0:128], RA, clt, cl)
                    if not last:
                        mm(bA[:, 128:256], RA, cl, clt)
                    yield
                    P.op("act", lambda e: e.activation(out=W(nl), in_=bA[:, 0:128], func=AF.Copy), r=[RA], w=[R_Wk[nl]])
                    if not last:
                        P.op("act", lambda e: e.activation(out=W(nlt), in_=bA[:, 128:256], func=AF.Copy), r=[RA], w=[R_Wk[nlt]])
                    yield
                    mm(bB[:, 0:128], RB, nl, XT_)
                    mm(bB[:, 128:256], RB, XT_, nl)
                    yield
                    P.op("dve", lambda e: e.tensor_tensor(out=W(XT_), in0=bB[:, 0:128], in1=W(XT_), op=ALU.add), r=[RB, R_Wk[XT_]], w=[R_Wk[XT_]])
                    P.op("dve", lambda e: e.tensor_tensor(out=W(X_), in0=bB[:, 128:256], in1=W(X_), op=ALU.add), r=[RB, R_Wk[X_]], w=[R_Wk[X_]])
                    yield
                    cl, clt, nl, nlt = nl, nlt, cl, clt
                mm(bA[:, 0:128], RA, O32T, X_)
                mm(bA[:, 128:256], RA, O32, XT_)
                yield
                P.op("act", lambda e: e.activation(out=W(Y_), in_=bA[:, 0:128], func=AF.Copy), r=[RA], w=[R_Wk[Y_]])
                P.op("act", lambda e: e.activation(out=W(Y2_), in_=bA[:, 128:256], func=AF.Copy), r=[RA], w=[R_Wk[Y2_]])
                yield
                mm(bB[:, 0:128], RB, XT_, Y_)
                mm(bB[:, 128:256], RB, X_, Y2_)
                yield
                P.op("dve", lambda e: e.tensor_tensor(out=W(X_), in0=W(X_), in1=bB[:, 0:128], op=ALU.subtract), r=[RB, R_Wk[X_]], w=[R_Wk[X_]])
                P.op("dve", lambda e: e.tensor_tensor(out=W(XT_), in0=W(XT_), in1=bB[:, 128:256], op=ALU.subtract), r=[RB, R_Wk[XT_]], w=[R_Wk[XT_]])
                yield
                mm(bA[:, 0:128], RA, O64, XT_)
                yield
                P.op("act", lambda e: e.activation(out=W(Y2_), in_=bA[:, 0:128], func=AF.Copy), r=[RA], w=[R_Wk[Y2_]])
                yield
                mm(bB[:, 0:128], RB, X_, Y2_)
                yield
                P.op("dve", lambda e: e.tensor_tensor(out=ATb_[:, sl_, :], in0=W(XT_), in1=bB[:, 0:128], op=ALU.subtract), r=[RB, R_Wk[XT_]], w=[R_ATbs[sl_]])
                yield
                AT, R_AT = ATb_[:, sl_, :], R_ATbs[sl_]
                P.op("dve", lambda e: e.tensor_scalar(out=nUT[:, sl_, :], in0=AT, scalar1=cc(5), scalar2=None, op0=ALU.mult), r=[R_AT, Rc], w=[R_x["nUT"][sl_]])
                yield
                P.op("pe", lambda e: e.matmul(out=bA[:, 0:128], lhsT=nUT[:, sl_, :], rhs=ktok[:, sl_, :], start=True, stop=True),
                     r=[R_x["nUT"][sl_], R_x["ktok"][sl_]], w=[RA])
                P.op("pe", lambda e: e.matmul(out=bA[:, 128:256], lhsT=AT, rhs=bv[:, sl_, :], start=True, stop=True), r=[R_AT, R_bvs[sl_]], w=[RA])
                yield
                P.op("act", lambda e: e.activation(out=W1n[:, sl_, :], in_=bA[:, 0:128], func=AF.Copy), r=[RA], w=[R_x["W1n"][sl_]])
                P.op("dve", lambda e: e.tensor_copy(out=V1s[:, sl_, :], in_=bA[:, 128:256]), r=[RA], w=[R_x["V1s"][sl_]])
                yield
                for ch in range(2):
                    P.op("pe", lambda e: e.matmul(out=bB[:, ch * 128:(ch + 1) * 128], lhsT=W1n[:, sl_, :], rhs=ktl[:, sl_, ch, :], start=True, stop=True),
                         r=[R_x["W1n"][sl_], R_ktls[sl_]], w=[RB])
                P.op("pe", lambda e: e.matmul(out=bA[:, 0:128], lhsT=W1n[:, sl_, :], rhs=aiT_[:, sl_, :], start=True, stop=True),
                     r=[R_x["W1n"][sl_], R_aiTs[sl_]], w=[RA])
                P.op("pe", lambda e: e.matmul(out=bA[:, 128:256], lhsT=aiT_[:, sl_, :], rhs=V1s[:, sl_, :], start=True, stop=True),
                     r=[R_aiTs[sl_], R_x["V1s"][sl_]], w=[RA])
                yield
                P.op("act", lambda e: e.activation(out=nDT[:, sl_, :, :].rearrange("p a b -> p (a b)"), in_=bB[:, 0:256], func=AF.Copy), r=[RB], w=[R_x["nDT"][sl_]])
                P.op("dve", lambda e: e.tensor_tensor(out=PTs[:, sl_, :], in0=bA[:, 0:128], in1=qg[:, sl_, :], op=ALU.add), r=[RA, R_qgs[sl_]], w=[R_x["PTs"][sl_]])
                P.op("act", lambda e: e.activation(out=Rsb[:, sl_, :], in_=bA[:, 128:256], func=AF.Copy), r=[RA], w=[R_x["Rsb"][sl_]])
                yield
                for ch in range(2):
                    P.op("pe", lambda e: e.matmul(out=bB[:, ch * 128:(ch + 1) * 128], lhsT=ktl[:, sl_, ch, :], rhs=V1s[:, sl_, :], start=True, stop=True),
                         r=[R_ktls[sl_], R_x["V1s"][sl_]], w=[RB])
                yield
                P.op("dve", lambda e: e.tensor_copy(out=Nsb[:, sl_, :, :].rearrange("p a b -> p (a b)"), in_=bB[:, 0:256]), r=[RB], w=[R_x["Nsb"][sl_]])
                yield

            def scan(gi, ntl, knT, R_knT):
                pb, R_pb = bk[4], R_bk[4]
                for tl in range(ntl):
                    t = gi * 4 + tl
                    sl_ = (gi % 2) * 4 + tl
                    for ch in range(2):
                        P.op("dve", lambda e: e.scalar_tensor_tensor(out=Tt[:, :], in0=S32[:, :], scalar=egl[:, sl_, ch:ch + 1], in1=Nsb[:, sl_, ch, :], op0=ALU.mult, op1=ALU.add),
                             r=[R_S32, R_egls[sl_], R_x["Nsb"][sl_]], w=[R_Tt])
                        P.op("pe", lambda e: e.matmul(out=pb[:, ch * 128:(ch + 1) * 128], lhsT=nDT[:, sl_, ch, :], rhs=Sbf[:, :], start=True, stop=True),
                             r=[R_x["nDT"][sl_], R_Sbf], w=[R_pb])
                        P.op("pe", lambda e: e.matmul(out=pb[:, 256 + ch * 128:256 + (ch + 1) * 128], lhsT=PTs[:, sl_, :], rhs=Sbf[:, :], start=True, stop=True),
                             r=[R_x["PTs"][sl_], R_Sbf], w=[R_pb])
                        yield
                        P.op("dve", lambda e: e.tensor_tensor(out=Sbf[:, :], in0=pb[:, ch * 128:(ch + 1) * 128], in1=Tt[:, :], op=ALU.add), r=[R_pb, R_Tt], w=[R_Sbf])
                        P.op("dve", lambda e: e.tensor_tensor(out=S32[:, :], in0=pb[:, ch * 128:(ch + 1) * 128], in1=Tt[:, :], op=ALU.add), r=[R_pb, R_Tt], w=[R_S32])
                        yield
                    P.op("dve", lambda e: e.tensor_tensor(out=osb_[0:64, :], in0=pb[0:64, 256:384], in1=Rsb[0:64, sl_, :], op=ALU.add), r=[R_pb, R_x["Rsb"][sl_]], w=[R_osb_])
                    P.op("dve", lambda e: e.tensor_tensor(out=osb_[64:128, :], in0=pb[64:128, 384:512], in1=Rsb[64:128, sl_, :], op=ALU.add), r=[R_pb, R_x["Rsb"][sl_]], w=[R_osb_])
                    P.op("act", lambda e: e.activation(out=junk[:, 0:128], in_=osb_[:, :], func=AF.Square, accum_out=stt[:, 1:2]), r=[R_osb_], w=[R_junk, R_stt2])
                    C.rstd(stt[:, 1:2], stt[:, 1:2], R_stt2, R_stt2, 128)
                    P.op("dve", lambda e: e.scalar_tensor_tensor(out=ogs[:, :], in0=osb_[:, :], scalar=stt[:, 1:2], in1=gz[:, sl_, :], op0=ALU.mult, op1=ALU.mult),
                         r=[R_osb_, R_stt2, R_gzs[sl_]], w=[R_ogs])
                    P.dma("pool", og_d[t * 128:(t + 1) * 128, :], ogs[:, :], r=[R_ogs])
                    yield

            for pair0 in range(0, ntl, 4):
                gens = [pre(tl) for tl in range(pair0, min(pair0 + 4, ntl))]
                last_pair = pair0 + 4 >= ntl
                while gens:
                    for g_ in list(gens):
                        try:
                            next(g_)
                        except StopIteration:
                            gens.remove(g_)
                    if pending_scan is not None:
                        try:
                            next(pending_scan)
                        except StopIteration:
                            pending_scan = None
                if last_pair and pending_scan is not None:
                    for _ in pending_scan:
                        pass
                    pending_scan = None
            pending_scan = scan(gi, ntl, knT, R_knT)
        for _ in pending_scan:
            pass
        P.finish()
    return nc


def run_mix_odd(hs_pad, I):
    nc = get_prog("mix_odd", build_mix_odd)
    w = I["w_in_odd"][0]
    cwv = I["dn_conv_w"][0]
    in_maps = []
    for h in range(NCORES):
        sl = lambda base: slice(base + h * 128, base + (h + 1) * 128)
        zc = np.zeros((D, 1), np.float32)
        wh = np.concatenate([w[:, sl(0)], w[:, sl(1024)], w[:, sl(2048)], w[:, sl(3072)], w[:, 4096 + h:4097 + h], zc, w[:, 4104 + h:4105 + h], zc], axis=1)
        cw = np.concatenate([cwv[:, sl(0)].T, cwv[:, sl(1024)].T, cwv[:, sl(2048)].T], axis=1)
        in_maps.append({"hs": hs_pad, "wh": np.ascontiguousarray(wh), "g_pre": I["pre_mix_norm"][1], "convw": np.ascontiguousarray(cw),
                        "a_log": np.ascontiguousarray(I["dn_a_log"][0, h:h + 1]), "dt_bias": np.ascontiguousarray(I["dn_dt_bias"][0, h:h + 1]),
                        "g_dn": I["dn_out_norm"][0]})
    res = run_bass_kernel_spmd(nc, in_maps, core_ids=list(range(NCORES)))
    return np.ascontiguousarray(np.concatenate([r["og"] for r in res.results], axis=1))


def kernel(**inputs):
    I = {k: np.ascontiguousarray(np.asarray(v, dtype=np.float32)) for k, v in inputs.items()}
    x = I["x"][0]
    hs0 = np.ascontiguousarray(np.concatenate([np.zeros((PADF, D), np.float32), I["meta_tokens"], x], axis=0))
    osb, ys = run_mix_even(hs0, I)
    W0 = {"wglu": I["s5_w_glu"][0], "bglu": I["s5_b_glu"][0],
          "gmerge": np.ascontiguousarray(np.concatenate([I["sb_out_norm"][0], I["s5_out_norm"][0]])),
          "wout": I["w_out_even"][0], "w1": I["mlp_w1"][0], "w2": I["mlp_w2"][0],
          "g_postmix": I["post_mix_norm"][0], "g_premlp": I["pre_mlp_norm"][0], "g_postmlp": I["post_mlp_norm"][0]}
    hs1 = run_post("even", hs0, (osb, ys), W0)
    og = run_mix_odd(hs1, I)
    W1 = {"wout": I["w_out_odd"][0], "w1": I["mlp_w1"][1], "w2": I["mlp_w2"][1],
          "g_postmix": I["post_mix_norm"][1], "g_premlp": I["pre_mlp_norm"][1], "g_postmlp": I["post_mlp_norm"][1]}
    out = run_post("odd", hs1, og, W1)
    return np.ascontiguousarray(out[128:].reshape(1, 16384, D).astype(np.float32))
```
